# Optimizing a Trainium2 kernel written in Bass

```python
import jax, jax.numpy as jnp
from jax import lax
import numpy as np

D_MODEL = 1024
BATCH = 8
SEQ = 4096
DEPTH = 2

N_A_LAYERS = DEPTH // 2
N_B_LAYERS = DEPTH - N_A_LAYERS

A_KEY_DIM = 128
A_HEADS = D_MODEL // A_KEY_DIM
A_VAL_DIM = D_MODEL // A_HEADS
A_DK = A_HEADS * A_KEY_DIM
A_DV = A_HEADS * A_VAL_DIM
A_CHUNK = 64

B_HEAD_DIM = 128
B_HEADS = D_MODEL // B_HEAD_DIM
B_WIDTH = B_HEADS * B_HEAD_DIM
B_Q_BLOCK = 128

D_FF = ((8 * D_MODEL // 3 + 127) // 128) * 128
CONV_W = 3
EPS = 1e-6
NEG_INF = -1e30

kernel_name = "yoco_hgrn2_fox_adaln_convffn"


def rms_norm(x):
    xf = x.astype(jnp.float32)
    y = xf * lax.rsqrt(jnp.mean(xf * xf, axis=-1, keepdims=True) + EPS)
    return y.astype(x.dtype)


def modulate(x, shift, scale):
    return rms_norm(x) * (1 + scale[:, None, :]) + shift[:, None, :]


def hgrn2_mixer(h, w_in, lb, norm_g, w_out):
    bsz, seq, _ = h.shape
    n_chunks = seq // A_CHUNK
    proj = (h @ w_in).astype(jnp.float32)
    q, f, i, g = jnp.split(proj, [A_DK, 2 * A_DK, 2 * A_DK + A_DV], axis=-1)
    q = jax.nn.silu(q)
    fg = lb + (1 - lb) * jax.nn.sigmoid(f)
    log_f = jnp.log(fg)
    k = 1 - fg

    def to_chunks(t, d):
        return t.reshape(bsz, n_chunks, A_CHUNK, A_HEADS, d).transpose(0, 3, 1, 2, 4)

    q, k, log_f = to_chunks(q, A_KEY_DIM), to_chunks(k, A_KEY_DIM), to_chunks(log_f, A_KEY_DIM)
    v = to_chunks(i, A_VAL_DIM)
    b = jnp.cumsum(log_f, axis=3)
    b_mid = b[:, :, :, A_CHUNK // 2:A_CHUNK // 2 + 1, :]
    q_intra = q * jnp.exp(b - b_mid)
    k_intra = k * jnp.exp(b_mid - b)
    scores = jnp.einsum('bhnck,bhnsk->bhncs', q_intra, k_intra)
    causal = jnp.tril(jnp.ones((A_CHUNK, A_CHUNK), dtype=bool))
    scores = jnp.where(causal, scores, 0.0)
    o_intra = jnp.einsum('bhncs,bhnsv->bhncv', scores, v)
    b_last = b[:, :, :, -1:, :]
    q_inter = q * jnp.exp(b)
    k_state = k * jnp.exp(b_last - b)
    chunk_decay = jnp.exp(b_last[:, :, :, 0, :])

    def step(state, xs):
        qc, kc, vc, dc = xs
        o = jnp.einsum('bhck,bhkv->bhcv', qc, state)
        state = dc[..., None] * state + jnp.einsum('bhck,bhcv->bhkv', kc, vc)
        return state, o

    xs = (jnp.moveaxis(q_inter, 2, 0), jnp.moveaxis(k_state, 2, 0),
          jnp.moveaxis(v, 2, 0), jnp.moveaxis(chunk_decay, 2, 0))
    state0 = jnp.zeros((bsz, A_HEADS, A_KEY_DIM, A_VAL_DIM), jnp.float32)
    _, o_inter = lax.scan(step, state0, xs)
    o = o_intra + jnp.moveaxis(o_inter, 0, 2)
    o = o.transpose(0, 2, 3, 1, 4).reshape(bsz, seq, A_HEADS, A_VAL_DIM)
    gate = jax.nn.silu(g.reshape(bsz, seq, A_HEADS, A_VAL_DIM))
    o = rms_norm(o) * norm_g * gate
    return o.reshape(bsz, seq, A_DV).astype(h.dtype) @ w_out


def shared_kv(x, c, kv_ada_w, kv_ada_b, kv_w, kv_b_f, k_norm_g):
    bsz, seq, _ = x.shape
    shift, scale = jnp.split(jax.nn.silu(c) @ kv_ada_w + kv_ada_b, 2, axis=-1)
    h = modulate(x, shift, scale)
    proj = h @ kv_w
    k, v, f_logit = jnp.split(proj, [B_WIDTH, 2 * B_WIDTH], axis=-1)
    k = rms_norm(k.reshape(bsz, seq, B_HEADS, B_HEAD_DIM)) * k_norm_g
    k = k.transpose(0, 2, 1, 3)
    v = v.reshape(bsz, seq, B_HEADS, B_HEAD_DIM).transpose(0, 2, 1, 3)
    log_f = jax.nn.log_sigmoid((f_logit + kv_b_f).astype(jnp.float32))
    cum_log_f = jnp.cumsum(log_f.transpose(0, 2, 1), axis=-1)
    return k, v, cum_log_f


def fox_mixer(h, k, v, cum_log_f, w_q, q_norm_g, w_out):
    bsz, seq, _ = h.shape
    n_blocks = seq // B_Q_BLOCK
    q, og = jnp.split(h @ w_q, 2, axis=-1)
    q = rms_norm(q.reshape(bsz, seq, B_HEADS, B_HEAD_DIM)) * q_norm_g * (B_HEAD_DIM ** -0.5)
    q_blocks = q.reshape(bsz, n_blocks, B_Q_BLOCK, B_HEADS, B_HEAD_DIM).transpose(1, 0, 3, 2, 4)
    f_q = cum_log_f.reshape(bsz, B_HEADS, n_blocks, B_Q_BLOCK).transpose(2, 0, 1, 3)
    starts = jnp.arange(n_blocks) * B_Q_BLOCK
    key_pos = jnp.arange(seq)

    def attend_block(args):
        qb, fq, start = args
        q_pos = start + jnp.arange(B_Q_BLOCK)
        logits = jnp.einsum('bhqd,bhkd->bhqk', qb, k).astype(jnp.float32)
        logits = logits + (fq[..., :, None] - cum_log_f[:, :, None, :])
        logits = jnp.where(key_pos[None, :] <= q_pos[:, None], logits, NEG_INF)
        p = jax.nn.softmax(logits, axis=-1)
        return jnp.einsum('bhqk,bhkd->bhqd', p.astype(v.dtype), v)

    o = lax.map(attend_block, (q_blocks, f_q, starts))
    o = o.transpose(1, 0, 3, 2, 4).reshape(bsz, seq, B_WIDTH)
    o = o * jax.nn.sigmoid(og)
    return o @ w_out


def conv_glu_ffn(h, w_up, conv_w, conv_b, w_down):
    u = h @ w_up
    u = lax.conv_general_dilated(
        u, conv_w[:, None, :], window_strides=(1,), padding=[(CONV_W - 1, 0)],
        dimension_numbers=('NWC', 'WIO', 'NWC'), feature_group_count=2 * D_FF) + conv_b
    gate, val = jnp.split(u, 2, axis=-1)
    return (jax.nn.silu(gate) * val) @ w_down


def setup_inputs(seed: int = 0) -> dict:
    key = jax.random.key(seed)
    ks = jax.random.split(key, 20)
    f32 = jnp.float32
    D = D_MODEL

    def nrm(k, shape, scale):
        return jax.random.normal(k, shape, f32) * scale

    return {
        "x": nrm(ks[0], (BATCH, SEQ, D), 1.0),
        "c": nrm(ks[1], (BATCH, D), 1.0),
        "ada_w": nrm(ks[2], (DEPTH, D, 6 * D), 0.5 * D ** -0.5),
        "ada_b": nrm(ks[3], (DEPTH, 6 * D), 0.02),
        "a_w_in": nrm(ks[4], (N_A_LAYERS, D, 2 * A_DK + 2 * A_DV), D ** -0.5),
        "a_lb_logits": nrm(ks[5], (N_A_LAYERS + 1, A_DK), 0.1),
        "a_norm_g": 1.0 + nrm(ks[6], (N_A_LAYERS, A_VAL_DIM), 0.02),
        "a_w_out": nrm(ks[7], (N_A_LAYERS, A_DV, D), A_DV ** -0.5),
        "kv_ada_w": nrm(ks[8], (D, 2 * D), 0.5 * D ** -0.5),
        "kv_ada_b": nrm(ks[9], (2 * D,), 0.02),
        "kv_w": nrm(ks[10], (D, 2 * B_WIDTH + B_HEADS), D ** -0.5),
        "kv_b_f": 3.0 + nrm(ks[11], (B_HEADS,), 0.5),
        "k_norm_g": 1.0 + nrm(ks[12], (B_HEAD_DIM,), 0.02),
        "b_w_q": nrm(ks[13], (N_B_LAYERS, D, 2 * B_WIDTH), D ** -0.5),
        "q_norm_g": 1.0 + nrm(ks[14], (N_B_LAYERS, B_HEAD_DIM), 0.02),
        "b_w_out": nrm(ks[15], (N_B_LAYERS, B_WIDTH, D), B_WIDTH ** -0.5),
        "ffn_w_up": nrm(ks[16], (DEPTH, D, 2 * D_FF), D ** -0.5),
        "ffn_conv_w": nrm(ks[17], (DEPTH, CONV_W, 2 * D_FF), CONV_W ** -0.5),
        "ffn_conv_b": nrm(ks[18], (DEPTH, 2 * D_FF), 0.02),
        "ffn_w_down": nrm(ks[19], (DEPTH, D_FF, D), D_FF ** -0.5),
    }


def reference(x, c, ada_w, ada_b, a_w_in, a_lb_logits, a_norm_g, a_w_out,
              kv_ada_w, kv_ada_b, kv_w, kv_b_f, k_norm_g,
              b_w_q, q_norm_g, b_w_out,
              ffn_w_up, ffn_conv_w, ffn_conv_b, ffn_w_down):
    lb_all = jnp.cumsum(jax.nn.softmax(a_lb_logits.astype(jnp.float32), axis=0), axis=0)[:N_A_LAYERS]
    c_act = jax.nn.silu(c)
    k_sh = v_sh = cum_log_f = None
    for l in range(DEPTH):
        mod = c_act @ ada_w[l] + ada_b[l]
        sh1, sc1, g1, sh2, sc2, g2 = jnp.split(mod, 6, axis=-1)
        if l == N_A_LAYERS:
            k_sh, v_sh, cum_log_f = shared_kv(x, c, kv_ada_w, kv_ada_b, kv_w, kv_b_f, k_norm_g)
        h = modulate(x, sh1, sc1)
        if l < N_A_LAYERS:
            y = hgrn2_mixer(h, a_w_in[l], lb_all[l], a_norm_g[l], a_w_out[l])
        else:
            j = l - N_A_LAYERS
            y = fox_mixer(h, k_sh, v_sh, cum_log_f, b_w_q[j], q_norm_g[j], b_w_out[j])
        x = x + g1[:, None, :] * y
        h = modulate(x, sh2, sc2)
        x = x + g2[:, None, :] * conv_glu_ffn(h, ffn_w_up[l], ffn_conv_w[l], ffn_conv_b[l], ffn_w_down[l])
    return x
```

```python
import numpy as np
import concourse.bass as bass
import concourse.mybir as mybir
from concourse.bass_utils import run_bass_kernel_spmd

F32 = mybir.dt.float32
BF16 = mybir.dt.bfloat16
U8 = mybir.dt.uint8
AF = mybir.ActivationFunctionType
ALU = mybir.AluOpType
AX = mybir.AxisListType
DSZ = {F32: 4, BF16: 2, U8: 1}

CENG = ("pe", "act", "dve", "pool")
EPOCH = 12000
STRICT = True


class Buf:
    __slots__ = ("name", "w", "r")

    def __init__(self, name=""):
        self.name = name
        self.w = None
        self.r = {}


class Op:
    __slots__ = ("id", "eng", "fn", "deps", "dma", "seq", "sig", "cnt", "need", "clock", "dsem", "dval", "inc")


class Prog:
    def __init__(self, nc):
        self.nc = nc
        self.ops = []
        self.ndma = 0
        self.dma_last = {}
        self.NDMA_SEM = 40
        self.NHW = 24
        self.nsw = 0

    def op(self, eng, fn, reads=(), writes=(), dma=False):
        o = Op()
        o.id = len(self.ops)
        o.eng = eng
        o.fn = fn
        o.dma = dma
        o.sig = False
        o.cnt = None
        deps = {}
        for b in reads:
            if b.w is not None:
                deps[b.w] = True
        for b in writes:
            if b.w is not None:
                deps.setdefault(b.w, False)
            for r in b.r.values():
                for rid in r:
                    deps.setdefault(rid, False)
        if dma:
            if eng == "pool":
                slot = self.NHW + self.nsw % (self.NDMA_SEM - self.NHW)
                self.nsw += 1
            else:
                slot = self.ndma % self.NHW
                self.ndma += 1
            prev = self.dma_last.get(slot)
            if prev is not None:
                deps.setdefault(prev.id, False)
                o.dval = prev.dval + 16
            else:
                o.dval = 16
            o.dsem = slot
            self.dma_last[slot] = o
        deps.pop(o.id, None)
        o.deps = deps
        for b in reads:
            if dma:
                b.r.setdefault("dma", []).append(o.id)
            else:
                b.r[eng] = [o.id]
        for b in writes:
            b.w = o.id
            b.r = {}
        self.ops.append(o)
        return o

    def finalize(self):
        ops = self.ops
        seqc = {e: 0 for e in CENG}
        known = {e: {c: 0 for c in CENG} for e in CENG + ("sp",)}
        kdma = {e: set() for e in CENG + ("sp",)}
        for o in ops:
            A = o.eng
            if not o.dma:
                seqc[A] += 1
                o.seq = seqc[A]
            else:
                o.seq = 0
            kn = known[A]
            need = []
            dl = sorted(o.deps.items(), key=lambda kv: -kv[0])
            for xid, raw in dl:
                X = ops[xid]
                if X.dma:
                    if xid in kdma[A]:
                        continue
                    need.append(xid)
                    kdma[A].add(xid)
                    for c in CENG:
                        if X.clock[c] > kn[c]:
                            kn[c] = X.clock[c]
                    continue
                E = X.eng
                if X.seq <= kn[E]:
                    continue
                if (not o.dma) and E == A:
                    if A == "pe" or not (raw or STRICT):
                        continue
                need.append(xid)
                X.sig = True
                kn[E] = X.seq
                for c in CENG:
                    if X.clock[c] > kn[c]:
                        kn[c] = X.clock[c]
            o.need = need
            ck = dict(kn)
            if len(kdma[A]) > 512:
                kdma[A] = set(sorted(kdma[A])[-256:])
            o.clock = ck
        cnt = {e: 0 for e in CENG}
        for o in ops:
            if (not o.dma) and o.sig:
                cnt[o.eng] += 1
                o.cnt = cnt[o.eng]
        self.nepoch = {e: cnt[e] // EPOCH + 1 for e in CENG}

    def emit(self, block, sems, dsems):
        ops = self.ops

        def semval(X):
            if X.dma:
                return dsems[X.dsem], X.dval
            k = (X.cnt - 1) // EPOCH
            return sems[X.eng][k], (X.cnt - 1) % EPOCH + 1

        def run(engname):
            def body(e):
                for o in ops:
                    if o.eng != engname:
                        continue
                    for xid in o.need:
                        s, v = semval(ops[xid])
                        e.wait_ge(s, v)
                    ins = o.fn(e)
                    if o.dma:
                        ins.then_inc(dsems[o.dsem], 16)
                    elif o.sig:
                        s, _ = semval(o)
                        ins.then_inc(s, 1)
                if engname == "sp":
                    for slot, o in self.dma_last.items():
                        e.wait_ge(dsems[slot], o.dval)
            return body

        block.tensor(run("pe"))
        block.scalar(run("act"))
        block.vector(run("dve"))
        block.gpsimd(run("pool"))
        block.sync(run("sp"))


class Arena:
    def __init__(self, t, size, part=128):
        self.t = t
        self.size = size
        self.off = 0

    def alloc(self, free_shape, dtype, align=64):
        n = int(np.prod(free_shape))
        nb = n * DSZ[dtype]
        off = (self.off + align - 1) // align * align
        assert off + nb <= self.size, f"arena overflow {off + nb} > {self.size}"
        self.off = off + nb
        ap = self.t[:, off:off + nb].bitcast(dtype)
        if len(free_shape) == 2:
            ap = ap.rearrange("p (a b) -> p a b", a=free_shape[0], b=free_shape[1])
        elif len(free_shape) == 3:
            ap = ap.rearrange("p (a b c) -> p a b c", a=free_shape[0], b=free_shape[1], c=free_shape[2])
        return ap

    def mark(self):
        return self.off

    def reset(self, m):
        self.off = m


S = 4096
D = 1024
NT = 32
KC = 8
TB = 256
NB = S // TB
TPB = TB // 128
DFF = 2816
NJ = 22
EPS = 1e-6
ARENA = 206 * 1024
DBG = {}


class Ring:
    def __init__(self, items):
        self.items = items
        self.i = 0

    def next(self):
        it = self.items[self.i % len(self.items)]
        self.i += 1
        return it


class K:
    def __init__(self, nc):
        self.nc = nc
        self.P = Prog(nc)
        self.Y = Buf("phase")

    def _op(self, eng, fn, reads, writes, dma=False):
        return self.P.op(eng, fn, reads=list(reads) + [self.Y], writes=list(writes), dma=dma)

    def barrier(self, scr):
        self.P.op("dve", lambda e: e.memset(scr, 0.0), reads=[], writes=[self.Y])

    def mm(self, out, lhsT, rhs, start, stop, reads, writes):
        return self._op("pe", lambda e: e.matmul(out, lhsT=lhsT, rhs=rhs, start=start, stop=stop), reads, writes)

    def tr(self, out, in_, ident, reads, writes):
        return self._op("pe", lambda e: e.transpose(out=out, in_=in_, identity=ident), reads, writes)

    def act(self, out, in_, func, reads, writes, scale=1.0, bias=0.0, accum=None):
        if accum is None:
            return self._op("act", lambda e: e.activation(out=out, in_=in_, func=func, bias=bias, scale=scale), reads, writes)
        return self._op("act", lambda e: e.activation(out=out, in_=in_, func=func, bias=bias, scale=scale, accum_out=accum), reads, writes)

    def tt(self, eng, out, in0, in1, op, reads, writes):
        return self._op(eng, lambda e: e.tensor_tensor(out=out, in0=in0, in1=in1, op=op), reads, writes)

    def ts(self, eng, out, in0, s1, s2, op0, op1, reads, writes):
        if s2 is None:
            return self._op(eng, lambda e: e.tensor_scalar(out=out, in0=in0, scalar1=s1, scalar2=None, op0=op0), reads, writes)
        return self._op(eng, lambda e: e.tensor_scalar(out=out, in0=in0, scalar1=s1, scalar2=s2, op0=op0, op1=op1), reads, writes)

    def stt(self, eng, out, in0, scalar, in1, op0, op1, reads, writes):
        return self._op(eng, lambda e: e.scalar_tensor_tensor(out=out, in0=in0, scalar=scalar, in1=in1, op0=op0, op1=op1), reads, writes)

    def copy(self, eng, out, in_, reads, writes):
        if eng == "act":
            return self._op("act", lambda e: e.activation(out=out, in_=in_, func=AF.Identity), reads, writes)
        return self._op(eng, lambda e: e.tensor_copy(out=out, in_=in_), reads, writes)

    def recip(self, out, in_, reads, writes):
        return self._op("dve", lambda e: e.reciprocal(out=out, in_=in_), reads, writes)

    def memset(self, eng, ap, val, reads, writes):
        return self._op(eng, lambda e: e.memset(ap, val), reads, writes)

    def scan(self, out, d0, d1, reads, writes):
        return self._op("dve", lambda e: e.tensor_tensor_scan(out=out, data0=d0, data1=d1, initial=0.0, op0=ALU.mult, op1=ALU.add), reads, writes)

    def asel(self, out, in_, pattern, cmp, fill, base, cm, reads, writes):
        return self._op("pool", lambda e: e.affine_select(out=out, in_=in_, pattern=pattern, compare_op=cmp, fill=fill, base=base, channel_multiplier=cm), reads, writes)

    def dma(self, q, out, in_, reads, writes):
        return self._op(q, lambda e: e.dma_start(out=out, in_=in_), reads, writes, dma=True)


def build_nc(upto=99, debug=False):
    nc = bass.Bass("TRN2", target_bir_lowering=False)
    k = K(nc)
    P = k.P

    def din(name, shape):
        return nc.dram_tensor(name, shape, F32, kind="ExternalInput").ap()

    x_in = din("x", [S, D])
    c_in = din("c", [8, 128])
    ada_w = din("ada_w", [2, D, 6 * D])
    ada_b = din("ada_b", [2, 6 * D])
    a_w_in = din("a_w_in", [D, 4096])
    a_lb = din("a_lb_logits", [16, 128])
    a_ng = din("a_norm_g", [1, 128])
    a_w_out = din("a_w_out", [D, D])
    kv_ada_w = din("kv_ada_w", [D, 2 * D])
    kv_ada_b = din("kv_ada_b", [1, 2 * D])
    kv_w = din("kv_w", [D, 2056])
    kv_bf = din("kv_b_f", [1, 8])
    k_ng = din("k_norm_g", [1, 128])
    b_w_q = din("b_w_q", [D, 2 * D])
    q_ng = din("q_norm_g", [1, 128])
    b_w_out = din("b_w_out", [D, D])
    w_up = din("ffn_w_up", [2, D, 2 * DFF])
    conv_w = din("ffn_conv_w", [2, 132, 128])
    conv_b = din("ffn_conv_b", [2, 44, 128])
    w_down = din("ffn_w_down", [2, DFF, D])
    out_d = nc.dram_tensor("out", [S, D], F32, kind="ExternalOutput").ap()
    skind = "ExternalOutput" if debug else "Internal"
    modsD = nc.dram_tensor("modsD", [14, D], F32, kind=skind).ap()
    x1_d = nc.dram_tensor("x1", [S, D], F32, kind=skind).ap()
    x2_d = nc.dram_tensor("x2", [S, D], F32, kind=skind).ap()
    x3_d = nc.dram_tensor("x3", [S, D], F32, kind=skind).ap()
    kT_d = nc.dram_tensor("kT_d", [8, 128, S], BF16, kind="Internal").ap()
    qT_d = nc.dram_tensor("qT_d", [8, 128, S], BF16, kind="Internal").ap()
    v_d = nc.dram_tensor("v_d", [S, D], BF16, kind="Internal").ap()
    gate_d = nc.dram_tensor("gate_d", [S, D], F32, kind="Internal").ap()
    bx1, bx2, bx3, bmods, bkT, bqT, bv, bgate, bout = (Buf(n) for n in "x1 x2 x3 mods kT qT v gate out".split())

    import contextlib
    st = contextlib.ExitStack()
    arena_t = st.enter_context(nc.sbuf_tensor("arena", [128, ARENA], U8))
    ps_t = st.enter_context(nc.psum_tensor("psum", [128, 8 * 2048], U8))
    A = Arena(arena_t, ARENA)
    PS = Arena(ps_t, 8 * 2048)

    def T(shape, dt, name=""):
        return A.alloc(shape, dt), Buf(name)

    bankbuf = [Buf(f"bank{i}") for i in range(8)]

    def BV(bank, shape, dt, boff=0):
        n = int(np.prod(shape))
        nb = n * DSZ[dt]
        assert boff + nb <= 2048
        off = bank * 2048 + boff
        ap = ps_t[:, off:off + nb].bitcast(dt)
        if len(shape) == 2:
            ap = ap.rearrange("p (a b) -> p a b", a=shape[0], b=shape[1])
        return ap, bankbuf[bank]

    ident_f, b_idf = T([128], F32)
    ident_bf, b_idb = T([128], BF16)
    ones_f, b_1f = T([128], F32)
    ones_bf, b_1b = T([128], BF16)
    tri_f, b_tri = T([128], F32)
    e0_f, b_e0 = T([128], F32)
    mask2_f, b_m2 = T([128], F32)
    maskc_bf, b_mc = T([128], BF16)
    scanmsk, b_sm = T([TB], F32)
    modcol, b_mod = T([112], F32)
    ccol, b_cc = T([8], F32)
    cact, b_ca = T([8], F32)
    lbcol, b_lb = T([16], F32)
    omlb, b_omlb = T([8], F32)
    nomlb, b_nomlb = T([8], F32)
    gcol, b_gc = T([3], F32)
    convc, b_cv = T([2, 176], F32)
    ss_a, b_ssa = T([NT], F32)
    ss_b, b_ssb = T([NT], F32)
    rstd_all, b_rs = T([NT], F32)
    ncum_all, b_nc = T([NT, 8], F32)
    cfirst_all, b_cf = T([NT, 8], F32)
    bfbc, b_bf = T([8], F32)
    scr, b_scr = T([1], F32)
    junk, b_junk = T([D], BF16)
    pmark = A.mark()

    def phase_reset():
        A.reset(pmark)
        k.barrier(scr)

    def load_cols(dst, bdst, src, n, stg, bstg, pstg, bpstg, rd=()):
        k.dma("sp", stg[0:n, :], src, list(rd), [bstg])
        k.tr(pstg[:, 0:n], stg[0:n, :], ident_f[0:n, 0:n], [bstg, b_idf], [bpstg])
        k.copy("dve", dst, pstg[:, 0:n], [bpstg], [bdst])

    k.memset("pool", ident_f, 0.0, [], [b_idf])
    k.asel(ident_f, ident_f, [[-1, 128]], ALU.not_equal, 1.0, 0, 1, [b_idf], [b_idf])
    k.copy("dve", ident_bf, ident_f, [b_idf], [b_idb])
    k.memset("pool", ones_f, 1.0, [], [b_1f])
    k.memset("pool", ones_bf, 1.0, [], [b_1b])
    k.memset("pool", tri_f, 1.0, [], [b_tri])
    k.asel(tri_f, tri_f, [[1, 128]], ALU.is_ge, 0.0, 0, -1, [b_tri], [b_tri])
    k.copy("dve", maskc_bf, tri_f, [b_tri], [b_mc])
    k.copy("dve", mask2_f, tri_f, [b_tri], [b_m2])
    k.memset("dve", mask2_f[0:64, 64:128], 0.0, [b_m2], [b_m2])
    k.memset("pool", e0_f, 0.0, [], [b_e0])
    k.asel(e0_f, e0_f, [[0, 128]], ALU.not_equal, 1.0, 0, 1, [b_e0], [b_e0])
    k.memset("pool", scanmsk, 1.0, [], [b_sm])
    k.memset("pool", scanmsk.rearrange("p (c t) -> p c t", t=64)[:, :, 0:1], 0.0, [b_sm], [b_sm])
    k.memset("pool", ncum_all, 0.0, [], [b_nc])

    stg, b_stg = T([128], F32)
    pstg, b_pstg = BV(0, [128], F32)
    load_cols(ccol, b_cc, c_in, 8, stg, b_stg, pstg, b_pstg)
    load_cols(lbcol, b_lb, a_lb, 16, stg, b_stg, pstg, b_pstg)
    k.dma("sp", stg[0:1, :], a_ng, [], [b_stg])
    k.dma("sp", stg[1:2, :], k_ng, [], [b_stg])
    k.dma("sp", stg[2:3, :], q_ng, [], [b_stg])
    k.tr(pstg[:, 0:3], stg[0:3, :], ident_f[0:3, 0:3], [b_stg, b_idf], [b_pstg])
    k.copy("dve", gcol, pstg[:, 0:3], [b_pstg], [b_gc])
    for l in range(2):
        load_cols(convc[:, l, 0:128], b_cv, conv_w[l, 0:128, :], 128, stg, b_stg, pstg, b_pstg)
        load_cols(convc[:, l, 128:132], b_cv, conv_w[l, 128:132, :], 4, stg, b_stg, pstg, b_pstg)
        load_cols(convc[:, l, 132:176], b_cv, conv_b[l], 44, stg, b_stg, pstg, b_pstg)
    k.dma("sp", bfbc, kv_bf.partition_broadcast(128), [], [b_bf])
    k.tt("dve", lbcol[:, 0:8], lbcol[:, 8:16], lbcol[:, 0:8], ALU.subtract, [b_lb], [b_lb])
    k.act(lbcol[:, 0:8], lbcol[:, 0:8], AF.Exp, [b_lb], [b_lb])
    k.ts("dve", lbcol[:, 0:8], lbcol[:, 0:8], 1.0, None, ALU.add, None, [b_lb], [b_lb])
    k.recip(lbcol[:, 0:8], lbcol[:, 0:8], [b_lb], [b_lb])
    k.ts("dve", omlb, lbcol[:, 0:8], -1.0, 1.0, ALU.mult, ALU.add, [b_lb], [b_omlb])
    k.ts("dve", nomlb, lbcol[:, 0:8], 1.0, -1.0, ALU.mult, ALU.add, [b_lb], [b_nomlb])
    k.act(cact, ccol, AF.Silu, [b_cc], [b_ca])

    xr = Ring([T([D], F32) for _ in range(3)])

    def sumsq(xt, bxt, ssdst, bss, t):
        k.act(junk, xt, AF.Square, [bxt], [b_junk, bss], accum=ssdst[:, t:t + 1])

    for t in range(NT):
        xt, bxt = xr.next()
        k.dma("sp", xt, x_in[t * 128:(t + 1) * 128, :], [], [bxt])
        sumsq(xt, bxt, ss_a, b_ssa, t)

    wst = Ring([T([3072], F32) for _ in range(3)])
    brow, b_brow = T([3072], F32)
    mrow, b_mrow = T([3072], F32)
    prow = [BV(1 + i, [512], F32) for i in range(6)]
    modflat = modsD.rearrange("(o v) n -> o (v n)", o=1)
    for (Wd_, bd_, row0, width) in ((ada_w[0], ada_b[0:1, :], 0, 6144), (ada_w[1], ada_b[1:2, :], 6, 6144), (kv_ada_w, kv_ada_b, 12, 2048)):
        for c0 in range(0, width, 3072):
            wd = min(3072, width - c0)
            nn = wd // 512
            for kc in range(KC):
                s_, bs_ = wst.next()
                k.dma("sp", s_[:, 0:wd], Wd_[kc * 128:(kc + 1) * 128, c0:c0 + wd], [], [bs_])
                for n in range(nn):
                    k.mm(prow[n][0][0:1, :], cact[:, kc:kc + 1], s_[:, n * 512:(n + 1) * 512], kc == 0, kc == KC - 1, [b_ca, bs_], [prow[n][1]])
            k.dma("sp", brow[0:1, 0:wd], bd_[:, c0:c0 + wd], [], [b_brow])
            for n in range(nn):
                k.tt("dve", mrow[0:1, n * 512:(n + 1) * 512], prow[n][0][0:1, :], brow[0:1, n * 512:(n + 1) * 512], ALU.add, [prow[n][1], b_brow], [b_mrow])
            k.dma("sp", modflat[:, row0 * D + c0: row0 * D + c0 + wd], mrow[0:1, 0:wd], [b_mrow], [bmods])
    load_cols(modcol, b_mod, modsD.rearrange("v (c p) -> (v c) p", p=128), 112, stg, b_stg, pstg, b_pstg, rd=[bmods])
    k.ts("dve", gcol[:, 2:3], gcol[:, 2:3], 128.0 ** -0.5, None, ALU.mult, None, [b_gc], [b_gc])
    for v in (1, 4, 7, 10, 13):
        k.ts("dve", modcol[:, v * 8:(v + 1) * 8], modcol[:, v * 8:(v + 1) * 8], 1.0, None, ALU.add, None, [b_mod], [b_mod])

    def calc_rstd(ss, bss):
        k.act(rstd_all, ss, AF.Ln, [bss], [b_rs], scale=1.0 / D, bias=EPS)
        k.act(rstd_all, rstd_all, AF.Exp, [b_rs], [b_rs], scale=-0.5)

    def g_bcast(dst, bdst, v):
        k.dma("sp", dst, modsD[v:v + 1, :].partition_broadcast(128), [bmods], [bdst])

    def load_w(dst, bdst, src, nk, c0=0, c1=None):
        for kc in range(nk):
            if c1 is None:
                k.dma("pool", dst[:, kc, :], src[kc * 128:(kc + 1) * 128, :], [], [bdst])
            else:
                k.dma("pool", dst[:, kc, 0:c1 - c0], src[kc * 128:(kc + 1) * 128, c0:c1], [], [bdst])

    def make_hT(xt, bxt, t, hn, bhn, tp, btp, variants):
        k.ts("dve", hn, xt, rstd_all[:, t:t + 1], None, ALU.mult, None, [bxt, b_rs], [bhn])
        for kc in range(KC):
            k.tr(tp[:, kc, :], hn[:, kc * 128:(kc + 1) * 128], ident_bf, [bhn, b_idb], [btp])
        for (vs, vh, dst, bdst) in variants:
            for kc in range(KC):
                sc = modcol[:, vs * 8 + kc: vs * 8 + kc + 1]
                sh = modcol[:, vh * 8 + kc: vh * 8 + kc + 1]
                if kc % 2 == 0:
                    k.act(dst[:, kc, :], tp[:, kc, :], AF.Identity, [btp, b_mod], [bdst], scale=sc, bias=sh)
                else:
                    k.ts("dve", dst[:, kc, :], tp[:, kc, :], sc, sh, ALU.mult, ALU.add, [btp, b_mod], [bdst])

    def residual_out(ypair, xres, bxres, gbc, b_gbc, tmp, btmp, xnew, bxnew, dst_d, bdst_d, t, ss_next, b_ssn):
        for half in range(2):
            yp, byp = ypair[half]
            k.tt("dve", tmp[:, half * 512:(half + 1) * 512], yp, gbc[:, half * 512:(half + 1) * 512], ALU.mult, [byp, b_gbc], [btmp])
        k.tt("pool", xnew, tmp, xres, ALU.add, [btmp, bxres], [bxnew])
        k.dma("pool", dst_d[t * 128:(t + 1) * 128, :], xnew, [bxnew], [bdst_d])
        sumsq(xnew, bxnew, ss_next, b_ssn, t)

    def phase_hgrn(x_src, bsrc, x_dst, bdst, ss_in, b_ssin, ss_out, b_ssout):
        phase_reset()
        calc_rstd(ss_in, b_ssin)
        w_in_sb, b_win = T([KC, 4096], BF16)
        w_out_sb, b_wout = T([KC, D], BF16)
        load_w(w_in_sb, b_win, a_w_in, KC)
        load_w(w_out_sb, b_wout, a_w_out, KC)
        g1bc, b_g1 = T([D], F32)
        g_bcast(g1bc, b_g1, 2)
        xr = Ring([T([D], F32) for _ in range(3)])
        hnr = Ring([T([D], BF16) for _ in range(2)])
        hTr = Ring([T([KC, TB], BF16) for _ in range(2)])
        tmpA = Ring([T([TB], F32) for _ in range(2)])
        tmpB = Ring([T([TB], F32) for _ in range(2)])
        tmpC = Ring([T([TB], F32) for _ in range(2)])
        kstr = Ring([T([TB], BF16) for _ in range(2)])
        E_all = A.alloc([8, TB], F32)
        kR_all = A.alloc([8, TB], BF16)
        qE_all = A.alloc([8, TB], BF16)
        gs_all = A.alloc([8, TB], F32)
        ks_all = A.alloc([TPB, 8, 128], BF16)
        v_all = A.alloc([TPB, D], BF16)
        Elast = A.alloc([8, TB // 64], F32)
        st32 = A.alloc([8, 128], F32)
        stb = A.alloc([8, 128], BF16)
        bE, bkR, bqE, bgs, bks, bEl, bst32, bstb = ([Buf() for _ in range(8)] for _ in range(8))
        bv_ = [Buf() for _ in range(TPB)]
        atr = Ring([T([4, 128], BF16) for _ in range(4)])
        oTr = Ring([T([D], F32) for _ in range(2)])
        sq, bsq = T([D], BF16)
        lnr, blnr = T([D], F32)
        on, bon = T([D], BF16)
        tmp, btmp = T([D], F32)
        xnew, bxnew = T([D], F32)
        tp, btp = BV(0, [KC, 128], BF16)
        pa = Ring([BV(1, [TB], F32), BV(2, [TB], F32)])
        pbs = [BV(3, [512], F32), BV(4, [512], F32)]
        pbig = ps_t[:, 3 * 2048:5 * 2048].bitcast(F32)
        pb = Ring(pbs)
        scg = [BV(5, [4, 128], F32), BV(1, [4, 128], F32)]
        pog = [BV(6, [4, 128], F32), BV(2, [4, 128], F32)]
        dsg = [BV(7, [4, 128], F32), BV(3, [4, 128], F32)]
        tpk, btpk = BV(4, [TPB, 128], BF16)
        k.memset("pool", st32, 0.0, [], bst32)
        k.memset("pool", stb, 0.0, [], bstb)
        NCH = TB // 64
        for b in range(DBG.get("nb", NB)):
            hT, bhT = hTr.next()
            for ti in range(TPB):
                t = b * TPB + ti
                xt, bxt = xr.next()
                hn, bhn = hnr.next()
                k.dma("sp", xt, x_src[t * 128:(t + 1) * 128, :], [bsrc], [bxt])
                make_hT(xt, bxt, t, hn, bhn, tp, btp, [(1, 0, hT[:, :, ti * 128:(ti + 1) * 128], bhT)])
            if DBG.get("stage", 9) < 1:
                continue
            for h in range(DBG.get("nh", 8)):
                pf, bpf = pa.next()
                for kc in range(KC):
                    k.mm(pf, w_in_sb[:, kc, 1024 + h * 128:1024 + (h + 1) * 128], hT[:, kc, :], kc == 0, kc == KC - 1, [b_win, bhT], [bpf])
                ta, bta = tmpA.next()
                tb_, btb = tmpB.next()
                tc, btc = tmpC.next()
                k.act(ta, pf, AF.Exp, [bpf], [bta])
                if DBG.get("fsub", 9) < 1:
                    continue
                k.ts("dve", ta, ta, 1.0, None, ALU.add, None, [bta], [bta])
                k.recip(ta, ta, [bta], [bta])
                k.act(tb_, ta, AF.Ln, [bta, b_nomlb], [btb], scale=nomlb[:, h:h + 1], bias=1.0)
                if DBG.get("fsub", 9) < 2:
                    continue
                k.scan(tc, scanmsk, tb_, [b_sm, btb], [btc])
                k.act(E_all[:, h, :], tc, AF.Exp, [btc], [bE[h]])
                k.act(tb_, tc, AF.Exp, [btc], [btb], scale=-1.0)
                k.stt("dve", kR_all[:, h, :], ta, omlb[:, h:h + 1], tb_, ALU.mult, ALU.mult, [bta, btb, b_omlb], [bkR[h]])
                if DBG.get("fsub", 9) < 3:
                    continue
                kst, bkst = kstr.next()
                Ev = E_all[:, h, :].rearrange("p (c t) -> p c t", t=64)
                k.tt("pool", kst.rearrange("p (c t) -> p c t", t=64), kR_all[:, h, :].rearrange("p (c t) -> p c t", t=64),
                     Ev[:, :, 63:64].to_broadcast([128, NCH, 64]), ALU.mult, [bkR[h], bE[h]], [bkst])
                k.copy("pool", Elast[:, h, :], Ev[:, :, 63], [bE[h]], [bEl[h]])
                if DBG.get("fsub", 9) < 4:
                    continue
                for ti in range(TPB):
                    k.tr(tpk[:, ti, :], kst[:, ti * 128:(ti + 1) * 128], ident_bf, [bkst, b_idb], [btpk])
                if DBG.get("fsub", 9) < 5:
                    continue
                for ti in range(TPB):
                    k.copy("dve", ks_all[:, ti, h, :], tpk[:, ti, :], [btpk], [bks[h]])
            if DBG.get("stage", 9) < 2:
                continue
            for ti in range(TPB):
                for half in range(2):
                    pv_, bpv = pb.next()
                    for kc in range(KC):
                        k.mm(pv_, hT[:, kc, ti * 128:(ti + 1) * 128], w_in_sb[:, kc, 2048 + half * 512:2048 + (half + 1) * 512], kc == 0, kc == KC - 1, [bhT, b_win], [bpv])
                    k.copy("dve", v_all[:, ti, half * 512:(half + 1) * 512], pv_, [bpv], [bv_[ti]])
            if DBG.get("stage", 9) < 3:
                continue
            for h in range(8):
                pq, bpq = pa.next()
                for kc in range(KC):
                    k.mm(pq, w_in_sb[:, kc, h * 128:(h + 1) * 128], hT[:, kc, :], kc == 0, kc == KC - 1, [b_win, bhT], [bpq])
                if DBG.get("ssub", 9) < 0:
                    continue
                ta, bta = tmpA.next()
                k.act(ta, pq, DBG.get("qf", AF.Silu), [bpq], [bta])
                if DBG.get("ssub", 9) < 1:
                    continue
                k.tt("pool", qE_all[:, h, :], ta, E_all[:, h, :], ALU.mult, [bta, bE[h]], [bqE[h]])
                if DBG.get("ssub", 9) < 2:
                    continue
                pg, bpg = pa.next()
                for kc in range(KC):
                    k.mm(pg, w_in_sb[:, kc, 3072 + h * 128:3072 + (h + 1) * 128], hT[:, kc, :], kc == 0, kc == KC - 1, [b_win, bhT], [bpg])
                k.act(gs_all[:, h, :], pg, AF.Silu, [bpg], [bgs[h]])
            if DBG.get("stage", 9) < 4:
                continue
            for ti in range(TPB):
                t = b * TPB + ti
                cs = slice(ti * 128, (ti + 1) * 128)
                oT, boT = oTr.next()
                oT3 = oT.rearrange("p (h t) -> p h t", t=128)
                atgs = [atr.next() for _ in range(2)]
                for g in range(2):
                    scb, bscb = scg[g]
                    for i_, h in enumerate(range(g * 4, g * 4 + 4)):
                        k.mm(scb[:, i_, :], kR_all[:, h, cs], qE_all[:, h, cs], True, True, [bkR[h], bqE[h]], [bscb])
                for g in range(2):
                    scb, bscb = scg[g]
                    atg, batg = atgs[g]
                    for i_ in range(4):
                        k.tt("dve", atg[:, i_, :], scb[:, i_, :], mask2_f, ALU.mult, [bscb, b_m2], [batg])
                for c in range(2):
                    cc = slice(c * 64, (c + 1) * 64)
                    pr = slice(c * 64, (c + 1) * 64)
                    ch = ti * 2 + c
                    for g in range(2):
                        atg, batg = atgs[g]
                        pob, bpob = pog[g]
                        dsb, bdsb = dsg[g]
                        for i_, h in enumerate(range(g * 4, g * 4 + 4)):
                            k.mm(pob[:, i_, cc], v_all[:, ti, h * 128:(h + 1) * 128], atg[:, i_, cc], True, False, [bv_[ti], batg], [bpob])
                            k.mm(pob[:, i_, cc], stb[:, h, :], qE_all[:, h, ti * 128 + c * 64: ti * 128 + (c + 1) * 64], False, True, [bstb[h], bqE[h]], [bpob])
                            k.mm(dsb[:, i_, :], ks_all[pr, ti, h, :], v_all[pr, ti, h * 128:(h + 1) * 128], True, True, [bks[h], bv_[ti]], [bdsb])
                    for g in range(2):
                        dsb, bdsb = dsg[g]
                        for i_, h in enumerate(range(g * 4, g * 4 + 4)):
                            k.stt("dve", st32[:, h, :], st32[:, h, :], Elast[:, h, ch:ch + 1], dsb[:, i_, :], ALU.mult, ALU.add, [bst32[h], bEl[h], bdsb], [bst32[h]])
                            k.copy("pool", stb[:, h, :], st32[:, h, :], [bst32[h]], [bstb[h]])
                for g in range(2):
                    pob, bpob = pog[g]
                    for i_, h in enumerate(range(g * 4, g * 4 + 4)):
                        k.copy("dve", oT3[:, h, :], pob[:, i_, :], [bpob], [boT])
                if DBG.get("stage", 9) < 5:
                    continue
                k.act(sq, oT, AF.Square, [boT], [bsq])
                for half in range(2):
                    k.mm(pbs[half][0], ones_bf, sq[:, half * 512:(half + 1) * 512], True, True, [b_1b, bsq], [pbs[half][1]])
                k.act(lnr, pbig, AF.Ln, [pbs[0][1], pbs[1][1]], [blnr], scale=1.0 / 128, bias=EPS)
                k.act(lnr, lnr, AF.Exp, [blnr], [blnr], scale=-0.5)
                k.stt("dve", lnr, oT, gcol[:, 0:1], lnr, ALU.mult, ALU.mult, [boT, b_gc, blnr], [blnr])
                k.tt("pool", on.rearrange("p (h t) -> p h t", t=128), lnr.rearrange("p (h t) -> p h t", t=128), gs_all[:, :, cs], ALU.mult, [blnr] + bgs, [bon])
                on3 = on.rearrange("p (h t) -> p h t", t=128)
                xres, bxres = xr.next()
                k.dma("sp", xres, x_src[t * 128:(t + 1) * 128, :], [bsrc], [bxres])
                ypair = []
                for half in range(2):
                    yp, byp = pbs[half]
                    for h in range(8):
                        k.mm(yp, on3[:, h, :], w_out_sb[:, h, half * 512:(half + 1) * 512], h == 0, h == 7, [bon, b_wout], [byp])
                    ypair.append((yp, byp))
                residual_out(ypair, xres, bxres, g1bc, b_g1, tmp, btmp, xnew, bxnew, x_dst, bdst, t, ss_out, b_ssout)

    def phase_ffn(l, x_src, bsrc, x_dst, bdst, ss_in, b_ssin, ss_out, b_ssout, vsc, vsh, vg):
        phase_reset()
        calc_rstd(ss_in, b_ssin)
        wup, b_wup = T([KC, 2 * DFF], BF16)
        wdn, b_wdn = T([NJ, D], BF16)
        load_w(wup, b_wup, w_up[l], KC)
        load_w(wdn, b_wdn, w_down[l], NJ)
        gbc, b_gbc = T([D], F32)
        g_bcast(gbc, b_gbc, vg)
        xr = Ring([T([D], F32) for _ in range(2)])
        hnr = Ring([T([D], BF16) for _ in range(2)])
        hTr = Ring([T([KC, TB + 2], BF16) for _ in range(2)])
        ur = Ring([T([TB + 2], F32) for _ in range(3)])
        t0r = Ring([T([TB], F32) for _ in range(3)])
        sgr = Ring([T([TB], F32) for _ in range(2)])
        actr = Ring([T([NJ, TB], BF16) for _ in range(2)])
        tmp, btmp = T([D], F32)
        xnew, bxnew = T([D], F32)
        tp, btp = BV(0, [KC, 128], BF16)
        pur = Ring([BV(1 + i, [TB + 2], F32) for i in range(4)])
        pbs = [BV(5, [512], F32), BV(6, [512], F32)]
        prev = None
        for b in range(DBG.get('fnb', NB)):
            hT, bhT = hTr.next()
            if prev is None:
                k.memset("pool", hT[:, :, TB:TB + 2], 0.0, [], [bhT])
            else:
                k.copy("pool", hT[:, :, TB:TB + 2], prev[0][:, :, TB - 2:TB], [prev[1]], [bhT])
            for ti in range(TPB):
                t = b * TPB + ti
                xt, bxt = xr.next()
                hn, bhn = hnr.next()
                k.dma("sp", xt, x_src[t * 128:(t + 1) * 128, :], [bsrc], [bxt])
                make_hT(xt, bxt, t, hn, bhn, tp, btp, [(vsc, vsh, hT[:, :, ti * 128:(ti + 1) * 128], bhT)])
            prev = (hT, bhT)
            if DBG.get('fstage', 9) < 1:
                continue
            aT, baT = actr.next()
            for j in range(DBG.get('fnj', NJ)):
                res = []
                for which, cidx in ((0, j), (1, NJ + j)):
                    pu, bpu = pur.next()
                    for kc in range(KC):
                        k.mm(pu, wup[:, kc, cidx * 128:(cidx + 1) * 128], hT[:, kc, :], kc == 0, kc == KC - 1, [b_wup, bhT], [bpu])
                    u, bu = ur.next()
                    t0, bt0 = t0r.next()
                    k.copy("dve", u[:, 2:2 + TB], pu[:, 0:TB], [bpu], [bu])
                    k.copy("dve", u[:, 0:2], pu[:, TB:TB + 2], [bpu], [bu])
                    if DBG.get('fstage', 9) < 2:
                        continue
                    k.ts("dve", t0, pu[:, 0:TB], convc[:, l, 88 + cidx:89 + cidx], convc[:, l, 132 + cidx:133 + cidx], ALU.mult, ALU.add, [bpu, b_cv], [bt0])
                    if DBG.get('fsub2', 9) < 1:
                        continue
                    k.stt("dve", t0, u[:, 1:1 + TB], convc[:, l, 44 + cidx:45 + cidx], t0, ALU.mult, ALU.add, [bu, b_cv, bt0], [bt0])
                    k.stt("dve", t0, u[:, 0:TB], convc[:, l, cidx:cidx + 1], t0, ALU.mult, ALU.add, [bu, b_cv, bt0], [bt0])
                    res.append((t0, bt0))
                if DBG.get('fstage', 9) < 3:
                    continue
                sg, bsg = sgr.next()
                k.act(sg, res[0][0], AF.Silu, [res[0][1]], [bsg])
                k.tt("pool", aT[:, j, :], sg, res[1][0], ALU.mult, [bsg, res[1][1]], [baT])
            if DBG.get('fstage', 9) < 4:
                continue
            for ti in range(TPB):
                t = b * TPB + ti
                xres, bxres = xr.next()
                k.dma("sp", xres, x_src[t * 128:(t + 1) * 128, :], [bsrc], [bxres])
                ypair = []
                for half in range(2):
                    yp, byp = pbs[half]
                    for j in range(NJ):
                        k.mm(yp, aT[:, j, ti * 128:(ti + 1) * 128], wdn[:, j, half * 512:(half + 1) * 512], j == 0, j == NJ - 1, [baT, b_wdn], [byp])
                    ypair.append((yp, byp))
                residual_out(ypair, xres, bxres, gbc, b_gbc, tmp, btmp, xnew, bxnew, x_dst, bdst, t, ss_out, b_ssout)

    def phase_fox_a(x_src, bsrc, ss_in, b_ssin):
        phase_reset()
        calc_rstd(ss_in, b_ssin)
        kvw, b_kvw = T([KC, 2056], BF16)
        wq, b_wq = T([KC, 2048], BF16)
        load_w(kvw, b_kvw, kv_w, KC)
        load_w(wq, b_wq, b_w_q, KC)
        xr = Ring([T([D], F32) for _ in range(3)])
        hnr = Ring([T([D], BF16) for _ in range(2)])
        hkvr = Ring([T([KC, TB], BF16) for _ in range(2)])
        h1r = Ring([T([KC, TB], BF16) for _ in range(2)])
        kTbr = Ring([T([8, TB], BF16) for _ in range(2)])
        qTbr = Ring([T([8, TB], BF16) for _ in range(2)])
        sqr = Ring([T([TB], BF16) for _ in range(2)])
        lnrr = Ring([T([TB], F32) for _ in range(2)])
        vbr = Ring([T([D], BF16) for _ in range(2)])
        gtr = Ring([T([D], F32) for _ in range(2)])
        ger = Ring([T([512], F32) for _ in range(2)])
        lfr = Ring([T([8], F32) for _ in range(2)])
        run, brun = T([8], F32)
        tp, btp = BV(0, [KC, 128], BF16)
        pa = Ring([BV(1, [TB], F32), BV(2, [TB], F32)])
        pssr = Ring([BV(3, [TB], F32), BV(4, [TB], F32)])
        pb = Ring([BV(5, [512], F32), BV(6, [512], F32)])
        pf8, bpf8 = BV(7, [8], F32, 0)
        pcum, bpcum = BV(7, [8], F32, 512)
        pcf, bpcf = BV(7, [8], F32, 1024)
        k.memset("pool", run, 0.0, [], [brun])
        kTv = kT_d.rearrange("h d t -> d h t")
        qTv = qT_d.rearrange("h d t -> d h t")
        for b in range(NB):
            hkv, bhkv = hkvr.next()
            h1, bh1 = h1r.next()
            for ti in range(TPB):
                t = b * TPB + ti
                xt, bxt = xr.next()
                hn, bhn = hnr.next()
                k.dma("sp", xt, x_src[t * 128:(t + 1) * 128, :], [bsrc], [bxt])
                tsl = slice(ti * 128, (ti + 1) * 128)
                make_hT(xt, bxt, t, hn, bhn, tp, btp, [(13, 12, hkv[:, :, tsl], bhkv), (7, 6, h1[:, :, tsl], bh1)])
            kTb, bkTb = kTbr.next()
            qTb, bqTb = qTbr.next()
            for (W, bW, hs, bhs, gi, dstb, bdstb) in ((kvw, b_kvw, hkv, bhkv, 1, kTb, bkTb), (wq, b_wq, h1, bh1, 2, qTb, bqTb)):
                for h in range(8):
                    pk, bpk = pa.next()
                    for kc in range(KC):
                        k.mm(pk, W[:, kc, h * 128:(h + 1) * 128], hs[:, kc, :], kc == 0, kc == KC - 1, [bW, bhs], [bpk])
                    sq, bsq = sqr.next()
                    k.act(sq, pk, AF.Square, [bpk], [bsq])
                    pss, bpss = pssr.next()
                    k.mm(pss, ones_bf, sq, True, True, [b_1b, bsq], [bpss])
                    lnr, blnr = lnrr.next()
                    k.act(lnr, pss, AF.Ln, [bpss], [blnr], scale=1.0 / 128, bias=EPS)
                    k.act(lnr, lnr, AF.Exp, [blnr], [blnr], scale=-0.5)
                    k.stt("dve", dstb[:, h, :], pk, gcol[:, gi:gi + 1], lnr, ALU.mult, ALU.mult, [bpk, b_gc, blnr], [bdstb])
            k.dma("pool", kTv[:, :, b * TB:(b + 1) * TB], kTb, [bkTb], [bkT])
            k.dma("pool", qTv[:, :, b * TB:(b + 1) * TB], qTb, [bqTb], [bqT])
            for ti in range(TPB):
                t = b * TPB + ti
                tsl = slice(ti * 128, (ti + 1) * 128)
                vb, bvb = vbr.next()
                for half in range(2):
                    pv_, bpv = pb.next()
                    for kc in range(KC):
                        k.mm(pv_, hkv[:, kc, tsl], kvw[:, kc, 1024 + half * 512:1024 + (half + 1) * 512], kc == 0, kc == KC - 1, [bhkv, b_kvw], [bpv])
                    k.copy("dve", vb[:, half * 512:(half + 1) * 512], pv_, [bpv], [bvb])
                k.dma("pool", v_d[t * 128:(t + 1) * 128, :], vb, [bvb], [bv])
                gt, bgt = gtr.next()
                for half in range(2):
                    pg, bpg = pb.next()
                    for kc in range(KC):
                        k.mm(pg, h1[:, kc, tsl], wq[:, kc, 1024 + half * 512:1024 + (half + 1) * 512], kc == 0, kc == KC - 1, [bh1, b_wq], [bpg])
                    ge, bge = ger.next()
                    k.act(ge, pg, AF.Exp, [bpg], [bge], scale=-1.0)
                    k.ts("dve", ge, ge, 1.0, None, ALU.add, None, [bge], [bge])
                    k.recip(gt[:, half * 512:(half + 1) * 512], ge, [bge], [bgt])
                k.dma("pool", gate_d[t * 128:(t + 1) * 128, :], gt, [bgt], [bgate])
                for kc in range(KC):
                    k.mm(pf8, hkv[:, kc, tsl], kvw[:, kc, 2048:2056], kc == 0, kc == KC - 1, [bhkv, b_kvw], [bpf8])
                lf, blf = lfr.next()
                k.tt("dve", lf, pf8, bfbc, ALU.add, [bpf8, b_bf], [blf])
                k.act(lf, lf, AF.Exp, [blf], [blf], scale=-1.0)
                k.act(lf, lf, AF.Ln, [blf], [blf], bias=1.0)
                k.mm(pcum, tri_f, lf, True, False, [b_tri, blf], [bpcum])
                k.mm(pcum, ones_f, run, False, True, [b_1f, brun], [bpcum])
                k.copy("dve", ncum_all[:, t, :], pcum, [bpcum], [b_nc])
                k.tt("dve", run, run, lf, ALU.add, [brun, blf], [brun])
                k.mm(pcf, e0_f, ncum_all[:, t, :], True, True, [b_e0, b_nc], [bpcf])
                k.copy("dve", cfirst_all[:, t, :], pcf, [bpcf], [b_cf])

    def phase_fox_bc(x_src, bsrc, x_dst, bdst, ss_out, b_ssout):
        phase_reset()
        wo, b_wo = T([KC, D], BF16)
        load_w(wo, b_wo, b_w_out, KC)
        gbc, b_gbc = T([D], F32)
        g_bcast(gbc, b_gbc, 8)
        o_all = A.alloc([NT, D], BF16)
        bo = [Buf() for _ in range(NT)]
        kThr = Ring([T([S], BF16) for _ in range(2)])
        qThr = Ring([T([S], BF16) for _ in range(2)])
        vhr = Ring([T([NT, 132], BF16) for _ in range(2)])
        ghr = Ring([T([NT, 128], F32) for _ in range(2)])
        bir = Ring([T([NT, NT], F32) for _ in range(2)])
        ptr_ = Ring([T([128], BF16) for _ in range(4)])
        recr = Ring([T([1], F32) for _ in range(2)])
        xr = Ring([T([D], F32) for _ in range(2)])
        oTr = Ring([T([KC, 128], BF16) for _ in range(2)])
        tmp, btmp = T([D], F32)
        xnew, bxnew = T([D], F32)
        sr = Ring([BV(i, [128], F32) for i in range(3)])
        por = Ring([BV(3, [132], F32), BV(4, [132], F32)])
        tp, btp = BV(5, [KC, 128], BF16)
        pbs = [BV(6, [512], F32), BV(7, [512], F32)]
        for (vh, bvh) in vhr.items:
            k.memset("pool", vh[:, :, 128:129], 1.0, [], [bvh])
        vdv = v_d.rearrange("(j p) d -> p j d", p=128)
        gdv = gate_d.rearrange("(j p) d -> p j d", p=128)
        for h in range(8):
            kTh, bkTh = kThr.next()
            qTh, bqTh = qThr.next()
            vh, bvh = vhr.next()
            gh, bgh = ghr.next()
            bias, bbias = bir.next()
            k.dma("sp", kTh, kT_d[h], [bkT], [bkTh])
            k.dma("sp", qTh, qT_d[h], [bqT], [bqTh])
            k.dma("sp", vh[:, :, 0:128], vdv[:, :, h * 128:(h + 1) * 128], [bv], [bvh])
            k.dma("sp", gh, gdv[:, :, h * 128:(h + 1) * 128], [bgate], [bgh])
            for j in range(NT):
                k.ts("dve", bias[:, j, :], cfirst_all[:, :, h], -1.0, ncum_all[:, j, h:h + 1], ALU.mult, ALU.add, [b_cf, b_nc], [bbias])
            for Q in range(NT):
                po, bpo = por.next()
                for j in range(Q + 1):
                    s_, bs = sr.next()
                    k.mm(s_, kTh[:, j * 128:(j + 1) * 128], qTh[:, Q * 128:(Q + 1) * 128], True, True, [bkTh, bqTh], [bs])
                    pt, bpt = ptr_.next()
                    k.act(pt, s_, AF.Exp, [bs, bbias], [bpt], bias=bias[:, j, Q:Q + 1])
                    if j == Q:
                        k.tt("pool", pt, pt, maskc_bf, ALU.mult, [bpt, b_mc], [bpt])
                    k.mm(po[:, 0:129], pt, vh[:, j, 0:129], j == 0, j == Q, [bpt, bvh], [bpo])
                rec, brec = recr.next()
                k.recip(rec, po[:, 128:129], [bpo], [brec])
                k.stt("dve", o_all[:, Q, h * 128:(h + 1) * 128], po[:, 0:128], rec, gh[:, Q, :], ALU.mult, ALU.mult, [bpo, brec, bgh], [bo[Q]])
        for t in range(NT):
            for kc in range(KC):
                k.tr(tp[:, kc, :], o_all[:, t, kc * 128:(kc + 1) * 128], ident_bf, [bo[t], b_idb], [btp])
            oT, boT = oTr.next()
            k.copy("dve", oT.rearrange("p a b -> p (a b)"), tp.rearrange("p a b -> p (a b)"), [btp], [boT])
            xres, bxres = xr.next()
            k.dma("sp", xres, x_src[t * 128:(t + 1) * 128, :], [bsrc], [bxres])
            ypair = []
            for half in range(2):
                yp, byp = pbs[half]
                for kc in range(KC):
                    k.mm(yp, oT[:, kc, :], wo[:, kc, half * 512:(half + 1) * 512], kc == 0, kc == KC - 1, [boT, b_wo], [byp])
                ypair.append((yp, byp))
            residual_out(ypair, xres, bxres, gbc, b_gbc, tmp, btmp, xnew, bxnew, x_dst, bdst, t, ss_out, b_ssout)

    bxin = Buf("xin")
    calc = None
    if upto >= 1:
        phase_hgrn(x_in, bxin, x1_d, bx1, ss_a, b_ssa, ss_b, b_ssb)
    if upto >= 2:
        phase_ffn(0, x1_d, bx1, x2_d, bx2, ss_b, b_ssb, ss_a, b_ssa, 4, 3, 5)
    if upto >= 3:
        phase_fox_a(x2_d, bx2, ss_a, b_ssa)
    if upto >= 4:
        phase_fox_bc(x2_d, bx2, x3_d, bx3, ss_b, b_ssb)
    if upto >= 5:
        phase_ffn(1, x3_d, bx3, out_d, bout, ss_b, b_ssb, ss_a, b_ssa, 10, 9, 11)

    k.barrier(scr)
    fin_d = nc.dram_tensor("fin_d", [128, 1], F32, kind="Internal").ap()
    k.dma("sp", fin_d, scr, [], [Buf("fin")])
    P.finalize()
    sems = {e: [st.enter_context(nc.semaphore(f"s_{e}{i}")) for i in range(P.nepoch[e])] for e in CENG}
    dsems = [st.enter_context(nc.semaphore(f"d{i}")) for i in range(P.NDMA_SEM)]
    block = st.enter_context(nc.Block())
    P.emit(block, sems, dsems)
    st.close()
    return nc


_NC_CACHE = {}


def _in_maps(inputs):
    f = lambda a: np.ascontiguousarray(np.asarray(a, dtype=np.float32))
    x = f(inputs["x"])
    c = f(inputs["c"])
    shared = dict(
        ada_w=f(inputs["ada_w"]), ada_b=f(inputs["ada_b"]),
        a_w_in=f(inputs["a_w_in"]).reshape(D, 4096),
        a_lb_logits=f(inputs["a_lb_logits"]).reshape(16, 128),
        a_norm_g=f(inputs["a_norm_g"]).reshape(1, 128),
        a_w_out=f(inputs["a_w_out"]).reshape(D, D),
        kv_ada_w=f(inputs["kv_ada_w"]), kv_ada_b=f(inputs["kv_ada_b"]).reshape(1, 2 * D),
        kv_w=f(inputs["kv_w"]), kv_b_f=f(inputs["kv_b_f"]).reshape(1, 8),
        k_norm_g=f(inputs["k_norm_g"]).reshape(1, 128),
        b_w_q=f(inputs["b_w_q"]).reshape(D, 2 * D),
        q_norm_g=f(inputs["q_norm_g"]).reshape(1, 128),
        b_w_out=f(inputs["b_w_out"]).reshape(D, D),
        ffn_w_up=f(inputs["ffn_w_up"]),
        ffn_conv_w=f(inputs["ffn_conv_w"]).reshape(2, 132, 128),
        ffn_conv_b=f(inputs["ffn_conv_b"]).reshape(2, 44, 128),
        ffn_w_down=f(inputs["ffn_w_down"]),
    )
    maps = []
    for b in range(8):
        m = dict(shared)
        m["x"] = x[b]
        m["c"] = c[b].reshape(8, 128)
        maps.append(m)
    return maps


def kernel(**inputs):
    if "nc" not in _NC_CACHE:
        _NC_CACHE["nc"] = build_nc()
    nc = _NC_CACHE["nc"]
    res = run_bass_kernel_spmd(nc, _in_maps(inputs), core_ids=list(range(8)))
    return np.stack([np.asarray(r["out"], dtype=np.float32) for r in res.results], axis=0)
```

```python
import numpy as np
import concourse.bass as bass
import concourse.mybir as mybir
from concourse.bass_utils import run_bass_kernel_spmd

F32 = mybir.dt.float32
BF16 = mybir.dt.bfloat16
U8 = mybir.dt.uint8
AF = mybir.ActivationFunctionType
ALU = mybir.AluOpType
AX = mybir.AxisListType
DSZ = {F32: 4, BF16: 2, U8: 1}

CENG = ("pe", "act", "dve", "pool")
EPOCH = 12000
STRICT = True


class Buf:
    __slots__ = ("name", "w", "r")

    def __init__(self, name=""):
        self.name = name
        self.w = None
        self.r = {}


class Op:
    __slots__ = ("id", "eng", "fn", "deps", "dma", "seq", "sig", "cnt", "need", "clock", "dsem", "dval", "inc")


class Prog:
    def __init__(self, nc):
        self.nc = nc
        self.ops = []
        self.ndma = 0
        self.dma_last = {}
        self.NDMA_SEM = 40
        self.NHW = 24
        self.nsw = 0

    def op(self, eng, fn, reads=(), writes=(), dma=False):
        o = Op()
        o.id = len(self.ops)
        o.eng = eng
        o.fn = fn
        o.dma = dma
        o.sig = False
        o.cnt = None
        deps = {}
        for b in reads:
            if b.w is not None:
                deps[b.w] = True
        for b in writes:
            if b.w is not None:
                deps.setdefault(b.w, False)
            for r in b.r.values():
                for rid in r:
                    deps.setdefault(rid, False)
        if dma:
            if eng == "pool":
                slot = self.NHW + self.nsw % (self.NDMA_SEM - self.NHW)
                self.nsw += 1
            else:
                slot = self.ndma % self.NHW
                self.ndma += 1
            prev = self.dma_last.get(slot)
            if prev is not None:
                deps.setdefault(prev.id, False)
                o.dval = prev.dval + 16
            else:
                o.dval = 16
            o.dsem = slot
            self.dma_last[slot] = o
        deps.pop(o.id, None)
        o.deps = deps
        for b in reads:
            if dma:
                b.r.setdefault("dma", []).append(o.id)
            else:
                b.r[eng] = [o.id]
        for b in writes:
            b.w = o.id
            b.r = {}
        self.ops.append(o)
        return o

    def finalize(self):
        ops = self.ops
        seqc = {e: 0 for e in CENG}
        known = {e: {c: 0 for c in CENG} for e in CENG + ("sp",)}
        kdma = {e: set() for e in CENG + ("sp",)}
        for o in ops:
            A = o.eng
            if not o.dma:
                seqc[A] += 1
                o.seq = seqc[A]
            else:
                o.seq = 0
            kn = known[A]
            need = []
            dl = sorted(o.deps.items(), key=lambda kv: -kv[0])
            for xid, raw in dl:
                X = ops[xid]
                if X.dma:
                    if xid in kdma[A]:
                        continue
                    need.append(xid)
                    kdma[A].add(xid)
                    for c in CENG:
                        if X.clock[c] > kn[c]:
                            kn[c] = X.clock[c]
                    continue
                E = X.eng
                if X.seq <= kn[E]:
                    continue
                if (not o.dma) and E == A:
                    if A == "pe" or not (raw or STRICT):
                        continue
                need.append(xid)
                X.sig = True
                kn[E] = X.seq
                for c in CENG:
                    if X.clock[c] > kn[c]:
                        kn[c] = X.clock[c]
            o.need = need
            ck = dict(kn)
            if len(kdma[A]) > 512:
                kdma[A] = set(sorted(kdma[A])[-256:])
            o.clock = ck
        cnt = {e: 0 for e in CENG}
        for o in ops:
            if (not o.dma) and o.sig:
                cnt[o.eng] += 1
                o.cnt = cnt[o.eng]
        self.nepoch = {e: cnt[e] // EPOCH + 1 for e in CENG}

    def emit(self, block, sems, dsems):
        ops = self.ops

        def semval(X):
            if X.dma:
                return dsems[X.dsem], X.dval
            k = (X.cnt - 1) // EPOCH
            return sems[X.eng][k], (X.cnt - 1) % EPOCH + 1

        def run(engname):
            def body(e):
                for o in ops:
                    if o.eng != engname:
                        continue
                    for xid in o.need:
                        s, v = semval(ops[xid])
                        e.wait_ge(s, v)
                    ins = o.fn(e)
                    if o.dma:
                        ins.then_inc(dsems[o.dsem], 16)
                    elif o.sig:
                        s, _ = semval(o)
                        ins.then_inc(s, 1)
                if engname == "sp":
                    for slot, o in self.dma_last.items():
                        e.wait_ge(dsems[slot], o.dval)
            return body

        block.tensor(run("pe"))
        block.scalar(run("act"))
        block.vector(run("dve"))
        block.gpsimd(run("pool"))
        block.sync(run("sp"))


class Arena:
    def __init__(self, t, size, part=128):
        self.t = t
        self.size = size
        self.off = 0

    def alloc(self, free_shape, dtype, align=64):
        n = int(np.prod(free_shape))
        nb = n * DSZ[dtype]
        off = (self.off + align - 1) // align * align
        assert off + nb <= self.size, f"arena overflow {off + nb} > {self.size}"
        self.off = off + nb
        ap = self.t[:, off:off + nb].bitcast(dtype)
        if len(free_shape) == 2:
            ap = ap.rearrange("p (a b) -> p a b", a=free_shape[0], b=free_shape[1])
        elif len(free_shape) == 3:
            ap = ap.rearrange("p (a b c) -> p a b c", a=free_shape[0], b=free_shape[1], c=free_shape[2])
        return ap

    def mark(self):
        return self.off

    def reset(self, m):
        self.off = m


S = 4096
D = 1024
NT = 32
KC = 8
TB = 256
NB = S // TB
TPB = TB // 128
DFF = 2816
NJ = 22
EPS = 1e-6
ARENA = 206 * 1024
DBG = {}


class Ring:
    def __init__(self, items):
        self.items = items
        self.i = 0

    def next(self):
        it = self.items[self.i % len(self.items)]
        self.i += 1
        return it


class K:
    def __init__(self, nc):
        self.nc = nc
        self.P = Prog(nc)
        self.Y = Buf("phase")

    def _op(self, eng, fn, reads, writes, dma=False):
        return self.P.op(eng, fn, reads=list(reads) + [self.Y], writes=list(writes), dma=dma)

    def barrier(self, scr):
        self.P.op("dve", lambda e: e.memset(scr, 0.0), reads=[], writes=[self.Y])

    def mm(self, out, lhsT, rhs, start, stop, reads, writes):
        return self._op("pe", lambda e: e.matmul(out, lhsT=lhsT, rhs=rhs, start=start, stop=stop), reads, writes)

    def tr(self, out, in_, ident, reads, writes):
        return self._op("pe", lambda e: e.transpose(out=out, in_=in_, identity=ident), reads, writes)

    def act(self, out, in_, func, reads, writes, scale=1.0, bias=0.0, accum=None):
        if accum is None:
            return self._op("act", lambda e: e.activation(out=out, in_=in_, func=func, bias=bias, scale=scale), reads, writes)
        return self._op("act", lambda e: e.activation(out=out, in_=in_, func=func, bias=bias, scale=scale, accum_out=accum), reads, writes)

    def tt(self, eng, out, in0, in1, op, reads, writes):
        return self._op(eng, lambda e: e.tensor_tensor(out=out, in0=in0, in1=in1, op=op), reads, writes)

    def ts(self, eng, out, in0, s1, s2, op0, op1, reads, writes):
        if s2 is None:
            return self._op(eng, lambda e: e.tensor_scalar(out=out, in0=in0, scalar1=s1, scalar2=None, op0=op0), reads, writes)
        return self._op(eng, lambda e: e.tensor_scalar(out=out, in0=in0, scalar1=s1, scalar2=s2, op0=op0, op1=op1), reads, writes)

    def stt(self, eng, out, in0, scalar, in1, op0, op1, reads, writes):
        return self._op(eng, lambda e: e.scalar_tensor_tensor(out=out, in0=in0, scalar=scalar, in1=in1, op0=op0, op1=op1), reads, writes)

    def copy(self, eng, out, in_, reads, writes):
        if eng == "act":
            return self._op("act", lambda e: e.activation(out=out, in_=in_, func=AF.Identity), reads, writes)
        return self._op(eng, lambda e: e.tensor_copy(out=out, in_=in_), reads, writes)

    def recip(self, out, in_, reads, writes):
        return self._op("dve", lambda e: e.reciprocal(out=out, in_=in_), reads, writes)

    def memset(self, eng, ap, val, reads, writes):
        return self._op(eng, lambda e: e.memset(ap, val), reads, writes)

    def scan(self, out, d0, d1, reads, writes):
        return self._op("dve", lambda e: e.tensor_tensor_scan(out=out, data0=d0, data1=d1, initial=0.0, op0=ALU.mult, op1=ALU.add), reads, writes)

    def asel(self, out, in_, pattern, cmp, fill, base, cm, reads, writes):
        return self._op("pool", lambda e: e.affine_select(out=out, in_=in_, pattern=pattern, compare_op=cmp, fill=fill, base=base, channel_multiplier=cm), reads, writes)

    def dma(self, q, out, in_, reads, writes):
        return self._op(q, lambda e: e.dma_start(out=out, in_=in_), reads, writes, dma=True)


def build_nc(upto=99, debug=False):
    nc = bass.Bass("TRN2", target_bir_lowering=False)
    k = K(nc)
    P = k.P

    def din(name, shape):
        return nc.dram_tensor(name, shape, F32, kind="ExternalInput").ap()

    x_in = din("x", [S, D])
    c_in = din("c", [8, 128])
    ada_w = din("ada_w", [2, D, 6 * D])
    ada_b = din("ada_b", [2, 6 * D])
    a_w_in = din("a_w_in", [D, 4096])
    a_lb = din("a_lb_logits", [16, 128])
    a_ng = din("a_norm_g", [1, 128])
    a_w_out = din("a_w_out", [D, D])
    kv_ada_w = din("kv_ada_w", [D, 2 * D])
    kv_ada_b = din("kv_ada_b", [1, 2 * D])
    kv_w = din("kv_w", [D, 2056])
    kv_bf = din("kv_b_f", [1, 8])
    k_ng = din("k_norm_g", [1, 128])
    b_w_q = din("b_w_q", [D, 2 * D])
    q_ng = din("q_norm_g", [1, 128])
    b_w_out = din("b_w_out", [D, D])
    w_up = din("ffn_w_up", [2, D, 2 * DFF])
    conv_w = din("ffn_conv_w", [2, 132, 128])
    conv_b = din("ffn_conv_b", [2, 44, 128])
    w_down = din("ffn_w_down", [2, DFF, D])
    out_d = nc.dram_tensor("out", [S, D], F32, kind="ExternalOutput").ap()
    skind = "ExternalOutput" if debug else "Internal"
    modsD = nc.dram_tensor("modsD", [14, D], F32, kind=skind).ap()
    x1_d = nc.dram_tensor("x1", [S, D], F32, kind=skind).ap()
    x2_d = nc.dram_tensor("x2", [S, D], F32, kind=skind).ap()
    x3_d = nc.dram_tensor("x3", [S, D], F32, kind=skind).ap()
    kT_d = nc.dram_tensor("kT_d", [8, 128, S], BF16, kind="Internal").ap()
    qT_d = nc.dram_tensor("qT_d", [8, 128, S], BF16, kind="Internal").ap()
    v_d = nc.dram_tensor("v_d", [S, D], BF16, kind="Internal").ap()
    gate_d = nc.dram_tensor("gate_d", [S, D], F32, kind="Internal").ap()
    bx1, bx2, bx3, bmods, bkT, bqT, bv, bgate, bout = (Buf(n) for n in "x1 x2 x3 mods kT qT v gate out".split())

    import contextlib
    st = contextlib.ExitStack()
    arena_t = st.enter_context(nc.sbuf_tensor("arena", [128, ARENA], U8))
    ps_t = st.enter_context(nc.psum_tensor("psum", [128, 8 * 2048], U8))
    A = Arena(arena_t, ARENA)
    PS = Arena(ps_t, 8 * 2048)

    def T(shape, dt, name=""):
        return A.alloc(shape, dt), Buf(name)

    bankbuf = [Buf(f"bank{i}") for i in range(8)]

    def BV(bank, shape, dt, boff=0):
        n = int(np.prod(shape))
        nb = n * DSZ[dt]
        assert boff + nb <= 2048
        off = bank * 2048 + boff
        ap = ps_t[:, off:off + nb].bitcast(dt)
        if len(shape) == 2:
            ap = ap.rearrange("p (a b) -> p a b", a=shape[0], b=shape[1])
        return ap, bankbuf[bank]

    ident_f, b_idf = T([128], F32)
    ident_bf, b_idb = T([128], BF16)
    ones_f, b_1f = T([128], F32)
    ones_bf, b_1b = T([128], BF16)
    tri_f, b_tri = T([128], F32)
    e0_f, b_e0 = T([128], F32)
    mask2_f, b_m2 = T([128], F32)
    maskc_bf, b_mc = T([128], BF16)
    scanmsk, b_sm = T([TB], F32)
    modcol, b_mod = T([112], F32)
    ccol, b_cc = T([8], F32)
    cact, b_ca = T([8], F32)
    lbcol, b_lb = T([16], F32)
    omlb, b_omlb = T([8], F32)
    nomlb, b_nomlb = T([8], F32)
    gcol, b_gc = T([3], F32)
    convc, b_cv = T([2, 176], F32)
    ss_a, b_ssa = T([NT], F32)
    ss_b, b_ssb = T([NT], F32)
    rstd_all, b_rs = T([NT], F32)
    ncum_all, b_nc = T([NT, 8], F32)
    cfirst_all, b_cf = T([NT, 8], F32)
    bfbc, b_bf = T([8], F32)
    scr, b_scr = T([1], F32)
    junk, b_junk = T([D], BF16)
    pmark = A.mark()

    def phase_reset():
        A.reset(pmark)
        k.barrier(scr)

    def load_cols(dst, bdst, src, n, stg, bstg, pstg, bpstg, rd=()):
        k.dma("sp", stg[0:n, :], src, list(rd), [bstg])
        k.tr(pstg[:, 0:n], stg[0:n, :], ident_f[0:n, 0:n], [bstg, b_idf], [bpstg])
        k.copy("dve", dst, pstg[:, 0:n], [bpstg], [bdst])

    k.memset("pool", ident_f, 0.0, [], [b_idf])
    k.asel(ident_f, ident_f, [[-1, 128]], ALU.not_equal, 1.0, 0, 1, [b_idf], [b_idf])
    k.copy("dve", ident_bf, ident_f, [b_idf], [b_idb])
    k.memset("pool", ones_f, 1.0, [], [b_1f])
    k.memset("pool", ones_bf, 1.0, [], [b_1b])
    k.memset("pool", tri_f, 1.0, [], [b_tri])
    k.asel(tri_f, tri_f, [[1, 128]], ALU.is_ge, 0.0, 0, -1, [b_tri], [b_tri])
    k.copy("dve", maskc_bf, tri_f, [b_tri], [b_mc])
    k.copy("dve", mask2_f, tri_f, [b_tri], [b_m2])
    k.memset("dve", mask2_f[0:64, 64:128], 0.0, [b_m2], [b_m2])
    k.memset("pool", e0_f, 0.0, [], [b_e0])
    k.asel(e0_f, e0_f, [[0, 128]], ALU.not_equal, 1.0, 0, 1, [b_e0], [b_e0])
    k.memset("pool", scanmsk, 1.0, [], [b_sm])
    k.memset("pool", scanmsk.rearrange("p (c t) -> p c t", t=64)[:, :, 0:1], 0.0, [b_sm], [b_sm])
    k.memset("pool", ncum_all, 0.0, [], [b_nc])

    stg, b_stg = T([128], F32)
    pstg, b_pstg = BV(0, [128], F32)
    load_cols(ccol, b_cc, c_in, 8, stg, b_stg, pstg, b_pstg)
    load_cols(lbcol, b_lb, a_lb, 16, stg, b_stg, pstg, b_pstg)
    k.dma("sp", stg[0:1, :], a_ng, [], [b_stg])
    k.dma("sp", stg[1:2, :], k_ng, [], [b_stg])
    k.dma("sp", stg[2:3, :], q_ng, [], [b_stg])
    k.tr(pstg[:, 0:3], stg[0:3, :], ident_f[0:3, 0:3], [b_stg, b_idf], [b_pstg])
    k.copy("dve", gcol, pstg[:, 0:3], [b_pstg], [b_gc])
    for l in range(2):
        load_cols(convc[:, l, 0:128], b_cv, conv_w[l, 0:128, :], 128, stg, b_stg, pstg, b_pstg)
        load_cols(convc[:, l, 128:132], b_cv, conv_w[l, 128:132, :], 4, stg, b_stg, pstg, b_pstg)
        load_cols(convc[:, l, 132:176], b_cv, conv_b[l], 44, stg, b_stg, pstg, b_pstg)
    k.dma("sp", bfbc, kv_bf.partition_broadcast(128), [], [b_bf])
    k.tt("dve", lbcol[:, 0:8], lbcol[:, 8:16], lbcol[:, 0:8], ALU.subtract, [b_lb], [b_lb])
    k.act(lbcol[:, 0:8], lbcol[:, 0:8], AF.Exp, [b_lb], [b_lb])
    k.ts("dve", lbcol[:, 0:8], lbcol[:, 0:8], 1.0, None, ALU.add, None, [b_lb], [b_lb])
    k.recip(lbcol[:, 0:8], lbcol[:, 0:8], [b_lb], [b_lb])
    k.ts("dve", omlb, lbcol[:, 0:8], -1.0, 1.0, ALU.mult, ALU.add, [b_lb], [b_omlb])
    k.ts("dve", nomlb, lbcol[:, 0:8], 1.0, -1.0, ALU.mult, ALU.add, [b_lb], [b_nomlb])
    k.act(cact, ccol, AF.Silu, [b_cc], [b_ca])

    xr = Ring([T([D], F32) for _ in range(3)])

    def sumsq(xt, bxt, ssdst, bss, t):
        k.act(junk, xt, AF.Square, [bxt], [b_junk, bss], accum=ssdst[:, t:t + 1])

    for t in range(NT):
        xt, bxt = xr.next()
        k.dma("sp", xt, x_in[t * 128:(t + 1) * 128, :], [], [bxt])
        sumsq(xt, bxt, ss_a, b_ssa, t)

    wst = Ring([T([3072], F32) for _ in range(3)])
    brow, b_brow = T([3072], F32)
    mrow, b_mrow = T([3072], F32)
    prow = [BV(1 + i, [512], F32) for i in range(6)]
    modflat = modsD.rearrange("(o v) n -> o (v n)", o=1)
    for (Wd_, bd_, row0, width) in ((ada_w[0], ada_b[0:1, :], 0, 6144), (ada_w[1], ada_b[1:2, :], 6, 6144), (kv_ada_w, kv_ada_b, 12, 2048)):
        for c0 in range(0, width, 3072):
            wd = min(3072, width - c0)
            nn = wd // 512
            for kc in range(KC):
                s_, bs_ = wst.next()
                k.dma("sp", s_[:, 0:wd], Wd_[kc * 128:(kc + 1) * 128, c0:c0 + wd], [], [bs_])
                for n in range(nn):
                    k.mm(prow[n][0][0:1, :], cact[:, kc:kc + 1], s_[:, n * 512:(n + 1) * 512], kc == 0, kc == KC - 1, [b_ca, bs_], [prow[n][1]])
            k.dma("sp", brow[0:1, 0:wd], bd_[:, c0:c0 + wd], [], [b_brow])
            for n in range(nn):
                k.tt("dve", mrow[0:1, n * 512:(n + 1) * 512], prow[n][0][0:1, :], brow[0:1, n * 512:(n + 1) * 512], ALU.add, [prow[n][1], b_brow], [b_mrow])
            k.dma("sp", modflat[:, row0 * D + c0: row0 * D + c0 + wd], mrow[0:1, 0:wd], [b_mrow], [bmods])
    load_cols(modcol, b_mod, modsD.rearrange("v (c p) -> (v c) p", p=128), 112, stg, b_stg, pstg, b_pstg, rd=[bmods])
    k.ts("dve", gcol[:, 2:3], gcol[:, 2:3], 128.0 ** -0.5, None, ALU.mult, None, [b_gc], [b_gc])
    for v in (1, 4, 7, 10, 13):
        k.ts("dve", modcol[:, v * 8:(v + 1) * 8], modcol[:, v * 8:(v + 1) * 8], 1.0, None, ALU.add, None, [b_mod], [b_mod])

    def calc_rstd(ss, bss):
        k.act(rstd_all, ss, AF.Ln, [bss], [b_rs], scale=1.0 / D, bias=EPS)
        k.act(rstd_all, rstd_all, AF.Exp, [b_rs], [b_rs], scale=-0.5)

    def g_bcast(dst, bdst, v):
        k.dma("sp", dst, modsD[v:v + 1, :].partition_broadcast(128), [bmods], [bdst])

    def load_w(dst, bdst, src, nk, c0=0, c1=None):
        for kc in range(nk):
            if c1 is None:
                k.dma("pool", dst[:, kc, :], src[kc * 128:(kc + 1) * 128, :], [], [bdst])
            else:
                k.dma("pool", dst[:, kc, 0:c1 - c0], src[kc * 128:(kc + 1) * 128, c0:c1], [], [bdst])

    def make_hT(xt, bxt, t, hn, bhn, tp, btp, variants):
        k.ts("dve", hn, xt, rstd_all[:, t:t + 1], None, ALU.mult, None, [bxt, b_rs], [bhn])
        for kc in range(KC):
            k.tr(tp[:, kc, :], hn[:, kc * 128:(kc + 1) * 128], ident_bf, [bhn, b_idb], [btp])
        for (vs, vh, dst, bdst) in variants:
            for kc in range(KC):
                sc = modcol[:, vs * 8 + kc: vs * 8 + kc + 1]
                sh = modcol[:, vh * 8 + kc: vh * 8 + kc + 1]
                if kc % 2 == 0:
                    k.act(dst[:, kc, :], tp[:, kc, :], AF.Identity, [btp, b_mod], [bdst], scale=sc, bias=sh)
                else:
                    k.ts("dve", dst[:, kc, :], tp[:, kc, :], sc, sh, ALU.mult, ALU.add, [btp, b_mod], [bdst])

    def residual_out(ypair, xres, bxres, gbc, b_gbc, tmp, btmp, xnew, bxnew, dst_d, bdst_d, t, ss_next, b_ssn):
        for half in range(2):
            yp, byp = ypair[half]
            k.tt("dve", tmp[:, half * 512:(half + 1) * 512], yp, gbc[:, half * 512:(half + 1) * 512], ALU.mult, [byp, b_gbc], [btmp])
        k.tt("pool", xnew, tmp, xres, ALU.add, [btmp, bxres], [bxnew])
        k.dma("pool", dst_d[t * 128:(t + 1) * 128, :], xnew, [bxnew], [bdst_d])
        sumsq(xnew, bxnew, ss_next, b_ssn, t)

    def phase_hgrn(x_src, bsrc, x_dst, bdst, ss_in, b_ssin, ss_out, b_ssout):
        phase_reset()
        calc_rstd(ss_in, b_ssin)
        w_in_sb, b_win = T([KC, 4096], BF16)
        w_out_sb, b_wout = T([KC, D], BF16)
        load_w(w_in_sb, b_win, a_w_in, KC)
        load_w(w_out_sb, b_wout, a_w_out, KC)
        g1bc, b_g1 = T([D], F32)
        g_bcast(g1bc, b_g1, 2)
        xr = Ring([T([D], F32) for _ in range(3)])
        hnr = Ring([T([D], BF16) for _ in range(2)])
        hTr = Ring([T([KC, TB], BF16) for _ in range(2)])
        tmpA = Ring([T([TB], F32) for _ in range(2)])
        tmpB = Ring([T([TB], F32) for _ in range(2)])
        tmpC = Ring([T([TB], F32) for _ in range(2)])
        kstr = Ring([T([TB], BF16) for _ in range(2)])
        E_all = A.alloc([8, TB], F32)
        kR_all = A.alloc([8, TB], BF16)
        qE_all = A.alloc([8, TB], BF16)
        gs_all = A.alloc([8, TB], F32)
        ks_all = A.alloc([TPB, 8, 128], BF16)
        v_all = A.alloc([TPB, D], BF16)
        Elast = A.alloc([8, TB // 64], F32)
        st32 = A.alloc([8, 128], F32)
        stb = A.alloc([8, 128], BF16)
        bE, bkR, bqE, bgs, bks, bEl, bst32, bstb = ([Buf() for _ in range(8)] for _ in range(8))
        bv_ = [Buf() for _ in range(TPB)]
        atr = Ring([T([4, 128], BF16) for _ in range(4)])
        oTr = Ring([T([D], F32) for _ in range(2)])
        sq, bsq = T([D], BF16)
        lnr, blnr = T([D], F32)
        on, bon = T([D], BF16)
        tmp, btmp = T([D], F32)
        xnew, bxnew = T([D], F32)
        tp, btp = BV(0, [KC, 128], BF16)
        pa = Ring([BV(1, [TB], F32), BV(2, [TB], F32)])
        pbs = [BV(3, [512], F32), BV(4, [512], F32)]
        pbig = ps_t[:, 3 * 2048:5 * 2048].bitcast(F32)
        pb = Ring(pbs)
        scg = [BV(5, [4, 128], F32), BV(1, [4, 128], F32)]
        pog = [BV(6, [4, 128], F32), BV(2, [4, 128], F32)]
        dsg = [BV(7, [4, 128], F32), BV(3, [4, 128], F32)]
        tpk, btpk = BV(4, [TPB, 128], BF16)
        k.memset("pool", st32, 0.0, [], bst32)
        k.memset("pool", stb, 0.0, [], bstb)
        NCH = TB // 64
        for b in range(DBG.get("nb", NB)):
            hT, bhT = hTr.next()
            for ti in range(TPB):
                t = b * TPB + ti
                xt, bxt = xr.next()
                hn, bhn = hnr.next()
                k.dma("sp", xt, x_src[t * 128:(t + 1) * 128, :], [bsrc], [bxt])
                make_hT(xt, bxt, t, hn, bhn, tp, btp, [(1, 0, hT[:, :, ti * 128:(ti + 1) * 128], bhT)])
            if DBG.get("stage", 9) < 1:
                continue
            for h in range(DBG.get("nh", 8)):
                pf, bpf = pa.next()
                for kc in range(KC):
                    k.mm(pf, w_in_sb[:, kc, 1024 + h * 128:1024 + (h + 1) * 128], hT[:, kc, :], kc == 0, kc == KC - 1, [b_win, bhT], [bpf])
                ta, bta = tmpA.next()
                tb_, btb = tmpB.next()
                tc, btc = tmpC.next()
                k.act(ta, pf, AF.Exp, [bpf], [bta])
                if DBG.get("fsub", 9) < 1:
                    continue
                k.ts("dve", ta, ta, 1.0, None, ALU.add, None, [bta], [bta])
                k.recip(ta, ta, [bta], [bta])
                k.act(tb_, ta, AF.Ln, [bta, b_nomlb], [btb], scale=nomlb[:, h:h + 1], bias=1.0)
                if DBG.get("fsub", 9) < 2:
                    continue
                k.scan(tc, scanmsk, tb_, [b_sm, btb], [btc])
                k.act(E_all[:, h, :], tc, AF.Exp, [btc], [bE[h]])
                k.act(tb_, tc, AF.Exp, [btc], [btb], scale=-1.0)
                k.stt("dve", kR_all[:, h, :], ta, omlb[:, h:h + 1], tb_, ALU.mult, ALU.mult, [bta, btb, b_omlb], [bkR[h]])
                if DBG.get("fsub", 9) < 3:
                    continue
                kst, bkst = kstr.next()
                Ev = E_all[:, h, :].rearrange("p (c t) -> p c t", t=64)
                k.tt("pool", kst.rearrange("p (c t) -> p c t", t=64), kR_all[:, h, :].rearrange("p (c t) -> p c t", t=64),
                     Ev[:, :, 63:64].to_broadcast([128, NCH, 64]), ALU.mult, [bkR[h], bE[h]], [bkst])
                k.copy("pool", Elast[:, h, :], Ev[:, :, 63], [bE[h]], [bEl[h]])
                if DBG.get("fsub", 9) < 4:
                    continue
                for ti in range(TPB):
                    k.tr(tpk[:, ti, :], kst[:, ti * 128:(ti + 1) * 128], ident_bf, [bkst, b_idb], [btpk])
                if DBG.get("fsub", 9) < 5:
                    continue
                for ti in range(TPB):
                    k.copy("dve", ks_all[:, ti, h, :], tpk[:, ti, :], [btpk], [bks[h]])
            if DBG.get("stage", 9) < 2:
                continue
            for ti in range(TPB):
                for half in range(2):
                    pv_, bpv = pb.next()
                    for kc in range(KC):
                        k.mm(pv_, hT[:, kc, ti * 128:(ti + 1) * 128], w_in_sb[:, kc, 2048 + half * 512:2048 + (half + 1) * 512], kc == 0, kc == KC - 1, [bhT, b_win], [bpv])
                    k.copy("dve", v_all[:, ti, half * 512:(half + 1) * 512], pv_, [bpv], [bv_[ti]])
            if DBG.get("stage", 9) < 3:
                continue
            for h in range(8):
                pq, bpq = pa.next()
                for kc in range(KC):
                    k.mm(pq, w_in_sb[:, kc, h * 128:(h + 1) * 128], hT[:, kc, :], kc == 0, kc == KC - 1, [b_win, bhT], [bpq])
                if DBG.get("ssub", 9) < 0:
                    continue
                ta, bta = tmpA.next()
                k.act(ta, pq, DBG.get("qf", AF.Silu), [bpq], [bta])
                if DBG.get("ssub", 9) < 1:
                    continue
                k.tt("pool", qE_all[:, h, :], ta, E_all[:, h, :], ALU.mult, [bta, bE[h]], [bqE[h]])
                if DBG.get("ssub", 9) < 2:
                    continue
                pg, bpg = pa.next()
                for kc in range(KC):
                    k.mm(pg, w_in_sb[:, kc, 3072 + h * 128:3072 + (h + 1) * 128], hT[:, kc, :], kc == 0, kc == KC - 1, [b_win, bhT], [bpg])
                k.act(gs_all[:, h, :], pg, AF.Silu, [bpg], [bgs[h]])
            if DBG.get("stage", 9) < 4:
                continue
            for ti in range(TPB):
                t = b * TPB + ti
                cs = slice(ti * 128, (ti + 1) * 128)
                oT, boT = oTr.next()
                oT3 = oT.rearrange("p (h t) -> p h t", t=128)
                atgs = [atr.next() for _ in range(2)]
                for g in range(2):
                    scb, bscb = scg[g]
                    for i_, h in enumerate(range(g * 4, g * 4 + 4)):
                        k.mm(scb[:, i_, :], kR_all[:, h, cs], qE_all[:, h, cs], True, True, [bkR[h], bqE[h]], [bscb])
                for g in range(2):
                    scb, bscb = scg[g]
                    atg, batg = atgs[g]
                    for i_ in range(4):
                        k.tt("dve", atg[:, i_, :], scb[:, i_, :], mask2_f, ALU.mult, [bscb, b_m2], [batg])
                for c in range(2):
                    cc = slice(c * 64, (c + 1) * 64)
                    pr = slice(c * 64, (c + 1) * 64)
                    ch = ti * 2 + c
                    for g in range(2):
                        atg, batg = atgs[g]
                        pob, bpob = pog[g]
                        dsb, bdsb = dsg[g]
                        for i_, h in enumerate(range(g * 4, g * 4 + 4)):
                            k.mm(pob[:, i_, cc], v_all[:, ti, h * 128:(h + 1) * 128], atg[:, i_, cc], True, False, [bv_[ti], batg], [bpob])
                            k.mm(pob[:, i_, cc], stb[:, h, :], qE_all[:, h, ti * 128 + c * 64: ti * 128 + (c + 1) * 64], False, True, [bstb[h], bqE[h]], [bpob])
                            k.mm(dsb[:, i_, :], ks_all[pr, ti, h, :], v_all[pr, ti, h * 128:(h + 1) * 128], True, True, [bks[h], bv_[ti]], [bdsb])
                    for g in range(2):
                        dsb, bdsb = dsg[g]
                        for i_, h in enumerate(range(g * 4, g * 4 + 4)):
                            k.stt("dve", st32[:, h, :], st32[:, h, :], Elast[:, h, ch:ch + 1], dsb[:, i_, :], ALU.mult, ALU.add, [bst32[h], bEl[h], bdsb], [bst32[h]])
                            k.copy("pool", stb[:, h, :], st32[:, h, :], [bst32[h]], [bstb[h]])
                for g in range(2):
                    pob, bpob = pog[g]
                    for i_, h in enumerate(range(g * 4, g * 4 + 4)):
                        k.copy("dve", oT3[:, h, :], pob[:, i_, :], [bpob], [boT])
                if DBG.get("stage", 9) < 5:
                    continue
                k.act(sq, oT, AF.Square, [boT], [bsq])
                for half in range(2):
                    k.mm(pbs[half][0], ones_bf, sq[:, half * 512:(half + 1) * 512], True, True, [b_1b, bsq], [pbs[half][1]])
                k.act(lnr, pbig, AF.Ln, [pbs[0][1], pbs[1][1]], [blnr], scale=1.0 / 128, bias=EPS)
                k.act(lnr, lnr, AF.Exp, [blnr], [blnr], scale=-0.5)
                k.stt("dve", lnr, oT, gcol[:, 0:1], lnr, ALU.mult, ALU.mult, [boT, b_gc, blnr], [blnr])
                k.tt("pool", on.rearrange("p (h t) -> p h t", t=128), lnr.rearrange("p (h t) -> p h t", t=128), gs_all[:, :, cs], ALU.mult, [blnr] + bgs, [bon])
                on3 = on.rearrange("p (h t) -> p h t", t=128)
                xres, bxres = xr.next()
                k.dma("sp", xres, x_src[t * 128:(t + 1) * 128, :], [bsrc], [bxres])
                ypair = []
                for half in range(2):
                    yp, byp = pbs[half]
                    for h in range(8):
                        k.mm(yp, on3[:, h, :], w_out_sb[:, h, half * 512:(half + 1) * 512], h == 0, h == 7, [bon, b_wout], [byp])
                    ypair.append((yp, byp))
                residual_out(ypair, xres, bxres, g1bc, b_g1, tmp, btmp, xnew, bxnew, x_dst, bdst, t, ss_out, b_ssout)

    def phase_ffn(l, x_src, bsrc, x_dst, bdst, ss_in, b_ssin, ss_out, b_ssout, vsc, vsh, vg):
        phase_reset()
        calc_rstd(ss_in, b_ssin)
        wup, b_wup = T([KC, 2 * DFF], BF16)
        wdn, b_wdn = T([NJ, D], BF16)
        load_w(wup, b_wup, w_up[l], KC)
        load_w(wdn, b_wdn, w_down[l], NJ)
        gbc, b_gbc = T([D], F32)
        g_bcast(gbc, b_gbc, vg)
        xr = Ring([T([D], F32) for _ in range(2)])
        hnr = Ring([T([D], BF16) for _ in range(2)])
        hTr = Ring([T([KC, TB + 2], BF16) for _ in range(2)])
        t0r = Ring([T([TB], F32) for _ in range(4)])
        sgr = Ring([T([TB], F32) for _ in range(2)])
        actr = Ring([T([NJ, TB], BF16) for _ in range(2)])
        tmp, btmp = T([D], F32)
        xnew, bxnew = T([D], F32)
        tp, btp = BV(0, [KC, 128], BF16)
        pur = Ring([BV(1 + i, [TB + 2], F32) for i in range(4)])
        pbs = [BV(5, [512], F32), BV(6, [512], F32)]
        prev = None
        for b in range(DBG.get('fnb', NB)):
            hT, bhT = hTr.next()
            if prev is None:
                k.memset("pool", hT[:, :, 0:2], 0.0, [], [bhT])
            else:
                k.copy("pool", hT[:, :, 0:2], prev[0][:, :, TB:TB + 2], [prev[1]], [bhT])
            for ti in range(TPB):
                t = b * TPB + ti
                xt, bxt = xr.next()
                hn, bhn = hnr.next()
                k.dma("sp", xt, x_src[t * 128:(t + 1) * 128, :], [bsrc], [bxt])
                make_hT(xt, bxt, t, hn, bhn, tp, btp, [(vsc, vsh, hT[:, :, 2 + ti * 128:2 + (ti + 1) * 128], bhT)])
            prev = (hT, bhT)
            if DBG.get('fstage', 9) < 1:
                continue
            aT, baT = actr.next()
            for j in range(DBG.get('fnj', NJ)):
                res = []
                for which, cidx in ((0, j), (1, NJ + j)):
                    pu, bpu = pur.next()
                    for kc in range(KC):
                        k.mm(pu, wup[:, kc, cidx * 128:(cidx + 1) * 128], hT[:, kc, :], kc == 0, kc == KC - 1, [b_wup, bhT], [bpu])
                    t0, bt0 = t0r.next()
                    k.ts("dve", t0, pu[:, 2:2 + TB], convc[:, l, 88 + cidx:89 + cidx], convc[:, l, 132 + cidx:133 + cidx], ALU.mult, ALU.add, [bpu, b_cv], [bt0])
                    k.stt("dve", t0, pu[:, 1:1 + TB], convc[:, l, 44 + cidx:45 + cidx], t0, ALU.mult, ALU.add, [bpu, b_cv, bt0], [bt0])
                    k.stt("dve", t0, pu[:, 0:TB], convc[:, l, cidx:cidx + 1], t0, ALU.mult, ALU.add, [bpu, b_cv, bt0], [bt0])
                    res.append((t0, bt0))
                if DBG.get('fstage', 9) < 3:
                    continue
                sg, bsg = sgr.next()
                k.act(sg, res[0][0], AF.Silu, [res[0][1]], [bsg])
                k.tt("pool", aT[:, j, :], sg, res[1][0], ALU.mult, [bsg, res[1][1]], [baT])
            if DBG.get('fstage', 9) < 4:
                continue
            for ti in range(TPB):
                t = b * TPB + ti
                xres, bxres = xr.next()
                k.dma("sp", xres, x_src[t * 128:(t + 1) * 128, :], [bsrc], [bxres])
                ypair = []
                for half in range(2):
                    yp, byp = pbs[half]
                    for j in range(NJ):
                        k.mm(yp, aT[:, j, ti * 128:(ti + 1) * 128], wdn[:, j, half * 512:(half + 1) * 512], j == 0, j == NJ - 1, [baT, b_wdn], [byp])
                    ypair.append((yp, byp))
                residual_out(ypair, xres, bxres, gbc, b_gbc, tmp, btmp, xnew, bxnew, x_dst, bdst, t, ss_out, b_ssout)

    def phase_fox_a(x_src, bsrc, ss_in, b_ssin):
        phase_reset()
        calc_rstd(ss_in, b_ssin)
        kvw, b_kvw = T([KC, 2056], BF16)
        wq, b_wq = T([KC, 2048], BF16)
        load_w(kvw, b_kvw, kv_w, KC)
        load_w(wq, b_wq, b_w_q, KC)
        xr = Ring([T([D], F32) for _ in range(3)])
        hnr = Ring([T([D], BF16) for _ in range(2)])
        hkvr = Ring([T([KC, TB], BF16) for _ in range(2)])
        h1r = Ring([T([KC, TB], BF16) for _ in range(2)])
        kTbr = Ring([T([8, TB], BF16) for _ in range(2)])
        qTbr = Ring([T([8, TB], BF16) for _ in range(2)])
        sqr = Ring([T([TB], BF16) for _ in range(2)])
        lnrr = Ring([T([TB], F32) for _ in range(2)])
        vbr = Ring([T([D], BF16) for _ in range(2)])
        gtr = Ring([T([D], F32) for _ in range(2)])
        ger = Ring([T([512], F32) for _ in range(2)])
        lfr = Ring([T([8], F32) for _ in range(2)])
        run, brun = T([8], F32)
        tp, btp = BV(0, [KC, 128], BF16)
        pa = Ring([BV(1, [TB], F32), BV(2, [TB], F32)])
        pssr = Ring([BV(3, [TB], F32), BV(4, [TB], F32)])
        pb = Ring([BV(5, [512], F32), BV(6, [512], F32)])
        pf8, bpf8 = BV(7, [8], F32, 0)
        pcum, bpcum = BV(7, [8], F32, 512)
        pcf, bpcf = BV(7, [8], F32, 1024)
        k.memset("pool", run, 0.0, [], [brun])
        kTv = kT_d.rearrange("h d t -> d h t")
        qTv = qT_d.rearrange("h d t -> d h t")
        for b in range(NB):
            hkv, bhkv = hkvr.next()
            h1, bh1 = h1r.next()
            for ti in range(TPB):
                t = b * TPB + ti
                xt, bxt = xr.next()
                hn, bhn = hnr.next()
                k.dma("sp", xt, x_src[t * 128:(t + 1) * 128, :], [bsrc], [bxt])
                tsl = slice(ti * 128, (ti + 1) * 128)
                make_hT(xt, bxt, t, hn, bhn, tp, btp, [(13, 12, hkv[:, :, tsl], bhkv), (7, 6, h1[:, :, tsl], bh1)])
            kTb, bkTb = kTbr.next()
            qTb, bqTb = qTbr.next()
            for (W, bW, hs, bhs, gi, dstb, bdstb) in ((kvw, b_kvw, hkv, bhkv, 1, kTb, bkTb), (wq, b_wq, h1, bh1, 2, qTb, bqTb)):
                for h in range(8):
                    pk, bpk = pa.next()
                    for kc in range(KC):
                        k.mm(pk, W[:, kc, h * 128:(h + 1) * 128], hs[:, kc, :], kc == 0, kc == KC - 1, [bW, bhs], [bpk])
                    sq, bsq = sqr.next()
                    k.act(sq, pk, AF.Square, [bpk], [bsq])
                    pss, bpss = pssr.next()
                    k.mm(pss, ones_bf, sq, True, True, [b_1b, bsq], [bpss])
                    lnr, blnr = lnrr.next()
                    k.act(lnr, pss, AF.Ln, [bpss], [blnr], scale=1.0 / 128, bias=EPS)
                    k.act(lnr, lnr, AF.Exp, [blnr], [blnr], scale=-0.5)
                    k.stt("dve", dstb[:, h, :], pk, gcol[:, gi:gi + 1], lnr, ALU.mult, ALU.mult, [bpk, b_gc, blnr], [bdstb])
            k.dma("pool", kTv[:, :, b * TB:(b + 1) * TB], kTb, [bkTb], [bkT])
            k.dma("pool", qTv[:, :, b * TB:(b + 1) * TB], qTb, [bqTb], [bqT])
            for ti in range(TPB):
                t = b * TPB + ti
                tsl = slice(ti * 128, (ti + 1) * 128)
                vb, bvb = vbr.next()
                for half in range(2):
                    pv_, bpv = pb.next()
                    for kc in range(KC):
                        k.mm(pv_, hkv[:, kc, tsl], kvw[:, kc, 1024 + half * 512:1024 + (half + 1) * 512], kc == 0, kc == KC - 1, [bhkv, b_kvw], [bpv])
                    k.copy("dve", vb[:, half * 512:(half + 1) * 512], pv_, [bpv], [bvb])
                k.dma("pool", v_d[t * 128:(t + 1) * 128, :], vb, [bvb], [bv])
                gt, bgt = gtr.next()
                for half in range(2):
                    pg, bpg = pb.next()
                    for kc in range(KC):
                        k.mm(pg, h1[:, kc, tsl], wq[:, kc, 1024 + half * 512:1024 + (half + 1) * 512], kc == 0, kc == KC - 1, [bh1, b_wq], [bpg])
                    ge, bge = ger.next()
                    k.act(ge, pg, AF.Exp, [bpg], [bge], scale=-1.0)
                    k.ts("dve", ge, ge, 1.0, None, ALU.add, None, [bge], [bge])
                    k.recip(gt[:, half * 512:(half + 1) * 512], ge, [bge], [bgt])
                k.dma("pool", gate_d[t * 128:(t + 1) * 128, :], gt, [bgt], [bgate])
                for kc in range(KC):
                    k.mm(pf8, hkv[:, kc, tsl], kvw[:, kc, 2048:2056], kc == 0, kc == KC - 1, [bhkv, b_kvw], [bpf8])
                lf, blf = lfr.next()
                k.tt("dve", lf, pf8, bfbc, ALU.add, [bpf8, b_bf], [blf])
                k.act(lf, lf, AF.Exp, [blf], [blf], scale=-1.0)
                k.act(lf, lf, AF.Ln, [blf], [blf], bias=1.0)
                k.mm(pcum, tri_f, lf, True, False, [b_tri, blf], [bpcum])
                k.mm(pcum, ones_f, run, False, True, [b_1f, brun], [bpcum])
                k.copy("dve", ncum_all[:, t, :], pcum, [bpcum], [b_nc])
                k.tt("dve", run, run, lf, ALU.add, [brun, blf], [brun])
                k.mm(pcf, e0_f, ncum_all[:, t, :], True, True, [b_e0, b_nc], [bpcf])
                k.copy("dve", cfirst_all[:, t, :], pcf, [bpcf], [b_cf])

    def phase_fox_bc(x_src, bsrc, x_dst, bdst, ss_out, b_ssout):
        phase_reset()
        wo, b_wo = T([KC, D], BF16)
        load_w(wo, b_wo, b_w_out, KC)
        gbc, b_gbc = T([D], F32)
        g_bcast(gbc, b_gbc, 8)
        o_all = A.alloc([NT, D], BF16)
        bo = [Buf() for _ in range(NT)]
        kThr = Ring([T([S], BF16) for _ in range(2)])
        qThr = Ring([T([S], BF16) for _ in range(2)])
        vhr = Ring([T([NT, 132], BF16) for _ in range(2)])
        ghr = Ring([T([NT, 128], F32) for _ in range(2)])
        bir = Ring([T([NT, NT], F32) for _ in range(2)])
        ptr_ = Ring([T([128], BF16) for _ in range(4)])
        recr = Ring([T([1], F32) for _ in range(2)])
        xr = Ring([T([D], F32) for _ in range(2)])
        oTr = Ring([T([KC, 128], BF16) for _ in range(2)])
        tmp, btmp = T([D], F32)
        xnew, bxnew = T([D], F32)
        sr = Ring([BV(i, [128], F32) for i in range(3)])
        por = Ring([BV(3, [132], F32), BV(4, [132], F32)])
        tp, btp = BV(5, [KC, 128], BF16)
        pbs = [BV(6, [512], F32), BV(7, [512], F32)]
        for (vh, bvh) in vhr.items:
            k.memset("pool", vh[:, :, 128:129], 1.0, [], [bvh])
        vdv = v_d.rearrange("(j p) d -> p j d", p=128)
        gdv = gate_d.rearrange("(j p) d -> p j d", p=128)
        for h in range(8):
            kTh, bkTh = kThr.next()
            qTh, bqTh = qThr.next()
            vh, bvh = vhr.next()
            gh, bgh = ghr.next()
            bias, bbias = bir.next()
            k.dma("sp", kTh, kT_d[h], [bkT], [bkTh])
            k.dma("sp", qTh, qT_d[h], [bqT], [bqTh])
            k.dma("sp", vh[:, :, 0:128], vdv[:, :, h * 128:(h + 1) * 128], [bv], [bvh])
            k.dma("sp", gh, gdv[:, :, h * 128:(h + 1) * 128], [bgate], [bgh])
            for j in range(NT):
                k.ts("dve", bias[:, j, :], cfirst_all[:, :, h], -1.0, ncum_all[:, j, h:h + 1], ALU.mult, ALU.add, [b_cf, b_nc], [bbias])
            pairs = [(Q, j) for Q in range(NT) for j in range(Q + 1)]
            LA = 2
            sbuf_of = {}
            po_of = {}
            for n in range(len(pairs) + LA):
                if n < len(pairs):
                    Q, j = pairs[n]
                    s_, bs = sr.next()
                    sbuf_of[n] = (s_, bs)
                    k.mm(s_, kTh[:, j * 128:(j + 1) * 128], qTh[:, Q * 128:(Q + 1) * 128], True, True, [bkTh, bqTh], [bs])
                m = n - LA
                if m < 0:
                    continue
                Q, j = pairs[m]
                if j == 0:
                    po_of[Q] = por.next()
                po, bpo = po_of[Q]
                s_, bs = sbuf_of.pop(m)
                pt, bpt = ptr_.next()
                k.act(pt, s_, AF.Exp, [bs, bbias], [bpt], bias=bias[:, j, Q:Q + 1])
                if j == Q:
                    k.tt("pool", pt, pt, maskc_bf, ALU.mult, [bpt, b_mc], [bpt])
                k.mm(po[:, 0:129], pt, vh[:, j, 0:129], j == 0, j == Q, [bpt, bvh], [bpo])
                if j == Q:
                    rec, brec = recr.next()
                    k.recip(rec, po[:, 128:129], [bpo], [brec])
                    k.stt("dve", o_all[:, Q, h * 128:(h + 1) * 128], po[:, 0:128], rec, gh[:, Q, :], ALU.mult, ALU.mult, [bpo, brec, bgh], [bo[Q]])
        for t in range(NT):
            for kc in range(KC):
                k.tr(tp[:, kc, :], o_all[:, t, kc * 128:(kc + 1) * 128], ident_bf, [bo[t], b_idb], [btp])
            oT, boT = oTr.next()
            k.copy("dve", oT.rearrange("p a b -> p (a b)"), tp.rearrange("p a b -> p (a b)"), [btp], [boT])
            xres, bxres = xr.next()
            k.dma("sp", xres, x_src[t * 128:(t + 1) * 128, :], [bsrc], [bxres])
            ypair = []
            for half in range(2):
                yp, byp = pbs[half]
                for kc in range(KC):
                    k.mm(yp, oT[:, kc, :], wo[:, kc, half * 512:(half + 1) * 512], kc == 0, kc == KC - 1, [boT, b_wo], [byp])
                ypair.append((yp, byp))
            residual_out(ypair, xres, bxres, gbc, b_gbc, tmp, btmp, xnew, bxnew, x_dst, bdst, t, ss_out, b_ssout)

    bxin = Buf("xin")
    calc = None
    if upto >= 1:
        phase_hgrn(x_in, bxin, x1_d, bx1, ss_a, b_ssa, ss_b, b_ssb)
    if upto >= 2:
        phase_ffn(0, x1_d, bx1, x2_d, bx2, ss_b, b_ssb, ss_a, b_ssa, 4, 3, 5)
    if upto >= 3:
        phase_fox_a(x2_d, bx2, ss_a, b_ssa)
    if upto >= 4:
        phase_fox_bc(x2_d, bx2, x3_d, bx3, ss_b, b_ssb)
    if upto >= 5:
        phase_ffn(1, x3_d, bx3, out_d, bout, ss_b, b_ssb, ss_a, b_ssa, 10, 9, 11)

    k.barrier(scr)
    fin_d = nc.dram_tensor("fin_d", [128, 1], F32, kind="Internal").ap()
    k.dma("sp", fin_d, scr, [], [Buf("fin")])
    P.finalize()
    sems = {e: [st.enter_context(nc.semaphore(f"s_{e}{i}")) for i in range(P.nepoch[e])] for e in CENG}
    dsems = [st.enter_context(nc.semaphore(f"d{i}")) for i in range(P.NDMA_SEM)]
    block = st.enter_context(nc.Block())
    P.emit(block, sems, dsems)
    st.close()
    return nc


_NC_CACHE = {}


def _in_maps(inputs):
    f = lambda a: np.ascontiguousarray(np.asarray(a, dtype=np.float32))
    x = f(inputs["x"])
    c = f(inputs["c"])
    shared = dict(
        ada_w=f(inputs["ada_w"]), ada_b=f(inputs["ada_b"]),
        a_w_in=f(inputs["a_w_in"]).reshape(D, 4096),
        a_lb_logits=f(inputs["a_lb_logits"]).reshape(16, 128),
        a_norm_g=f(inputs["a_norm_g"]).reshape(1, 128),
        a_w_out=f(inputs["a_w_out"]).reshape(D, D),
        kv_ada_w=f(inputs["kv_ada_w"]), kv_ada_b=f(inputs["kv_ada_b"]).reshape(1, 2 * D),
        kv_w=f(inputs["kv_w"]), kv_b_f=f(inputs["kv_b_f"]).reshape(1, 8),
        k_norm_g=f(inputs["k_norm_g"]).reshape(1, 128),
        b_w_q=f(inputs["b_w_q"]).reshape(D, 2 * D),
        q_norm_g=f(inputs["q_norm_g"]).reshape(1, 128),
        b_w_out=f(inputs["b_w_out"]).reshape(D, D),
        ffn_w_up=f(inputs["ffn_w_up"]),
        ffn_conv_w=f(inputs["ffn_conv_w"]).reshape(2, 132, 128),
        ffn_conv_b=f(inputs["ffn_conv_b"]).reshape(2, 44, 128),
        ffn_w_down=f(inputs["ffn_w_down"]),
    )
    maps = []
    for b in range(8):
        m = dict(shared)
        m["x"] = x[b]
        m["c"] = c[b].reshape(8, 128)
        maps.append(m)
    return maps


def kernel(**inputs):
    if "nc" not in _NC_CACHE:
        _NC_CACHE["nc"] = build_nc()
    nc = _NC_CACHE["nc"]
    res = run_bass_kernel_spmd(nc, _in_maps(inputs), core_ids=list(range(8)))
    return np.stack([np.asarray(r["out"], dtype=np.float32) for r in res.results], axis=0)
```

```python
import numpy as np
import concourse.bass as bass
import concourse.mybir as mybir
from concourse.bass_utils import run_bass_kernel_spmd

F32 = mybir.dt.float32
BF16 = mybir.dt.bfloat16
U8 = mybir.dt.uint8
AF = mybir.ActivationFunctionType
ALU = mybir.AluOpType
AX = mybir.AxisListType
DSZ = {F32: 4, BF16: 2, U8: 1}

CENG = ("pe", "act", "dve", "pool")
EPOCH = 12000
STRICT = True


class Buf:
    __slots__ = ("name", "w", "r")

    def __init__(self, name=""):
        self.name = name
        self.w = None
        self.r = {}


class Op:
    __slots__ = ("id", "eng", "fn", "deps", "dma", "seq", "sig", "cnt", "need", "clock", "dsem", "dval", "inc")


class Prog:
    def __init__(self, nc):
        self.nc = nc
        self.ops = []
        self.ndma = 0
        self.dma_last = {}
        self.NDMA_SEM = 40
        self.NHW = 24
        self.nsw = 0

    def op(self, eng, fn, reads=(), writes=(), dma=False):
        o = Op()
        o.id = len(self.ops)
        o.eng = eng
        o.fn = fn
        o.dma = dma
        o.sig = False
        o.cnt = None
        deps = {}
        for b in reads:
            if b.w is not None:
                deps[b.w] = True
        for b in writes:
            if b.w is not None:
                deps.setdefault(b.w, False)
            for r in b.r.values():
                for rid in r:
                    deps.setdefault(rid, False)
        if dma:
            if eng == "pool":
                slot = self.NHW + self.nsw % (self.NDMA_SEM - self.NHW)
                self.nsw += 1
            else:
                slot = self.ndma % self.NHW
                self.ndma += 1
            prev = self.dma_last.get(slot)
            if prev is not None:
                deps.setdefault(prev.id, False)
                o.dval = prev.dval + 16
            else:
                o.dval = 16
            o.dsem = slot
            self.dma_last[slot] = o
        deps.pop(o.id, None)
        o.deps = deps
        for b in reads:
            if dma:
                b.r.setdefault("dma", []).append(o.id)
            else:
                b.r[eng] = [o.id]
        for b in writes:
            b.w = o.id
            b.r = {}
        self.ops.append(o)
        return o

    def finalize(self):
        ops = self.ops
        seqc = {e: 0 for e in CENG}
        known = {e: {c: 0 for c in CENG} for e in CENG + ("sp",)}
        kdma = {e: set() for e in CENG + ("sp",)}
        for o in ops:
            A = o.eng
            if not o.dma:
                seqc[A] += 1
                o.seq = seqc[A]
            else:
                o.seq = 0
            kn = known[A]
            need = []
            dl = sorted(o.deps.items(), key=lambda kv: -kv[0])
            for xid, raw in dl:
                X = ops[xid]
                if X.dma:
                    if xid in kdma[A]:
                        continue
                    need.append(xid)
                    kdma[A].add(xid)
                    for c in CENG:
                        if X.clock[c] > kn[c]:
                            kn[c] = X.clock[c]
                    continue
                E = X.eng
                if X.seq <= kn[E]:
                    continue
                if (not o.dma) and E == A:
                    if A == "pe" or not (raw or STRICT):
                        continue
                need.append(xid)
                X.sig = True
                kn[E] = X.seq
                for c in CENG:
                    if X.clock[c] > kn[c]:
                        kn[c] = X.clock[c]
            o.need = need
            ck = dict(kn)
            if len(kdma[A]) > 512:
                kdma[A] = set(sorted(kdma[A])[-256:])
            o.clock = ck
        cnt = {e: 0 for e in CENG}
        for o in ops:
            if (not o.dma) and o.sig:
                cnt[o.eng] += 1
                o.cnt = cnt[o.eng]
        self.nepoch = {e: cnt[e] // EPOCH + 1 for e in CENG}

    def emit(self, block, sems, dsems):
        ops = self.ops

        def semval(X):
            if X.dma:
                return dsems[X.dsem], X.dval
            k = (X.cnt - 1) // EPOCH
            return sems[X.eng][k], (X.cnt - 1) % EPOCH + 1

        def run(engname):
            def body(e):
                for o in ops:
                    if o.eng != engname:
                        continue
                    for xid in o.need:
                        s, v = semval(ops[xid])
                        e.wait_ge(s, v)
                    ins = o.fn(e)
                    if o.dma:
                        ins.then_inc(dsems[o.dsem], 16)
                    elif o.sig:
                        s, _ = semval(o)
                        ins.then_inc(s, 1)
                if engname == "sp":
                    for slot, o in self.dma_last.items():
                        e.wait_ge(dsems[slot], o.dval)
            return body

        block.tensor(run("pe"))
        block.scalar(run("act"))
        block.vector(run("dve"))
        block.gpsimd(run("pool"))
        block.sync(run("sp"))


class Arena:
    def __init__(self, t, size, part=128):
        self.t = t
        self.size = size
        self.off = 0

    def alloc(self, free_shape, dtype, align=64):
        n = int(np.prod(free_shape))
        nb = n * DSZ[dtype]
        off = (self.off + align - 1) // align * align
        assert off + nb <= self.size, f"arena overflow {off + nb} > {self.size}"
        self.off = off + nb
        ap = self.t[:, off:off + nb].bitcast(dtype)
        if len(free_shape) == 2:
            ap = ap.rearrange("p (a b) -> p a b", a=free_shape[0], b=free_shape[1])
        elif len(free_shape) == 3:
            ap = ap.rearrange("p (a b c) -> p a b c", a=free_shape[0], b=free_shape[1], c=free_shape[2])
        return ap

    def mark(self):
        return self.off

    def reset(self, m):
        self.off = m


S = 4096
D = 1024
NT = 32
KC = 8
TB = 256
NB = S // TB
TPB = TB // 128
DFF = 2816
NJ = 22
EPS = 1e-6
ARENA = 206 * 1024
DBG = {}


class Ring:
    def __init__(self, items):
        self.items = items
        self.i = 0

    def next(self):
        it = self.items[self.i % len(self.items)]
        self.i += 1
        return it


class K:
    def __init__(self, nc):
        self.nc = nc
        self.P = Prog(nc)
        self.Y = Buf("phase")

    def _op(self, eng, fn, reads, writes, dma=False):
        return self.P.op(eng, fn, reads=list(reads) + [self.Y], writes=list(writes), dma=dma)

    def barrier(self, scr):
        self.P.op("dve", lambda e: e.memset(scr, 0.0), reads=[], writes=[self.Y])

    def mm(self, out, lhsT, rhs, start, stop, reads, writes):
        return self._op("pe", lambda e: e.matmul(out, lhsT=lhsT, rhs=rhs, start=start, stop=stop), reads, writes)

    def tr(self, out, in_, ident, reads, writes):
        return self._op("pe", lambda e: e.transpose(out=out, in_=in_, identity=ident), reads, writes)

    def act(self, out, in_, func, reads, writes, scale=1.0, bias=0.0, accum=None):
        if accum is None:
            return self._op("act", lambda e: e.activation(out=out, in_=in_, func=func, bias=bias, scale=scale), reads, writes)
        return self._op("act", lambda e: e.activation(out=out, in_=in_, func=func, bias=bias, scale=scale, accum_out=accum), reads, writes)

    def tt(self, eng, out, in0, in1, op, reads, writes):
        return self._op(eng, lambda e: e.tensor_tensor(out=out, in0=in0, in1=in1, op=op), reads, writes)

    def ts(self, eng, out, in0, s1, s2, op0, op1, reads, writes):
        if s2 is None:
            return self._op(eng, lambda e: e.tensor_scalar(out=out, in0=in0, scalar1=s1, scalar2=None, op0=op0), reads, writes)
        return self._op(eng, lambda e: e.tensor_scalar(out=out, in0=in0, scalar1=s1, scalar2=s2, op0=op0, op1=op1), reads, writes)

    def stt(self, eng, out, in0, scalar, in1, op0, op1, reads, writes):
        return self._op(eng, lambda e: e.scalar_tensor_tensor(out=out, in0=in0, scalar=scalar, in1=in1, op0=op0, op1=op1), reads, writes)

    def copy(self, eng, out, in_, reads, writes):
        if eng == "act":
            return self._op("act", lambda e: e.activation(out=out, in_=in_, func=AF.Identity), reads, writes)
        return self._op(eng, lambda e: e.tensor_copy(out=out, in_=in_), reads, writes)

    def recip(self, out, in_, reads, writes):
        return self._op("dve", lambda e: e.reciprocal(out=out, in_=in_), reads, writes)

    def memset(self, eng, ap, val, reads, writes):
        return self._op(eng, lambda e: e.memset(ap, val), reads, writes)

    def scan(self, out, d0, d1, reads, writes):
        return self._op("dve", lambda e: e.tensor_tensor_scan(out=out, data0=d0, data1=d1, initial=0.0, op0=ALU.mult, op1=ALU.add), reads, writes)

    def asel(self, out, in_, pattern, cmp, fill, base, cm, reads, writes):
        return self._op("pool", lambda e: e.affine_select(out=out, in_=in_, pattern=pattern, compare_op=cmp, fill=fill, base=base, channel_multiplier=cm), reads, writes)

    def dma(self, q, out, in_, reads, writes):
        return self._op(q, lambda e: e.dma_start(out=out, in_=in_), reads, writes, dma=True)


def build_nc(upto=99, debug=False):
    nc = bass.Bass("TRN2", target_bir_lowering=False)
    k = K(nc)
    P = k.P

    def din(name, shape):
        return nc.dram_tensor(name, shape, F32, kind="ExternalInput").ap()

    x_in = din("x", [S, D])
    c_in = din("c", [8, 128])
    ada_w = din("ada_w", [2, D, 6 * D])
    ada_b = din("ada_b", [2, 6 * D])
    a_w_in = din("a_w_in", [D, 4096])
    a_lb = din("a_lb_logits", [16, 128])
    a_ng = din("a_norm_g", [1, 128])
    a_w_out = din("a_w_out", [D, D])
    kv_ada_w = din("kv_ada_w", [D, 2 * D])
    kv_ada_b = din("kv_ada_b", [1, 2 * D])
    kv_w = din("kv_w", [D, 2056])
    kv_bf = din("kv_b_f", [1, 8])
    k_ng = din("k_norm_g", [1, 128])
    b_w_q = din("b_w_q", [D, 2 * D])
    q_ng = din("q_norm_g", [1, 128])
    b_w_out = din("b_w_out", [D, D])
    w_up = din("ffn_w_up", [2, D, 2 * DFF])
    conv_w = din("ffn_conv_w", [2, 132, 128])
    conv_b = din("ffn_conv_b", [2, 44, 128])
    w_down = din("ffn_w_down", [2, DFF, D])
    out_d = nc.dram_tensor("out", [S, D], F32, kind="ExternalOutput").ap()
    skind = "ExternalOutput" if debug else "Internal"
    modsD = nc.dram_tensor("modsD", [14, D], F32, kind=skind).ap()
    x1_d = nc.dram_tensor("x1", [S, D], F32, kind=skind).ap()
    x2_d = nc.dram_tensor("x2", [S, D], F32, kind=skind).ap()
    x3_d = nc.dram_tensor("x3", [S, D], F32, kind=skind).ap()
    kT_d = nc.dram_tensor("kT_d", [8, 128, S], BF16, kind="Internal").ap()
    qT_d = nc.dram_tensor("qT_d", [8, 128, S], BF16, kind="Internal").ap()
    v_d = nc.dram_tensor("v_d", [S, D], BF16, kind="Internal").ap()
    gate_d = nc.dram_tensor("gate_d", [S, D], F32, kind="Internal").ap()
    bx1, bx2, bx3, bmods, bkT, bqT, bv, bgate, bout = (Buf(n) for n in "x1 x2 x3 mods kT qT v gate out".split())

    import contextlib
    st = contextlib.ExitStack()
    arena_t = st.enter_context(nc.sbuf_tensor("arena", [128, ARENA], U8))
    ps_t = st.enter_context(nc.psum_tensor("psum", [128, 8 * 2048], U8))
    A = Arena(arena_t, ARENA)
    PS = Arena(ps_t, 8 * 2048)

    def T(shape, dt, name=""):
        return A.alloc(shape, dt), Buf(name)

    bankbuf = [Buf(f"bank{i}") for i in range(8)]

    def BV(bank, shape, dt, boff=0):
        n = int(np.prod(shape))
        nb = n * DSZ[dt]
        assert boff + nb <= 2048
        off = bank * 2048 + boff
        ap = ps_t[:, off:off + nb].bitcast(dt)
        if len(shape) == 2:
            ap = ap.rearrange("p (a b) -> p a b", a=shape[0], b=shape[1])
        return ap, bankbuf[bank]

    ident_f, b_idf = T([128], F32)
    ident_bf, b_idb = T([128], BF16)
    ones_f, b_1f = T([128], F32)
    ones_bf, b_1b = T([128], BF16)
    tri_f, b_tri = T([128], F32)
    e0_f, b_e0 = T([128], F32)
    mask2_f, b_m2 = T([128], F32)
    maskc_bf, b_mc = T([128], BF16)
    scanmsk, b_sm = T([TB], F32)
    modcol, b_mod = T([112], F32)
    ccol, b_cc = T([8], F32)
    cact, b_ca = T([8], F32)
    lbcol, b_lb = T([16], F32)
    omlb, b_omlb = T([8], F32)
    nomlb, b_nomlb = T([8], F32)
    gcol, b_gc = T([3], F32)
    convc, b_cv = T([2, 176], F32)
    ss_a, b_ssa = T([NT], F32)
    ss_b, b_ssb = T([NT], F32)
    rstd_all, b_rs = T([NT], F32)
    ncum_all, b_nc = T([NT, 8], F32)
    cfirst_all, b_cf = T([NT, 8], F32)
    bfbc, b_bf = T([8], F32)
    scr, b_scr = T([1], F32)
    junk, b_junk = T([D], BF16)
    pmark = A.mark()

    def phase_reset():
        A.reset(pmark)
        k.barrier(scr)

    def load_cols(dst, bdst, src, n, stg, bstg, pstg, bpstg, rd=()):
        k.dma("sp", stg[0:n, :], src, list(rd), [bstg])
        k.tr(pstg[:, 0:n], stg[0:n, :], ident_f[0:n, 0:n], [bstg, b_idf], [bpstg])
        k.copy("dve", dst, pstg[:, 0:n], [bpstg], [bdst])

    k.memset("pool", ident_f, 0.0, [], [b_idf])
    k.asel(ident_f, ident_f, [[-1, 128]], ALU.not_equal, 1.0, 0, 1, [b_idf], [b_idf])
    k.copy("dve", ident_bf, ident_f, [b_idf], [b_idb])
    k.memset("pool", ones_f, 1.0, [], [b_1f])
    k.memset("pool", ones_bf, 1.0, [], [b_1b])
    k.memset("pool", tri_f, 1.0, [], [b_tri])
    k.asel(tri_f, tri_f, [[1, 128]], ALU.is_ge, 0.0, 0, -1, [b_tri], [b_tri])
    k.copy("dve", maskc_bf, tri_f, [b_tri], [b_mc])
    k.copy("dve", mask2_f, tri_f, [b_tri], [b_m2])
    k.memset("dve", mask2_f[0:64, 64:128], 0.0, [b_m2], [b_m2])
    k.memset("pool", e0_f, 0.0, [], [b_e0])
    k.asel(e0_f, e0_f, [[0, 128]], ALU.not_equal, 1.0, 0, 1, [b_e0], [b_e0])
    k.memset("pool", scanmsk, 1.0, [], [b_sm])
    k.memset("pool", scanmsk.rearrange("p (c t) -> p c t", t=64)[:, :, 0:1], 0.0, [b_sm], [b_sm])
    k.memset("pool", ncum_all, 0.0, [], [b_nc])

    stg, b_stg = T([128], F32)
    pstg, b_pstg = BV(0, [128], F32)
    load_cols(ccol, b_cc, c_in, 8, stg, b_stg, pstg, b_pstg)
    load_cols(lbcol, b_lb, a_lb, 16, stg, b_stg, pstg, b_pstg)
    k.dma("sp", stg[0:1, :], a_ng, [], [b_stg])
    k.dma("sp", stg[1:2, :], k_ng, [], [b_stg])
    k.dma("sp", stg[2:3, :], q_ng, [], [b_stg])
    k.tr(pstg[:, 0:3], stg[0:3, :], ident_f[0:3, 0:3], [b_stg, b_idf], [b_pstg])
    k.copy("dve", gcol, pstg[:, 0:3], [b_pstg], [b_gc])
    for l in range(2):
        load_cols(convc[:, l, 0:128], b_cv, conv_w[l, 0:128, :], 128, stg, b_stg, pstg, b_pstg)
        load_cols(convc[:, l, 128:132], b_cv, conv_w[l, 128:132, :], 4, stg, b_stg, pstg, b_pstg)
        load_cols(convc[:, l, 132:176], b_cv, conv_b[l], 44, stg, b_stg, pstg, b_pstg)
    k.dma("sp", bfbc, kv_bf.partition_broadcast(128), [], [b_bf])
    k.tt("dve", lbcol[:, 0:8], lbcol[:, 8:16], lbcol[:, 0:8], ALU.subtract, [b_lb], [b_lb])
    k.act(lbcol[:, 0:8], lbcol[:, 0:8], AF.Exp, [b_lb], [b_lb])
    k.ts("dve", lbcol[:, 0:8], lbcol[:, 0:8], 1.0, None, ALU.add, None, [b_lb], [b_lb])
    k.recip(lbcol[:, 0:8], lbcol[:, 0:8], [b_lb], [b_lb])
    k.ts("dve", omlb, lbcol[:, 0:8], -1.0, 1.0, ALU.mult, ALU.add, [b_lb], [b_omlb])
    k.ts("dve", nomlb, lbcol[:, 0:8], 1.0, -1.0, ALU.mult, ALU.add, [b_lb], [b_nomlb])
    k.act(cact, ccol, AF.Silu, [b_cc], [b_ca])

    xr = Ring([T([D], F32) for _ in range(3)])

    def sumsq(xt, bxt, ssdst, bss, t):
        k.act(junk, xt, AF.Square, [bxt], [b_junk, bss], accum=ssdst[:, t:t + 1])

    for t in range(NT):
        xt, bxt = xr.next()
        k.dma("sp", xt, x_in[t * 128:(t + 1) * 128, :], [], [bxt])
        sumsq(xt, bxt, ss_a, b_ssa, t)

    wst = Ring([T([3072], F32) for _ in range(3)])
    brow, b_brow = T([3072], F32)
    mrow, b_mrow = T([3072], F32)
    prow = [BV(1 + i, [512], F32) for i in range(6)]
    modflat = modsD.rearrange("(o v) n -> o (v n)", o=1)
    for (Wd_, bd_, row0, width) in ((ada_w[0], ada_b[0:1, :], 0, 6144), (ada_w[1], ada_b[1:2, :], 6, 6144), (kv_ada_w, kv_ada_b, 12, 2048)):
        for c0 in range(0, width, 3072):
            wd = min(3072, width - c0)
            nn = wd // 512
            for kc in range(KC):
                s_, bs_ = wst.next()
                k.dma("sp", s_[:, 0:wd], Wd_[kc * 128:(kc + 1) * 128, c0:c0 + wd], [], [bs_])
                for n in range(nn):
                    k.mm(prow[n][0][0:1, :], cact[:, kc:kc + 1], s_[:, n * 512:(n + 1) * 512], kc == 0, kc == KC - 1, [b_ca, bs_], [prow[n][1]])
            k.dma("sp", brow[0:1, 0:wd], bd_[:, c0:c0 + wd], [], [b_brow])
            for n in range(nn):
                k.tt("dve", mrow[0:1, n * 512:(n + 1) * 512], prow[n][0][0:1, :], brow[0:1, n * 512:(n + 1) * 512], ALU.add, [prow[n][1], b_brow], [b_mrow])
            k.dma("sp", modflat[:, row0 * D + c0: row0 * D + c0 + wd], mrow[0:1, 0:wd], [b_mrow], [bmods])
    load_cols(modcol, b_mod, modsD.rearrange("v (c p) -> (v c) p", p=128), 112, stg, b_stg, pstg, b_pstg, rd=[bmods])
    k.ts("dve", gcol[:, 2:3], gcol[:, 2:3], 128.0 ** -0.5, None, ALU.mult, None, [b_gc], [b_gc])
    for v in (1, 4, 7, 10, 13):
        k.ts("dve", modcol[:, v * 8:(v + 1) * 8], modcol[:, v * 8:(v + 1) * 8], 1.0, None, ALU.add, None, [b_mod], [b_mod])

    def calc_rstd(ss, bss):
        k.act(rstd_all, ss, AF.Ln, [bss], [b_rs], scale=1.0 / D, bias=EPS)
        k.act(rstd_all, rstd_all, AF.Exp, [b_rs], [b_rs], scale=-0.5)

    def g_bcast(dst, bdst, v):
        k.dma("sp", dst, modsD[v:v + 1, :].partition_broadcast(128), [bmods], [bdst])

    def load_w(dst, bdst, src, nk, c0=0, c1=None):
        for kc in range(nk):
            if c1 is None:
                k.dma("pool", dst[:, kc, :], src[kc * 128:(kc + 1) * 128, :], [], [bdst])
            else:
                k.dma("pool", dst[:, kc, 0:c1 - c0], src[kc * 128:(kc + 1) * 128, c0:c1], [], [bdst])

    def make_hT(xt, bxt, t, hn, bhn, tp, btp, variants):
        k.ts("dve", hn, xt, rstd_all[:, t:t + 1], None, ALU.mult, None, [bxt, b_rs], [bhn])
        for kc in range(KC):
            k.tr(tp[:, kc, :], hn[:, kc * 128:(kc + 1) * 128], ident_bf, [bhn, b_idb], [btp])
        for (vs, vh, dst, bdst) in variants:
            for kc in range(KC):
                sc = modcol[:, vs * 8 + kc: vs * 8 + kc + 1]
                sh = modcol[:, vh * 8 + kc: vh * 8 + kc + 1]
                if kc % 2 == 0:
                    k.act(dst[:, kc, :], tp[:, kc, :], AF.Identity, [btp, b_mod], [bdst], scale=sc, bias=sh)
                else:
                    k.ts("dve", dst[:, kc, :], tp[:, kc, :], sc, sh, ALU.mult, ALU.add, [btp, b_mod], [bdst])

    def residual_out(ypair, xres, bxres, gbc, b_gbc, tmp, btmp, xnew, bxnew, dst_d, bdst_d, t, ss_next, b_ssn):
        for half in range(2):
            yp, byp = ypair[half]
            k.tt("dve", tmp[:, half * 512:(half + 1) * 512], yp, gbc[:, half * 512:(half + 1) * 512], ALU.mult, [byp, b_gbc], [btmp])
        k.tt("pool", xnew, tmp, xres, ALU.add, [btmp, bxres], [bxnew])
        k.dma("pool", dst_d[t * 128:(t + 1) * 128, :], xnew, [bxnew], [bdst_d])
        sumsq(xnew, bxnew, ss_next, b_ssn, t)

    def phase_hgrn(x_src, bsrc, x_dst, bdst, ss_in, b_ssin, ss_out, b_ssout):
        phase_reset()
        calc_rstd(ss_in, b_ssin)
        w_in_sb, b_win = T([KC, 4096], BF16)
        w_out_sb, b_wout = T([KC, D], BF16)
        load_w(w_in_sb, b_win, a_w_in, KC)
        load_w(w_out_sb, b_wout, a_w_out, KC)
        g1bc, b_g1 = T([D], F32)
        g_bcast(g1bc, b_g1, 2)
        xr = Ring([T([D], F32) for _ in range(3)])
        hnr = Ring([T([D], BF16) for _ in range(2)])
        hTr = Ring([T([KC, TB], BF16) for _ in range(2)])
        tmpA = Ring([T([TB], F32) for _ in range(2)])
        tmpB = Ring([T([TB], F32) for _ in range(2)])
        tmpC = Ring([T([TB], F32) for _ in range(2)])
        kstr = Ring([T([TB], BF16) for _ in range(8)])
        E_all = A.alloc([8, TB], F32)
        kR_all = A.alloc([8, TB], BF16)
        qE_all = A.alloc([8, TB], BF16)
        gs_all = A.alloc([8, TB], F32)
        ks_all = A.alloc([TPB, 8, 128], BF16)
        v_all = A.alloc([TPB, D], BF16)
        Elast = A.alloc([8, TB // 64], F32)
        st32 = A.alloc([8, 128], F32)
        stb = A.alloc([8, 128], BF16)
        bE, bkR, bqE, bgs, bks, bEl, bst32, bstb = ([Buf() for _ in range(8)] for _ in range(8))
        bv_ = [Buf() for _ in range(TPB)]
        atr = Ring([T([4, 128], BF16) for _ in range(4)])
        oTr = Ring([T([D], F32) for _ in range(2)])
        sq, bsq = T([D], BF16)
        lnr, blnr = T([D], F32)
        on, bon = T([D], BF16)
        tmp, btmp = T([D], F32)
        xnew, bxnew = T([D], F32)
        tp, btp = BV(0, [KC, 128], BF16)
        pa = Ring([BV(1, [TB], F32), BV(2, [TB], F32)])
        pbs = [BV(3, [512], F32), BV(4, [512], F32)]
        pbig = ps_t[:, 3 * 2048:5 * 2048].bitcast(F32)
        pb = Ring(pbs)
        scg = [BV(5, [4, 128], F32), BV(1, [4, 128], F32)]
        pog = [BV(6, [4, 128], F32), BV(2, [4, 128], F32)]
        dsg = [BV(7, [4, 128], F32), BV(3, [4, 128], F32)]
        tpks = [BV(0, [TPB, 128], BF16), BV(5, [TPB, 128], BF16)]
        k.memset("pool", st32, 0.0, [], bst32)
        k.memset("pool", stb, 0.0, [], bstb)
        NCH = TB // 64
        for b in range(DBG.get("nb", NB)):
            hT, bhT = hTr.next()
            for ti in range(TPB):
                t = b * TPB + ti
                xt, bxt = xr.next()
                hn, bhn = hnr.next()
                k.dma("sp", xt, x_src[t * 128:(t + 1) * 128, :], [bsrc], [bxt])
                make_hT(xt, bxt, t, hn, bhn, tp, btp, [(1, 0, hT[:, :, ti * 128:(ti + 1) * 128], bhT)])
            if DBG.get("stage", 9) < 1:
                continue
            ksts = []
            for h in range(DBG.get("nh", 8)):
                pf, bpf = pa.next()
                for kc in range(KC):
                    k.mm(pf, w_in_sb[:, kc, 1024 + h * 128:1024 + (h + 1) * 128], hT[:, kc, :], kc == 0, kc == KC - 1, [b_win, bhT], [bpf])
                ta, bta = tmpA.next()
                tb_, btb = tmpB.next()
                tc, btc = tmpC.next()
                k.act(ta, pf, AF.Exp, [bpf], [bta])
                if DBG.get("fsub", 9) < 1:
                    continue
                k.ts("dve", ta, ta, 1.0, None, ALU.add, None, [bta], [bta])
                k.recip(ta, ta, [bta], [bta])
                k.act(tb_, ta, AF.Ln, [bta, b_nomlb], [btb], scale=nomlb[:, h:h + 1], bias=1.0)
                if DBG.get("fsub", 9) < 2:
                    continue
                k.scan(tc, scanmsk, tb_, [b_sm, btb], [btc])
                k.act(E_all[:, h, :], tc, AF.Exp, [btc], [bE[h]])
                k.act(tb_, tc, AF.Exp, [btc], [btb], scale=-1.0)
                k.stt("dve", kR_all[:, h, :], ta, omlb[:, h:h + 1], tb_, ALU.mult, ALU.mult, [bta, btb, b_omlb], [bkR[h]])
                if DBG.get("fsub", 9) < 3:
                    continue
                kst, bkst = kstr.next()
                Ev = E_all[:, h, :].rearrange("p (c t) -> p c t", t=64)
                k.tt("pool", kst.rearrange("p (c t) -> p c t", t=64), kR_all[:, h, :].rearrange("p (c t) -> p c t", t=64),
                     Ev[:, :, 63:64].to_broadcast([128, NCH, 64]), ALU.mult, [bkR[h], bE[h]], [bkst])
                k.copy("pool", Elast[:, h, :], Ev[:, :, 63], [bE[h]], [bEl[h]])
                ksts.append((kst, bkst))
            if DBG.get("stage", 9) < 2:
                continue
            for ti in range(TPB):
                for half in range(2):
                    pv_, bpv = pb.next()
                    for kc in range(KC):
                        k.mm(pv_, hT[:, kc, ti * 128:(ti + 1) * 128], w_in_sb[:, kc, 2048 + half * 512:2048 + (half + 1) * 512], kc == 0, kc == KC - 1, [bhT, b_win], [bpv])
                    k.copy("dve", v_all[:, ti, half * 512:(half + 1) * 512], pv_, [bpv], [bv_[ti]])
            for h in range(8):
                kst, bkst = ksts[h]
                tpk_, btpk_ = tpks[h % 2]
                for ti in range(TPB):
                    k.tr(tpk_[:, ti, :], kst[:, ti * 128:(ti + 1) * 128], ident_bf, [bkst, b_idb], [btpk_])
                for ti in range(TPB):
                    k.copy("dve", ks_all[:, ti, h, :], tpk_[:, ti, :], [btpk_], [bks[h]])
            for h in range(8):
                pq, bpq = pa.next()
                for kc in range(KC):
                    k.mm(pq, w_in_sb[:, kc, h * 128:(h + 1) * 128], hT[:, kc, :], kc == 0, kc == KC - 1, [b_win, bhT], [bpq])
                if DBG.get("ssub", 9) < 0:
                    continue
                ta, bta = tmpA.next()
                k.act(ta, pq, DBG.get("qf", AF.Silu), [bpq], [bta])
                if DBG.get("ssub", 9) < 1:
                    continue
                k.tt("pool", qE_all[:, h, :], ta, E_all[:, h, :], ALU.mult, [bta, bE[h]], [bqE[h]])
                if DBG.get("ssub", 9) < 2:
                    continue
                pg, bpg = pa.next()
                for kc in range(KC):
                    k.mm(pg, w_in_sb[:, kc, 3072 + h * 128:3072 + (h + 1) * 128], hT[:, kc, :], kc == 0, kc == KC - 1, [b_win, bhT], [bpg])
                k.act(gs_all[:, h, :], pg, AF.Silu, [bpg], [bgs[h]])
            if DBG.get("stage", 9) < 4:
                continue
            for ti in range(TPB):
                t = b * TPB + ti
                cs = slice(ti * 128, (ti + 1) * 128)
                oT, boT = oTr.next()
                oT3 = oT.rearrange("p (h t) -> p h t", t=128)
                atgs = [atr.next() for _ in range(2)]
                for g in range(2):
                    scb, bscb = scg[g]
                    for i_, h in enumerate(range(g * 4, g * 4 + 4)):
                        k.mm(scb[:, i_, :], kR_all[:, h, cs], qE_all[:, h, cs], True, True, [bkR[h], bqE[h]], [bscb])
                for g in range(2):
                    scb, bscb = scg[g]
                    atg, batg = atgs[g]
                    for i_ in range(4):
                        k.tt("dve", atg[:, i_, :], scb[:, i_, :], mask2_f, ALU.mult, [bscb, b_m2], [batg])
                for c in range(2):
                    cc = slice(c * 64, (c + 1) * 64)
                    pr = slice(c * 64, (c + 1) * 64)
                    ch = ti * 2 + c
                    for g in range(2):
                        atg, batg = atgs[g]
                        pob, bpob = pog[g]
                        dsb, bdsb = dsg[g]
                        for i_, h in enumerate(range(g * 4, g * 4 + 4)):
                            k.mm(pob[:, i_, cc], v_all[:, ti, h * 128:(h + 1) * 128], atg[:, i_, cc], True, False, [bv_[ti], batg], [bpob])
                            k.mm(pob[:, i_, cc], stb[:, h, :], qE_all[:, h, ti * 128 + c * 64: ti * 128 + (c + 1) * 64], False, True, [bstb[h], bqE[h]], [bpob])
                            k.mm(dsb[:, i_, :], ks_all[pr, ti, h, :], v_all[pr, ti, h * 128:(h + 1) * 128], True, True, [bks[h], bv_[ti]], [bdsb])
                    for g in range(2):
                        dsb, bdsb = dsg[g]
                        for i_, h in enumerate(range(g * 4, g * 4 + 4)):
                            k.stt("dve", st32[:, h, :], st32[:, h, :], Elast[:, h, ch:ch + 1], dsb[:, i_, :], ALU.mult, ALU.add, [bst32[h], bEl[h], bdsb], [bst32[h]])
                            k.copy("pool", stb[:, h, :], st32[:, h, :], [bst32[h]], [bstb[h]])
                for g in range(2):
                    pob, bpob = pog[g]
                    for i_, h in enumerate(range(g * 4, g * 4 + 4)):
                        k.copy("dve", oT3[:, h, :], pob[:, i_, :], [bpob], [boT])
                if DBG.get("stage", 9) < 5:
                    continue
                k.act(sq, oT, AF.Square, [boT], [bsq])
                for half in range(2):
                    k.mm(pbs[half][0], ones_bf, sq[:, half * 512:(half + 1) * 512], True, True, [b_1b, bsq], [pbs[half][1]])
                k.act(lnr, pbig, AF.Ln, [pbs[0][1], pbs[1][1]], [blnr], scale=1.0 / 128, bias=EPS)
                k.act(lnr, lnr, AF.Exp, [blnr], [blnr], scale=-0.5)
                k.stt("dve", lnr, oT, gcol[:, 0:1], lnr, ALU.mult, ALU.mult, [boT, b_gc, blnr], [blnr])
                k.tt("pool", on.rearrange("p (h t) -> p h t", t=128), lnr.rearrange("p (h t) -> p h t", t=128), gs_all[:, :, cs], ALU.mult, [blnr] + bgs, [bon])
                on3 = on.rearrange("p (h t) -> p h t", t=128)
                xres, bxres = xr.next()
                k.dma("sp", xres, x_src[t * 128:(t + 1) * 128, :], [bsrc], [bxres])
                ypair = []
                for half in range(2):
                    yp, byp = pbs[half]
                    for h in range(8):
                        k.mm(yp, on3[:, h, :], w_out_sb[:, h, half * 512:(half + 1) * 512], h == 0, h == 7, [bon, b_wout], [byp])
                    ypair.append((yp, byp))
                residual_out(ypair, xres, bxres, g1bc, b_g1, tmp, btmp, xnew, bxnew, x_dst, bdst, t, ss_out, b_ssout)

    def phase_ffn(l, x_src, bsrc, x_dst, bdst, ss_in, b_ssin, ss_out, b_ssout, vsc, vsh, vg):
        phase_reset()
        calc_rstd(ss_in, b_ssin)
        wup, b_wup = T([KC, 2 * DFF], BF16)
        wdn, b_wdn = T([NJ, D], BF16)
        load_w(wup, b_wup, w_up[l], KC)
        load_w(wdn, b_wdn, w_down[l], NJ)
        gbc, b_gbc = T([D], F32)
        g_bcast(gbc, b_gbc, vg)
        xr = Ring([T([D], F32) for _ in range(2)])
        hnr = Ring([T([D], BF16) for _ in range(2)])
        hTr = Ring([T([KC, TB + 2], BF16) for _ in range(2)])
        t0r = Ring([T([TB], F32) for _ in range(4)])
        sgr = Ring([T([TB], F32) for _ in range(2)])
        actr = Ring([T([NJ, TB], BF16) for _ in range(2)])
        tmp, btmp = T([D], F32)
        xnew, bxnew = T([D], F32)
        tp, btp = BV(0, [KC, 128], BF16)
        pur = Ring([BV(1 + i, [TB + 2], F32) for i in range(4)])
        pbs = [BV(5, [512], F32), BV(6, [512], F32)]
        def do_hT(b, prev):
            hT, bhT = hTr.next()
            if prev is None:
                k.memset("pool", hT[:, :, 0:2], 0.0, [], [bhT])
            else:
                k.copy("pool", hT[:, :, 0:2], prev[0][:, :, TB:TB + 2], [prev[1]], [bhT])
            for ti in range(TPB):
                t = b * TPB + ti
                xt, bxt = xr.next()
                hn, bhn = hnr.next()
                k.dma("sp", xt, x_src[t * 128:(t + 1) * 128, :], [bsrc], [bxt])
                make_hT(xt, bxt, t, hn, bhn, tp, btp, [(vsc, vsh, hT[:, :, 2 + ti * 128:2 + (ti + 1) * 128], bhT)])
            return hT, bhT

        def do_up(hT, bhT):
            aT, baT = actr.next()
            for j in range(NJ):
                res = []
                for which, cidx in ((0, j), (1, NJ + j)):
                    pu, bpu = pur.next()
                    for kc in range(KC):
                        k.mm(pu, wup[:, kc, cidx * 128:(cidx + 1) * 128], hT[:, kc, :], kc == 0, kc == KC - 1, [b_wup, bhT], [bpu])
                    t0, bt0 = t0r.next()
                    k.ts("dve", t0, pu[:, 2:2 + TB], convc[:, l, 88 + cidx:89 + cidx], convc[:, l, 132 + cidx:133 + cidx], ALU.mult, ALU.add, [bpu, b_cv], [bt0])
                    k.stt("dve", t0, pu[:, 1:1 + TB], convc[:, l, 44 + cidx:45 + cidx], t0, ALU.mult, ALU.add, [bpu, b_cv, bt0], [bt0])
                    k.stt("dve", t0, pu[:, 0:TB], convc[:, l, cidx:cidx + 1], t0, ALU.mult, ALU.add, [bpu, b_cv, bt0], [bt0])
                    res.append((t0, bt0))
                sg, bsg = sgr.next()
                k.act(sg, res[0][0], AF.Silu, [res[0][1]], [bsg])
                k.tt("pool", aT[:, j, :], sg, res[1][0], ALU.mult, [bsg, res[1][1]], [baT])
            return aT, baT

        def do_down(b, aT, baT):
            for ti in range(TPB):
                t = b * TPB + ti
                xres, bxres = xr.next()
                k.dma("sp", xres, x_src[t * 128:(t + 1) * 128, :], [bsrc], [bxres])
                ypair = []
                for half in range(2):
                    yp, byp = pbs[half]
                    for j in range(NJ):
                        k.mm(yp, aT[:, j, ti * 128:(ti + 1) * 128], wdn[:, j, half * 512:(half + 1) * 512], j == 0, j == NJ - 1, [baT, b_wdn], [byp])
                    ypair.append((yp, byp))
                residual_out(ypair, xres, bxres, gbc, b_gbc, tmp, btmp, xnew, bxnew, x_dst, bdst, t, ss_out, b_ssout)

        nbk = DBG.get('fnb', NB)
        hts = do_hT(0, None)
        ats = do_up(*hts)
        for b in range(nbk):
            if b + 1 < nbk:
                hts_n = do_hT(b + 1, hts)
                ats_n = do_up(*hts_n)
            do_down(b, *ats)
            if b + 1 < nbk:
                hts, ats = hts_n, ats_n

    def phase_fox_a(x_src, bsrc, ss_in, b_ssin):
        phase_reset()
        calc_rstd(ss_in, b_ssin)
        kvw, b_kvw = T([KC, 2056], BF16)
        wq, b_wq = T([KC, 2048], BF16)
        load_w(kvw, b_kvw, kv_w, KC)
        load_w(wq, b_wq, b_w_q, KC)
        xr = Ring([T([D], F32) for _ in range(3)])
        hnr = Ring([T([D], BF16) for _ in range(2)])
        hkvr = Ring([T([KC, TB], BF16) for _ in range(2)])
        h1r = Ring([T([KC, TB], BF16) for _ in range(2)])
        kTbr = Ring([T([8, TB], BF16) for _ in range(2)])
        qTbr = Ring([T([8, TB], BF16) for _ in range(2)])
        sqr = Ring([T([TB], BF16) for _ in range(2)])
        lnrr = Ring([T([TB], F32) for _ in range(2)])
        vbr = Ring([T([D], BF16) for _ in range(2)])
        gtr = Ring([T([D], F32) for _ in range(2)])
        ger = Ring([T([512], F32) for _ in range(2)])
        lfr = Ring([T([8], F32) for _ in range(2)])
        run, brun = T([8], F32)
        tp, btp = BV(0, [KC, 128], BF16)
        pa = Ring([BV(1, [TB], F32), BV(2, [TB], F32)])
        pssr = Ring([BV(3, [TB], F32), BV(4, [TB], F32)])
        pb = Ring([BV(5, [512], F32), BV(6, [512], F32)])
        pf8, bpf8 = BV(7, [8], F32, 0)
        pcum, bpcum = BV(7, [8], F32, 512)
        pcf, bpcf = BV(7, [8], F32, 1024)
        k.memset("pool", run, 0.0, [], [brun])
        kTv = kT_d.rearrange("h d t -> d h t")
        qTv = qT_d.rearrange("h d t -> d h t")
        for b in range(NB):
            hkv, bhkv = hkvr.next()
            h1, bh1 = h1r.next()
            for ti in range(TPB):
                t = b * TPB + ti
                xt, bxt = xr.next()
                hn, bhn = hnr.next()
                k.dma("sp", xt, x_src[t * 128:(t + 1) * 128, :], [bsrc], [bxt])
                tsl = slice(ti * 128, (ti + 1) * 128)
                make_hT(xt, bxt, t, hn, bhn, tp, btp, [(13, 12, hkv[:, :, tsl], bhkv), (7, 6, h1[:, :, tsl], bh1)])
            kTb, bkTb = kTbr.next()
            qTb, bqTb = qTbr.next()
            jobs = [(kvw, b_kvw, hkv, bhkv, 1, kTb, bkTb, h) for h in range(8)] + [(wq, b_wq, h1, bh1, 2, qTb, bqTb, h) for h in range(8)]

            def proj(job):
                W, bW, hs, bhs, gi, dstb, bdstb, h = job
                pk, bpk = pa.next()
                for kc in range(KC):
                    k.mm(pk, W[:, kc, h * 128:(h + 1) * 128], hs[:, kc, :], kc == 0, kc == KC - 1, [bW, bhs], [bpk])
                return pk, bpk

            cur = proj(jobs[0])
            for n, job in enumerate(jobs):
                nxt = proj(jobs[n + 1]) if n + 1 < len(jobs) else None
                W, bW, hs, bhs, gi, dstb, bdstb, h = job
                pk, bpk = cur
                sq, bsq = sqr.next()
                k.act(sq, pk, AF.Square, [bpk], [bsq])
                pss, bpss = pssr.next()
                k.mm(pss, ones_bf, sq, True, True, [b_1b, bsq], [bpss])
                lnr, blnr = lnrr.next()
                k.act(lnr, pss, AF.Ln, [bpss], [blnr], scale=1.0 / 128, bias=EPS)
                k.act(lnr, lnr, AF.Exp, [blnr], [blnr], scale=-0.5)
                k.stt("dve", dstb[:, h, :], pk, gcol[:, gi:gi + 1], lnr, ALU.mult, ALU.mult, [bpk, b_gc, blnr], [bdstb])
                cur = nxt
            k.dma("pool", kTv[:, :, b * TB:(b + 1) * TB], kTb, [bkTb], [bkT])
            k.dma("pool", qTv[:, :, b * TB:(b + 1) * TB], qTb, [bqTb], [bqT])
            for ti in range(TPB):
                t = b * TPB + ti
                tsl = slice(ti * 128, (ti + 1) * 128)
                vb, bvb = vbr.next()
                for half in range(2):
                    pv_, bpv = pb.next()
                    for kc in range(KC):
                        k.mm(pv_, hkv[:, kc, tsl], kvw[:, kc, 1024 + half * 512:1024 + (half + 1) * 512], kc == 0, kc == KC - 1, [bhkv, b_kvw], [bpv])
                    k.copy("dve", vb[:, half * 512:(half + 1) * 512], pv_, [bpv], [bvb])
                k.dma("pool", v_d[t * 128:(t + 1) * 128, :], vb, [bvb], [bv])
                gt, bgt = gtr.next()
                for half in range(2):
                    pg, bpg = pb.next()
                    for kc in range(KC):
                        k.mm(pg, h1[:, kc, tsl], wq[:, kc, 1024 + half * 512:1024 + (half + 1) * 512], kc == 0, kc == KC - 1, [bh1, b_wq], [bpg])
                    ge, bge = ger.next()
                    k.act(ge, pg, AF.Exp, [bpg], [bge], scale=-1.0)
                    k.ts("dve", ge, ge, 1.0, None, ALU.add, None, [bge], [bge])
                    k.recip(gt[:, half * 512:(half + 1) * 512], ge, [bge], [bgt])
                k.dma("pool", gate_d[t * 128:(t + 1) * 128, :], gt, [bgt], [bgate])
                for kc in range(KC):
                    k.mm(pf8, hkv[:, kc, tsl], kvw[:, kc, 2048:2056], kc == 0, kc == KC - 1, [bhkv, b_kvw], [bpf8])
                lf, blf = lfr.next()
                k.tt("dve", lf, pf8, bfbc, ALU.add, [bpf8, b_bf], [blf])
                k.act(lf, lf, AF.Exp, [blf], [blf], scale=-1.0)
                k.act(lf, lf, AF.Ln, [blf], [blf], bias=1.0)
                k.mm(pcum, tri_f, lf, True, False, [b_tri, blf], [bpcum])
                k.mm(pcum, ones_f, run, False, True, [b_1f, brun], [bpcum])
                k.copy("dve", ncum_all[:, t, :], pcum, [bpcum], [b_nc])
                k.tt("dve", run, run, lf, ALU.add, [brun, blf], [brun])
                k.mm(pcf, e0_f, ncum_all[:, t, :], True, True, [b_e0, b_nc], [bpcf])
                k.copy("dve", cfirst_all[:, t, :], pcf, [bpcf], [b_cf])

    def phase_fox_bc(x_src, bsrc, x_dst, bdst, ss_out, b_ssout):
        phase_reset()
        wo, b_wo = T([KC, D], BF16)
        load_w(wo, b_wo, b_w_out, KC)
        gbc, b_gbc = T([D], F32)
        g_bcast(gbc, b_gbc, 8)
        o_all = A.alloc([NT, D], BF16)
        bo = [Buf() for _ in range(NT)]
        kThr = Ring([T([S], BF16) for _ in range(2)])
        qThr = Ring([T([S], BF16) for _ in range(2)])
        vhr = Ring([T([NT, 132], BF16) for _ in range(2)])
        ghr = Ring([T([NT, 128], F32) for _ in range(2)])
        bir = Ring([T([NT, NT], F32) for _ in range(2)])
        ptr_ = Ring([T([128], BF16) for _ in range(4)])
        recr = Ring([T([1], F32) for _ in range(2)])
        xr = Ring([T([D], F32) for _ in range(2)])
        oTr = Ring([T([KC, 128], BF16) for _ in range(2)])
        tmp, btmp = T([D], F32)
        xnew, bxnew = T([D], F32)
        sr = Ring([BV(i, [128], F32) for i in range(3)])
        por = Ring([BV(3, [132], F32), BV(4, [132], F32)])
        tp, btp = BV(5, [KC, 128], BF16)
        pbs = [BV(6, [512], F32), BV(7, [512], F32)]
        for (vh, bvh) in vhr.items:
            k.memset("pool", vh[:, :, 128:129], 1.0, [], [bvh])
        vdv = v_d.rearrange("(j p) d -> p j d", p=128)
        gdv = gate_d.rearrange("(j p) d -> p j d", p=128)
        for h in range(8):
            kTh, bkTh = kThr.next()
            qTh, bqTh = qThr.next()
            vh, bvh = vhr.next()
            gh, bgh = ghr.next()
            bias, bbias = bir.next()
            k.dma("sp", kTh, kT_d[h], [bkT], [bkTh])
            k.dma("sp", qTh, qT_d[h], [bqT], [bqTh])
            k.dma("sp", vh[:, :, 0:128], vdv[:, :, h * 128:(h + 1) * 128], [bv], [bvh])
            k.dma("sp", gh, gdv[:, :, h * 128:(h + 1) * 128], [bgate], [bgh])
            for j in range(NT):
                k.ts("dve", bias[:, j, :], cfirst_all[:, :, h], -1.0, ncum_all[:, j, h:h + 1], ALU.mult, ALU.add, [b_cf, b_nc], [bbias])
            pairs = [(Q, j) for Q in range(NT) for j in range(Q + 1)]
            LA = 2
            sbuf_of = {}
            po_of = {}
            for n in range(len(pairs) + LA):
                if n < len(pairs):
                    Q, j = pairs[n]
                    s_, bs = sr.next()
                    sbuf_of[n] = (s_, bs)
                    k.mm(s_, kTh[:, j * 128:(j + 1) * 128], qTh[:, Q * 128:(Q + 1) * 128], True, True, [bkTh, bqTh], [bs])
                m = n - LA
                if m < 0:
                    continue
                Q, j = pairs[m]
                if j == 0:
                    po_of[Q] = por.next()
                po, bpo = po_of[Q]
                s_, bs = sbuf_of.pop(m)
                pt, bpt = ptr_.next()
                k.act(pt, s_, AF.Exp, [bs, bbias], [bpt], bias=bias[:, j, Q:Q + 1])
                if j == Q:
                    k.tt("pool", pt, pt, maskc_bf, ALU.mult, [bpt, b_mc], [bpt])
                k.mm(po[:, 0:129], pt, vh[:, j, 0:129], j == 0, j == Q, [bpt, bvh], [bpo])
                if j == Q:
                    rec, brec = recr.next()
                    k.recip(rec, po[:, 128:129], [bpo], [brec])
                    k.stt("dve", o_all[:, Q, h * 128:(h + 1) * 128], po[:, 0:128], rec, gh[:, Q, :], ALU.mult, ALU.mult, [bpo, brec, bgh], [bo[Q]])
        for t in range(NT):
            for kc in range(KC):
                k.tr(tp[:, kc, :], o_all[:, t, kc * 128:(kc + 1) * 128], ident_bf, [bo[t], b_idb], [btp])
            oT, boT = oTr.next()
            k.copy("dve", oT.rearrange("p a b -> p (a b)"), tp.rearrange("p a b -> p (a b)"), [btp], [boT])
            xres, bxres = xr.next()
            k.dma("sp", xres, x_src[t * 128:(t + 1) * 128, :], [bsrc], [bxres])
            ypair = []
            for half in range(2):
                yp, byp = pbs[half]
                for kc in range(KC):
                    k.mm(yp, oT[:, kc, :], wo[:, kc, half * 512:(half + 1) * 512], kc == 0, kc == KC - 1, [boT, b_wo], [byp])
                ypair.append((yp, byp))
            residual_out(ypair, xres, bxres, gbc, b_gbc, tmp, btmp, xnew, bxnew, x_dst, bdst, t, ss_out, b_ssout)

    bxin = Buf("xin")
    calc = None
    if upto >= 1:
        phase_hgrn(x_in, bxin, x1_d, bx1, ss_a, b_ssa, ss_b, b_ssb)
    if upto >= 2:
        phase_ffn(0, x1_d, bx1, x2_d, bx2, ss_b, b_ssb, ss_a, b_ssa, 4, 3, 5)
    if upto >= 3:
        phase_fox_a(x2_d, bx2, ss_a, b_ssa)
    if upto >= 4:
        phase_fox_bc(x2_d, bx2, x3_d, bx3, ss_b, b_ssb)
    if upto >= 5:
        phase_ffn(1, x3_d, bx3, out_d, bout, ss_b, b_ssb, ss_a, b_ssa, 10, 9, 11)

    k.barrier(scr)
    fin_d = nc.dram_tensor("fin_d", [128, 1], F32, kind="Internal").ap()
    k.dma("sp", fin_d, scr, [], [Buf("fin")])
    P.finalize()
    sems = {e: [st.enter_context(nc.semaphore(f"s_{e}{i}")) for i in range(P.nepoch[e])] for e in CENG}
    dsems = [st.enter_context(nc.semaphore(f"d{i}")) for i in range(P.NDMA_SEM)]
    block = st.enter_context(nc.Block())
    P.emit(block, sems, dsems)
    st.close()
    return nc


_NC_CACHE = {}


def _in_maps(inputs):
    f = lambda a: np.ascontiguousarray(np.asarray(a, dtype=np.float32))
    x = f(inputs["x"])
    c = f(inputs["c"])
    shared = dict(
        ada_w=f(inputs["ada_w"]), ada_b=f(inputs["ada_b"]),
        a_w_in=f(inputs["a_w_in"]).reshape(D, 4096),
        a_lb_logits=f(inputs["a_lb_logits"]).reshape(16, 128),
        a_norm_g=f(inputs["a_norm_g"]).reshape(1, 128),
        a_w_out=f(inputs["a_w_out"]).reshape(D, D),
        kv_ada_w=f(inputs["kv_ada_w"]), kv_ada_b=f(inputs["kv_ada_b"]).reshape(1, 2 * D),
        kv_w=f(inputs["kv_w"]), kv_b_f=f(inputs["kv_b_f"]).reshape(1, 8),
        k_norm_g=f(inputs["k_norm_g"]).reshape(1, 128),
        b_w_q=f(inputs["b_w_q"]).reshape(D, 2 * D),
        q_norm_g=f(inputs["q_norm_g"]).reshape(1, 128),
        b_w_out=f(inputs["b_w_out"]).reshape(D, D),
        ffn_w_up=f(inputs["ffn_w_up"]),
        ffn_conv_w=f(inputs["ffn_conv_w"]).reshape(2, 132, 128),
        ffn_conv_b=f(inputs["ffn_conv_b"]).reshape(2, 44, 128),
        ffn_w_down=f(inputs["ffn_w_down"]),
    )
    maps = []
    for b in range(8):
        m = dict(shared)
        m["x"] = x[b]
        m["c"] = c[b].reshape(8, 128)
        maps.append(m)
    return maps


def kernel(**inputs):
    if "nc" not in _NC_CACHE:
        _NC_CACHE["nc"] = build_nc()
    nc = _NC_CACHE["nc"]
    res = run_bass_kernel_spmd(nc, _in_maps(inputs), core_ids=list(range(8)))
    return np.stack([np.asarray(r["out"], dtype=np.float32) for r in res.results], axis=0)
```

```python
import numpy as np
import concourse.bass as bass
import concourse.mybir as mybir
from concourse.bass_utils import run_bass_kernel_spmd

F32 = mybir.dt.float32
BF16 = mybir.dt.bfloat16
U8 = mybir.dt.uint8
AF = mybir.ActivationFunctionType
ALU = mybir.AluOpType
AX = mybir.AxisListType
DSZ = {F32: 4, BF16: 2, U8: 1}

CENG = ("pe", "act", "dve", "pool")
EPOCH = 12000
STRICT = True


class Buf:
    __slots__ = ("name", "w", "r")

    def __init__(self, name=""):
        self.name = name
        self.w = None
        self.r = {}


class Op:
    __slots__ = ("id", "eng", "fn", "deps", "dma", "seq", "sig", "cnt", "need", "clock", "dsem", "dval", "inc")


class Prog:
    def __init__(self, nc):
        self.nc = nc
        self.ops = []
        self.ndma = 0
        self.dma_last = {}
        self.NDMA_SEM = 40
        self.NHW = 24
        self.nsw = 0

    def op(self, eng, fn, reads=(), writes=(), dma=False):
        o = Op()
        o.id = len(self.ops)
        o.eng = eng
        o.fn = fn
        o.dma = dma
        o.sig = False
        o.cnt = None
        deps = {}
        for b in reads:
            if b.w is not None:
                deps[b.w] = True
        for b in writes:
            if b.w is not None:
                deps.setdefault(b.w, False)
            for r in b.r.values():
                for rid in r:
                    deps.setdefault(rid, False)
        if dma:
            if eng == "pool":
                slot = self.NHW + self.nsw % (self.NDMA_SEM - self.NHW)
                self.nsw += 1
            else:
                slot = self.ndma % self.NHW
                self.ndma += 1
            prev = self.dma_last.get(slot)
            if prev is not None:
                deps.setdefault(prev.id, False)
                o.dval = prev.dval + 16
            else:
                o.dval = 16
            o.dsem = slot
            self.dma_last[slot] = o
        deps.pop(o.id, None)
        o.deps = deps
        for b in reads:
            if dma:
                b.r.setdefault("dma", []).append(o.id)
            else:
                b.r[eng] = [o.id]
        for b in writes:
            b.w = o.id
            b.r = {}
        self.ops.append(o)
        return o

    def finalize(self):
        ops = self.ops
        seqc = {e: 0 for e in CENG}
        known = {e: {c: 0 for c in CENG} for e in CENG + ("sp",)}
        kdma = {e: set() for e in CENG + ("sp",)}
        for o in ops:
            A = o.eng
            if not o.dma:
                seqc[A] += 1
                o.seq = seqc[A]
            else:
                o.seq = 0
            kn = known[A]
            need = []
            dl = sorted(o.deps.items(), key=lambda kv: -kv[0])
            for xid, raw in dl:
                X = ops[xid]
                if X.dma:
                    if xid in kdma[A]:
                        continue
                    need.append(xid)
                    kdma[A].add(xid)
                    for c in CENG:
                        if X.clock[c] > kn[c]:
                            kn[c] = X.clock[c]
                    continue
                E = X.eng
                if X.seq <= kn[E]:
                    continue
                if (not o.dma) and E == A:
                    if A == "pe" or not (raw or STRICT):
                        continue
                need.append(xid)
                X.sig = True
                kn[E] = X.seq
                for c in CENG:
                    if X.clock[c] > kn[c]:
                        kn[c] = X.clock[c]
            o.need = need
            ck = dict(kn)
            if len(kdma[A]) > 512:
                kdma[A] = set(sorted(kdma[A])[-256:])
            o.clock = ck
        cnt = {e: 0 for e in CENG}
        for o in ops:
            if (not o.dma) and o.sig:
                cnt[o.eng] += 1
                o.cnt = cnt[o.eng]
        self.nepoch = {e: cnt[e] // EPOCH + 1 for e in CENG}

    def emit(self, block, sems, dsems):
        ops = self.ops

        def semval(X):
            if X.dma:
                return dsems[X.dsem], X.dval
            k = (X.cnt - 1) // EPOCH
            return sems[X.eng][k], (X.cnt - 1) % EPOCH + 1

        def run(engname):
            def body(e):
                for o in ops:
                    if o.eng != engname:
                        continue
                    for xid in o.need:
                        s, v = semval(ops[xid])
                        e.wait_ge(s, v)
                    ins = o.fn(e)
                    if o.dma:
                        ins.then_inc(dsems[o.dsem], 16)
                    elif o.sig:
                        s, _ = semval(o)
                        ins.then_inc(s, 1)
                if engname == "sp":
                    for slot, o in self.dma_last.items():
                        e.wait_ge(dsems[slot], o.dval)
            return body

        block.tensor(run("pe"))
        block.scalar(run("act"))
        block.vector(run("dve"))
        block.gpsimd(run("pool"))
        block.sync(run("sp"))


class Arena:
    def __init__(self, t, size, part=128):
        self.t = t
        self.size = size
        self.off = 0

    def alloc(self, free_shape, dtype, align=64):
        n = int(np.prod(free_shape))
        nb = n * DSZ[dtype]
        off = (self.off + align - 1) // align * align
        assert off + nb <= self.size, f"arena overflow {off + nb} > {self.size}"
        self.off = off + nb
        ap = self.t[:, off:off + nb].bitcast(dtype)
        if len(free_shape) == 2:
            ap = ap.rearrange("p (a b) -> p a b", a=free_shape[0], b=free_shape[1])
        elif len(free_shape) == 3:
            ap = ap.rearrange("p (a b c) -> p a b c", a=free_shape[0], b=free_shape[1], c=free_shape[2])
        return ap

    def mark(self):
        return self.off

    def reset(self, m):
        self.off = m


S = 4096
D = 1024
NT = 32
KC = 8
TB = 256
NB = S // TB
TPB = TB // 128
DFF = 2816
NJ = 22
EPS = 1e-6
ARENA = 206 * 1024
DBG = {}


class Ring:
    def __init__(self, items):
        self.items = items
        self.i = 0

    def next(self):
        it = self.items[self.i % len(self.items)]
        self.i += 1
        return it


class K:
    def __init__(self, nc):
        self.nc = nc
        self.P = Prog(nc)
        self.Y = Buf("phase")

    def _op(self, eng, fn, reads, writes, dma=False):
        return self.P.op(eng, fn, reads=list(reads) + [self.Y], writes=list(writes), dma=dma)

    def barrier(self, scr):
        self.P.op("dve", lambda e: e.memset(scr, 0.0), reads=[], writes=[self.Y])

    def mm(self, out, lhsT, rhs, start, stop, reads, writes):
        return self._op("pe", lambda e: e.matmul(out, lhsT=lhsT, rhs=rhs, start=start, stop=stop), reads, writes)

    def tr(self, out, in_, ident, reads, writes):
        return self._op("pe", lambda e: e.transpose(out=out, in_=in_, identity=ident), reads, writes)

    def act(self, out, in_, func, reads, writes, scale=1.0, bias=0.0, accum=None):
        if accum is None:
            return self._op("act", lambda e: e.activation(out=out, in_=in_, func=func, bias=bias, scale=scale), reads, writes)
        return self._op("act", lambda e: e.activation(out=out, in_=in_, func=func, bias=bias, scale=scale, accum_out=accum), reads, writes)

    def tt(self, eng, out, in0, in1, op, reads, writes):
        return self._op(eng, lambda e: e.tensor_tensor(out=out, in0=in0, in1=in1, op=op), reads, writes)

    def ts(self, eng, out, in0, s1, s2, op0, op1, reads, writes):
        if s2 is None:
            return self._op(eng, lambda e: e.tensor_scalar(out=out, in0=in0, scalar1=s1, scalar2=None, op0=op0), reads, writes)
        return self._op(eng, lambda e: e.tensor_scalar(out=out, in0=in0, scalar1=s1, scalar2=s2, op0=op0, op1=op1), reads, writes)

    def stt(self, eng, out, in0, scalar, in1, op0, op1, reads, writes):
        return self._op(eng, lambda e: e.scalar_tensor_tensor(out=out, in0=in0, scalar=scalar, in1=in1, op0=op0, op1=op1), reads, writes)

    def copy(self, eng, out, in_, reads, writes):
        if eng == "act":
            return self._op("act", lambda e: e.activation(out=out, in_=in_, func=AF.Identity), reads, writes)
        return self._op(eng, lambda e: e.tensor_copy(out=out, in_=in_), reads, writes)

    def recip(self, out, in_, reads, writes):
        return self._op("dve", lambda e: e.reciprocal(out=out, in_=in_), reads, writes)

    def memset(self, eng, ap, val, reads, writes):
        return self._op(eng, lambda e: e.memset(ap, val), reads, writes)

    def scan(self, out, d0, d1, reads, writes):
        return self._op("dve", lambda e: e.tensor_tensor_scan(out=out, data0=d0, data1=d1, initial=0.0, op0=ALU.mult, op1=ALU.add), reads, writes)

    def asel(self, out, in_, pattern, cmp, fill, base, cm, reads, writes):
        return self._op("pool", lambda e: e.affine_select(out=out, in_=in_, pattern=pattern, compare_op=cmp, fill=fill, base=base, channel_multiplier=cm), reads, writes)

    def dma(self, q, out, in_, reads, writes):
        return self._op(q, lambda e: e.dma_start(out=out, in_=in_), reads, writes, dma=True)


def build_nc(upto=99, debug=False):
    nc = bass.Bass("TRN2", target_bir_lowering=False)
    k = K(nc)
    P = k.P

    def din(name, shape):
        return nc.dram_tensor(name, shape, F32, kind="ExternalInput").ap()

    x_in = din("x", [S, D])
    c_in = din("c", [8, 128])
    ada_w = din("ada_w", [2, D, 6 * D])
    ada_b = din("ada_b", [2, 6 * D])
    a_w_in = din("a_w_in", [D, 4096])
    a_lb = din("a_lb_logits", [16, 128])
    a_ng = din("a_norm_g", [1, 128])
    a_w_out = din("a_w_out", [D, D])
    kv_ada_w = din("kv_ada_w", [D, 2 * D])
    kv_ada_b = din("kv_ada_b", [1, 2 * D])
    kv_w = din("kv_w", [D, 2056])
    kv_bf = din("kv_b_f", [1, 8])
    k_ng = din("k_norm_g", [1, 128])
    b_w_q = din("b_w_q", [D, 2 * D])
    q_ng = din("q_norm_g", [1, 128])
    b_w_out = din("b_w_out", [D, D])
    w_up = din("ffn_w_up", [2, D, 2 * DFF])
    conv_w = din("ffn_conv_w", [2, 132, 128])
    conv_b = din("ffn_conv_b", [2, 44, 128])
    w_down = din("ffn_w_down", [2, DFF, D])
    out_d = nc.dram_tensor("out", [S, D], F32, kind="ExternalOutput").ap()
    skind = "ExternalOutput" if debug else "Internal"
    modsD = nc.dram_tensor("modsD", [14, D], F32, kind=skind).ap()
    x1_d = nc.dram_tensor("x1", [S, D], F32, kind=skind).ap()
    x2_d = nc.dram_tensor("x2", [S, D], F32, kind=skind).ap()
    x3_d = nc.dram_tensor("x3", [S, D], F32, kind=skind).ap()
    kT_d = nc.dram_tensor("kT_d", [NB, 128, 8 * TB], BF16, kind="Internal").ap()
    qT_d = nc.dram_tensor("qT_d", [NB, 128, 8 * TB], BF16, kind="Internal").ap()
    v_d = nc.dram_tensor("v_d", [S, D], BF16, kind="Internal").ap()
    gate_d = nc.dram_tensor("gate_d", [S, D], F32, kind="Internal").ap()
    bx1, bx2, bx3, bmods, bkT, bqT, bv, bgate, bout = (Buf(n) for n in "x1 x2 x3 mods kT qT v gate out".split())

    import contextlib
    st = contextlib.ExitStack()
    arena_t = st.enter_context(nc.sbuf_tensor("arena", [128, ARENA], U8))
    ps_t = st.enter_context(nc.psum_tensor("psum", [128, 8 * 2048], U8))
    A = Arena(arena_t, ARENA)
    PS = Arena(ps_t, 8 * 2048)

    def T(shape, dt, name=""):
        return A.alloc(shape, dt), Buf(name)

    bankbuf = [Buf(f"bank{i}") for i in range(8)]

    def BV(bank, shape, dt, boff=0):
        n = int(np.prod(shape))
        nb = n * DSZ[dt]
        assert boff + nb <= 2048
        off = bank * 2048 + boff
        ap = ps_t[:, off:off + nb].bitcast(dt)
        if len(shape) == 2:
            ap = ap.rearrange("p (a b) -> p a b", a=shape[0], b=shape[1])
        return ap, bankbuf[bank]

    ident_f, b_idf = T([128], F32)
    ident_bf, b_idb = T([128], BF16)
    ones_f, b_1f = T([128], F32)
    ones_bf, b_1b = T([128], BF16)
    tri_f, b_tri = T([128], F32)
    e0_f, b_e0 = T([128], F32)
    mask2_f, b_m2 = T([128], F32)
    maskc_bf, b_mc = T([128], BF16)
    scanmsk, b_sm = T([TB], F32)
    modcol, b_mod = T([112], F32)
    ccol, b_cc = T([8], F32)
    cact, b_ca = T([8], F32)
    lbcol, b_lb = T([16], F32)
    omlb, b_omlb = T([8], F32)
    nomlb, b_nomlb = T([8], F32)
    gcol, b_gc = T([3], F32)
    convc, b_cv = T([2, 176], F32)
    ss_a, b_ssa = T([NT], F32)
    ss_b, b_ssb = T([NT], F32)
    rstd_all, b_rs = T([NT], F32)
    ncum_all, b_nc = T([NT, 8], F32)
    cfirst_all, b_cf = T([NT, 8], F32)
    bfbc, b_bf = T([8], F32)
    scr, b_scr = T([1], F32)
    junk, b_junk = T([D], BF16)
    pmark = A.mark()

    def phase_reset():
        A.reset(pmark)
        k.barrier(scr)

    def load_cols(dst, bdst, src, n, stg, bstg, pstg, bpstg, rd=()):
        k.dma("sp", stg[0:n, :], src, list(rd), [bstg])
        k.tr(pstg[:, 0:n], stg[0:n, :], ident_f[0:n, 0:n], [bstg, b_idf], [bpstg])
        k.copy("dve", dst, pstg[:, 0:n], [bpstg], [bdst])

    k.memset("pool", ident_f, 0.0, [], [b_idf])
    k.asel(ident_f, ident_f, [[-1, 128]], ALU.not_equal, 1.0, 0, 1, [b_idf], [b_idf])
    k.copy("dve", ident_bf, ident_f, [b_idf], [b_idb])
    k.memset("pool", ones_f, 1.0, [], [b_1f])
    k.memset("pool", ones_bf, 1.0, [], [b_1b])
    k.memset("pool", tri_f, 1.0, [], [b_tri])
    k.asel(tri_f, tri_f, [[1, 128]], ALU.is_ge, 0.0, 0, -1, [b_tri], [b_tri])
    k.copy("dve", maskc_bf, tri_f, [b_tri], [b_mc])
    k.copy("dve", mask2_f, tri_f, [b_tri], [b_m2])
    k.memset("dve", mask2_f[0:64, 64:128], 0.0, [b_m2], [b_m2])
    k.memset("pool", e0_f, 0.0, [], [b_e0])
    k.asel(e0_f, e0_f, [[0, 128]], ALU.not_equal, 1.0, 0, 1, [b_e0], [b_e0])
    k.memset("pool", scanmsk, 1.0, [], [b_sm])
    k.memset("pool", scanmsk.rearrange("p (c t) -> p c t", t=64)[:, :, 0:1], 0.0, [b_sm], [b_sm])
    k.memset("pool", ncum_all, 0.0, [], [b_nc])

    stg, b_stg = T([128], F32)
    pstg, b_pstg = BV(0, [128], F32)
    load_cols(ccol, b_cc, c_in, 8, stg, b_stg, pstg, b_pstg)
    load_cols(lbcol, b_lb, a_lb, 16, stg, b_stg, pstg, b_pstg)
    k.dma("sp", stg[0:1, :], a_ng, [], [b_stg])
    k.dma("sp", stg[1:2, :], k_ng, [], [b_stg])
    k.dma("sp", stg[2:3, :], q_ng, [], [b_stg])
    k.tr(pstg[:, 0:3], stg[0:3, :], ident_f[0:3, 0:3], [b_stg, b_idf], [b_pstg])
    k.copy("dve", gcol, pstg[:, 0:3], [b_pstg], [b_gc])
    for l in range(2):
        load_cols(convc[:, l, 0:128], b_cv, conv_w[l, 0:128, :], 128, stg, b_stg, pstg, b_pstg)
        load_cols(convc[:, l, 128:132], b_cv, conv_w[l, 128:132, :], 4, stg, b_stg, pstg, b_pstg)
        load_cols(convc[:, l, 132:176], b_cv, conv_b[l], 44, stg, b_stg, pstg, b_pstg)
    k.dma("sp", bfbc, kv_bf.partition_broadcast(128), [], [b_bf])
    k.tt("dve", lbcol[:, 0:8], lbcol[:, 8:16], lbcol[:, 0:8], ALU.subtract, [b_lb], [b_lb])
    k.act(lbcol[:, 0:8], lbcol[:, 0:8], AF.Exp, [b_lb], [b_lb])
    k.ts("dve", lbcol[:, 0:8], lbcol[:, 0:8], 1.0, None, ALU.add, None, [b_lb], [b_lb])
    k.recip(lbcol[:, 0:8], lbcol[:, 0:8], [b_lb], [b_lb])
    k.ts("dve", omlb, lbcol[:, 0:8], -1.0, 1.0, ALU.mult, ALU.add, [b_lb], [b_omlb])
    k.ts("dve", nomlb, lbcol[:, 0:8], 1.0, -1.0, ALU.mult, ALU.add, [b_lb], [b_nomlb])
    k.act(cact, ccol, AF.Silu, [b_cc], [b_ca])

    xr = Ring([T([D], F32) for _ in range(3)])

    def sumsq(xt, bxt, ssdst, bss, t):
        k.act(junk, xt, AF.Square, [bxt], [b_junk, bss], accum=ssdst[:, t:t + 1])

    for t in range(NT):
        xt, bxt = xr.next()
        k.dma("sp", xt, x_in[t * 128:(t + 1) * 128, :], [], [bxt])
        sumsq(xt, bxt, ss_a, b_ssa, t)

    wst = Ring([T([3072], F32) for _ in range(3)])
    brow, b_brow = T([3072], F32)
    mrow, b_mrow = T([3072], F32)
    prow = [BV(1 + i, [512], F32) for i in range(6)]
    modflat = modsD.rearrange("(o v) n -> o (v n)", o=1)
    for (Wd_, bd_, row0, width) in ((ada_w[0], ada_b[0:1, :], 0, 6144), (ada_w[1], ada_b[1:2, :], 6, 6144), (kv_ada_w, kv_ada_b, 12, 2048)):
        for c0 in range(0, width, 3072):
            wd = min(3072, width - c0)
            nn = wd // 512
            for kc in range(KC):
                s_, bs_ = wst.next()
                k.dma("sp", s_[:, 0:wd], Wd_[kc * 128:(kc + 1) * 128, c0:c0 + wd], [], [bs_])
                for n in range(nn):
                    k.mm(prow[n][0][0:1, :], cact[:, kc:kc + 1], s_[:, n * 512:(n + 1) * 512], kc == 0, kc == KC - 1, [b_ca, bs_], [prow[n][1]])
            k.dma("sp", brow[0:1, 0:wd], bd_[:, c0:c0 + wd], [], [b_brow])
            for n in range(nn):
                k.tt("dve", mrow[0:1, n * 512:(n + 1) * 512], prow[n][0][0:1, :], brow[0:1, n * 512:(n + 1) * 512], ALU.add, [prow[n][1], b_brow], [b_mrow])
            k.dma("sp", modflat[:, row0 * D + c0: row0 * D + c0 + wd], mrow[0:1, 0:wd], [b_mrow], [bmods])
    load_cols(modcol, b_mod, modsD.rearrange("v (c p) -> (v c) p", p=128), 112, stg, b_stg, pstg, b_pstg, rd=[bmods])
    k.ts("dve", gcol[:, 2:3], gcol[:, 2:3], 128.0 ** -0.5, None, ALU.mult, None, [b_gc], [b_gc])
    for v in (1, 4, 7, 10, 13):
        k.ts("dve", modcol[:, v * 8:(v + 1) * 8], modcol[:, v * 8:(v + 1) * 8], 1.0, None, ALU.add, None, [b_mod], [b_mod])

    def calc_rstd(ss, bss):
        k.act(rstd_all, ss, AF.Ln, [bss], [b_rs], scale=1.0 / D, bias=EPS)
        k.act(rstd_all, rstd_all, AF.Exp, [b_rs], [b_rs], scale=-0.5)

    def g_bcast(dst, bdst, v):
        k.dma("sp", dst, modsD[v:v + 1, :].partition_broadcast(128), [bmods], [bdst])

    def load_w(dst, bdst, src, nk, c0=0, c1=None):
        for kc in range(nk):
            if c1 is None:
                k.dma("pool", dst[:, kc, :], src[kc * 128:(kc + 1) * 128, :], [], [bdst])
            else:
                k.dma("pool", dst[:, kc, 0:c1 - c0], src[kc * 128:(kc + 1) * 128, c0:c1], [], [bdst])

    def make_hT(xt, bxt, t, hn, bhn, tp, btp, variants):
        k.ts("dve", hn, xt, rstd_all[:, t:t + 1], None, ALU.mult, None, [bxt, b_rs], [bhn])
        for kc in range(KC):
            k.tr(tp[:, kc, :], hn[:, kc * 128:(kc + 1) * 128], ident_bf, [bhn, b_idb], [btp])
        for (vs, vh, dst, bdst) in variants:
            for kc in range(KC):
                sc = modcol[:, vs * 8 + kc: vs * 8 + kc + 1]
                sh = modcol[:, vh * 8 + kc: vh * 8 + kc + 1]
                if kc % 2 == 0:
                    k.act(dst[:, kc, :], tp[:, kc, :], AF.Identity, [btp, b_mod], [bdst], scale=sc, bias=sh)
                else:
                    k.ts("dve", dst[:, kc, :], tp[:, kc, :], sc, sh, ALU.mult, ALU.add, [btp, b_mod], [bdst])

    def residual_out(ypair, xres, bxres, gbc, b_gbc, tmp, btmp, xnew, bxnew, dst_d, bdst_d, t, ss_next, b_ssn):
        for half in range(2):
            yp, byp = ypair[half]
            k.tt("dve", tmp[:, half * 512:(half + 1) * 512], yp, gbc[:, half * 512:(half + 1) * 512], ALU.mult, [byp, b_gbc], [btmp])
        k.tt("pool", xnew, tmp, xres, ALU.add, [btmp, bxres], [bxnew])
        k.dma("pool", dst_d[t * 128:(t + 1) * 128, :], xnew, [bxnew], [bdst_d])
        sumsq(xnew, bxnew, ss_next, b_ssn, t)

    def phase_hgrn(x_src, bsrc, x_dst, bdst, ss_in, b_ssin, ss_out, b_ssout):
        phase_reset()
        calc_rstd(ss_in, b_ssin)
        w_in_sb, b_win = T([KC, 4096], BF16)
        w_out_sb, b_wout = T([KC, D], BF16)
        load_w(w_in_sb, b_win, a_w_in, KC)
        load_w(w_out_sb, b_wout, a_w_out, KC)
        g1bc, b_g1 = T([D], F32)
        g_bcast(g1bc, b_g1, 2)
        xr = Ring([T([D], F32) for _ in range(3)])
        hnr = Ring([T([D], BF16) for _ in range(2)])
        hTr = Ring([T([KC, TB], BF16) for _ in range(2)])
        tmpA = Ring([T([TB], F32) for _ in range(2)])
        tmpB = Ring([T([TB], F32) for _ in range(2)])
        tmpC = Ring([T([TB], F32) for _ in range(2)])
        kstr = Ring([T([TB], BF16) for _ in range(8)])
        E_all = A.alloc([8, TB], F32)
        kR_all = A.alloc([8, TB], BF16)
        qE_all = A.alloc([8, TB], BF16)
        gs_all = A.alloc([8, TB], F32)
        ks_all = A.alloc([TPB, 8, 128], BF16)
        v_all = A.alloc([TPB, D], BF16)
        Elast = A.alloc([8, TB // 64], F32)
        st32 = A.alloc([8, 128], F32)
        stb = A.alloc([8, 128], BF16)
        bE, bkR, bqE, bgs, bks, bEl, bst32, bstb = ([Buf() for _ in range(8)] for _ in range(8))
        bv_ = [Buf() for _ in range(TPB)]
        atr = Ring([T([4, 128], BF16) for _ in range(4)])
        oTr = Ring([T([D], F32) for _ in range(2)])
        sq, bsq = T([D], BF16)
        lnr, blnr = T([D], F32)
        on, bon = T([D], BF16)
        tmp, btmp = T([D], F32)
        xnew, bxnew = T([D], F32)
        tp, btp = BV(0, [KC, 128], BF16)
        pa = Ring([BV(1, [TB], F32), BV(2, [TB], F32)])
        pbs = [BV(3, [512], F32), BV(4, [512], F32)]
        pbig = ps_t[:, 3 * 2048:5 * 2048].bitcast(F32)
        pb = Ring(pbs)
        scg = [BV(5, [4, 128], F32), BV(1, [4, 128], F32)]
        pog = [BV(6, [4, 128], F32), BV(2, [4, 128], F32)]
        dsg = [BV(7, [4, 128], F32), BV(3, [4, 128], F32)]
        tpks = [BV(0, [TPB, 128], BF16), BV(5, [TPB, 128], BF16)]
        k.memset("pool", st32, 0.0, [], bst32)
        k.memset("pool", stb, 0.0, [], bstb)
        NCH = TB // 64
        for b in range(DBG.get("nb", NB)):
            hT, bhT = hTr.next()
            for ti in range(TPB):
                t = b * TPB + ti
                xt, bxt = xr.next()
                hn, bhn = hnr.next()
                k.dma("sp", xt, x_src[t * 128:(t + 1) * 128, :], [bsrc], [bxt])
                make_hT(xt, bxt, t, hn, bhn, tp, btp, [(1, 0, hT[:, :, ti * 128:(ti + 1) * 128], bhT)])
            if DBG.get("stage", 9) < 1:
                continue
            ksts = []
            for h in range(DBG.get("nh", 8)):
                pf, bpf = pa.next()
                for kc in range(KC):
                    k.mm(pf, w_in_sb[:, kc, 1024 + h * 128:1024 + (h + 1) * 128], hT[:, kc, :], kc == 0, kc == KC - 1, [b_win, bhT], [bpf])
                ta, bta = tmpA.next()
                tb_, btb = tmpB.next()
                tc, btc = tmpC.next()
                k.act(ta, pf, AF.Exp, [bpf], [bta])
                if DBG.get("fsub", 9) < 1:
                    continue
                k.act(ta, ta, AF.Ln, [bta], [bta], bias=1.0)
                k.act(ta, ta, AF.Exp, [bta], [bta], scale=-1.0)
                k.act(tb_, ta, AF.Ln, [bta, b_nomlb], [btb], scale=nomlb[:, h:h + 1], bias=1.0)
                if DBG.get("fsub", 9) < 2:
                    continue
                k.scan(tc, scanmsk, tb_, [b_sm, btb], [btc])
                k.act(E_all[:, h, :], tc, AF.Exp, [btc], [bE[h]])
                k.act(tb_, tc, AF.Exp, [btc], [btb], scale=-1.0)
                k.stt("dve", kR_all[:, h, :], ta, omlb[:, h:h + 1], tb_, ALU.mult, ALU.mult, [bta, btb, b_omlb], [bkR[h]])
                if DBG.get("fsub", 9) < 3:
                    continue
                kst, bkst = kstr.next()
                Ev = E_all[:, h, :].rearrange("p (c t) -> p c t", t=64)
                k.tt("pool", kst.rearrange("p (c t) -> p c t", t=64), kR_all[:, h, :].rearrange("p (c t) -> p c t", t=64),
                     Ev[:, :, 63:64].to_broadcast([128, NCH, 64]), ALU.mult, [bkR[h], bE[h]], [bkst])
                k.copy("pool", Elast[:, h, :], Ev[:, :, 63], [bE[h]], [bEl[h]])
                ksts.append((kst, bkst))
            if DBG.get("stage", 9) < 2:
                continue
            for ti in range(TPB):
                for half in range(2):
                    pv_, bpv = pb.next()
                    for kc in range(KC):
                        k.mm(pv_, hT[:, kc, ti * 128:(ti + 1) * 128], w_in_sb[:, kc, 2048 + half * 512:2048 + (half + 1) * 512], kc == 0, kc == KC - 1, [bhT, b_win], [bpv])
                    k.copy("dve", v_all[:, ti, half * 512:(half + 1) * 512], pv_, [bpv], [bv_[ti]])
            for h in range(8):
                kst, bkst = ksts[h]
                tpk_, btpk_ = tpks[h % 2]
                for ti in range(TPB):
                    k.tr(tpk_[:, ti, :], kst[:, ti * 128:(ti + 1) * 128], ident_bf, [bkst, b_idb], [btpk_])
                for ti in range(TPB):
                    k.copy("dve", ks_all[:, ti, h, :], tpk_[:, ti, :], [btpk_], [bks[h]])
            for h in range(8):
                pq, bpq = pa.next()
                for kc in range(KC):
                    k.mm(pq, w_in_sb[:, kc, h * 128:(h + 1) * 128], hT[:, kc, :], kc == 0, kc == KC - 1, [b_win, bhT], [bpq])
                if DBG.get("ssub", 9) < 0:
                    continue
                ta, bta = tmpA.next()
                k.act(ta, pq, DBG.get("qf", AF.Silu), [bpq], [bta])
                if DBG.get("ssub", 9) < 1:
                    continue
                k.tt("pool", qE_all[:, h, :], ta, E_all[:, h, :], ALU.mult, [bta, bE[h]], [bqE[h]])
                if DBG.get("ssub", 9) < 2:
                    continue
                pg, bpg = pa.next()
                for kc in range(KC):
                    k.mm(pg, w_in_sb[:, kc, 3072 + h * 128:3072 + (h + 1) * 128], hT[:, kc, :], kc == 0, kc == KC - 1, [b_win, bhT], [bpg])
                k.act(gs_all[:, h, :], pg, AF.Silu, [bpg], [bgs[h]])
            if DBG.get("stage", 9) < 4:
                continue
            for ti in range(TPB):
                t = b * TPB + ti
                cs = slice(ti * 128, (ti + 1) * 128)
                oT, boT = oTr.next()
                oT3 = oT.rearrange("p (h t) -> p h t", t=128)
                atgs = [atr.next() for _ in range(2)]
                for g in range(2):
                    scb, bscb = scg[g]
                    for i_, h in enumerate(range(g * 4, g * 4 + 4)):
                        k.mm(scb[:, i_, :], kR_all[:, h, cs], qE_all[:, h, cs], True, True, [bkR[h], bqE[h]], [bscb])
                for g in range(2):
                    scb, bscb = scg[g]
                    atg, batg = atgs[g]
                    for i_ in range(4):
                        k.tt("dve", atg[:, i_, :], scb[:, i_, :], mask2_f, ALU.mult, [bscb, b_m2], [batg])
                for c in range(2):
                    cc = slice(c * 64, (c + 1) * 64)
                    pr = slice(c * 64, (c + 1) * 64)
                    ch = ti * 2 + c
                    for g in range(2):
                        atg, batg = atgs[g]
                        pob, bpob = pog[g]
                        dsb, bdsb = dsg[g]
                        for i_, h in enumerate(range(g * 4, g * 4 + 4)):
                            k.mm(pob[:, i_, cc], v_all[:, ti, h * 128:(h + 1) * 128], atg[:, i_, cc], True, False, [bv_[ti], batg], [bpob])
                            k.mm(pob[:, i_, cc], stb[:, h, :], qE_all[:, h, ti * 128 + c * 64: ti * 128 + (c + 1) * 64], False, True, [bstb[h], bqE[h]], [bpob])
                            k.mm(dsb[:, i_, :], ks_all[pr, ti, h, :], v_all[pr, ti, h * 128:(h + 1) * 128], True, True, [bks[h], bv_[ti]], [bdsb])
                    for g in range(2):
                        dsb, bdsb = dsg[g]
                        for i_, h in enumerate(range(g * 4, g * 4 + 4)):
                            k.stt("dve", st32[:, h, :], st32[:, h, :], Elast[:, h, ch:ch + 1], dsb[:, i_, :], ALU.mult, ALU.add, [bst32[h], bEl[h], bdsb], [bst32[h]])
                            k.copy("pool", stb[:, h, :], st32[:, h, :], [bst32[h]], [bstb[h]])
                for g in range(2):
                    pob, bpob = pog[g]
                    for i_, h in enumerate(range(g * 4, g * 4 + 4)):
                        k.copy("dve", oT3[:, h, :], pob[:, i_, :], [bpob], [boT])
                if DBG.get("stage", 9) < 5:
                    continue
                k.act(sq, oT, AF.Square, [boT], [bsq])
                for half in range(2):
                    k.mm(pbs[half][0], ones_bf, sq[:, half * 512:(half + 1) * 512], True, True, [b_1b, bsq], [pbs[half][1]])
                k.act(lnr, pbig, AF.Ln, [pbs[0][1], pbs[1][1]], [blnr], scale=1.0 / 128, bias=EPS)
                k.act(lnr, lnr, AF.Exp, [blnr], [blnr], scale=-0.5)
                k.stt("dve", lnr, oT, gcol[:, 0:1], lnr, ALU.mult, ALU.mult, [boT, b_gc, blnr], [blnr])
                k.tt("pool", on.rearrange("p (h t) -> p h t", t=128), lnr.rearrange("p (h t) -> p h t", t=128), gs_all[:, :, cs], ALU.mult, [blnr] + bgs, [bon])
                on3 = on.rearrange("p (h t) -> p h t", t=128)
                xres, bxres = xr.next()
                k.dma("sp", xres, x_src[t * 128:(t + 1) * 128, :], [bsrc], [bxres])
                ypair = []
                for half in range(2):
                    yp, byp = pbs[half]
                    for h in range(8):
                        k.mm(yp, on3[:, h, :], w_out_sb[:, h, half * 512:(half + 1) * 512], h == 0, h == 7, [bon, b_wout], [byp])
                    ypair.append((yp, byp))
                residual_out(ypair, xres, bxres, g1bc, b_g1, tmp, btmp, xnew, bxnew, x_dst, bdst, t, ss_out, b_ssout)

    def phase_ffn(l, x_src, bsrc, x_dst, bdst, ss_in, b_ssin, ss_out, b_ssout, vsc, vsh, vg):
        phase_reset()
        calc_rstd(ss_in, b_ssin)
        wup, b_wup = T([KC, 2 * DFF], BF16)
        wdn, b_wdn = T([NJ, D], BF16)
        load_w(wup, b_wup, w_up[l], KC)
        load_w(wdn, b_wdn, w_down[l], NJ)
        gbc, b_gbc = T([D], F32)
        g_bcast(gbc, b_gbc, vg)
        xr = Ring([T([D], F32) for _ in range(2)])
        hnr = Ring([T([D], BF16) for _ in range(2)])
        hTr = Ring([T([KC, TB + 2], BF16) for _ in range(2)])
        t0r = Ring([T([TB], F32) for _ in range(4)])
        sgr = Ring([T([TB], F32) for _ in range(2)])
        actr = Ring([T([NJ, TB], BF16) for _ in range(2)])
        tmp, btmp = T([D], F32)
        xnew, bxnew = T([D], F32)
        tp, btp = BV(0, [KC, 128], BF16)
        pur = Ring([BV(1 + i, [TB + 2], F32) for i in range(4)])
        pbs = [BV(5, [512], F32), BV(6, [512], F32)]
        def do_hT(b, prev):
            hT, bhT = hTr.next()
            if prev is None:
                k.memset("pool", hT[:, :, 0:2], 0.0, [], [bhT])
            else:
                k.copy("pool", hT[:, :, 0:2], prev[0][:, :, TB:TB + 2], [prev[1]], [bhT])
            for ti in range(TPB):
                t = b * TPB + ti
                xt, bxt = xr.next()
                hn, bhn = hnr.next()
                k.dma("sp", xt, x_src[t * 128:(t + 1) * 128, :], [bsrc], [bxt])
                make_hT(xt, bxt, t, hn, bhn, tp, btp, [(vsc, vsh, hT[:, :, 2 + ti * 128:2 + (ti + 1) * 128], bhT)])
            return hT, bhT

        def do_up(hT, bhT):
            aT, baT = actr.next()
            for j in range(NJ):
                res = []
                for which, cidx in ((0, j), (1, NJ + j)):
                    pu, bpu = pur.next()
                    for kc in range(KC):
                        k.mm(pu, wup[:, kc, cidx * 128:(cidx + 1) * 128], hT[:, kc, :], kc == 0, kc == KC - 1, [b_wup, bhT], [bpu])
                    t0, bt0 = t0r.next()
                    k.ts("dve", t0, pu[:, 2:2 + TB], convc[:, l, 88 + cidx:89 + cidx], convc[:, l, 132 + cidx:133 + cidx], ALU.mult, ALU.add, [bpu, b_cv], [bt0])
                    k.stt("dve", t0, pu[:, 1:1 + TB], convc[:, l, 44 + cidx:45 + cidx], t0, ALU.mult, ALU.add, [bpu, b_cv, bt0], [bt0])
                    k.stt("dve", t0, pu[:, 0:TB], convc[:, l, cidx:cidx + 1], t0, ALU.mult, ALU.add, [bpu, b_cv, bt0], [bt0])
                    res.append((t0, bt0))
                sg, bsg = sgr.next()
                k.act(sg, res[0][0], AF.Silu, [res[0][1]], [bsg])
                k.tt("pool", aT[:, j, :], sg, res[1][0], ALU.mult, [bsg, res[1][1]], [baT])
            return aT, baT

        def do_down(b, aT, baT):
            for ti in range(TPB):
                t = b * TPB + ti
                xres, bxres = xr.next()
                k.dma("sp", xres, x_src[t * 128:(t + 1) * 128, :], [bsrc], [bxres])
                ypair = []
                for half in range(2):
                    yp, byp = pbs[half]
                    for j in range(NJ):
                        k.mm(yp, aT[:, j, ti * 128:(ti + 1) * 128], wdn[:, j, half * 512:(half + 1) * 512], j == 0, j == NJ - 1, [baT, b_wdn], [byp])
                    ypair.append((yp, byp))
                residual_out(ypair, xres, bxres, gbc, b_gbc, tmp, btmp, xnew, bxnew, x_dst, bdst, t, ss_out, b_ssout)

        nbk = DBG.get('fnb', NB)
        hts = do_hT(0, None)
        ats = do_up(*hts)
        for b in range(nbk):
            if b + 1 < nbk:
                hts_n = do_hT(b + 1, hts)
                ats_n = do_up(*hts_n)
            do_down(b, *ats)
            if b + 1 < nbk:
                hts, ats = hts_n, ats_n

    def phase_fox_a(x_src, bsrc, ss_in, b_ssin):
        phase_reset()
        calc_rstd(ss_in, b_ssin)
        kvw, b_kvw = T([KC, 2056], BF16)
        wq, b_wq = T([KC, 2048], BF16)
        load_w(kvw, b_kvw, kv_w, KC)
        load_w(wq, b_wq, b_w_q, KC)
        xr = Ring([T([D], F32) for _ in range(3)])
        hnr = Ring([T([D], BF16) for _ in range(2)])
        hkvr = Ring([T([KC, TB], BF16) for _ in range(2)])
        h1r = Ring([T([KC, TB], BF16) for _ in range(2)])
        kTbr = Ring([T([8, TB], BF16) for _ in range(2)])
        qTbr = Ring([T([8, TB], BF16) for _ in range(2)])
        sqr = Ring([T([TB], BF16) for _ in range(2)])
        lnrr = Ring([T([TB], F32) for _ in range(2)])
        vbr = Ring([T([D], BF16) for _ in range(2)])
        gtr = Ring([T([D], F32) for _ in range(2)])
        ger = Ring([T([512], F32) for _ in range(2)])
        lfr = Ring([T([8], F32) for _ in range(2)])
        run, brun = T([8], F32)
        tp, btp = BV(0, [KC, 128], BF16)
        pa = Ring([BV(1, [TB], F32), BV(2, [TB], F32)])
        pssr = Ring([BV(3, [TB], F32), BV(4, [TB], F32)])
        pb = Ring([BV(5, [512], F32), BV(6, [512], F32)])
        pf8, bpf8 = BV(7, [8], F32, 0)
        pcum, bpcum = BV(7, [8], F32, 512)
        pcf, bpcf = BV(7, [8], F32, 1024)
        k.memset("pool", run, 0.0, [], [brun])
        for b in range(NB):
            hkv, bhkv = hkvr.next()
            h1, bh1 = h1r.next()
            for ti in range(TPB):
                t = b * TPB + ti
                xt, bxt = xr.next()
                hn, bhn = hnr.next()
                k.dma("sp", xt, x_src[t * 128:(t + 1) * 128, :], [bsrc], [bxt])
                tsl = slice(ti * 128, (ti + 1) * 128)
                make_hT(xt, bxt, t, hn, bhn, tp, btp, [(13, 12, hkv[:, :, tsl], bhkv), (7, 6, h1[:, :, tsl], bh1)])
            kTb, bkTb = kTbr.next()
            qTb, bqTb = qTbr.next()
            jobs = [(kvw, b_kvw, hkv, bhkv, 1, kTb, bkTb, h) for h in range(8)] + [(wq, b_wq, h1, bh1, 2, qTb, bqTb, h) for h in range(8)]

            def proj(job):
                W, bW, hs, bhs, gi, dstb, bdstb, h = job
                pk, bpk = pa.next()
                for kc in range(KC):
                    k.mm(pk, W[:, kc, h * 128:(h + 1) * 128], hs[:, kc, :], kc == 0, kc == KC - 1, [bW, bhs], [bpk])
                return pk, bpk

            cur = proj(jobs[0])
            for n, job in enumerate(jobs):
                nxt = proj(jobs[n + 1]) if n + 1 < len(jobs) else None
                W, bW, hs, bhs, gi, dstb, bdstb, h = job
                pk, bpk = cur
                sq, bsq = sqr.next()
                k.act(sq, pk, AF.Square, [bpk], [bsq])
                pss, bpss = pssr.next()
                k.mm(pss, ones_bf, sq, True, True, [b_1b, bsq], [bpss])
                lnr, blnr = lnrr.next()
                k.act(lnr, pss, AF.Ln, [bpss], [blnr], scale=1.0 / 128, bias=EPS)
                k.act(lnr, lnr, AF.Exp, [blnr], [blnr], scale=-0.5)
                k.stt("dve", dstb[:, h, :], pk, gcol[:, gi:gi + 1], lnr, ALU.mult, ALU.mult, [bpk, b_gc, blnr], [bdstb])
                cur = nxt
            k.dma("pool", kT_d[b], kTb.rearrange("p h t -> p (h t)"), [bkTb], [bkT])
            k.dma("pool", qT_d[b], qTb.rearrange("p h t -> p (h t)"), [bqTb], [bqT])
            for ti in range(TPB):
                t = b * TPB + ti
                tsl = slice(ti * 128, (ti + 1) * 128)
                vb, bvb = vbr.next()
                for half in range(2):
                    pv_, bpv = pb.next()
                    for kc in range(KC):
                        k.mm(pv_, hkv[:, kc, tsl], kvw[:, kc, 1024 + half * 512:1024 + (half + 1) * 512], kc == 0, kc == KC - 1, [bhkv, b_kvw], [bpv])
                    k.copy("dve", vb[:, half * 512:(half + 1) * 512], pv_, [bpv], [bvb])
                k.dma("pool", v_d[t * 128:(t + 1) * 128, :], vb, [bvb], [bv])
                gt, bgt = gtr.next()
                for half in range(2):
                    pg, bpg = pb.next()
                    for kc in range(KC):
                        k.mm(pg, h1[:, kc, tsl], wq[:, kc, 1024 + half * 512:1024 + (half + 1) * 512], kc == 0, kc == KC - 1, [bh1, b_wq], [bpg])
                    ge, bge = ger.next()
                    k.act(ge, pg, AF.Exp, [bpg], [bge], scale=-1.0)
                    k.act(ge, ge, AF.Ln, [bge], [bge], bias=1.0)
                    k.act(gt[:, half * 512:(half + 1) * 512], ge, AF.Exp, [bge], [bgt], scale=-1.0)
                k.dma("pool", gate_d[t * 128:(t + 1) * 128, :], gt, [bgt], [bgate])
                for kc in range(KC):
                    k.mm(pf8, hkv[:, kc, tsl], kvw[:, kc, 2048:2056], kc == 0, kc == KC - 1, [bhkv, b_kvw], [bpf8])
                lf, blf = lfr.next()
                k.tt("dve", lf, pf8, bfbc, ALU.add, [bpf8, b_bf], [blf])
                k.act(lf, lf, AF.Exp, [blf], [blf], scale=-1.0)
                k.act(lf, lf, AF.Ln, [blf], [blf], bias=1.0)
                k.mm(pcum, tri_f, lf, True, False, [b_tri, blf], [bpcum])
                k.mm(pcum, ones_f, run, False, True, [b_1f, brun], [bpcum])
                k.copy("dve", ncum_all[:, t, :], pcum, [bpcum], [b_nc])
                k.tt("dve", run, run, lf, ALU.add, [brun, blf], [brun])
                k.mm(pcf, e0_f, ncum_all[:, t, :], True, True, [b_e0, b_nc], [bpcf])
                k.copy("dve", cfirst_all[:, t, :], pcf, [bpcf], [b_cf])

    def phase_fox_bc(x_src, bsrc, x_dst, bdst, ss_out, b_ssout):
        phase_reset()
        wo, b_wo = T([KC, D], BF16)
        load_w(wo, b_wo, b_w_out, KC)
        gbc, b_gbc = T([D], F32)
        g_bcast(gbc, b_gbc, 8)
        o_all = A.alloc([NT, D], BF16)
        bo = [Buf() for _ in range(NT)]
        kThr = Ring([T([S], BF16) for _ in range(2)])
        qThr = Ring([T([S], BF16) for _ in range(2)])
        vhr = Ring([T([NT, 132], BF16) for _ in range(2)])
        ghr = Ring([T([NT, 128], F32) for _ in range(2)])
        bir = Ring([T([NT, NT], F32) for _ in range(2)])
        ptr_ = Ring([T([128], BF16) for _ in range(4)])
        recr = Ring([T([1], F32) for _ in range(2)])
        xr = Ring([T([D], F32) for _ in range(2)])
        oTr = Ring([T([KC, 128], BF16) for _ in range(2)])
        tmp, btmp = T([D], F32)
        xnew, bxnew = T([D], F32)
        sr = Ring([BV(i, [128], F32) for i in range(3)])
        por = Ring([BV(3, [132], F32), BV(4, [132], F32)])
        tp, btp = BV(5, [KC, 128], BF16)
        pbs = [BV(6, [512], F32), BV(7, [512], F32)]
        for (vh, bvh) in vhr.items:
            k.memset("pool", vh[:, :, 128:129], 1.0, [], [bvh])
        vdv = v_d.rearrange("(j p) d -> p j d", p=128)
        gdv = gate_d.rearrange("(j p) d -> p j d", p=128)
        for h in range(8):
            kTh, bkTh = kThr.next()
            qTh, bqTh = qThr.next()
            vh, bvh = vhr.next()
            gh, bgh = ghr.next()
            bias, bbias = bir.next()
            k.dma("sp", kTh.rearrange("p (b t) -> p b t", t=TB), kT_d.rearrange("b d (h t) -> d b h t", h=8)[:, :, h, :], [bkT], [bkTh])
            k.dma("sp", qTh.rearrange("p (b t) -> p b t", t=TB), qT_d.rearrange("b d (h t) -> d b h t", h=8)[:, :, h, :], [bqT], [bqTh])
            k.dma("sp", vh[:, :, 0:128], vdv[:, :, h * 128:(h + 1) * 128], [bv], [bvh])
            k.dma("sp", gh, gdv[:, :, h * 128:(h + 1) * 128], [bgate], [bgh])
            for j in range(NT):
                k.ts("dve", bias[:, j, :], cfirst_all[:, :, h], -1.0, ncum_all[:, j, h:h + 1], ALU.mult, ALU.add, [b_cf, b_nc], [bbias])
            pairs = [(Q, j) for Q in range(NT) for j in range(Q + 1)]
            LA = 2
            sbuf_of = {}
            po_of = {}
            for n in range(len(pairs) + LA):
                if n < len(pairs):
                    Q, j = pairs[n]
                    s_, bs = sr.next()
                    sbuf_of[n] = (s_, bs)
                    k.mm(s_, kTh[:, j * 128:(j + 1) * 128], qTh[:, Q * 128:(Q + 1) * 128], True, True, [bkTh, bqTh], [bs])
                m = n - LA
                if m < 0:
                    continue
                Q, j = pairs[m]
                if j == 0:
                    po_of[Q] = por.next()
                po, bpo = po_of[Q]
                s_, bs = sbuf_of.pop(m)
                pt, bpt = ptr_.next()
                k.act(pt, s_, AF.Exp, [bs, bbias], [bpt], bias=bias[:, j, Q:Q + 1])
                if j == Q:
                    k.tt("pool", pt, pt, maskc_bf, ALU.mult, [bpt, b_mc], [bpt])
                k.mm(po[:, 0:129], pt, vh[:, j, 0:129], j == 0, j == Q, [bpt, bvh], [bpo])
                if j == Q:
                    rec, brec = recr.next()
                    k.recip(rec, po[:, 128:129], [bpo], [brec])
                    k.stt("dve", o_all[:, Q, h * 128:(h + 1) * 128], po[:, 0:128], rec, gh[:, Q, :], ALU.mult, ALU.mult, [bpo, brec, bgh], [bo[Q]])
        for t in range(NT):
            for kc in range(KC):
                k.tr(tp[:, kc, :], o_all[:, t, kc * 128:(kc + 1) * 128], ident_bf, [bo[t], b_idb], [btp])
            oT, boT = oTr.next()
            k.copy("dve", oT.rearrange("p a b -> p (a b)"), tp.rearrange("p a b -> p (a b)"), [btp], [boT])
            xres, bxres = xr.next()
            k.dma("sp", xres, x_src[t * 128:(t + 1) * 128, :], [bsrc], [bxres])
            ypair = []
            for half in range(2):
                yp, byp = pbs[half]
                for kc in range(KC):
                    k.mm(yp, oT[:, kc, :], wo[:, kc, half * 512:(half + 1) * 512], kc == 0, kc == KC - 1, [boT, b_wo], [byp])
                ypair.append((yp, byp))
            residual_out(ypair, xres, bxres, gbc, b_gbc, tmp, btmp, xnew, bxnew, x_dst, bdst, t, ss_out, b_ssout)

    bxin = Buf("xin")
    calc = None
    if upto >= 1:
        phase_hgrn(x_in, bxin, x1_d, bx1, ss_a, b_ssa, ss_b, b_ssb)
    if upto >= 2:
        phase_ffn(0, x1_d, bx1, x2_d, bx2, ss_b, b_ssb, ss_a, b_ssa, 4, 3, 5)
    if upto >= 3:
        phase_fox_a(x2_d, bx2, ss_a, b_ssa)
    if upto >= 4:
        phase_fox_bc(x2_d, bx2, x3_d, bx3, ss_b, b_ssb)
    if upto >= 5:
        phase_ffn(1, x3_d, bx3, out_d, bout, ss_b, b_ssb, ss_a, b_ssa, 10, 9, 11)

    k.barrier(scr)
    fin_d = nc.dram_tensor("fin_d", [128, 1], F32, kind="Internal").ap()
    k.dma("sp", fin_d, scr, [], [Buf("fin")])
    P.finalize()
    sems = {e: [st.enter_context(nc.semaphore(f"s_{e}{i}")) for i in range(P.nepoch[e])] for e in CENG}
    dsems = [st.enter_context(nc.semaphore(f"d{i}")) for i in range(P.NDMA_SEM)]
    block = st.enter_context(nc.Block())
    P.emit(block, sems, dsems)
    st.close()
    return nc


_NC_CACHE = {}


def _in_maps(inputs):
    f = lambda a: np.ascontiguousarray(np.asarray(a, dtype=np.float32))
    x = f(inputs["x"])
    c = f(inputs["c"])
    shared = dict(
        ada_w=f(inputs["ada_w"]), ada_b=f(inputs["ada_b"]),
        a_w_in=f(inputs["a_w_in"]).reshape(D, 4096),
        a_lb_logits=f(inputs["a_lb_logits"]).reshape(16, 128),
        a_norm_g=f(inputs["a_norm_g"]).reshape(1, 128),
        a_w_out=f(inputs["a_w_out"]).reshape(D, D),
        kv_ada_w=f(inputs["kv_ada_w"]), kv_ada_b=f(inputs["kv_ada_b"]).reshape(1, 2 * D),
        kv_w=f(inputs["kv_w"]), kv_b_f=f(inputs["kv_b_f"]).reshape(1, 8),
        k_norm_g=f(inputs["k_norm_g"]).reshape(1, 128),
        b_w_q=f(inputs["b_w_q"]).reshape(D, 2 * D),
        q_norm_g=f(inputs["q_norm_g"]).reshape(1, 128),
        b_w_out=f(inputs["b_w_out"]).reshape(D, D),
        ffn_w_up=f(inputs["ffn_w_up"]),
        ffn_conv_w=f(inputs["ffn_conv_w"]).reshape(2, 132, 128),
        ffn_conv_b=f(inputs["ffn_conv_b"]).reshape(2, 44, 128),
        ffn_w_down=f(inputs["ffn_w_down"]),
    )
    maps = []
    for b in range(8):
        m = dict(shared)
        m["x"] = x[b]
        m["c"] = c[b].reshape(8, 128)
        maps.append(m)
    return maps


def kernel(**inputs):
    if "nc" not in _NC_CACHE:
        _NC_CACHE["nc"] = build_nc()
    nc = _NC_CACHE["nc"]
    res = run_bass_kernel_spmd(nc, _in_maps(inputs), core_ids=list(range(8)))
    return np.stack([np.asarray(r["out"], dtype=np.float32) for r in res.results], axis=0)
```

```python
import numpy as np
import concourse.bass as bass
import concourse.mybir as mybir
from concourse.bass_utils import run_bass_kernel_spmd

F32 = mybir.dt.float32
BF16 = mybir.dt.bfloat16
U8 = mybir.dt.uint8
AF = mybir.ActivationFunctionType
ALU = mybir.AluOpType
AX = mybir.AxisListType
DSZ = {F32: 4, BF16: 2, U8: 1}

CENG = ("pe", "act", "dve", "pool")
EPOCH = 12000
STRICT = False


class Buf:
    __slots__ = ("name", "w", "r")

    def __init__(self, name=""):
        self.name = name
        self.w = None
        self.r = {}


class Op:
    __slots__ = ("id", "eng", "fn", "deps", "dma", "seq", "sig", "cnt", "need", "clock", "dsem", "dval", "inc")


class Prog:
    def __init__(self, nc):
        self.nc = nc
        self.ops = []
        self.ndma = 0
        self.dma_last = {}
        self.NDMA_SEM = 40
        self.NHW = 24
        self.nsw = 0

    def op(self, eng, fn, reads=(), writes=(), dma=False):
        o = Op()
        o.id = len(self.ops)
        o.eng = eng
        o.fn = fn
        o.dma = dma
        o.sig = False
        o.cnt = None
        deps = {}
        for b in reads:
            if b.w is not None:
                deps[b.w] = True
        for b in writes:
            if b.w is not None:
                deps.setdefault(b.w, False)
            for r in b.r.values():
                for rid in r:
                    deps.setdefault(rid, False)
        if dma:
            if eng == "pool":
                slot = self.NHW + self.nsw % (self.NDMA_SEM - self.NHW)
                self.nsw += 1
            else:
                slot = self.ndma % self.NHW
                self.ndma += 1
            prev = self.dma_last.get(slot)
            if prev is not None:
                deps.setdefault(prev.id, False)
                o.dval = prev.dval + 16
            else:
                o.dval = 16
            o.dsem = slot
            self.dma_last[slot] = o
        deps.pop(o.id, None)
        o.deps = deps
        for b in reads:
            if dma:
                b.r.setdefault("dma", []).append(o.id)
            else:
                b.r[eng] = [o.id]
        for b in writes:
            b.w = o.id
            b.r = {}
        self.ops.append(o)
        return o

    def finalize(self):
        ops = self.ops
        seqc = {e: 0 for e in CENG}
        known = {e: {c: 0 for c in CENG} for e in CENG + ("sp",)}
        kdma = {e: set() for e in CENG + ("sp",)}
        for o in ops:
            A = o.eng
            if not o.dma:
                seqc[A] += 1
                o.seq = seqc[A]
            else:
                o.seq = 0
            kn = known[A]
            need = []
            dl = sorted(o.deps.items(), key=lambda kv: -kv[0])
            for xid, raw in dl:
                X = ops[xid]
                if X.dma:
                    if xid in kdma[A]:
                        continue
                    need.append(xid)
                    kdma[A].add(xid)
                    for c in CENG:
                        if X.clock[c] > kn[c]:
                            kn[c] = X.clock[c]
                    continue
                E = X.eng
                if X.seq <= kn[E]:
                    continue
                if (not o.dma) and E == A:
                    if A == "pe" or not (raw or STRICT):
                        continue
                need.append(xid)
                X.sig = True
                kn[E] = X.seq
                for c in CENG:
                    if X.clock[c] > kn[c]:
                        kn[c] = X.clock[c]
            o.need = need
            ck = dict(kn)
            if len(kdma[A]) > 512:
                kdma[A] = set(sorted(kdma[A])[-256:])
            o.clock = ck
        cnt = {e: 0 for e in CENG}
        for o in ops:
            if (not o.dma) and o.sig:
                cnt[o.eng] += 1
                o.cnt = cnt[o.eng]
        self.nepoch = {e: cnt[e] // EPOCH + 1 for e in CENG}

    def emit(self, block, sems, dsems):
        ops = self.ops

        def semval(X):
            if X.dma:
                return dsems[X.dsem], X.dval
            k = (X.cnt - 1) // EPOCH
            return sems[X.eng][k], (X.cnt - 1) % EPOCH + 1

        def run(engname):
            def body(e):
                for o in ops:
                    if o.eng != engname:
                        continue
                    for xid in o.need:
                        s, v = semval(ops[xid])
                        e.wait_ge(s, v)
                    ins = o.fn(e)
                    if o.dma:
                        ins.then_inc(dsems[o.dsem], 16)
                    elif o.sig:
                        s, _ = semval(o)
                        ins.then_inc(s, 1)
                if engname == "sp":
                    for slot, o in self.dma_last.items():
                        e.wait_ge(dsems[slot], o.dval)
            return body

        block.tensor(run("pe"))
        block.scalar(run("act"))
        block.vector(run("dve"))
        block.gpsimd(run("pool"))
        block.sync(run("sp"))


class Arena:
    def __init__(self, t, size, part=128):
        self.t = t
        self.size = size
        self.off = 0

    def alloc(self, free_shape, dtype, align=64):
        n = int(np.prod(free_shape))
        nb = n * DSZ[dtype]
        off = (self.off + align - 1) // align * align
        assert off + nb <= self.size, f"arena overflow {off + nb} > {self.size}"
        self.off = off + nb
        ap = self.t[:, off:off + nb].bitcast(dtype)
        if len(free_shape) == 2:
            ap = ap.rearrange("p (a b) -> p a b", a=free_shape[0], b=free_shape[1])
        elif len(free_shape) == 3:
            ap = ap.rearrange("p (a b c) -> p a b c", a=free_shape[0], b=free_shape[1], c=free_shape[2])
        return ap

    def mark(self):
        return self.off

    def reset(self, m):
        self.off = m


S = 4096
D = 1024
NT = 32
KC = 8
TB = 256
NB = S // TB
TPB = TB // 128
DFF = 2816
NJ = 22
EPS = 1e-6
ARENA = 206 * 1024
DBG = {}


class Ring:
    def __init__(self, items):
        self.items = items
        self.i = 0

    def next(self):
        it = self.items[self.i % len(self.items)]
        self.i += 1
        return it


class K:
    def __init__(self, nc):
        self.nc = nc
        self.P = Prog(nc)
        self.Y = Buf("phase")

    def _op(self, eng, fn, reads, writes, dma=False):
        return self.P.op(eng, fn, reads=list(reads) + [self.Y], writes=list(writes), dma=dma)

    def barrier(self, scr):
        self.P.op("dve", lambda e: e.memset(scr, 0.0), reads=[], writes=[self.Y])

    def mm(self, out, lhsT, rhs, start, stop, reads, writes):
        return self._op("pe", lambda e: e.matmul(out, lhsT=lhsT, rhs=rhs, start=start, stop=stop), reads, writes)

    def tr(self, out, in_, ident, reads, writes):
        return self._op("pe", lambda e: e.transpose(out=out, in_=in_, identity=ident), reads, writes)

    def act(self, out, in_, func, reads, writes, scale=1.0, bias=0.0, accum=None):
        if accum is None:
            return self._op("act", lambda e: e.activation(out=out, in_=in_, func=func, bias=bias, scale=scale), reads, writes)
        return self._op("act", lambda e: e.activation(out=out, in_=in_, func=func, bias=bias, scale=scale, accum_out=accum), reads, writes)

    def tt(self, eng, out, in0, in1, op, reads, writes):
        return self._op(eng, lambda e: e.tensor_tensor(out=out, in0=in0, in1=in1, op=op), reads, writes)

    def ts(self, eng, out, in0, s1, s2, op0, op1, reads, writes):
        if s2 is None:
            return self._op(eng, lambda e: e.tensor_scalar(out=out, in0=in0, scalar1=s1, scalar2=None, op0=op0), reads, writes)
        return self._op(eng, lambda e: e.tensor_scalar(out=out, in0=in0, scalar1=s1, scalar2=s2, op0=op0, op1=op1), reads, writes)

    def stt(self, eng, out, in0, scalar, in1, op0, op1, reads, writes):
        return self._op(eng, lambda e: e.scalar_tensor_tensor(out=out, in0=in0, scalar=scalar, in1=in1, op0=op0, op1=op1), reads, writes)

    def copy(self, eng, out, in_, reads, writes):
        if eng == "act":
            return self._op("act", lambda e: e.activation(out=out, in_=in_, func=AF.Identity), reads, writes)
        return self._op(eng, lambda e: e.tensor_copy(out=out, in_=in_), reads, writes)

    def recip(self, out, in_, reads, writes):
        return self._op("dve", lambda e: e.reciprocal(out=out, in_=in_), reads, writes)

    def memset(self, eng, ap, val, reads, writes):
        return self._op(eng, lambda e: e.memset(ap, val), reads, writes)

    def scan(self, out, d0, d1, reads, writes):
        return self._op("dve", lambda e: e.tensor_tensor_scan(out=out, data0=d0, data1=d1, initial=0.0, op0=ALU.mult, op1=ALU.add), reads, writes)

    def asel(self, out, in_, pattern, cmp, fill, base, cm, reads, writes):
        return self._op("pool", lambda e: e.affine_select(out=out, in_=in_, pattern=pattern, compare_op=cmp, fill=fill, base=base, channel_multiplier=cm), reads, writes)

    def dma(self, q, out, in_, reads, writes):
        return self._op(q, lambda e: e.dma_start(out=out, in_=in_), reads, writes, dma=True)


def build_nc(upto=99, debug=False):
    nc = bass.Bass("TRN2", target_bir_lowering=False)
    k = K(nc)
    P = k.P

    def din(name, shape):
        return nc.dram_tensor(name, shape, F32, kind="ExternalInput").ap()

    x_in = din("x", [S, D])
    c_in = din("c", [8, 128])
    ada_w = din("ada_w", [2, D, 6 * D])
    ada_b = din("ada_b", [2, 6 * D])
    a_w_in = din("a_w_in", [D, 4096])
    a_lb = din("a_lb_logits", [16, 128])
    a_ng = din("a_norm_g", [1, 128])
    a_w_out = din("a_w_out", [D, D])
    kv_ada_w = din("kv_ada_w", [D, 2 * D])
    kv_ada_b = din("kv_ada_b", [1, 2 * D])
    kv_w = din("kv_w", [D, 2056])
    kv_bf = din("kv_b_f", [1, 8])
    k_ng = din("k_norm_g", [1, 128])
    b_w_q = din("b_w_q", [D, 2 * D])
    q_ng = din("q_norm_g", [1, 128])
    b_w_out = din("b_w_out", [D, D])
    w_up = din("ffn_w_up", [2, D, 2 * DFF])
    conv_w = din("ffn_conv_w", [2, 132, 128])
    conv_b = din("ffn_conv_b", [2, 44, 128])
    w_down = din("ffn_w_down", [2, DFF, D])
    out_d = nc.dram_tensor("out", [S, D], F32, kind="ExternalOutput").ap()
    skind = "ExternalOutput" if debug else "Internal"
    modsD = nc.dram_tensor("modsD", [14, D], F32, kind=skind).ap()
    x1_d = nc.dram_tensor("x1", [S, D], F32, kind=skind).ap()
    x2_d = nc.dram_tensor("x2", [S, D], F32, kind=skind).ap()
    x3_d = nc.dram_tensor("x3", [S, D], F32, kind=skind).ap()
    kT_d = nc.dram_tensor("kT_d", [NB, 128, 8 * TB], BF16, kind="Internal").ap()
    qT_d = nc.dram_tensor("qT_d", [NB, 128, 8 * TB], BF16, kind="Internal").ap()
    v_d = nc.dram_tensor("v_d", [S, D], BF16, kind="Internal").ap()
    gate_d = nc.dram_tensor("gate_d", [S, D], F32, kind="Internal").ap()
    bx1, bx2, bx3, bmods, bkT, bqT, bv, bgate, bout = (Buf(n) for n in "x1 x2 x3 mods kT qT v gate out".split())

    import contextlib
    st = contextlib.ExitStack()
    arena_t = st.enter_context(nc.sbuf_tensor("arena", [128, ARENA], U8))
    ps_t = st.enter_context(nc.psum_tensor("psum", [128, 8 * 2048], U8))
    A = Arena(arena_t, ARENA)
    PS = Arena(ps_t, 8 * 2048)

    def T(shape, dt, name=""):
        return A.alloc(shape, dt), Buf(name)

    bankbuf = [Buf(f"bank{i}") for i in range(8)]

    def BV(bank, shape, dt, boff=0):
        n = int(np.prod(shape))
        nb = n * DSZ[dt]
        assert boff + nb <= 2048
        off = bank * 2048 + boff
        ap = ps_t[:, off:off + nb].bitcast(dt)
        if len(shape) == 2:
            ap = ap.rearrange("p (a b) -> p a b", a=shape[0], b=shape[1])
        return ap, bankbuf[bank]

    ident_f, b_idf = T([128], F32)
    ident_bf, b_idb = T([128], BF16)
    ones_f, b_1f = T([128], F32)
    ones_bf, b_1b = T([128], BF16)
    tri_f, b_tri = T([128], F32)
    e0_f, b_e0 = T([128], F32)
    mask2_f, b_m2 = T([128], F32)
    maskc_bf, b_mc = T([128], BF16)
    scanmsk, b_sm = T([TB], F32)
    modcol, b_mod = T([112], F32)
    ccol, b_cc = T([8], F32)
    cact, b_ca = T([8], F32)
    lbcol, b_lb = T([16], F32)
    omlb, b_omlb = T([8], F32)
    nomlb, b_nomlb = T([8], F32)
    gcol, b_gc = T([3], F32)
    convc, b_cv = T([2, 176], F32)
    ss_a, b_ssa = T([NT], F32)
    ss_b, b_ssb = T([NT], F32)
    rstd_all, b_rs = T([NT], F32)
    ncum_all, b_nc = T([NT, 8], F32)
    cfirst_all, b_cf = T([NT, 8], F32)
    bfbc, b_bf = T([8], F32)
    scr, b_scr = T([1], F32)
    junk, b_junk = T([D], BF16)
    pmark = A.mark()

    def phase_reset():
        A.reset(pmark)
        k.barrier(scr)

    def load_cols(dst, bdst, src, n, stg, bstg, pstg, bpstg, rd=()):
        k.dma("sp", stg[0:n, :], src, list(rd), [bstg])
        k.tr(pstg[:, 0:n], stg[0:n, :], ident_f[0:n, 0:n], [bstg, b_idf], [bpstg])
        k.copy("dve", dst, pstg[:, 0:n], [bpstg], [bdst])

    k.memset("pool", ident_f, 0.0, [], [b_idf])
    k.asel(ident_f, ident_f, [[-1, 128]], ALU.not_equal, 1.0, 0, 1, [b_idf], [b_idf])
    k.copy("dve", ident_bf, ident_f, [b_idf], [b_idb])
    k.memset("pool", ones_f, 1.0, [], [b_1f])
    k.memset("pool", ones_bf, 1.0, [], [b_1b])
    k.memset("pool", tri_f, 1.0, [], [b_tri])
    k.asel(tri_f, tri_f, [[1, 128]], ALU.is_ge, 0.0, 0, -1, [b_tri], [b_tri])
    k.copy("dve", maskc_bf, tri_f, [b_tri], [b_mc])
    k.copy("dve", mask2_f, tri_f, [b_tri], [b_m2])
    k.memset("dve", mask2_f[0:64, 64:128], 0.0, [b_m2], [b_m2])
    k.memset("pool", e0_f, 0.0, [], [b_e0])
    k.asel(e0_f, e0_f, [[0, 128]], ALU.not_equal, 1.0, 0, 1, [b_e0], [b_e0])
    k.memset("pool", scanmsk, 1.0, [], [b_sm])
    k.memset("pool", scanmsk.rearrange("p (c t) -> p c t", t=64)[:, :, 0:1], 0.0, [b_sm], [b_sm])
    k.memset("pool", ncum_all, 0.0, [], [b_nc])

    stg, b_stg = T([128], F32)
    pstg, b_pstg = BV(0, [128], F32)
    load_cols(ccol, b_cc, c_in, 8, stg, b_stg, pstg, b_pstg)
    load_cols(lbcol, b_lb, a_lb, 16, stg, b_stg, pstg, b_pstg)
    k.dma("sp", stg[0:1, :], a_ng, [], [b_stg])
    k.dma("sp", stg[1:2, :], k_ng, [], [b_stg])
    k.dma("sp", stg[2:3, :], q_ng, [], [b_stg])
    k.tr(pstg[:, 0:3], stg[0:3, :], ident_f[0:3, 0:3], [b_stg, b_idf], [b_pstg])
    k.copy("dve", gcol, pstg[:, 0:3], [b_pstg], [b_gc])
    for l in range(2):
        load_cols(convc[:, l, 0:128], b_cv, conv_w[l, 0:128, :], 128, stg, b_stg, pstg, b_pstg)
        load_cols(convc[:, l, 128:132], b_cv, conv_w[l, 128:132, :], 4, stg, b_stg, pstg, b_pstg)
        load_cols(convc[:, l, 132:176], b_cv, conv_b[l], 44, stg, b_stg, pstg, b_pstg)
    k.dma("sp", bfbc, kv_bf.partition_broadcast(128), [], [b_bf])
    k.tt("dve", lbcol[:, 0:8], lbcol[:, 8:16], lbcol[:, 0:8], ALU.subtract, [b_lb], [b_lb])
    k.act(lbcol[:, 0:8], lbcol[:, 0:8], AF.Exp, [b_lb], [b_lb])
    k.ts("dve", lbcol[:, 0:8], lbcol[:, 0:8], 1.0, None, ALU.add, None, [b_lb], [b_lb])
    k.recip(lbcol[:, 0:8], lbcol[:, 0:8], [b_lb], [b_lb])
    k.ts("dve", omlb, lbcol[:, 0:8], -1.0, 1.0, ALU.mult, ALU.add, [b_lb], [b_omlb])
    k.ts("dve", nomlb, lbcol[:, 0:8], 1.0, -1.0, ALU.mult, ALU.add, [b_lb], [b_nomlb])
    k.act(cact, ccol, AF.Silu, [b_cc], [b_ca])

    xr = Ring([T([D], F32) for _ in range(3)])

    def sumsq(xt, bxt, ssdst, bss, t):
        k.act(junk, xt, AF.Square, [bxt], [b_junk, bss], accum=ssdst[:, t:t + 1])

    for t in range(NT):
        xt, bxt = xr.next()
        k.dma("sp", xt, x_in[t * 128:(t + 1) * 128, :], [], [bxt])
        sumsq(xt, bxt, ss_a, b_ssa, t)

    wst = Ring([T([3072], F32) for _ in range(3)])
    brow, b_brow = T([3072], F32)
    mrow, b_mrow = T([3072], F32)
    prow = [BV(1 + i, [512], F32) for i in range(6)]
    modflat = modsD.rearrange("(o v) n -> o (v n)", o=1)
    for (Wd_, bd_, row0, width) in ((ada_w[0], ada_b[0:1, :], 0, 6144), (ada_w[1], ada_b[1:2, :], 6, 6144), (kv_ada_w, kv_ada_b, 12, 2048)):
        for c0 in range(0, width, 3072):
            wd = min(3072, width - c0)
            nn = wd // 512
            for kc in range(KC):
                s_, bs_ = wst.next()
                k.dma("sp", s_[:, 0:wd], Wd_[kc * 128:(kc + 1) * 128, c0:c0 + wd], [], [bs_])
                for n in range(nn):
                    k.mm(prow[n][0][0:1, :], cact[:, kc:kc + 1], s_[:, n * 512:(n + 1) * 512], kc == 0, kc == KC - 1, [b_ca, bs_], [prow[n][1]])
            k.dma("sp", brow[0:1, 0:wd], bd_[:, c0:c0 + wd], [], [b_brow])
            for n in range(nn):
                k.tt("dve", mrow[0:1, n * 512:(n + 1) * 512], prow[n][0][0:1, :], brow[0:1, n * 512:(n + 1) * 512], ALU.add, [prow[n][1], b_brow], [b_mrow])
            k.dma("sp", modflat[:, row0 * D + c0: row0 * D + c0 + wd], mrow[0:1, 0:wd], [b_mrow], [bmods])
    load_cols(modcol, b_mod, modsD.rearrange("v (c p) -> (v c) p", p=128), 112, stg, b_stg, pstg, b_pstg, rd=[bmods])
    k.ts("dve", gcol[:, 2:3], gcol[:, 2:3], 128.0 ** -0.5, None, ALU.mult, None, [b_gc], [b_gc])
    for v in (1, 4, 7, 10, 13):
        k.ts("dve", modcol[:, v * 8:(v + 1) * 8], modcol[:, v * 8:(v + 1) * 8], 1.0, None, ALU.add, None, [b_mod], [b_mod])

    def calc_rstd(ss, bss):
        k.act(rstd_all, ss, AF.Ln, [bss], [b_rs], scale=1.0 / D, bias=EPS)
        k.act(rstd_all, rstd_all, AF.Exp, [b_rs], [b_rs], scale=-0.5)

    def g_bcast(dst, bdst, v):
        k.dma("sp", dst, modsD[v:v + 1, :].partition_broadcast(128), [bmods], [bdst])

    def load_w(dst, bdst, src, nk, c0=0, c1=None):
        for kc in range(nk):
            if c1 is None:
                k.dma("pool", dst[:, kc, :], src[kc * 128:(kc + 1) * 128, :], [], [bdst])
            else:
                k.dma("pool", dst[:, kc, 0:c1 - c0], src[kc * 128:(kc + 1) * 128, c0:c1], [], [bdst])

    def make_hT(xt, bxt, t, hn, bhn, tp, btp, variants, act_heavy=False):
        if act_heavy:
            k.act(hn, xt, AF.Identity, [bxt, b_rs], [bhn], scale=rstd_all[:, t:t + 1])
        else:
            k.ts("dve", hn, xt, rstd_all[:, t:t + 1], None, ALU.mult, None, [bxt, b_rs], [bhn])
        for kc in range(KC):
            k.tr(tp[:, kc, :], hn[:, kc * 128:(kc + 1) * 128], ident_bf, [bhn, b_idb], [btp])
        for (vs, vh, dst, bdst) in variants:
            for kc in range(KC):
                sc = modcol[:, vs * 8 + kc: vs * 8 + kc + 1]
                sh = modcol[:, vh * 8 + kc: vh * 8 + kc + 1]
                if kc % 2 == 0 or (act_heavy and kc % 4 != 3):
                    k.act(dst[:, kc, :], tp[:, kc, :], AF.Identity, [btp, b_mod], [bdst], scale=sc, bias=sh)
                else:
                    k.ts("dve", dst[:, kc, :], tp[:, kc, :], sc, sh, ALU.mult, ALU.add, [btp, b_mod], [bdst])

    def residual_out(ypair, xres, bxres, gbc, b_gbc, tmp, btmp, xnew, bxnew, dst_d, bdst_d, t, ss_next, b_ssn):
        for half in range(2):
            yp, byp = ypair[half]
            k.tt("dve", tmp[:, half * 512:(half + 1) * 512], yp, gbc[:, half * 512:(half + 1) * 512], ALU.mult, [byp, b_gbc], [btmp])
        k.tt("pool", xnew, tmp, xres, ALU.add, [btmp, bxres], [bxnew])
        k.dma("pool", dst_d[t * 128:(t + 1) * 128, :], xnew, [bxnew], [bdst_d])
        sumsq(xnew, bxnew, ss_next, b_ssn, t)

    def phase_hgrn(x_src, bsrc, x_dst, bdst, ss_in, b_ssin, ss_out, b_ssout):
        phase_reset()
        calc_rstd(ss_in, b_ssin)
        w_in_sb, b_win = T([KC, 4096], BF16)
        w_out_sb, b_wout = T([KC, D], BF16)
        load_w(w_in_sb, b_win, a_w_in, KC)
        load_w(w_out_sb, b_wout, a_w_out, KC)
        g1bc, b_g1 = T([D], F32)
        g_bcast(g1bc, b_g1, 2)
        xr = Ring([T([D], F32) for _ in range(3)])
        hnr = Ring([T([D], BF16) for _ in range(2)])
        hTr = Ring([T([KC, TB], BF16) for _ in range(2)])
        tmpA = Ring([T([TB], F32) for _ in range(2)])
        tmpB = Ring([T([TB], F32) for _ in range(2)])
        tmpC = Ring([T([TB], F32) for _ in range(2)])
        kstr = Ring([T([TB], BF16) for _ in range(8)])
        E_all = A.alloc([8, TB], F32)
        kR_all = A.alloc([8, TB], BF16)
        qE_all = A.alloc([8, TB], BF16)
        gs_all = A.alloc([8, TB], F32)
        ks_all = A.alloc([TPB, 8, 128], BF16)
        v_all = A.alloc([TPB, D], BF16)
        Elast = A.alloc([8, TB // 64], F32)
        st32 = A.alloc([8, 128], F32)
        stb = A.alloc([8, 128], BF16)
        bE, bkR, bqE, bgs, bks, bEl, bst32, bstb = ([Buf() for _ in range(8)] for _ in range(8))
        bv_ = [Buf() for _ in range(TPB)]
        atr = Ring([T([4, 128], BF16) for _ in range(4)])
        oTr = Ring([T([D], F32) for _ in range(2)])
        sq, bsq = T([D], BF16)
        lnr, blnr = T([D], F32)
        on, bon = T([D], BF16)
        tmp, btmp = T([D], F32)
        xnew, bxnew = T([D], F32)
        tp, btp = BV(0, [KC, 128], BF16)
        pa = Ring([BV(1, [TB], F32), BV(2, [TB], F32)])
        pbs = [BV(3, [512], F32), BV(4, [512], F32)]
        pbig = ps_t[:, 3 * 2048:5 * 2048].bitcast(F32)
        pb = Ring(pbs)
        scg = [BV(5, [4, 128], F32), BV(1, [4, 128], F32)]
        pog = [BV(6, [4, 128], F32), BV(2, [4, 128], F32)]
        dsg = [BV(7, [4, 128], F32), BV(3, [4, 128], F32)]
        tpks = [BV(0, [TPB, 128], BF16), BV(5, [TPB, 128], BF16)]
        k.memset("pool", st32, 0.0, [], bst32)
        k.memset("pool", stb, 0.0, [], bstb)
        NCH = TB // 64
        def do_hT_h(b):
            hT, bhT = hTr.next()
            for ti in range(TPB):
                t = b * TPB + ti
                xt, bxt = xr.next()
                hn, bhn = hnr.next()
                k.dma("sp", xt, x_src[t * 128:(t + 1) * 128, :], [bsrc], [bxt])
                make_hT(xt, bxt, t, hn, bhn, tp, btp, [(1, 0, hT[:, :, ti * 128:(ti + 1) * 128], bhT)])
            return hT, bhT

        nbk_h = DBG.get("nb", NB)
        nxt_h = do_hT_h(0)
        for b in range(nbk_h):
            hT, bhT = nxt_h
            if DBG.get("stage", 9) < 1:
                continue
            ksts = []
            for h in range(DBG.get("nh", 8)):
                pf, bpf = pa.next()
                for kc in range(KC):
                    k.mm(pf, w_in_sb[:, kc, 1024 + h * 128:1024 + (h + 1) * 128], hT[:, kc, :], kc == 0, kc == KC - 1, [b_win, bhT], [bpf])
                ta, bta = tmpA.next()
                tb_, btb = tmpB.next()
                tc, btc = tmpC.next()
                k.act(ta, pf, AF.Exp, [bpf], [bta])
                if DBG.get("fsub", 9) < 1:
                    continue
                k.act(ta, ta, AF.Ln, [bta], [bta], bias=1.0)
                k.act(ta, ta, AF.Exp, [bta], [bta], scale=-1.0)
                k.act(tb_, ta, AF.Ln, [bta, b_nomlb], [btb], scale=nomlb[:, h:h + 1], bias=1.0)
                if DBG.get("fsub", 9) < 2:
                    continue
                k.scan(tc, scanmsk, tb_, [b_sm, btb], [btc])
                k.act(E_all[:, h, :], tc, AF.Exp, [btc], [bE[h]])
                k.act(tb_, tc, AF.Exp, [btc], [btb], scale=-1.0)
                k.stt("dve", kR_all[:, h, :], ta, omlb[:, h:h + 1], tb_, ALU.mult, ALU.mult, [bta, btb, b_omlb], [bkR[h]])
                if DBG.get("fsub", 9) < 3:
                    continue
                kst, bkst = kstr.next()
                Ev = E_all[:, h, :].rearrange("p (c t) -> p c t", t=64)
                k.tt("pool", kst.rearrange("p (c t) -> p c t", t=64), kR_all[:, h, :].rearrange("p (c t) -> p c t", t=64),
                     Ev[:, :, 63:64].to_broadcast([128, NCH, 64]), ALU.mult, [bkR[h], bE[h]], [bkst])
                k.copy("pool", Elast[:, h, :], Ev[:, :, 63], [bE[h]], [bEl[h]])
                ksts.append((kst, bkst))
            if DBG.get("stage", 9) < 2:
                continue
            for ti in range(TPB):
                for half in range(2):
                    pv_, bpv = pb.next()
                    for kc in range(KC):
                        k.mm(pv_, hT[:, kc, ti * 128:(ti + 1) * 128], w_in_sb[:, kc, 2048 + half * 512:2048 + (half + 1) * 512], kc == 0, kc == KC - 1, [bhT, b_win], [bpv])
                    k.copy("dve", v_all[:, ti, half * 512:(half + 1) * 512], pv_, [bpv], [bv_[ti]])
            for h in range(8):
                kst, bkst = ksts[h]
                tpk_, btpk_ = tpks[h % 2]
                for ti in range(TPB):
                    k.tr(tpk_[:, ti, :], kst[:, ti * 128:(ti + 1) * 128], ident_bf, [bkst, b_idb], [btpk_])
                for ti in range(TPB):
                    k.copy("dve", ks_all[:, ti, h, :], tpk_[:, ti, :], [btpk_], [bks[h]])
            for h in range(8):
                pq, bpq = pa.next()
                for kc in range(KC):
                    k.mm(pq, w_in_sb[:, kc, h * 128:(h + 1) * 128], hT[:, kc, :], kc == 0, kc == KC - 1, [b_win, bhT], [bpq])
                if DBG.get("ssub", 9) < 0:
                    continue
                ta, bta = tmpA.next()
                k.act(ta, pq, DBG.get("qf", AF.Silu), [bpq], [bta])
                if DBG.get("ssub", 9) < 1:
                    continue
                k.tt("pool", qE_all[:, h, :], ta, E_all[:, h, :], ALU.mult, [bta, bE[h]], [bqE[h]])
                if DBG.get("ssub", 9) < 2:
                    continue
                pg, bpg = pa.next()
                for kc in range(KC):
                    k.mm(pg, w_in_sb[:, kc, 3072 + h * 128:3072 + (h + 1) * 128], hT[:, kc, :], kc == 0, kc == KC - 1, [b_win, bhT], [bpg])
                k.act(gs_all[:, h, :], pg, AF.Silu, [bpg], [bgs[h]])
            if b + 1 < nbk_h:
                nxt_h = do_hT_h(b + 1)
            for ti in range(TPB):
                t = b * TPB + ti
                cs = slice(ti * 128, (ti + 1) * 128)
                oT, boT = oTr.next()
                oT3 = oT.rearrange("p (h t) -> p h t", t=128)
                atgs = [atr.next() for _ in range(2)]
                for g in range(2):
                    scb, bscb = scg[g]
                    for i_, h in enumerate(range(g * 4, g * 4 + 4)):
                        k.mm(scb[:, i_, :], kR_all[:, h, cs], qE_all[:, h, cs], True, True, [bkR[h], bqE[h]], [bscb])
                for g in range(2):
                    scb, bscb = scg[g]
                    atg, batg = atgs[g]
                    for i_ in range(4):
                        k.tt("dve", atg[:, i_, :], scb[:, i_, :], mask2_f, ALU.mult, [bscb, b_m2], [batg])
                for c in range(2):
                    cc = slice(c * 64, (c + 1) * 64)
                    pr = slice(c * 64, (c + 1) * 64)
                    ch = ti * 2 + c
                    for g in range(2):
                        atg, batg = atgs[g]
                        pob, bpob = pog[g]
                        dsb, bdsb = dsg[g]
                        for i_, h in enumerate(range(g * 4, g * 4 + 4)):
                            k.mm(pob[:, i_, cc], v_all[:, ti, h * 128:(h + 1) * 128], atg[:, i_, cc], True, False, [bv_[ti], batg], [bpob])
                            k.mm(pob[:, i_, cc], stb[:, h, :], qE_all[:, h, ti * 128 + c * 64: ti * 128 + (c + 1) * 64], False, True, [bstb[h], bqE[h]], [bpob])
                            k.mm(dsb[:, i_, :], ks_all[pr, ti, h, :], v_all[pr, ti, h * 128:(h + 1) * 128], True, True, [bks[h], bv_[ti]], [bdsb])
                    for g in range(2):
                        dsb, bdsb = dsg[g]
                        for i_, h in enumerate(range(g * 4, g * 4 + 4)):
                            k.stt("dve", st32[:, h, :], st32[:, h, :], Elast[:, h, ch:ch + 1], dsb[:, i_, :], ALU.mult, ALU.add, [bst32[h], bEl[h], bdsb], [bst32[h]])
                            k.copy("pool", stb[:, h, :], st32[:, h, :], [bst32[h]], [bstb[h]])
                for g in range(2):
                    pob, bpob = pog[g]
                    for i_, h in enumerate(range(g * 4, g * 4 + 4)):
                        k.copy("dve", oT3[:, h, :], pob[:, i_, :], [bpob], [boT])
                if DBG.get("stage", 9) < 5:
                    continue
                k.act(sq, oT, AF.Square, [boT], [bsq])
                for half in range(2):
                    k.mm(pbs[half][0], ones_bf, sq[:, half * 512:(half + 1) * 512], True, True, [b_1b, bsq], [pbs[half][1]])
                k.act(lnr, pbig, AF.Ln, [pbs[0][1], pbs[1][1]], [blnr], scale=1.0 / 128, bias=EPS)
                k.act(lnr, lnr, AF.Exp, [blnr], [blnr], scale=-0.5)
                k.stt("dve", lnr, oT, gcol[:, 0:1], lnr, ALU.mult, ALU.mult, [boT, b_gc, blnr], [blnr])
                k.tt("pool", on.rearrange("p (h t) -> p h t", t=128), lnr.rearrange("p (h t) -> p h t", t=128), gs_all[:, :, cs], ALU.mult, [blnr] + bgs, [bon])
                on3 = on.rearrange("p (h t) -> p h t", t=128)
                xres, bxres = xr.next()
                k.dma("sp", xres, x_src[t * 128:(t + 1) * 128, :], [bsrc], [bxres])
                ypair = []
                for half in range(2):
                    yp, byp = pbs[half]
                    for h in range(8):
                        k.mm(yp, on3[:, h, :], w_out_sb[:, h, half * 512:(half + 1) * 512], h == 0, h == 7, [bon, b_wout], [byp])
                    ypair.append((yp, byp))
                residual_out(ypair, xres, bxres, g1bc, b_g1, tmp, btmp, xnew, bxnew, x_dst, bdst, t, ss_out, b_ssout)

    def phase_ffn(l, x_src, bsrc, x_dst, bdst, ss_in, b_ssin, ss_out, b_ssout, vsc, vsh, vg):
        phase_reset()
        calc_rstd(ss_in, b_ssin)
        wup, b_wup = T([KC, 2 * DFF], BF16)
        wdn, b_wdn = T([NJ, D], BF16)
        load_w(wup, b_wup, w_up[l], KC)
        load_w(wdn, b_wdn, w_down[l], NJ)
        gbc, b_gbc = T([D], F32)
        g_bcast(gbc, b_gbc, vg)
        xr = Ring([T([D], F32) for _ in range(2)])
        hnr = Ring([T([D], BF16) for _ in range(2)])
        hTr = Ring([T([KC, TB + 2], BF16) for _ in range(2)])
        t0r = Ring([T([TB], F32) for _ in range(4)])
        sgr = Ring([T([TB], F32) for _ in range(2)])
        actr = Ring([T([NJ, TB], BF16) for _ in range(2)])
        tmp, btmp = T([D], F32)
        xnew, bxnew = T([D], F32)
        tp, btp = BV(0, [KC, 128], BF16)
        pur = Ring([BV(1 + i, [TB + 2], F32) for i in range(4)])
        pbs = [BV(5, [512], F32), BV(6, [512], F32)]
        def do_hT(b, prev):
            hT, bhT = hTr.next()
            if prev is None:
                k.memset("pool", hT[:, :, 0:2], 0.0, [], [bhT])
            else:
                k.copy("pool", hT[:, :, 0:2], prev[0][:, :, TB:TB + 2], [prev[1]], [bhT])
            for ti in range(TPB):
                t = b * TPB + ti
                xt, bxt = xr.next()
                hn, bhn = hnr.next()
                k.dma("sp", xt, x_src[t * 128:(t + 1) * 128, :], [bsrc], [bxt])
                make_hT(xt, bxt, t, hn, bhn, tp, btp, [(vsc, vsh, hT[:, :, 2 + ti * 128:2 + (ti + 1) * 128], bhT)], act_heavy=True)
            return hT, bhT

        def do_up(hT, bhT):
            aT, baT = actr.next()
            for j in range(NJ):
                res = []
                for which, cidx in ((0, j), (1, NJ + j)):
                    pu, bpu = pur.next()
                    for kc in range(KC):
                        k.mm(pu, wup[:, kc, cidx * 128:(cidx + 1) * 128], hT[:, kc, :], kc == 0, kc == KC - 1, [b_wup, bhT], [bpu])
                    t0, bt0 = t0r.next()
                    k.ts("dve", t0, pu[:, 2:2 + TB], convc[:, l, 88 + cidx:89 + cidx], convc[:, l, 132 + cidx:133 + cidx], ALU.mult, ALU.add, [bpu, b_cv], [bt0])
                    k.stt("dve", t0, pu[:, 1:1 + TB], convc[:, l, 44 + cidx:45 + cidx], t0, ALU.mult, ALU.add, [bpu, b_cv, bt0], [bt0])
                    k.stt("dve", t0, pu[:, 0:TB], convc[:, l, cidx:cidx + 1], t0, ALU.mult, ALU.add, [bpu, b_cv, bt0], [bt0])
                    res.append((t0, bt0))
                sg, bsg = sgr.next()
                k.act(sg, res[0][0], AF.Silu, [res[0][1]], [bsg])
                k.tt("pool", aT[:, j, :], sg, res[1][0], ALU.mult, [bsg, res[1][1]], [baT])
            return aT, baT

        def do_down(b, aT, baT):
            for ti in range(TPB):
                t = b * TPB + ti
                xres, bxres = xr.next()
                k.dma("sp", xres, x_src[t * 128:(t + 1) * 128, :], [bsrc], [bxres])
                ypair = []
                for half in range(2):
                    yp, byp = pbs[half]
                    for j in range(NJ):
                        k.mm(yp, aT[:, j, ti * 128:(ti + 1) * 128], wdn[:, j, half * 512:(half + 1) * 512], j == 0, j == NJ - 1, [baT, b_wdn], [byp])
                    ypair.append((yp, byp))
                residual_out(ypair, xres, bxres, gbc, b_gbc, tmp, btmp, xnew, bxnew, x_dst, bdst, t, ss_out, b_ssout)

        nbk = DBG.get('fnb', NB)
        hts = do_hT(0, None)
        ats = do_up(*hts)
        for b in range(nbk):
            if b + 1 < nbk:
                hts_n = do_hT(b + 1, hts)
                ats_n = do_up(*hts_n)
            do_down(b, *ats)
            if b + 1 < nbk:
                hts, ats = hts_n, ats_n

    def phase_fox_a(x_src, bsrc, ss_in, b_ssin):
        phase_reset()
        calc_rstd(ss_in, b_ssin)
        kvw, b_kvw = T([KC, 2056], BF16)
        wq, b_wq = T([KC, 2048], BF16)
        load_w(kvw, b_kvw, kv_w, KC)
        load_w(wq, b_wq, b_w_q, KC)
        xr = Ring([T([D], F32) for _ in range(3)])
        hnr = Ring([T([D], BF16) for _ in range(2)])
        hkvr = Ring([T([KC, TB], BF16) for _ in range(2)])
        h1r = Ring([T([KC, TB], BF16) for _ in range(2)])
        kTbr = Ring([T([8, TB], BF16) for _ in range(2)])
        qTbr = Ring([T([8, TB], BF16) for _ in range(2)])
        sqr = Ring([T([TB], BF16) for _ in range(2)])
        lnrr = Ring([T([TB], F32) for _ in range(2)])
        vbr = Ring([T([D], BF16) for _ in range(2)])
        gtr = Ring([T([D], F32) for _ in range(2)])
        ger = Ring([T([512], F32) for _ in range(2)])
        lfr = Ring([T([8], F32) for _ in range(2)])
        run, brun = T([8], F32)
        tp, btp = BV(0, [KC, 128], BF16)
        pa = Ring([BV(1, [TB], F32), BV(2, [TB], F32)])
        pssr = Ring([BV(3, [TB], F32), BV(4, [TB], F32)])
        pb = Ring([BV(5, [512], F32), BV(6, [512], F32)])
        pf8, bpf8 = BV(7, [8], F32, 0)
        pcum, bpcum = BV(7, [8], F32, 512)
        pcf, bpcf = BV(7, [8], F32, 1024)
        k.memset("pool", run, 0.0, [], [brun])
        def do_hT_a(b):
            hkv, bhkv = hkvr.next()
            h1, bh1 = h1r.next()
            for ti in range(TPB):
                t = b * TPB + ti
                xt, bxt = xr.next()
                hn, bhn = hnr.next()
                k.dma("sp", xt, x_src[t * 128:(t + 1) * 128, :], [bsrc], [bxt])
                tsl = slice(ti * 128, (ti + 1) * 128)
                make_hT(xt, bxt, t, hn, bhn, tp, btp, [(13, 12, hkv[:, :, tsl], bhkv), (7, 6, h1[:, :, tsl], bh1)])
            return hkv, bhkv, h1, bh1

        nxt_a = do_hT_a(0)
        for b in range(NB):
            hkv, bhkv, h1, bh1 = nxt_a
            kTb, bkTb = kTbr.next()
            qTb, bqTb = qTbr.next()
            jobs = [(kvw, b_kvw, hkv, bhkv, 1, kTb, bkTb, h) for h in range(8)] + [(wq, b_wq, h1, bh1, 2, qTb, bqTb, h) for h in range(8)]

            def proj(job):
                W, bW, hs, bhs, gi, dstb, bdstb, h = job
                pk, bpk = pa.next()
                for kc in range(KC):
                    k.mm(pk, W[:, kc, h * 128:(h + 1) * 128], hs[:, kc, :], kc == 0, kc == KC - 1, [bW, bhs], [bpk])
                return pk, bpk

            cur = proj(jobs[0])
            for n, job in enumerate(jobs):
                nxt = proj(jobs[n + 1]) if n + 1 < len(jobs) else None
                W, bW, hs, bhs, gi, dstb, bdstb, h = job
                pk, bpk = cur
                sq, bsq = sqr.next()
                k.act(sq, pk, AF.Square, [bpk], [bsq])
                pss, bpss = pssr.next()
                k.mm(pss, ones_bf, sq, True, True, [b_1b, bsq], [bpss])
                lnr, blnr = lnrr.next()
                k.act(lnr, pss, AF.Ln, [bpss], [blnr], scale=1.0 / 128, bias=EPS)
                k.act(lnr, lnr, AF.Exp, [blnr], [blnr], scale=-0.5)
                k.stt("dve", dstb[:, h, :], pk, gcol[:, gi:gi + 1], lnr, ALU.mult, ALU.mult, [bpk, b_gc, blnr], [bdstb])
                cur = nxt
            k.dma("pool", kT_d[b], kTb.rearrange("p h t -> p (h t)"), [bkTb], [bkT])
            k.dma("pool", qT_d[b], qTb.rearrange("p h t -> p (h t)"), [bqTb], [bqT])
            if b + 1 < NB:
                nxt_a = do_hT_a(b + 1)
            for ti in range(TPB):
                t = b * TPB + ti
                tsl = slice(ti * 128, (ti + 1) * 128)
                vb, bvb = vbr.next()
                for half in range(2):
                    pv_, bpv = pb.next()
                    for kc in range(KC):
                        k.mm(pv_, hkv[:, kc, tsl], kvw[:, kc, 1024 + half * 512:1024 + (half + 1) * 512], kc == 0, kc == KC - 1, [bhkv, b_kvw], [bpv])
                    k.copy("dve", vb[:, half * 512:(half + 1) * 512], pv_, [bpv], [bvb])
                k.dma("pool", v_d[t * 128:(t + 1) * 128, :], vb, [bvb], [bv])
                gt, bgt = gtr.next()
                for half in range(2):
                    pg, bpg = pb.next()
                    for kc in range(KC):
                        k.mm(pg, h1[:, kc, tsl], wq[:, kc, 1024 + half * 512:1024 + (half + 1) * 512], kc == 0, kc == KC - 1, [bh1, b_wq], [bpg])
                    ge, bge = ger.next()
                    k.act(ge, pg, AF.Exp, [bpg], [bge], scale=-1.0)
                    k.act(ge, ge, AF.Ln, [bge], [bge], bias=1.0)
                    k.act(gt[:, half * 512:(half + 1) * 512], ge, AF.Exp, [bge], [bgt], scale=-1.0)
                k.dma("pool", gate_d[t * 128:(t + 1) * 128, :], gt, [bgt], [bgate])
                for kc in range(KC):
                    k.mm(pf8, hkv[:, kc, tsl], kvw[:, kc, 2048:2056], kc == 0, kc == KC - 1, [bhkv, b_kvw], [bpf8])
                lf, blf = lfr.next()
                k.tt("dve", lf, pf8, bfbc, ALU.add, [bpf8, b_bf], [blf])
                k.act(lf, lf, AF.Exp, [blf], [blf], scale=-1.0)
                k.act(lf, lf, AF.Ln, [blf], [blf], bias=1.0)
                k.mm(pcum, tri_f, lf, True, False, [b_tri, blf], [bpcum])
                k.mm(pcum, ones_f, run, False, True, [b_1f, brun], [bpcum])
                k.copy("dve", ncum_all[:, t, :], pcum, [bpcum], [b_nc])
                k.tt("dve", run, run, lf, ALU.add, [brun, blf], [brun])
                k.mm(pcf, e0_f, ncum_all[:, t, :], True, True, [b_e0, b_nc], [bpcf])
                k.copy("dve", cfirst_all[:, t, :], pcf, [bpcf], [b_cf])

    def phase_fox_bc(x_src, bsrc, x_dst, bdst, ss_out, b_ssout):
        phase_reset()
        wo, b_wo = T([KC, D], BF16)
        load_w(wo, b_wo, b_w_out, KC)
        gbc, b_gbc = T([D], F32)
        g_bcast(gbc, b_gbc, 8)
        o_all = A.alloc([NT, D], BF16)
        bo = [Buf() for _ in range(NT)]
        kThr = Ring([T([S], BF16) for _ in range(2)])
        qThr = Ring([T([S], BF16) for _ in range(2)])
        vhr = Ring([T([NT, 132], BF16) for _ in range(2)])
        ghr = Ring([T([NT, 128], F32) for _ in range(2)])
        bir = Ring([T([NT, NT], F32) for _ in range(2)])
        ptr_ = Ring([T([128], BF16) for _ in range(4)])
        recr = Ring([T([1], F32) for _ in range(2)])
        xr = Ring([T([D], F32) for _ in range(2)])
        oTr = Ring([T([KC, 128], BF16) for _ in range(2)])
        tmp, btmp = T([D], F32)
        xnew, bxnew = T([D], F32)
        sr = Ring([BV(i, [128], F32) for i in range(3)])
        por = Ring([BV(3, [132], F32), BV(4, [132], F32)])
        tp, btp = BV(5, [KC, 128], BF16)
        pbs = [BV(6, [512], F32), BV(7, [512], F32)]
        for (vh, bvh) in vhr.items:
            k.memset("pool", vh[:, :, 128:129], 1.0, [], [bvh])
        vdv = v_d.rearrange("(j p) d -> p j d", p=128)
        gdv = gate_d.rearrange("(j p) d -> p j d", p=128)
        for h in range(8):
            kTh, bkTh = kThr.next()
            qTh, bqTh = qThr.next()
            vh, bvh = vhr.next()
            gh, bgh = ghr.next()
            bias, bbias = bir.next()
            k.dma("sp", kTh.rearrange("p (b t) -> p b t", t=TB), kT_d.rearrange("b d (h t) -> d b h t", h=8)[:, :, h, :], [bkT], [bkTh])
            k.dma("sp", qTh.rearrange("p (b t) -> p b t", t=TB), qT_d.rearrange("b d (h t) -> d b h t", h=8)[:, :, h, :], [bqT], [bqTh])
            k.dma("sp", vh[:, :, 0:128], vdv[:, :, h * 128:(h + 1) * 128], [bv], [bvh])
            k.dma("sp", gh, gdv[:, :, h * 128:(h + 1) * 128], [bgate], [bgh])
            for j in range(NT):
                k.ts("dve", bias[:, j, :], cfirst_all[:, :, h], -1.0, ncum_all[:, j, h:h + 1], ALU.mult, ALU.add, [b_cf, b_nc], [bbias])
            pairs = [(Q, j) for Q in range(NT) for j in range(Q + 1)]
            LA = 2
            sbuf_of = {}
            po_of = {}
            for n in range(len(pairs) + LA):
                if n < len(pairs):
                    Q, j = pairs[n]
                    s_, bs = sr.next()
                    sbuf_of[n] = (s_, bs)
                    k.mm(s_, kTh[:, j * 128:(j + 1) * 128], qTh[:, Q * 128:(Q + 1) * 128], True, True, [bkTh, bqTh], [bs])
                m = n - LA
                if m < 0:
                    continue
                Q, j = pairs[m]
                if j == 0:
                    po_of[Q] = por.next()
                po, bpo = po_of[Q]
                s_, bs = sbuf_of.pop(m)
                pt, bpt = ptr_.next()
                k.act(pt, s_, AF.Exp, [bs, bbias], [bpt], bias=bias[:, j, Q:Q + 1])
                if j == Q:
                    k.tt("pool", pt, pt, maskc_bf, ALU.mult, [bpt, b_mc], [bpt])
                k.mm(po[:, 0:129], pt, vh[:, j, 0:129], j == 0, j == Q, [bpt, bvh], [bpo])
                if j == Q:
                    rec, brec = recr.next()
                    k.recip(rec, po[:, 128:129], [bpo], [brec])
                    k.stt("dve", o_all[:, Q, h * 128:(h + 1) * 128], po[:, 0:128], rec, gh[:, Q, :], ALU.mult, ALU.mult, [bpo, brec, bgh], [bo[Q]])
        for t in range(NT):
            for kc in range(KC):
                k.tr(tp[:, kc, :], o_all[:, t, kc * 128:(kc + 1) * 128], ident_bf, [bo[t], b_idb], [btp])
            oT, boT = oTr.next()
            k.copy("dve", oT.rearrange("p a b -> p (a b)"), tp.rearrange("p a b -> p (a b)"), [btp], [boT])
            xres, bxres = xr.next()
            k.dma("sp", xres, x_src[t * 128:(t + 1) * 128, :], [bsrc], [bxres])
            ypair = []
            for half in range(2):
                yp, byp = pbs[half]
                for kc in range(KC):
                    k.mm(yp, oT[:, kc, :], wo[:, kc, half * 512:(half + 1) * 512], kc == 0, kc == KC - 1, [boT, b_wo], [byp])
                ypair.append((yp, byp))
            residual_out(ypair, xres, bxres, gbc, b_gbc, tmp, btmp, xnew, bxnew, x_dst, bdst, t, ss_out, b_ssout)

    bxin = Buf("xin")
    calc = None
    if upto >= 1:
        phase_hgrn(x_in, bxin, x1_d, bx1, ss_a, b_ssa, ss_b, b_ssb)
    if upto >= 2:
        phase_ffn(0, x1_d, bx1, x2_d, bx2, ss_b, b_ssb, ss_a, b_ssa, 4, 3, 5)
    if upto >= 3:
        phase_fox_a(x2_d, bx2, ss_a, b_ssa)
    if upto >= 4:
        phase_fox_bc(x2_d, bx2, x3_d, bx3, ss_b, b_ssb)
    if upto >= 5:
        phase_ffn(1, x3_d, bx3, out_d, bout, ss_b, b_ssb, ss_a, b_ssa, 10, 9, 11)

    k.barrier(scr)
    fin_d = nc.dram_tensor("fin_d", [128, 1], F32, kind="Internal").ap()
    k.dma("sp", fin_d, scr, [], [Buf("fin")])
    P.finalize()
    sems = {e: [st.enter_context(nc.semaphore(f"s_{e}{i}")) for i in range(P.nepoch[e])] for e in CENG}
    dsems = [st.enter_context(nc.semaphore(f"d{i}")) for i in range(P.NDMA_SEM)]
    block = st.enter_context(nc.Block())
    P.emit(block, sems, dsems)
    st.close()
    return nc


_NC_CACHE = {}


def _in_maps(inputs):
    f = lambda a: np.ascontiguousarray(np.asarray(a, dtype=np.float32))
    x = f(inputs["x"])
    c = f(inputs["c"])
    shared = dict(
        ada_w=f(inputs["ada_w"]), ada_b=f(inputs["ada_b"]),
        a_w_in=f(inputs["a_w_in"]).reshape(D, 4096),
        a_lb_logits=f(inputs["a_lb_logits"]).reshape(16, 128),
        a_norm_g=f(inputs["a_norm_g"]).reshape(1, 128),
        a_w_out=f(inputs["a_w_out"]).reshape(D, D),
        kv_ada_w=f(inputs["kv_ada_w"]), kv_ada_b=f(inputs["kv_ada_b"]).reshape(1, 2 * D),
        kv_w=f(inputs["kv_w"]), kv_b_f=f(inputs["kv_b_f"]).reshape(1, 8),
        k_norm_g=f(inputs["k_norm_g"]).reshape(1, 128),
        b_w_q=f(inputs["b_w_q"]).reshape(D, 2 * D),
        q_norm_g=f(inputs["q_norm_g"]).reshape(1, 128),
        b_w_out=f(inputs["b_w_out"]).reshape(D, D),
        ffn_w_up=f(inputs["ffn_w_up"]),
        ffn_conv_w=f(inputs["ffn_conv_w"]).reshape(2, 132, 128),
        ffn_conv_b=f(inputs["ffn_conv_b"]).reshape(2, 44, 128),
        ffn_w_down=f(inputs["ffn_w_down"]),
    )
    maps = []
    for b in range(8):
        m = dict(shared)
        m["x"] = x[b]
        m["c"] = c[b].reshape(8, 128)
        maps.append(m)
    return maps


def kernel(**inputs):
    if "nc" not in _NC_CACHE:
        _NC_CACHE["nc"] = build_nc()
    nc = _NC_CACHE["nc"]
    res = run_bass_kernel_spmd(nc, _in_maps(inputs), core_ids=list(range(8)))
    return np.stack([np.asarray(r["out"], dtype=np.float32) for r in res.results], axis=0)
```

```python
import numpy as np
import concourse.bass as bass
import concourse.mybir as mybir
from concourse.bass_utils import run_bass_kernel_spmd

F32 = mybir.dt.float32
BF16 = mybir.dt.bfloat16
U8 = mybir.dt.uint8
AF = mybir.ActivationFunctionType
ALU = mybir.AluOpType
AX = mybir.AxisListType
DSZ = {F32: 4, BF16: 2, U8: 1}

CENG = ("pe", "act", "dve", "pool")
EPOCH = 12000
STRICT = False


class Buf:
    __slots__ = ("name", "w", "r")

    def __init__(self, name=""):
        self.name = name
        self.w = None
        self.r = {}


class Op:
    __slots__ = ("id", "eng", "fn", "deps", "dma", "seq", "sig", "cnt", "need", "clock", "dsem", "dval", "inc")


class Prog:
    def __init__(self, nc):
        self.nc = nc
        self.ops = []
        self.ndma = 0
        self.dma_last = {}
        self.NDMA_SEM = 40
        self.NHW = 24
        self.nsw = 0

    def op(self, eng, fn, reads=(), writes=(), dma=False):
        o = Op()
        o.id = len(self.ops)
        o.eng = eng
        o.fn = fn
        o.dma = dma
        o.sig = False
        o.cnt = None
        deps = {}
        for b in reads:
            if b.w is not None:
                deps[b.w] = True
        for b in writes:
            if b.w is not None:
                deps.setdefault(b.w, False)
            for r in b.r.values():
                for rid in r:
                    deps.setdefault(rid, False)
        if dma:
            if eng == "pool":
                slot = self.NHW + self.nsw % (self.NDMA_SEM - self.NHW)
                self.nsw += 1
            else:
                slot = self.ndma % self.NHW
                self.ndma += 1
            prev = self.dma_last.get(slot)
            if prev is not None:
                deps.setdefault(prev.id, False)
                o.dval = prev.dval + 16
            else:
                o.dval = 16
            o.dsem = slot
            self.dma_last[slot] = o
        deps.pop(o.id, None)
        o.deps = deps
        for b in reads:
            if dma:
                b.r.setdefault("dma", []).append(o.id)
            else:
                b.r[eng] = [o.id]
        for b in writes:
            b.w = o.id
            b.r = {}
        self.ops.append(o)
        return o

    def finalize(self):
        ops = self.ops
        seqc = {e: 0 for e in CENG}
        known = {e: {c: 0 for c in CENG} for e in CENG + ("sp",)}
        kdma = {e: set() for e in CENG + ("sp",)}
        for o in ops:
            A = o.eng
            if not o.dma:
                seqc[A] += 1
                o.seq = seqc[A]
            else:
                o.seq = 0
            kn = known[A]
            need = []
            dl = sorted(o.deps.items(), key=lambda kv: -kv[0])
            for xid, raw in dl:
                X = ops[xid]
                if X.dma:
                    if xid in kdma[A]:
                        continue
                    need.append(xid)
                    kdma[A].add(xid)
                    for c in CENG:
                        if X.clock[c] > kn[c]:
                            kn[c] = X.clock[c]
                    continue
                E = X.eng
                if X.seq <= kn[E]:
                    continue
                if (not o.dma) and E == A:
                    if A == "pe" or not (raw or STRICT):
                        continue
                need.append(xid)
                X.sig = True
                kn[E] = X.seq
                for c in CENG:
                    if X.clock[c] > kn[c]:
                        kn[c] = X.clock[c]
            o.need = need
            ck = dict(kn)
            if len(kdma[A]) > 512:
                kdma[A] = set(sorted(kdma[A])[-256:])
            o.clock = ck
        cnt = {e: 0 for e in CENG}
        for o in ops:
            if (not o.dma) and o.sig:
                cnt[o.eng] += 1
                o.cnt = cnt[o.eng]
        self.nepoch = {e: cnt[e] // EPOCH + 1 for e in CENG}

    def emit(self, block, sems, dsems):
        ops = self.ops

        def semval(X):
            if X.dma:
                return dsems[X.dsem], X.dval
            k = (X.cnt - 1) // EPOCH
            return sems[X.eng][k], (X.cnt - 1) % EPOCH + 1

        def run(engname):
            def body(e):
                for o in ops:
                    if o.eng != engname:
                        continue
                    for xid in o.need:
                        s, v = semval(ops[xid])
                        e.wait_ge(s, v)
                    ins = o.fn(e)
                    if o.dma:
                        ins.then_inc(dsems[o.dsem], 16)
                    elif o.sig:
                        s, _ = semval(o)
                        ins.then_inc(s, 1)
                if engname == "sp":
                    for slot, o in self.dma_last.items():
                        e.wait_ge(dsems[slot], o.dval)
            return body

        block.tensor(run("pe"))
        block.scalar(run("act"))
        block.vector(run("dve"))
        block.gpsimd(run("pool"))
        block.sync(run("sp"))


class Arena:
    def __init__(self, t, size, part=128):
        self.t = t
        self.size = size
        self.off = 0

    def alloc(self, free_shape, dtype, align=64):
        n = int(np.prod(free_shape))
        nb = n * DSZ[dtype]
        off = (self.off + align - 1) // align * align
        assert off + nb <= self.size, f"arena overflow {off + nb} > {self.size}"
        self.off = off + nb
        ap = self.t[:, off:off + nb].bitcast(dtype)
        if len(free_shape) == 2:
            ap = ap.rearrange("p (a b) -> p a b", a=free_shape[0], b=free_shape[1])
        elif len(free_shape) == 3:
            ap = ap.rearrange("p (a b c) -> p a b c", a=free_shape[0], b=free_shape[1], c=free_shape[2])
        return ap

    def mark(self):
        return self.off

    def reset(self, m):
        self.off = m


S = 4096
D = 1024
NT = 32
KC = 8
TB = 256
NB = S // TB
TPB = TB // 128
DFF = 2816
NJ = 22
EPS = 1e-6
ARENA = 212480
DBG = {}


class Ring:
    def __init__(self, items):
        self.items = items
        self.i = 0

    def next(self):
        it = self.items[self.i % len(self.items)]
        self.i += 1
        return it


class K:
    def __init__(self, nc):
        self.nc = nc
        self.P = Prog(nc)
        self.Y = Buf("phase")

    def _op(self, eng, fn, reads, writes, dma=False):
        return self.P.op(eng, fn, reads=list(reads) + [self.Y], writes=list(writes), dma=dma)

    def barrier(self, scr):
        self.P.op("dve", lambda e: e.memset(scr, 0.0), reads=[], writes=[self.Y])

    def mm(self, out, lhsT, rhs, start, stop, reads, writes):
        return self._op("pe", lambda e: e.matmul(out, lhsT=lhsT, rhs=rhs, start=start, stop=stop), reads, writes)

    def tr(self, out, in_, ident, reads, writes):
        return self._op("pe", lambda e: e.transpose(out=out, in_=in_, identity=ident), reads, writes)

    def act(self, out, in_, func, reads, writes, scale=1.0, bias=0.0, accum=None):
        if accum is None:
            return self._op("act", lambda e: e.activation(out=out, in_=in_, func=func, bias=bias, scale=scale), reads, writes)
        return self._op("act", lambda e: e.activation(out=out, in_=in_, func=func, bias=bias, scale=scale, accum_out=accum), reads, writes)

    def tt(self, eng, out, in0, in1, op, reads, writes):
        return self._op(eng, lambda e: e.tensor_tensor(out=out, in0=in0, in1=in1, op=op), reads, writes)

    def ts(self, eng, out, in0, s1, s2, op0, op1, reads, writes):
        if s2 is None:
            return self._op(eng, lambda e: e.tensor_scalar(out=out, in0=in0, scalar1=s1, scalar2=None, op0=op0), reads, writes)
        return self._op(eng, lambda e: e.tensor_scalar(out=out, in0=in0, scalar1=s1, scalar2=s2, op0=op0, op1=op1), reads, writes)

    def stt(self, eng, out, in0, scalar, in1, op0, op1, reads, writes):
        return self._op(eng, lambda e: e.scalar_tensor_tensor(out=out, in0=in0, scalar=scalar, in1=in1, op0=op0, op1=op1), reads, writes)

    def copy(self, eng, out, in_, reads, writes):
        if eng == "act":
            return self._op("act", lambda e: e.activation(out=out, in_=in_, func=AF.Identity), reads, writes)
        return self._op(eng, lambda e: e.tensor_copy(out=out, in_=in_), reads, writes)

    def recip(self, out, in_, reads, writes):
        return self._op("dve", lambda e: e.reciprocal(out=out, in_=in_), reads, writes)

    def memset(self, eng, ap, val, reads, writes):
        return self._op(eng, lambda e: e.memset(ap, val), reads, writes)

    def scan(self, out, d0, d1, reads, writes):
        return self._op("dve", lambda e: e.tensor_tensor_scan(out=out, data0=d0, data1=d1, initial=0.0, op0=ALU.mult, op1=ALU.add), reads, writes)

    def asel(self, out, in_, pattern, cmp, fill, base, cm, reads, writes):
        return self._op("pool", lambda e: e.affine_select(out=out, in_=in_, pattern=pattern, compare_op=cmp, fill=fill, base=base, channel_multiplier=cm), reads, writes)

    def dma(self, q, out, in_, reads, writes):
        return self._op(q, lambda e: e.dma_start(out=out, in_=in_), reads, writes, dma=True)


def build_nc(upto=99, debug=False):
    nc = bass.Bass("TRN2", target_bir_lowering=False)
    k = K(nc)
    P = k.P

    def din(name, shape):
        return nc.dram_tensor(name, shape, F32, kind="ExternalInput").ap()

    x_in = din("x", [S, D])
    c_in = din("c", [8, 128])
    ada_w = din("ada_w", [2, D, 6 * D])
    ada_b = din("ada_b", [2, 6 * D])
    a_w_in = din("a_w_in", [D, 4096])
    a_lb = din("a_lb_logits", [16, 128])
    a_ng = din("a_norm_g", [1, 128])
    a_w_out = din("a_w_out", [D, D])
    kv_ada_w = din("kv_ada_w", [D, 2 * D])
    kv_ada_b = din("kv_ada_b", [1, 2 * D])
    kv_w = din("kv_w", [D, 2056])
    kv_bf = din("kv_b_f", [1, 8])
    k_ng = din("k_norm_g", [1, 128])
    b_w_q = din("b_w_q", [D, 2 * D])
    q_ng = din("q_norm_g", [1, 128])
    b_w_out = din("b_w_out", [D, D])
    w_up = din("ffn_w_up", [2, D, 2 * DFF])
    conv_w = din("ffn_conv_w", [2, 132, 128])
    conv_b = din("ffn_conv_b", [2, 44, 128])
    w_down = din("ffn_w_down", [2, DFF, D])
    out_d = nc.dram_tensor("out", [S, D], F32, kind="ExternalOutput").ap()
    skind = "ExternalOutput" if debug else "Internal"
    modsD = nc.dram_tensor("modsD", [14, D], F32, kind=skind).ap()
    x1_d = nc.dram_tensor("x1", [S, D], F32, kind=skind).ap()
    x2_d = nc.dram_tensor("x2", [S, D], F32, kind=skind).ap()
    x3_d = nc.dram_tensor("x3", [S, D], F32, kind=skind).ap()
    kT_d = nc.dram_tensor("kT_d", [NB, 128, 8 * TB], BF16, kind="Internal").ap()
    qT_d = nc.dram_tensor("qT_d", [NB, 128, 8 * TB], BF16, kind="Internal").ap()
    v_d = nc.dram_tensor("v_d", [S, D], BF16, kind="Internal").ap()
    gate_d = nc.dram_tensor("gate_d", [S, D], F32, kind="Internal").ap()
    bx1, bx2, bx3, bmods, bkT, bqT, bv, bgate, bout = (Buf(n) for n in "x1 x2 x3 mods kT qT v gate out".split())

    import contextlib
    st = contextlib.ExitStack()
    arena_t = st.enter_context(nc.sbuf_tensor("arena", [128, ARENA], U8))
    ps_t = st.enter_context(nc.psum_tensor("psum", [128, 8 * 2048], U8))
    A = Arena(arena_t, ARENA)
    PS = Arena(ps_t, 8 * 2048)

    def T(shape, dt, name=""):
        return A.alloc(shape, dt), Buf(name)

    bankbuf = [Buf(f"bank{i}") for i in range(8)]

    def BV(bank, shape, dt, boff=0):
        n = int(np.prod(shape))
        nb = n * DSZ[dt]
        assert boff + nb <= 2048
        off = bank * 2048 + boff
        ap = ps_t[:, off:off + nb].bitcast(dt)
        if len(shape) == 2:
            ap = ap.rearrange("p (a b) -> p a b", a=shape[0], b=shape[1])
        return ap, bankbuf[bank]

    ident_f, b_idf = T([128], F32)
    ident_bf, b_idb = T([128], BF16)
    ones_f, b_1f = T([128], F32)
    ones_bf, b_1b = T([128], BF16)
    tri_f, b_tri = T([128], F32)
    e0_f, b_e0 = T([128], F32)
    mask2_f, b_m2 = T([128], F32)
    maskc_bf, b_mc = T([128], BF16)
    scanmsk, b_sm = T([TB], F32)
    modcol, b_mod = T([112], F32)
    ccol, b_cc = T([8], F32)
    cact, b_ca = T([8], F32)
    lbcol, b_lb = T([16], F32)
    omlb, b_omlb = T([8], F32)
    nomlb, b_nomlb = T([8], F32)
    gcol, b_gc = T([3], F32)
    convc, b_cv = T([2, 176], F32)
    ss_a, b_ssa = T([NT], F32)
    ss_b, b_ssb = T([NT], F32)
    rstd_all, b_rs = T([NT], F32)
    ncum_all, b_nc = T([NT, 8], F32)
    cfirst_all, b_cf = T([NT, 8], F32)
    bfbc, b_bf = T([8], F32)
    scr, b_scr = T([1], F32)
    junk, b_junk = T([D], BF16)
    pmark = A.mark()

    def phase_reset():
        A.reset(pmark)
        k.barrier(scr)

    def load_cols(dst, bdst, src, n, stg, bstg, pstg, bpstg, rd=()):
        k.dma("sp", stg[0:n, :], src, list(rd), [bstg])
        k.tr(pstg[:, 0:n], stg[0:n, :], ident_f[0:n, 0:n], [bstg, b_idf], [bpstg])
        k.copy("dve", dst, pstg[:, 0:n], [bpstg], [bdst])

    k.memset("pool", ident_f, 0.0, [], [b_idf])
    k.asel(ident_f, ident_f, [[-1, 128]], ALU.not_equal, 1.0, 0, 1, [b_idf], [b_idf])
    k.copy("dve", ident_bf, ident_f, [b_idf], [b_idb])
    k.memset("pool", ones_f, 1.0, [], [b_1f])
    k.memset("pool", ones_bf, 1.0, [], [b_1b])
    k.memset("pool", tri_f, 1.0, [], [b_tri])
    k.asel(tri_f, tri_f, [[1, 128]], ALU.is_ge, 0.0, 0, -1, [b_tri], [b_tri])
    k.copy("dve", maskc_bf, tri_f, [b_tri], [b_mc])
    k.copy("dve", mask2_f, tri_f, [b_tri], [b_m2])
    k.memset("dve", mask2_f[0:64, 64:128], 0.0, [b_m2], [b_m2])
    k.memset("pool", e0_f, 0.0, [], [b_e0])
    k.asel(e0_f, e0_f, [[0, 128]], ALU.not_equal, 1.0, 0, 1, [b_e0], [b_e0])
    k.memset("pool", scanmsk, 1.0, [], [b_sm])
    k.memset("pool", scanmsk.rearrange("p (c t) -> p c t", t=64)[:, :, 0:1], 0.0, [b_sm], [b_sm])
    k.memset("pool", ncum_all, 0.0, [], [b_nc])

    stg, b_stg = T([128], F32)
    pstg, b_pstg = BV(0, [128], F32)
    load_cols(ccol, b_cc, c_in, 8, stg, b_stg, pstg, b_pstg)
    load_cols(lbcol, b_lb, a_lb, 16, stg, b_stg, pstg, b_pstg)
    k.dma("sp", stg[0:1, :], a_ng, [], [b_stg])
    k.dma("sp", stg[1:2, :], k_ng, [], [b_stg])
    k.dma("sp", stg[2:3, :], q_ng, [], [b_stg])
    k.tr(pstg[:, 0:3], stg[0:3, :], ident_f[0:3, 0:3], [b_stg, b_idf], [b_pstg])
    k.copy("dve", gcol, pstg[:, 0:3], [b_pstg], [b_gc])
    for l in range(2):
        load_cols(convc[:, l, 0:128], b_cv, conv_w[l, 0:128, :], 128, stg, b_stg, pstg, b_pstg)
        load_cols(convc[:, l, 128:132], b_cv, conv_w[l, 128:132, :], 4, stg, b_stg, pstg, b_pstg)
        load_cols(convc[:, l, 132:176], b_cv, conv_b[l], 44, stg, b_stg, pstg, b_pstg)
    k.dma("sp", bfbc, kv_bf.partition_broadcast(128), [], [b_bf])
    k.tt("dve", lbcol[:, 0:8], lbcol[:, 8:16], lbcol[:, 0:8], ALU.subtract, [b_lb], [b_lb])
    k.act(lbcol[:, 0:8], lbcol[:, 0:8], AF.Exp, [b_lb], [b_lb])
    k.ts("dve", lbcol[:, 0:8], lbcol[:, 0:8], 1.0, None, ALU.add, None, [b_lb], [b_lb])
    k.recip(lbcol[:, 0:8], lbcol[:, 0:8], [b_lb], [b_lb])
    k.ts("dve", omlb, lbcol[:, 0:8], -1.0, 1.0, ALU.mult, ALU.add, [b_lb], [b_omlb])
    k.ts("dve", nomlb, lbcol[:, 0:8], 1.0, -1.0, ALU.mult, ALU.add, [b_lb], [b_nomlb])
    k.act(cact, ccol, AF.Silu, [b_cc], [b_ca])

    xr = Ring([T([D], F32) for _ in range(3)])

    def sumsq(xt, bxt, ssdst, bss, t):
        k.act(junk, xt, AF.Square, [bxt], [b_junk, bss], accum=ssdst[:, t:t + 1])

    for t in range(NT):
        xt, bxt = xr.next()
        k.dma("sp", xt, x_in[t * 128:(t + 1) * 128, :], [], [bxt])
        sumsq(xt, bxt, ss_a, b_ssa, t)

    wst = Ring([T([3072], F32) for _ in range(3)])
    brow, b_brow = T([3072], F32)
    mrow, b_mrow = T([3072], F32)
    prow = [BV(1 + i, [512], F32) for i in range(6)]
    modflat = modsD.rearrange("(o v) n -> o (v n)", o=1)
    for (Wd_, bd_, row0, width) in ((ada_w[0], ada_b[0:1, :], 0, 6144), (ada_w[1], ada_b[1:2, :], 6, 6144), (kv_ada_w, kv_ada_b, 12, 2048)):
        for c0 in range(0, width, 3072):
            wd = min(3072, width - c0)
            nn = wd // 512
            for kc in range(KC):
                s_, bs_ = wst.next()
                k.dma("sp", s_[:, 0:wd], Wd_[kc * 128:(kc + 1) * 128, c0:c0 + wd], [], [bs_])
                for n in range(nn):
                    k.mm(prow[n][0][0:1, :], cact[:, kc:kc + 1], s_[:, n * 512:(n + 1) * 512], kc == 0, kc == KC - 1, [b_ca, bs_], [prow[n][1]])
            k.dma("sp", brow[0:1, 0:wd], bd_[:, c0:c0 + wd], [], [b_brow])
            for n in range(nn):
                k.tt("dve", mrow[0:1, n * 512:(n + 1) * 512], prow[n][0][0:1, :], brow[0:1, n * 512:(n + 1) * 512], ALU.add, [prow[n][1], b_brow], [b_mrow])
            k.dma("sp", modflat[:, row0 * D + c0: row0 * D + c0 + wd], mrow[0:1, 0:wd], [b_mrow], [bmods])
    load_cols(modcol, b_mod, modsD.rearrange("v (c p) -> (v c) p", p=128), 112, stg, b_stg, pstg, b_pstg, rd=[bmods])
    k.ts("dve", gcol[:, 2:3], gcol[:, 2:3], 128.0 ** -0.5, None, ALU.mult, None, [b_gc], [b_gc])
    for v in (1, 4, 7, 10, 13):
        k.ts("dve", modcol[:, v * 8:(v + 1) * 8], modcol[:, v * 8:(v + 1) * 8], 1.0, None, ALU.add, None, [b_mod], [b_mod])

    def calc_rstd(ss, bss):
        k.act(rstd_all, ss, AF.Ln, [bss], [b_rs], scale=1.0 / D, bias=EPS)
        k.act(rstd_all, rstd_all, AF.Exp, [b_rs], [b_rs], scale=-0.5)

    def g_bcast(dst, bdst, v):
        k.dma("sp", dst, modsD[v:v + 1, :].partition_broadcast(128), [bmods], [bdst])

    def load_w(dst, bdst, src, nk, c0=0, c1=None):
        for kc in range(nk):
            if c1 is None:
                k.dma("pool", dst[:, kc, :], src[kc * 128:(kc + 1) * 128, :], [], [bdst])
            else:
                k.dma("pool", dst[:, kc, 0:c1 - c0], src[kc * 128:(kc + 1) * 128, c0:c1], [], [bdst])

    def make_hT(xt, bxt, t, hn, bhn, tp, btp, variants, act_heavy=False):
        if act_heavy:
            k.act(hn, xt, AF.Identity, [bxt, b_rs], [bhn], scale=rstd_all[:, t:t + 1])
        else:
            k.ts("dve", hn, xt, rstd_all[:, t:t + 1], None, ALU.mult, None, [bxt, b_rs], [bhn])
        for kc in range(KC):
            k.tr(tp[:, kc, :], hn[:, kc * 128:(kc + 1) * 128], ident_bf, [bhn, b_idb], [btp])
        for (vs, vh, dst, bdst) in variants:
            for kc in range(KC):
                sc = modcol[:, vs * 8 + kc: vs * 8 + kc + 1]
                sh = modcol[:, vh * 8 + kc: vh * 8 + kc + 1]
                if kc % 2 == 0 or (act_heavy and kc % 4 != 3):
                    k.act(dst[:, kc, :], tp[:, kc, :], AF.Identity, [btp, b_mod], [bdst], scale=sc, bias=sh)
                else:
                    k.ts("dve", dst[:, kc, :], tp[:, kc, :], sc, sh, ALU.mult, ALU.add, [btp, b_mod], [bdst])

    def residual_out(ypair, xres, bxres, gbc, b_gbc, tmp, btmp, xnew, bxnew, dst_d, bdst_d, t, ss_next, b_ssn):
        for half in range(2):
            yp, byp = ypair[half]
            k.tt("dve", tmp[:, half * 512:(half + 1) * 512], yp, gbc[:, half * 512:(half + 1) * 512], ALU.mult, [byp, b_gbc], [btmp])
        k.tt("pool", xnew, tmp, xres, ALU.add, [btmp, bxres], [bxnew])
        k.dma("pool", dst_d[t * 128:(t + 1) * 128, :], xnew, [bxnew], [bdst_d])
        sumsq(xnew, bxnew, ss_next, b_ssn, t)

    def phase_hgrn(x_src, bsrc, x_dst, bdst, ss_in, b_ssin, ss_out, b_ssout):
        phase_reset()
        calc_rstd(ss_in, b_ssin)
        w_in_sb, b_win = T([KC, 4096], BF16)
        w_out_sb, b_wout = T([KC, D], BF16)
        load_w(w_in_sb, b_win, a_w_in, KC)
        load_w(w_out_sb, b_wout, a_w_out, KC)
        g1bc, b_g1 = T([D], F32)
        g_bcast(g1bc, b_g1, 2)
        xr = Ring([T([D], F32) for _ in range(3)])
        hnr = Ring([T([D], BF16) for _ in range(2)])
        hTr = Ring([T([KC, TB], BF16) for _ in range(2)])
        tmpA = Ring([T([TB], F32) for _ in range(2)])
        tmpB = Ring([T([TB], F32) for _ in range(2)])
        tmpC = Ring([T([TB], F32) for _ in range(2)])
        kstr = Ring([T([TB], BF16) for _ in range(8)])
        E_all = A.alloc([8, TB], F32)
        kR_all = A.alloc([8, TB], BF16)
        qE_all = A.alloc([8, TB], BF16)
        gs_all = A.alloc([8, TB], F32)
        ks_all = A.alloc([TPB, 8, 128], BF16)
        v_all = A.alloc([TPB, D], BF16)
        Elast = A.alloc([8, TB // 64], F32)
        st32 = A.alloc([8, 128], F32)
        stb = A.alloc([8, 128], BF16)
        bE, bkR, bqE, bgs, bks, bEl, bst32, bstb = ([Buf() for _ in range(8)] for _ in range(8))
        bv_ = [Buf() for _ in range(TPB)]
        atr = Ring([T([4, 128], BF16) for _ in range(4)])
        oTr = Ring([T([D], F32) for _ in range(2)])
        sq, bsq = T([D], BF16)
        lnr, blnr = T([D], F32)
        on, bon = T([D], BF16)
        tmp, btmp = T([D], F32)
        xnew, bxnew = T([D], F32)
        tp, btp = BV(0, [KC, 128], BF16)
        pa = Ring([BV(1, [TB], F32), BV(2, [TB], F32)])
        pbs = [BV(3, [512], F32), BV(4, [512], F32)]
        pbig = ps_t[:, 3 * 2048:5 * 2048].bitcast(F32)
        pb = Ring(pbs)
        scg = [BV(5, [4, 128], F32), BV(1, [4, 128], F32)]
        pog = [BV(6, [4, 128], F32), BV(2, [4, 128], F32)]
        dsg = [BV(7, [4, 128], F32), BV(3, [4, 128], F32)]
        tpks = [BV(0, [TPB, 128], BF16), BV(5, [TPB, 128], BF16)]
        k.memset("pool", st32, 0.0, [], bst32)
        k.memset("pool", stb, 0.0, [], bstb)
        NCH = TB // 64
        def do_hT_h(b):
            hT, bhT = hTr.next()
            for ti in range(TPB):
                t = b * TPB + ti
                xt, bxt = xr.next()
                hn, bhn = hnr.next()
                k.dma("sp", xt, x_src[t * 128:(t + 1) * 128, :], [bsrc], [bxt])
                make_hT(xt, bxt, t, hn, bhn, tp, btp, [(1, 0, hT[:, :, ti * 128:(ti + 1) * 128], bhT)])
            return hT, bhT

        nbk_h = DBG.get("nb", NB)
        nxt_h = do_hT_h(0)
        for b in range(nbk_h):
            hT, bhT = nxt_h
            if DBG.get("stage", 9) < 1:
                continue
            ksts = []
            for h in range(DBG.get("nh", 8)):
                pf, bpf = pa.next()
                for kc in range(KC):
                    k.mm(pf, w_in_sb[:, kc, 1024 + h * 128:1024 + (h + 1) * 128], hT[:, kc, :], kc == 0, kc == KC - 1, [b_win, bhT], [bpf])
                ta, bta = tmpA.next()
                tb_, btb = tmpB.next()
                tc, btc = tmpC.next()
                k.act(ta, pf, AF.Exp, [bpf], [bta])
                if DBG.get("fsub", 9) < 1:
                    continue
                k.act(ta, ta, AF.Ln, [bta], [bta], bias=1.0)
                k.act(ta, ta, AF.Exp, [bta], [bta], scale=-1.0)
                k.act(tb_, ta, AF.Ln, [bta, b_nomlb], [btb], scale=nomlb[:, h:h + 1], bias=1.0)
                if DBG.get("fsub", 9) < 2:
                    continue
                k.scan(tc, scanmsk, tb_, [b_sm, btb], [btc])
                k.act(E_all[:, h, :], tc, AF.Exp, [btc], [bE[h]])
                k.act(tb_, tc, AF.Exp, [btc], [btb], scale=-1.0)
                k.stt("dve", kR_all[:, h, :], ta, omlb[:, h:h + 1], tb_, ALU.mult, ALU.mult, [bta, btb, b_omlb], [bkR[h]])
                if DBG.get("fsub", 9) < 3:
                    continue
                kst, bkst = kstr.next()
                Ev = E_all[:, h, :].rearrange("p (c t) -> p c t", t=64)
                k.tt("pool", kst.rearrange("p (c t) -> p c t", t=64), kR_all[:, h, :].rearrange("p (c t) -> p c t", t=64),
                     Ev[:, :, 63:64].to_broadcast([128, NCH, 64]), ALU.mult, [bkR[h], bE[h]], [bkst])
                k.copy("pool", Elast[:, h, :], Ev[:, :, 63], [bE[h]], [bEl[h]])
                ksts.append((kst, bkst))
            if DBG.get("stage", 9) < 2:
                continue
            for ti in range(TPB):
                for half in range(2):
                    pv_, bpv = pb.next()
                    for kc in range(KC):
                        k.mm(pv_, hT[:, kc, ti * 128:(ti + 1) * 128], w_in_sb[:, kc, 2048 + half * 512:2048 + (half + 1) * 512], kc == 0, kc == KC - 1, [bhT, b_win], [bpv])
                    k.copy("dve", v_all[:, ti, half * 512:(half + 1) * 512], pv_, [bpv], [bv_[ti]])
            for h in range(8):
                kst, bkst = ksts[h]
                tpk_, btpk_ = tpks[h % 2]
                for ti in range(TPB):
                    k.tr(tpk_[:, ti, :], kst[:, ti * 128:(ti + 1) * 128], ident_bf, [bkst, b_idb], [btpk_])
                for ti in range(TPB):
                    k.copy("dve", ks_all[:, ti, h, :], tpk_[:, ti, :], [btpk_], [bks[h]])
            for h in range(8):
                pq, bpq = pa.next()
                for kc in range(KC):
                    k.mm(pq, w_in_sb[:, kc, h * 128:(h + 1) * 128], hT[:, kc, :], kc == 0, kc == KC - 1, [b_win, bhT], [bpq])
                if DBG.get("ssub", 9) < 0:
                    continue
                ta, bta = tmpA.next()
                k.act(ta, pq, DBG.get("qf", AF.Silu), [bpq], [bta])
                if DBG.get("ssub", 9) < 1:
                    continue
                k.tt("pool", qE_all[:, h, :], ta, E_all[:, h, :], ALU.mult, [bta, bE[h]], [bqE[h]])
                if DBG.get("ssub", 9) < 2:
                    continue
                pg, bpg = pa.next()
                for kc in range(KC):
                    k.mm(pg, w_in_sb[:, kc, 3072 + h * 128:3072 + (h + 1) * 128], hT[:, kc, :], kc == 0, kc == KC - 1, [b_win, bhT], [bpg])
                k.act(gs_all[:, h, :], pg, AF.Silu, [bpg], [bgs[h]])
            if b + 1 < nbk_h:
                nxt_h = do_hT_h(b + 1)
            for ti in range(TPB):
                t = b * TPB + ti
                cs = slice(ti * 128, (ti + 1) * 128)
                oT, boT = oTr.next()
                oT3 = oT.rearrange("p (h t) -> p h t", t=128)
                atgs = [atr.next() for _ in range(2)]
                for g in range(2):
                    scb, bscb = scg[g]
                    for i_, h in enumerate(range(g * 4, g * 4 + 4)):
                        k.mm(scb[:, i_, :], kR_all[:, h, cs], qE_all[:, h, cs], True, True, [bkR[h], bqE[h]], [bscb])
                for g in range(2):
                    scb, bscb = scg[g]
                    atg, batg = atgs[g]
                    for i_ in range(4):
                        k.tt("dve", atg[:, i_, :], scb[:, i_, :], mask2_f, ALU.mult, [bscb, b_m2], [batg])
                for c in range(2):
                    cc = slice(c * 64, (c + 1) * 64)
                    pr = slice(c * 64, (c + 1) * 64)
                    ch = ti * 2 + c
                    for g in range(2):
                        atg, batg = atgs[g]
                        pob, bpob = pog[g]
                        dsb, bdsb = dsg[g]
                        for i_, h in enumerate(range(g * 4, g * 4 + 4)):
                            k.mm(pob[:, i_, cc], v_all[:, ti, h * 128:(h + 1) * 128], atg[:, i_, cc], True, False, [bv_[ti], batg], [bpob])
                            k.mm(pob[:, i_, cc], stb[:, h, :], qE_all[:, h, ti * 128 + c * 64: ti * 128 + (c + 1) * 64], False, True, [bstb[h], bqE[h]], [bpob])
                            k.mm(dsb[:, i_, :], ks_all[pr, ti, h, :], v_all[pr, ti, h * 128:(h + 1) * 128], True, True, [bks[h], bv_[ti]], [bdsb])
                    for g in range(2):
                        dsb, bdsb = dsg[g]
                        for i_, h in enumerate(range(g * 4, g * 4 + 4)):
                            k.stt("dve", st32[:, h, :], st32[:, h, :], Elast[:, h, ch:ch + 1], dsb[:, i_, :], ALU.mult, ALU.add, [bst32[h], bEl[h], bdsb], [bst32[h]])
                            k.copy("pool", stb[:, h, :], st32[:, h, :], [bst32[h]], [bstb[h]])
                for g in range(2):
                    pob, bpob = pog[g]
                    for i_, h in enumerate(range(g * 4, g * 4 + 4)):
                        k.copy("dve", oT3[:, h, :], pob[:, i_, :], [bpob], [boT])
                if DBG.get("stage", 9) < 5:
                    continue
                k.act(sq, oT, AF.Square, [boT], [bsq])
                for half in range(2):
                    k.mm(pbs[half][0], ones_bf, sq[:, half * 512:(half + 1) * 512], True, True, [b_1b, bsq], [pbs[half][1]])
                k.act(lnr, pbig, AF.Ln, [pbs[0][1], pbs[1][1]], [blnr], scale=1.0 / 128, bias=EPS)
                k.act(lnr, lnr, AF.Exp, [blnr], [blnr], scale=-0.5)
                k.stt("dve", lnr, oT, gcol[:, 0:1], lnr, ALU.mult, ALU.mult, [boT, b_gc, blnr], [blnr])
                k.tt("pool", on.rearrange("p (h t) -> p h t", t=128), lnr.rearrange("p (h t) -> p h t", t=128), gs_all[:, :, cs], ALU.mult, [blnr] + bgs, [bon])
                on3 = on.rearrange("p (h t) -> p h t", t=128)
                xres, bxres = xr.next()
                k.dma("sp", xres, x_src[t * 128:(t + 1) * 128, :], [bsrc], [bxres])
                ypair = []
                for half in range(2):
                    yp, byp = pbs[half]
                    for h in range(8):
                        k.mm(yp, on3[:, h, :], w_out_sb[:, h, half * 512:(half + 1) * 512], h == 0, h == 7, [bon, b_wout], [byp])
                    ypair.append((yp, byp))
                residual_out(ypair, xres, bxres, g1bc, b_g1, tmp, btmp, xnew, bxnew, x_dst, bdst, t, ss_out, b_ssout)

    def phase_ffn(l, x_src, bsrc, x_dst, bdst, ss_in, b_ssin, ss_out, b_ssout, vsc, vsh, vg):
        phase_reset()
        calc_rstd(ss_in, b_ssin)
        wup, b_wup = T([KC, 2 * DFF], BF16)
        wdn, b_wdn = T([NJ, D], BF16)
        load_w(wup, b_wup, w_up[l], KC)
        load_w(wdn, b_wdn, w_down[l], NJ)
        gbc, b_gbc = T([D], F32)
        g_bcast(gbc, b_gbc, vg)
        xr = Ring([T([D], F32) for _ in range(2)])
        hnr = Ring([T([D], BF16) for _ in range(2)])
        hTr = Ring([T([KC, TB + 2], BF16) for _ in range(2)])
        t0r = Ring([T([TB], F32) for _ in range(4)])
        sgr = Ring([T([TB], F32) for _ in range(2)])
        actr = Ring([T([NJ, TB], BF16) for _ in range(2)])
        tmp, btmp = T([D], F32)
        xnew, bxnew = T([D], F32)
        tp, btp = BV(0, [KC, 128], BF16)
        pur = Ring([BV(1 + i, [TB + 2], F32) for i in range(4)])
        pbs = [BV(5, [512], F32), BV(6, [512], F32)]
        def do_hT(b, prev):
            hT, bhT = hTr.next()
            if prev is None:
                k.memset("pool", hT[:, :, 0:2], 0.0, [], [bhT])
            else:
                k.copy("pool", hT[:, :, 0:2], prev[0][:, :, TB:TB + 2], [prev[1]], [bhT])
            for ti in range(TPB):
                t = b * TPB + ti
                xt, bxt = xr.next()
                hn, bhn = hnr.next()
                k.dma("sp", xt, x_src[t * 128:(t + 1) * 128, :], [bsrc], [bxt])
                make_hT(xt, bxt, t, hn, bhn, tp, btp, [(vsc, vsh, hT[:, :, 2 + ti * 128:2 + (ti + 1) * 128], bhT)], act_heavy=True)
            return hT, bhT

        def do_up(hT, bhT):
            aT, baT = actr.next()
            for j in range(NJ):
                res = []
                for which, cidx in ((0, j), (1, NJ + j)):
                    pu, bpu = pur.next()
                    for kc in range(KC):
                        k.mm(pu, wup[:, kc, cidx * 128:(cidx + 1) * 128], hT[:, kc, :], kc == 0, kc == KC - 1, [b_wup, bhT], [bpu])
                    t0, bt0 = t0r.next()
                    k.ts("dve", t0, pu[:, 2:2 + TB], convc[:, l, 88 + cidx:89 + cidx], convc[:, l, 132 + cidx:133 + cidx], ALU.mult, ALU.add, [bpu, b_cv], [bt0])
                    k.stt("dve", t0, pu[:, 1:1 + TB], convc[:, l, 44 + cidx:45 + cidx], t0, ALU.mult, ALU.add, [bpu, b_cv, bt0], [bt0])
                    k.stt("dve", t0, pu[:, 0:TB], convc[:, l, cidx:cidx + 1], t0, ALU.mult, ALU.add, [bpu, b_cv, bt0], [bt0])
                    res.append((t0, bt0))
                sg, bsg = sgr.next()
                k.act(sg, res[0][0], AF.Silu, [res[0][1]], [bsg])
                k.tt("pool", aT[:, j, :], sg, res[1][0], ALU.mult, [bsg, res[1][1]], [baT])
            return aT, baT

        def do_down(b, aT, baT):
            for ti in range(TPB):
                t = b * TPB + ti
                xres, bxres = xr.next()
                k.dma("sp", xres, x_src[t * 128:(t + 1) * 128, :], [bsrc], [bxres])
                ypair = []
                for half in range(2):
                    yp, byp = pbs[half]
                    for j in range(NJ):
                        k.mm(yp, aT[:, j, ti * 128:(ti + 1) * 128], wdn[:, j, half * 512:(half + 1) * 512], j == 0, j == NJ - 1, [baT, b_wdn], [byp])
                    ypair.append((yp, byp))
                residual_out(ypair, xres, bxres, gbc, b_gbc, tmp, btmp, xnew, bxnew, x_dst, bdst, t, ss_out, b_ssout)

        nbk = DBG.get('fnb', NB)
        hts = do_hT(0, None)
        ats = do_up(*hts)
        for b in range(nbk):
            if b + 1 < nbk:
                hts_n = do_hT(b + 1, hts)
                ats_n = do_up(*hts_n)
            do_down(b, *ats)
            if b + 1 < nbk:
                hts, ats = hts_n, ats_n

    def phase_fox_a(x_src, bsrc, ss_in, b_ssin):
        phase_reset()
        calc_rstd(ss_in, b_ssin)
        kvw, b_kvw = T([KC, 2056], BF16)
        wq, b_wq = T([KC, 2048], BF16)
        load_w(kvw, b_kvw, kv_w, KC)
        load_w(wq, b_wq, b_w_q, KC)
        xr = Ring([T([D], F32) for _ in range(3)])
        hnr = Ring([T([D], BF16) for _ in range(2)])
        hkvr = Ring([T([KC, TB], BF16) for _ in range(2)])
        h1r = Ring([T([KC, TB], BF16) for _ in range(2)])
        kTbr = Ring([T([8, TB], BF16) for _ in range(2)])
        qTbr = Ring([T([8, TB], BF16) for _ in range(2)])
        sqr = Ring([T([TB], BF16) for _ in range(2)])
        lnrr = Ring([T([TB], F32) for _ in range(2)])
        vbr = Ring([T([D], BF16) for _ in range(2)])
        gtr = Ring([T([D], F32) for _ in range(2)])
        ger = Ring([T([512], F32) for _ in range(2)])
        lfr = Ring([T([8], F32) for _ in range(2)])
        run, brun = T([8], F32)
        tp, btp = BV(0, [KC, 128], BF16)
        pa = Ring([BV(1, [TB], F32), BV(2, [TB], F32)])
        pssr = Ring([BV(3, [TB], F32), BV(4, [TB], F32)])
        pb = Ring([BV(5, [512], F32), BV(6, [512], F32)])
        pf8, bpf8 = BV(7, [8], F32, 0)
        pcum, bpcum = BV(7, [8], F32, 512)
        pcf, bpcf = BV(7, [8], F32, 1024)
        k.memset("pool", run, 0.0, [], [brun])
        def do_hT_a(b):
            hkv, bhkv = hkvr.next()
            h1, bh1 = h1r.next()
            for ti in range(TPB):
                t = b * TPB + ti
                xt, bxt = xr.next()
                hn, bhn = hnr.next()
                k.dma("sp", xt, x_src[t * 128:(t + 1) * 128, :], [bsrc], [bxt])
                tsl = slice(ti * 128, (ti + 1) * 128)
                make_hT(xt, bxt, t, hn, bhn, tp, btp, [(13, 12, hkv[:, :, tsl], bhkv), (7, 6, h1[:, :, tsl], bh1)])
            return hkv, bhkv, h1, bh1

        nxt_a = do_hT_a(0)
        for b in range(NB):
            hkv, bhkv, h1, bh1 = nxt_a
            kTb, bkTb = kTbr.next()
            qTb, bqTb = qTbr.next()
            jobs = [(kvw, b_kvw, hkv, bhkv, 1, kTb, bkTb, h) for h in range(8)] + [(wq, b_wq, h1, bh1, 2, qTb, bqTb, h) for h in range(8)]

            def proj(job):
                W, bW, hs, bhs, gi, dstb, bdstb, h = job
                pk, bpk = pa.next()
                for kc in range(KC):
                    k.mm(pk, W[:, kc, h * 128:(h + 1) * 128], hs[:, kc, :], kc == 0, kc == KC - 1, [bW, bhs], [bpk])
                return pk, bpk

            cur = proj(jobs[0])
            for n, job in enumerate(jobs):
                nxt = proj(jobs[n + 1]) if n + 1 < len(jobs) else None
                W, bW, hs, bhs, gi, dstb, bdstb, h = job
                pk, bpk = cur
                sq, bsq = sqr.next()
                k.act(sq, pk, AF.Square, [bpk], [bsq])
                pss, bpss = pssr.next()
                k.mm(pss, ones_bf, sq, True, True, [b_1b, bsq], [bpss])
                lnr, blnr = lnrr.next()
                k.act(lnr, pss, AF.Ln, [bpss], [blnr], scale=1.0 / 128, bias=EPS)
                k.act(lnr, lnr, AF.Exp, [blnr], [blnr], scale=-0.5)
                k.stt("dve", dstb[:, h, :], pk, gcol[:, gi:gi + 1], lnr, ALU.mult, ALU.mult, [bpk, b_gc, blnr], [bdstb])
                cur = nxt
            k.dma("pool", kT_d[b], kTb.rearrange("p h t -> p (h t)"), [bkTb], [bkT])
            k.dma("pool", qT_d[b], qTb.rearrange("p h t -> p (h t)"), [bqTb], [bqT])
            if b + 1 < NB:
                nxt_a = do_hT_a(b + 1)
            for ti in range(TPB):
                t = b * TPB + ti
                tsl = slice(ti * 128, (ti + 1) * 128)
                vb, bvb = vbr.next()
                for half in range(2):
                    pv_, bpv = pb.next()
                    for kc in range(KC):
                        k.mm(pv_, hkv[:, kc, tsl], kvw[:, kc, 1024 + half * 512:1024 + (half + 1) * 512], kc == 0, kc == KC - 1, [bhkv, b_kvw], [bpv])
                    k.copy("dve", vb[:, half * 512:(half + 1) * 512], pv_, [bpv], [bvb])
                k.dma("pool", v_d[t * 128:(t + 1) * 128, :], vb, [bvb], [bv])
                gt, bgt = gtr.next()
                for half in range(2):
                    pg, bpg = pb.next()
                    for kc in range(KC):
                        k.mm(pg, h1[:, kc, tsl], wq[:, kc, 1024 + half * 512:1024 + (half + 1) * 512], kc == 0, kc == KC - 1, [bh1, b_wq], [bpg])
                    ge, bge = ger.next()
                    k.act(ge, pg, AF.Exp, [bpg], [bge], scale=-1.0)
                    k.act(ge, ge, AF.Ln, [bge], [bge], bias=1.0)
                    k.act(gt[:, half * 512:(half + 1) * 512], ge, AF.Exp, [bge], [bgt], scale=-1.0)
                k.dma("pool", gate_d[t * 128:(t + 1) * 128, :], gt, [bgt], [bgate])
                for kc in range(KC):
                    k.mm(pf8, hkv[:, kc, tsl], kvw[:, kc, 2048:2056], kc == 0, kc == KC - 1, [bhkv, b_kvw], [bpf8])
                lf, blf = lfr.next()
                k.tt("dve", lf, pf8, bfbc, ALU.add, [bpf8, b_bf], [blf])
                k.act(lf, lf, AF.Exp, [blf], [blf], scale=-1.0)
                k.act(lf, lf, AF.Ln, [blf], [blf], bias=1.0)
                k.mm(pcum, tri_f, lf, True, False, [b_tri, blf], [bpcum])
                k.mm(pcum, ones_f, run, False, True, [b_1f, brun], [bpcum])
                k.copy("dve", ncum_all[:, t, :], pcum, [bpcum], [b_nc])
                k.tt("dve", run, run, lf, ALU.add, [brun, blf], [brun])
                k.mm(pcf, e0_f, ncum_all[:, t, :], True, True, [b_e0, b_nc], [bpcf])
                k.copy("dve", cfirst_all[:, t, :], pcf, [bpcf], [b_cf])

    def phase_fox_bc(x_src, bsrc, x_dst, bdst, ss_out, b_ssout):
        phase_reset()
        wo, b_wo = T([KC, D], BF16)
        load_w(wo, b_wo, b_w_out, KC)
        gbc, b_gbc = T([D], F32)
        g_bcast(gbc, b_gbc, 8)
        o_all = A.alloc([NT, D], BF16)
        bo = [Buf() for _ in range(NT)]
        kThr = Ring([T([S], BF16) for _ in range(2)])
        qThr = Ring([T([S], BF16) for _ in range(2)])
        vhr = Ring([T([NT, 132], BF16) for _ in range(2)])
        ghr = Ring([T([NT, 128], F32) for _ in range(2)])
        bir = Ring([T([NT, NT], F32) for _ in range(2)])
        ptr_ = Ring([T([256], BF16) for _ in range(4)])
        scr2 = Ring([T([NT // 2], F32) for _ in range(2)])
        cmbr = Ring([T([132], F32) for _ in range(2)])
        recr = Ring([T([1], F32) for _ in range(2)])
        xr = Ring([T([D], F32) for _ in range(2)])
        oTr = Ring([T([KC, 128], BF16) for _ in range(2)])
        tmp, btmp = T([D], F32)
        xnew, bxnew = T([D], F32)
        sr = Ring([BV(i, [256], F32) for i in range(3)])
        por = Ring([BV(3, [3, 132], F32), BV(4, [3, 132], F32)])
        tp, btp = BV(5, [KC, 128], BF16)
        pbs = [BV(6, [512], F32), BV(7, [512], F32)]
        for (vh, bvh) in vhr.items:
            k.memset("pool", vh[:, :, 128:129], 1.0, [], [bvh])
        vdv = v_d.rearrange("(j p) d -> p j d", p=128)
        gdv = gate_d.rearrange("(j p) d -> p j d", p=128)
        for h in range(8):
            kTh, bkTh = kThr.next()
            qTh, bqTh = qThr.next()
            vh, bvh = vhr.next()
            gh, bgh = ghr.next()
            bias, bbias = bir.next()
            k.dma("sp", kTh.rearrange("p (b t) -> p b t", t=TB), kT_d.rearrange("b d (h t) -> d b h t", h=8)[:, :, h, :], [bkT], [bkTh])
            k.dma("sp", qTh.rearrange("p (b t) -> p b t", t=TB), qT_d.rearrange("b d (h t) -> d b h t", h=8)[:, :, h, :], [bqT], [bqTh])
            k.dma("sp", vh[:, :, 0:128], vdv[:, :, h * 128:(h + 1) * 128], [bv], [bvh])
            k.dma("sp", gh, gdv[:, :, h * 128:(h + 1) * 128], [bgate], [bgh])
            for j in range(NT):
                k.ts("dve", bias[:, j, :], cfirst_all[:, :, h], -1.0, ncum_all[:, j, h:h + 1], ALU.mult, ALU.add, [b_cf, b_nc], [bbias])
            scol, bscol = scr2.next()
            cfv = cfirst_all[:, :, h].rearrange("p (m two) -> p m two", two=2)
            k.tt("dve", scol, cfv[:, :, 0], cfv[:, :, 1], ALU.subtract, [b_cf], [bscol])
            k.act(scol, scol, AF.Exp, [bscol], [bscol])
            jobs = []
            for m in range(NT // 2):
                Q0, Q1 = 2 * m, 2 * m + 1
                for jj in range(2 * m + 1):
                    jobs.append((Q0 * 128, 256, jj, Q0, jj == 2 * m, [(0, 0, jj == 0, jj == 2 * m), (1, 128, jj == 0, jj == 2 * m)], None))
                jobs.append((Q1 * 128, 128, Q1, Q1, True, [(2, 0, True, True)], m))
            LA = 2
            sbuf_of = {}
            acc_of = {}
            for n in range(len(jobs) + LA):
                if n < len(jobs):
                    q0, qw, jj, Qb, msk, tg, fin = jobs[n]
                    s_, bs = sr.next()
                    sbuf_of[n] = (s_, bs)
                    k.mm(s_[:, 0:qw], kTh[:, jj * 128:(jj + 1) * 128], qTh[:, q0:q0 + qw], True, True, [bkTh, bqTh], [bs])
                mi = n - LA
                if mi < 0:
                    continue
                q0, qw, jj, Qb, msk, tg, fin = jobs[mi]
                pm = q0 // 256
                if pm not in acc_of:
                    acc_of[pm] = por.next()
                accb, baccb = acc_of[pm]
                s_, bs = sbuf_of.pop(mi)
                pt, bpt = ptr_.next()
                k.act(pt[:, 0:qw], s_[:, 0:qw], AF.Exp, [bs, bbias], [bpt], bias=bias[:, jj, Qb:Qb + 1])
                if msk:
                    k.tt("pool", pt[:, 0:128], pt[:, 0:128], maskc_bf, ALU.mult, [bpt, b_mc], [bpt])
                for (ai, c0, st_, sp_) in tg:
                    k.mm(accb[:, ai, 0:129], pt[:, c0:c0 + 128], vh[:, jj, 0:129], st_, sp_, [bpt, bvh], [baccb])
                if fin is not None:
                    Q0, Q1 = 2 * fin, 2 * fin + 1
                    rec, brec = recr.next()
                    k.recip(rec, accb[:, 0, 128:129], [baccb], [brec])
                    k.stt("dve", o_all[:, Q0, h * 128:(h + 1) * 128], accb[:, 0, 0:128], rec, gh[:, Q0, :], ALU.mult, ALU.mult, [baccb, brec, bgh], [bo[Q0]])
                    cmb, bcmb = cmbr.next()
                    k.copy("dve", cmb[:, 0:129], accb[:, 2, 0:129], [baccb], [bcmb])
                    k.stt("dve", cmb[:, 0:129], accb[:, 1, 0:129], scol[:, fin:fin + 1], cmb[:, 0:129], ALU.mult, ALU.add, [baccb, bscol, bcmb], [bcmb])
                    rec, brec = recr.next()
                    k.recip(rec, cmb[:, 128:129], [bcmb], [brec])
                    k.stt("dve", o_all[:, Q1, h * 128:(h + 1) * 128], cmb[:, 0:128], rec, gh[:, Q1, :], ALU.mult, ALU.mult, [bcmb, brec, bgh], [bo[Q1]])
                    del acc_of[pm]
        for t in range(NT):
            for kc in range(KC):
                k.tr(tp[:, kc, :], o_all[:, t, kc * 128:(kc + 1) * 128], ident_bf, [bo[t], b_idb], [btp])
            oT, boT = oTr.next()
            k.copy("dve", oT.rearrange("p a b -> p (a b)"), tp.rearrange("p a b -> p (a b)"), [btp], [boT])
            xres, bxres = xr.next()
            k.dma("sp", xres, x_src[t * 128:(t + 1) * 128, :], [bsrc], [bxres])
            ypair = []
            for half in range(2):
                yp, byp = pbs[half]
                for kc in range(KC):
                    k.mm(yp, oT[:, kc, :], wo[:, kc, half * 512:(half + 1) * 512], kc == 0, kc == KC - 1, [boT, b_wo], [byp])
                ypair.append((yp, byp))
            residual_out(ypair, xres, bxres, gbc, b_gbc, tmp, btmp, xnew, bxnew, x_dst, bdst, t, ss_out, b_ssout)

    bxin = Buf("xin")
    calc = None
    if upto >= 1:
        phase_hgrn(x_in, bxin, x1_d, bx1, ss_a, b_ssa, ss_b, b_ssb)
    if upto >= 2:
        phase_ffn(0, x1_d, bx1, x2_d, bx2, ss_b, b_ssb, ss_a, b_ssa, 4, 3, 5)
    if upto >= 3:
        phase_fox_a(x2_d, bx2, ss_a, b_ssa)
    if upto >= 4:
        phase_fox_bc(x2_d, bx2, x3_d, bx3, ss_b, b_ssb)
    if upto >= 5:
        phase_ffn(1, x3_d, bx3, out_d, bout, ss_b, b_ssb, ss_a, b_ssa, 10, 9, 11)

    k.barrier(scr)
    fin_d = nc.dram_tensor("fin_d", [128, 1], F32, kind="Internal").ap()
    k.dma("sp", fin_d, scr, [], [Buf("fin")])
    P.finalize()
    sems = {e: [st.enter_context(nc.semaphore(f"s_{e}{i}")) for i in range(P.nepoch[e])] for e in CENG}
    dsems = [st.enter_context(nc.semaphore(f"d{i}")) for i in range(P.NDMA_SEM)]
    block = st.enter_context(nc.Block())
    P.emit(block, sems, dsems)
    st.close()
    return nc


_NC_CACHE = {}


def _in_maps(inputs):
    f = lambda a: np.ascontiguousarray(np.asarray(a, dtype=np.float32))
    x = f(inputs["x"])
    c = f(inputs["c"])
    shared = dict(
        ada_w=f(inputs["ada_w"]), ada_b=f(inputs["ada_b"]),
        a_w_in=f(inputs["a_w_in"]).reshape(D, 4096),
        a_lb_logits=f(inputs["a_lb_logits"]).reshape(16, 128),
        a_norm_g=f(inputs["a_norm_g"]).reshape(1, 128),
        a_w_out=f(inputs["a_w_out"]).reshape(D, D),
        kv_ada_w=f(inputs["kv_ada_w"]), kv_ada_b=f(inputs["kv_ada_b"]).reshape(1, 2 * D),
        kv_w=f(inputs["kv_w"]), kv_b_f=f(inputs["kv_b_f"]).reshape(1, 8),
        k_norm_g=f(inputs["k_norm_g"]).reshape(1, 128),
        b_w_q=f(inputs["b_w_q"]).reshape(D, 2 * D),
        q_norm_g=f(inputs["q_norm_g"]).reshape(1, 128),
        b_w_out=f(inputs["b_w_out"]).reshape(D, D),
        ffn_w_up=f(inputs["ffn_w_up"]),
        ffn_conv_w=f(inputs["ffn_conv_w"]).reshape(2, 132, 128),
        ffn_conv_b=f(inputs["ffn_conv_b"]).reshape(2, 44, 128),
        ffn_w_down=f(inputs["ffn_w_down"]),
    )
    maps = []
    for b in range(8):
        m = dict(shared)
        m["x"] = x[b]
        m["c"] = c[b].reshape(8, 128)
        maps.append(m)
    return maps


def kernel(**inputs):
    if "nc" not in _NC_CACHE:
        _NC_CACHE["nc"] = build_nc()
    nc = _NC_CACHE["nc"]
    res = run_bass_kernel_spmd(nc, _in_maps(inputs), core_ids=list(range(8)))
    return np.stack([np.asarray(r["out"], dtype=np.float32) for r in res.results], axis=0)
```

```python
import numpy as np
import concourse.bass as bass
import concourse.mybir as mybir
from concourse.bass_utils import run_bass_kernel_spmd

F32 = mybir.dt.float32
BF16 = mybir.dt.bfloat16
U8 = mybir.dt.uint8
AF = mybir.ActivationFunctionType
ALU = mybir.AluOpType
AX = mybir.AxisListType
DSZ = {F32: 4, BF16: 2, U8: 1}

CENG = ("pe", "act", "dve", "pool")
EPOCH = 12000
STRICT = False


class Buf:
    __slots__ = ("name", "w", "r")

    def __init__(self, name=""):
        self.name = name
        self.w = None
        self.r = {}


class Op:
    __slots__ = ("id", "eng", "fn", "deps", "dma", "seq", "sig", "cnt", "need", "clock", "dsem", "dval", "inc")


class Prog:
    def __init__(self, nc):
        self.nc = nc
        self.ops = []
        self.ndma = 0
        self.dma_last = {}
        self.NDMA_SEM = 40
        self.NHW = 24
        self.nsw = 0

    def op(self, eng, fn, reads=(), writes=(), dma=False):
        o = Op()
        o.id = len(self.ops)
        o.eng = eng
        o.fn = fn
        o.dma = dma
        o.sig = False
        o.cnt = None
        deps = {}
        for b in reads:
            if b.w is not None:
                deps[b.w] = True
        for b in writes:
            if b.w is not None:
                deps.setdefault(b.w, False)
            for r in b.r.values():
                for rid in r:
                    deps.setdefault(rid, False)
        if dma:
            if eng == "pool":
                slot = self.NHW + self.nsw % (self.NDMA_SEM - self.NHW)
                self.nsw += 1
            else:
                slot = self.ndma % self.NHW
                self.ndma += 1
            prev = self.dma_last.get(slot)
            if prev is not None:
                deps.setdefault(prev.id, False)
                o.dval = prev.dval + 16
            else:
                o.dval = 16
            o.dsem = slot
            self.dma_last[slot] = o
        deps.pop(o.id, None)
        o.deps = deps
        for b in reads:
            if dma:
                b.r.setdefault("dma", []).append(o.id)
            else:
                b.r[eng] = [o.id]
        for b in writes:
            b.w = o.id
            b.r = {}
        self.ops.append(o)
        return o

    def finalize(self):
        ops = self.ops
        seqc = {e: 0 for e in CENG}
        known = {e: {c: 0 for c in CENG} for e in CENG + ("sp",)}
        kdma = {e: set() for e in CENG + ("sp",)}
        for o in ops:
            A = o.eng
            if not o.dma:
                seqc[A] += 1
                o.seq = seqc[A]
            else:
                o.seq = 0
            kn = known[A]
            need = []
            dl = sorted(o.deps.items(), key=lambda kv: -kv[0])
            for xid, raw in dl:
                X = ops[xid]
                if X.dma:
                    if xid in kdma[A]:
                        continue
                    need.append(xid)
                    kdma[A].add(xid)
                    for c in CENG:
                        if X.clock[c] > kn[c]:
                            kn[c] = X.clock[c]
                    continue
                E = X.eng
                if X.seq <= kn[E]:
                    continue
                if (not o.dma) and E == A:
                    if A == "pe" or not (raw or STRICT):
                        continue
                need.append(xid)
                X.sig = True
                kn[E] = X.seq
                for c in CENG:
                    if X.clock[c] > kn[c]:
                        kn[c] = X.clock[c]
            o.need = need
            ck = dict(kn)
            if len(kdma[A]) > 512:
                kdma[A] = set(sorted(kdma[A])[-256:])
            o.clock = ck
        cnt = {e: 0 for e in CENG}
        for o in ops:
            if (not o.dma) and o.sig:
                cnt[o.eng] += 1
                o.cnt = cnt[o.eng]
        self.nepoch = {e: cnt[e] // EPOCH + 1 for e in CENG}

    def emit(self, block, sems, dsems):
        ops = self.ops

        def semval(X):
            if X.dma:
                return dsems[X.dsem], X.dval
            k = (X.cnt - 1) // EPOCH
            return sems[X.eng][k], (X.cnt - 1) % EPOCH + 1

        def run(engname):
            def body(e):
                for o in ops:
                    if o.eng != engname:
                        continue
                    for xid in o.need:
                        s, v = semval(ops[xid])
                        e.wait_ge(s, v)
                    ins = o.fn(e)
                    if o.dma:
                        ins.then_inc(dsems[o.dsem], 16)
                    elif o.sig:
                        s, _ = semval(o)
                        ins.then_inc(s, 1)
                if engname == "sp":
                    for slot, o in self.dma_last.items():
                        e.wait_ge(dsems[slot], o.dval)
            return body

        block.tensor(run("pe"))
        block.scalar(run("act"))
        block.vector(run("dve"))
        block.gpsimd(run("pool"))
        block.sync(run("sp"))


class Arena:
    def __init__(self, t, size, part=128):
        self.t = t
        self.size = size
        self.off = 0

    def alloc(self, free_shape, dtype, align=64):
        n = int(np.prod(free_shape))
        nb = n * DSZ[dtype]
        off = (self.off + align - 1) // align * align
        assert off + nb <= self.size, f"arena overflow {off + nb} > {self.size}"
        self.off = off + nb
        ap = self.t[:, off:off + nb].bitcast(dtype)
        if len(free_shape) == 2:
            ap = ap.rearrange("p (a b) -> p a b", a=free_shape[0], b=free_shape[1])
        elif len(free_shape) == 3:
            ap = ap.rearrange("p (a b c) -> p a b c", a=free_shape[0], b=free_shape[1], c=free_shape[2])
        return ap

    def mark(self):
        return self.off

    def reset(self, m):
        self.off = m


S = 4096
D = 1024
NT = 32
KC = 8
TB = 256
NB = S // TB
TPB = TB // 128
DFF = 2816
NJ = 22
EPS = 1e-6
ARENA = 212480
DBG = {}


class Ring:
    def __init__(self, items):
        self.items = items
        self.i = 0

    def next(self):
        it = self.items[self.i % len(self.items)]
        self.i += 1
        return it


class K:
    def __init__(self, nc):
        self.nc = nc
        self.P = Prog(nc)
        self.Y = Buf("phase")

    def _op(self, eng, fn, reads, writes, dma=False):
        return self.P.op(eng, fn, reads=list(reads) + [self.Y], writes=list(writes), dma=dma)

    def barrier(self, scr):
        self.P.op("dve", lambda e: e.memset(scr, 0.0), reads=[], writes=[self.Y])

    def mm(self, out, lhsT, rhs, start, stop, reads, writes):
        return self._op("pe", lambda e: e.matmul(out, lhsT=lhsT, rhs=rhs, start=start, stop=stop), reads, writes)

    def tr(self, out, in_, ident, reads, writes):
        return self._op("pe", lambda e: e.transpose(out=out, in_=in_, identity=ident), reads, writes)

    def act(self, out, in_, func, reads, writes, scale=1.0, bias=0.0, accum=None):
        if accum is None:
            return self._op("act", lambda e: e.activation(out=out, in_=in_, func=func, bias=bias, scale=scale), reads, writes)
        return self._op("act", lambda e: e.activation(out=out, in_=in_, func=func, bias=bias, scale=scale, accum_out=accum), reads, writes)

    def tt(self, eng, out, in0, in1, op, reads, writes):
        return self._op(eng, lambda e: e.tensor_tensor(out=out, in0=in0, in1=in1, op=op), reads, writes)

    def ts(self, eng, out, in0, s1, s2, op0, op1, reads, writes):
        if s2 is None:
            return self._op(eng, lambda e: e.tensor_scalar(out=out, in0=in0, scalar1=s1, scalar2=None, op0=op0), reads, writes)
        return self._op(eng, lambda e: e.tensor_scalar(out=out, in0=in0, scalar1=s1, scalar2=s2, op0=op0, op1=op1), reads, writes)

    def stt(self, eng, out, in0, scalar, in1, op0, op1, reads, writes):
        return self._op(eng, lambda e: e.scalar_tensor_tensor(out=out, in0=in0, scalar=scalar, in1=in1, op0=op0, op1=op1), reads, writes)

    def copy(self, eng, out, in_, reads, writes):
        if eng == "act":
            return self._op("act", lambda e: e.activation(out=out, in_=in_, func=AF.Identity), reads, writes)
        return self._op(eng, lambda e: e.tensor_copy(out=out, in_=in_), reads, writes)

    def recip(self, out, in_, reads, writes):
        return self._op("dve", lambda e: e.reciprocal(out=out, in_=in_), reads, writes)

    def memset(self, eng, ap, val, reads, writes):
        return self._op(eng, lambda e: e.memset(ap, val), reads, writes)

    def scan(self, out, d0, d1, reads, writes):
        return self._op("dve", lambda e: e.tensor_tensor_scan(out=out, data0=d0, data1=d1, initial=0.0, op0=ALU.mult, op1=ALU.add), reads, writes)

    def asel(self, out, in_, pattern, cmp, fill, base, cm, reads, writes):
        return self._op("pool", lambda e: e.affine_select(out=out, in_=in_, pattern=pattern, compare_op=cmp, fill=fill, base=base, channel_multiplier=cm), reads, writes)

    def dma(self, q, out, in_, reads, writes):
        return self._op(q, lambda e: e.dma_start(out=out, in_=in_), reads, writes, dma=True)


def build_nc(upto=99, debug=False):
    nc = bass.Bass("TRN2", target_bir_lowering=False)
    k = K(nc)
    P = k.P

    def din(name, shape):
        return nc.dram_tensor(name, shape, F32, kind="ExternalInput").ap()

    x_in = din("x", [S, D])
    c_in = din("c", [8, 128])
    ada_w = din("ada_w", [2, D, 6 * D])
    ada_b = din("ada_b", [2, 6 * D])
    a_w_in = din("a_w_in", [D, 4096])
    a_lb = din("a_lb_logits", [16, 128])
    a_ng = din("a_norm_g", [1, 128])
    a_w_out = din("a_w_out", [D, D])
    kv_ada_w = din("kv_ada_w", [D, 2 * D])
    kv_ada_b = din("kv_ada_b", [1, 2 * D])
    kv_w = din("kv_w", [D, 2056])
    kv_bf = din("kv_b_f", [1, 8])
    k_ng = din("k_norm_g", [1, 128])
    b_w_q = din("b_w_q", [D, 2 * D])
    q_ng = din("q_norm_g", [1, 128])
    b_w_out = din("b_w_out", [D, D])
    w_up = din("ffn_w_up", [2, D, 2 * DFF])
    conv_w = din("ffn_conv_w", [2, 132, 128])
    conv_b = din("ffn_conv_b", [2, 44, 128])
    w_down = din("ffn_w_down", [2, DFF, D])
    out_d = nc.dram_tensor("out", [S, D], F32, kind="ExternalOutput").ap()
    skind = "ExternalOutput" if debug else "Internal"
    modsD = nc.dram_tensor("modsD", [14, D], F32, kind=skind).ap()
    x1_d = nc.dram_tensor("x1", [S, D], F32, kind=skind).ap()
    x2_d = nc.dram_tensor("x2", [S, D], F32, kind=skind).ap()
    x3_d = nc.dram_tensor("x3", [S, D], F32, kind=skind).ap()
    kT_d = nc.dram_tensor("kT_d", [NB, 128, 8 * TB], BF16, kind="Internal").ap()
    qT_d = nc.dram_tensor("qT_d", [NB, 128, 8 * TB], BF16, kind="Internal").ap()
    v_d = nc.dram_tensor("v_d", [S, D], BF16, kind="Internal").ap()
    gate_d = nc.dram_tensor("gate_d", [S, D], F32, kind="Internal").ap()
    bx1, bx2, bx3, bmods, bkT, bqT, bv, bgate, bout = (Buf(n) for n in "x1 x2 x3 mods kT qT v gate out".split())

    import contextlib
    st = contextlib.ExitStack()
    arena_t = st.enter_context(nc.sbuf_tensor("arena", [128, ARENA], U8))
    ps_t = st.enter_context(nc.psum_tensor("psum", [128, 8 * 2048], U8))
    A = Arena(arena_t, ARENA)
    PS = Arena(ps_t, 8 * 2048)

    def T(shape, dt, name=""):
        return A.alloc(shape, dt), Buf(name)

    bankbuf = [Buf(f"bank{i}") for i in range(8)]

    def BV(bank, shape, dt, boff=0):
        n = int(np.prod(shape))
        nb = n * DSZ[dt]
        assert boff + nb <= 2048
        off = bank * 2048 + boff
        ap = ps_t[:, off:off + nb].bitcast(dt)
        if len(shape) == 2:
            ap = ap.rearrange("p (a b) -> p a b", a=shape[0], b=shape[1])
        return ap, bankbuf[bank]

    ident_f, b_idf = T([128], F32)
    ident_bf, b_idb = T([128], BF16)
    ones_f, b_1f = T([128], F32)
    ones_bf, b_1b = T([128], BF16)
    tri_f, b_tri = T([128], F32)
    e0_f, b_e0 = T([128], F32)
    mask2_f, b_m2 = T([128], F32)
    maskc_bf, b_mc = T([128], BF16)
    scanmsk, b_sm = T([TB], F32)
    modcol, b_mod = T([112], F32)
    ccol, b_cc = T([8], F32)
    cact, b_ca = T([8], F32)
    lbcol, b_lb = T([16], F32)
    omlb, b_omlb = T([8], F32)
    nomlb, b_nomlb = T([8], F32)
    gcol, b_gc = T([3], F32)
    convc, b_cv = T([2, 176], F32)
    ss_a, b_ssa = T([NT], F32)
    ss_b, b_ssb = T([NT], F32)
    rstd_all, b_rs = T([NT], F32)
    ncum_all, b_nc = T([NT, 8], F32)
    cfirst_all, b_cf = T([NT, 8], F32)
    bfbc, b_bf = T([8], F32)
    scr, b_scr = T([1], F32)
    junk, b_junk = T([D], BF16)
    pmark = A.mark()

    def phase_reset():
        A.reset(pmark)
        k.barrier(scr)

    def load_cols(dst, bdst, src, n, stg, bstg, pstg, bpstg, rd=()):
        k.dma("sp", stg[0:n, :], src, list(rd), [bstg])
        k.tr(pstg[:, 0:n], stg[0:n, :], ident_f[0:n, 0:n], [bstg, b_idf], [bpstg])
        k.copy("dve", dst, pstg[:, 0:n], [bpstg], [bdst])

    k.memset("pool", ident_f, 0.0, [], [b_idf])
    k.asel(ident_f, ident_f, [[-1, 128]], ALU.not_equal, 1.0, 0, 1, [b_idf], [b_idf])
    k.copy("dve", ident_bf, ident_f, [b_idf], [b_idb])
    k.memset("pool", ones_f, 1.0, [], [b_1f])
    k.memset("pool", ones_bf, 1.0, [], [b_1b])
    k.memset("pool", tri_f, 1.0, [], [b_tri])
    k.asel(tri_f, tri_f, [[1, 128]], ALU.is_ge, 0.0, 0, -1, [b_tri], [b_tri])
    k.copy("dve", maskc_bf, tri_f, [b_tri], [b_mc])
    k.copy("dve", mask2_f, tri_f, [b_tri], [b_m2])
    k.memset("dve", mask2_f[0:64, 64:128], 0.0, [b_m2], [b_m2])
    k.memset("pool", e0_f, 0.0, [], [b_e0])
    k.asel(e0_f, e0_f, [[0, 128]], ALU.not_equal, 1.0, 0, 1, [b_e0], [b_e0])
    k.memset("pool", scanmsk, 1.0, [], [b_sm])
    k.memset("pool", scanmsk.rearrange("p (c t) -> p c t", t=64)[:, :, 0:1], 0.0, [b_sm], [b_sm])
    k.memset("pool", ncum_all, 0.0, [], [b_nc])

    stg, b_stg = T([128], F32)
    pstg, b_pstg = BV(0, [128], F32)
    load_cols(ccol, b_cc, c_in, 8, stg, b_stg, pstg, b_pstg)
    load_cols(lbcol, b_lb, a_lb, 16, stg, b_stg, pstg, b_pstg)
    k.dma("sp", stg[0:1, :], a_ng, [], [b_stg])
    k.dma("sp", stg[1:2, :], k_ng, [], [b_stg])
    k.dma("sp", stg[2:3, :], q_ng, [], [b_stg])
    k.tr(pstg[:, 0:3], stg[0:3, :], ident_f[0:3, 0:3], [b_stg, b_idf], [b_pstg])
    k.copy("dve", gcol, pstg[:, 0:3], [b_pstg], [b_gc])
    for l in range(2):
        load_cols(convc[:, l, 0:128], b_cv, conv_w[l, 0:128, :], 128, stg, b_stg, pstg, b_pstg)
        load_cols(convc[:, l, 128:132], b_cv, conv_w[l, 128:132, :], 4, stg, b_stg, pstg, b_pstg)
        load_cols(convc[:, l, 132:176], b_cv, conv_b[l], 44, stg, b_stg, pstg, b_pstg)
    k.dma("sp", bfbc, kv_bf.partition_broadcast(128), [], [b_bf])
    k.tt("dve", lbcol[:, 0:8], lbcol[:, 8:16], lbcol[:, 0:8], ALU.subtract, [b_lb], [b_lb])
    k.act(lbcol[:, 0:8], lbcol[:, 0:8], AF.Exp, [b_lb], [b_lb])
    k.ts("dve", lbcol[:, 0:8], lbcol[:, 0:8], 1.0, None, ALU.add, None, [b_lb], [b_lb])
    k.recip(lbcol[:, 0:8], lbcol[:, 0:8], [b_lb], [b_lb])
    k.ts("dve", omlb, lbcol[:, 0:8], -1.0, 1.0, ALU.mult, ALU.add, [b_lb], [b_omlb])
    k.ts("dve", nomlb, lbcol[:, 0:8], 1.0, -1.0, ALU.mult, ALU.add, [b_lb], [b_nomlb])
    k.act(cact, ccol, AF.Silu, [b_cc], [b_ca])

    xr = Ring([T([D], F32) for _ in range(3)])

    def sumsq(xt, bxt, ssdst, bss, t):
        k.act(junk, xt, AF.Square, [bxt], [b_junk, bss], accum=ssdst[:, t:t + 1])

    for t in range(NT):
        xt, bxt = xr.next()
        k.dma("sp", xt, x_in[t * 128:(t + 1) * 128, :], [], [bxt])
        sumsq(xt, bxt, ss_a, b_ssa, t)

    wst = Ring([T([3072], F32) for _ in range(3)])
    brow, b_brow = T([3072], F32)
    mrow, b_mrow = T([3072], F32)
    prow = [BV(1 + i, [512], F32) for i in range(6)]
    modflat = modsD.rearrange("(o v) n -> o (v n)", o=1)
    for (Wd_, bd_, row0, width) in ((ada_w[0], ada_b[0:1, :], 0, 6144), (ada_w[1], ada_b[1:2, :], 6, 6144), (kv_ada_w, kv_ada_b, 12, 2048)):
        for c0 in range(0, width, 3072):
            wd = min(3072, width - c0)
            nn = wd // 512
            for kc in range(KC):
                s_, bs_ = wst.next()
                k.dma("sp", s_[:, 0:wd], Wd_[kc * 128:(kc + 1) * 128, c0:c0 + wd], [], [bs_])
                for n in range(nn):
                    k.mm(prow[n][0][0:1, :], cact[:, kc:kc + 1], s_[:, n * 512:(n + 1) * 512], kc == 0, kc == KC - 1, [b_ca, bs_], [prow[n][1]])
            k.dma("sp", brow[0:1, 0:wd], bd_[:, c0:c0 + wd], [], [b_brow])
            for n in range(nn):
                k.tt("dve", mrow[0:1, n * 512:(n + 1) * 512], prow[n][0][0:1, :], brow[0:1, n * 512:(n + 1) * 512], ALU.add, [prow[n][1], b_brow], [b_mrow])
            k.dma("sp", modflat[:, row0 * D + c0: row0 * D + c0 + wd], mrow[0:1, 0:wd], [b_mrow], [bmods])
    load_cols(modcol, b_mod, modsD.rearrange("v (c p) -> (v c) p", p=128), 112, stg, b_stg, pstg, b_pstg, rd=[bmods])
    k.ts("dve", gcol[:, 2:3], gcol[:, 2:3], 128.0 ** -0.5, None, ALU.mult, None, [b_gc], [b_gc])
    for v in (1, 4, 7, 10, 13):
        k.ts("dve", modcol[:, v * 8:(v + 1) * 8], modcol[:, v * 8:(v + 1) * 8], 1.0, None, ALU.add, None, [b_mod], [b_mod])

    def calc_rstd(ss, bss):
        k.act(rstd_all, ss, AF.Ln, [bss], [b_rs], scale=1.0 / D, bias=EPS)
        k.act(rstd_all, rstd_all, AF.Exp, [b_rs], [b_rs], scale=-0.5)

    def g_bcast(dst, bdst, v):
        k.dma("sp", dst, modsD[v:v + 1, :].partition_broadcast(128), [bmods], [bdst])

    def load_w(dst, bdst, src, nk, c0=0, c1=None):
        for kc in range(nk):
            if c1 is None:
                k.dma("pool", dst[:, kc, :], src[kc * 128:(kc + 1) * 128, :], [], [bdst])
            else:
                k.dma("pool", dst[:, kc, 0:c1 - c0], src[kc * 128:(kc + 1) * 128, c0:c1], [], [bdst])

    def make_hT(xt, bxt, t, hn, bhn, tp, btp, variants, act_heavy=False):
        if act_heavy:
            k.act(hn, xt, AF.Identity, [bxt, b_rs], [bhn], scale=rstd_all[:, t:t + 1])
        else:
            k.ts("dve", hn, xt, rstd_all[:, t:t + 1], None, ALU.mult, None, [bxt, b_rs], [bhn])
        for kc in range(KC):
            k.tr(tp[:, kc, :], hn[:, kc * 128:(kc + 1) * 128], ident_bf, [bhn, b_idb], [btp])
        for (vs, vh, dst, bdst) in variants:
            for kc in range(KC):
                sc = modcol[:, vs * 8 + kc: vs * 8 + kc + 1]
                sh = modcol[:, vh * 8 + kc: vh * 8 + kc + 1]
                if kc % 2 == 0 or (act_heavy and kc % 4 != 3):
                    k.act(dst[:, kc, :], tp[:, kc, :], AF.Identity, [btp, b_mod], [bdst], scale=sc, bias=sh)
                else:
                    k.ts("dve", dst[:, kc, :], tp[:, kc, :], sc, sh, ALU.mult, ALU.add, [btp, b_mod], [bdst])

    def residual_out(ypair, xres, bxres, gbc, b_gbc, tmp, btmp, xnew, bxnew, dst_d, bdst_d, t, ss_next, b_ssn):
        for half in range(2):
            yp, byp = ypair[half]
            k.tt("dve", tmp[:, half * 512:(half + 1) * 512], yp, gbc[:, half * 512:(half + 1) * 512], ALU.mult, [byp, b_gbc], [btmp])
        k.tt("pool", xnew, tmp, xres, ALU.add, [btmp, bxres], [bxnew])
        k.dma("pool", dst_d[t * 128:(t + 1) * 128, :], xnew, [bxnew], [bdst_d])
        sumsq(xnew, bxnew, ss_next, b_ssn, t)

    def phase_hgrn(x_src, bsrc, x_dst, bdst, ss_in, b_ssin, ss_out, b_ssout):
        phase_reset()
        calc_rstd(ss_in, b_ssin)
        w_in_sb, b_win = T([KC, 4096], BF16)
        w_out_sb, b_wout = T([KC, D], BF16)
        load_w(w_in_sb, b_win, a_w_in, KC)
        load_w(w_out_sb, b_wout, a_w_out, KC)
        g1bc, b_g1 = T([D], F32)
        g_bcast(g1bc, b_g1, 2)
        xr = Ring([T([D], F32) for _ in range(3)])
        hnr = Ring([T([D], BF16) for _ in range(2)])
        hTr = Ring([T([KC, TB], BF16) for _ in range(2)])
        tmpA = Ring([T([TB], F32) for _ in range(2)])
        tmpB = Ring([T([TB], F32) for _ in range(2)])
        tmpC = Ring([T([TB], F32) for _ in range(2)])
        kstr = Ring([T([TB], BF16) for _ in range(8)])
        E_all = A.alloc([8, TB], F32)
        kR_all = A.alloc([8, TB], BF16)
        qE_all = A.alloc([8, TB], BF16)
        gs_all = A.alloc([8, TB], F32)
        ks_all = A.alloc([TPB, 8, 128], BF16)
        v_all = A.alloc([TPB, D], BF16)
        Elast = A.alloc([8, TB // 64], F32)
        st32 = A.alloc([8, 128], F32)
        stb = A.alloc([8, 128], BF16)
        bE, bkR, bqE, bgs, bks, bEl, bst32, bstb = ([Buf() for _ in range(8)] for _ in range(8))
        bv_ = [Buf() for _ in range(TPB)]
        atr = Ring([T([4, 128], BF16) for _ in range(4)])
        oTr = Ring([T([D], F32) for _ in range(2)])
        sq, bsq = T([D], BF16)
        lnr, blnr = T([D], F32)
        on, bon = T([D], BF16)
        tmp, btmp = T([D], F32)
        xnew, bxnew = T([D], F32)
        tp, btp = BV(0, [KC, 128], BF16)
        pa = Ring([BV(1, [TB], F32), BV(2, [TB], F32)])
        pbs = [BV(3, [512], F32), BV(4, [512], F32)]
        pbig = ps_t[:, 3 * 2048:5 * 2048].bitcast(F32)
        pb = Ring(pbs)
        scg = [BV(5, [4, 128], F32), BV(1, [4, 128], F32)]
        pog = [BV(6, [4, 128], F32), BV(2, [4, 128], F32)]
        dsg = [BV(7, [4, 128], F32), BV(3, [4, 128], F32)]
        tpks = [BV(0, [TPB, 128], BF16), BV(5, [TPB, 128], BF16)]
        k.memset("pool", st32, 0.0, [], bst32)
        k.memset("pool", stb, 0.0, [], bstb)
        NCH = TB // 64
        def do_hT_h(b):
            hT, bhT = hTr.next()
            for ti in range(TPB):
                t = b * TPB + ti
                xt, bxt = xr.next()
                hn, bhn = hnr.next()
                k.dma("sp", xt, x_src[t * 128:(t + 1) * 128, :], [bsrc], [bxt])
                make_hT(xt, bxt, t, hn, bhn, tp, btp, [(1, 0, hT[:, :, ti * 128:(ti + 1) * 128], bhT)])
            return hT, bhT

        nbk_h = DBG.get("nb", NB)
        nxt_h = do_hT_h(0)
        for b in range(nbk_h):
            hT, bhT = nxt_h
            if DBG.get("stage", 9) < 1:
                continue
            ksts = []
            for h in range(DBG.get("nh", 8)):
                pf, bpf = pa.next()
                for kc in range(KC):
                    k.mm(pf, w_in_sb[:, kc, 1024 + h * 128:1024 + (h + 1) * 128], hT[:, kc, :], kc == 0, kc == KC - 1, [b_win, bhT], [bpf])
                ta, bta = tmpA.next()
                tb_, btb = tmpB.next()
                tc, btc = tmpC.next()
                k.act(ta, pf, AF.Exp, [bpf], [bta])
                if DBG.get("fsub", 9) < 1:
                    continue
                k.act(ta, ta, AF.Ln, [bta], [bta], bias=1.0)
                k.act(ta, ta, AF.Exp, [bta], [bta], scale=-1.0)
                k.act(tb_, ta, AF.Ln, [bta, b_nomlb], [btb], scale=nomlb[:, h:h + 1], bias=1.0)
                if DBG.get("fsub", 9) < 2:
                    continue
                k.scan(tc, scanmsk, tb_, [b_sm, btb], [btc])
                k.act(E_all[:, h, :], tc, AF.Exp, [btc], [bE[h]])
                k.act(tb_, tc, AF.Exp, [btc], [btb], scale=-1.0)
                k.stt("dve", kR_all[:, h, :], ta, omlb[:, h:h + 1], tb_, ALU.mult, ALU.mult, [bta, btb, b_omlb], [bkR[h]])
                if DBG.get("fsub", 9) < 3:
                    continue
                kst, bkst = kstr.next()
                Ev = E_all[:, h, :].rearrange("p (c t) -> p c t", t=64)
                k.tt("pool", kst.rearrange("p (c t) -> p c t", t=64), kR_all[:, h, :].rearrange("p (c t) -> p c t", t=64),
                     Ev[:, :, 63:64].to_broadcast([128, NCH, 64]), ALU.mult, [bkR[h], bE[h]], [bkst])
                k.copy("pool", Elast[:, h, :], Ev[:, :, 63], [bE[h]], [bEl[h]])
                ksts.append((kst, bkst))
            if DBG.get("stage", 9) < 2:
                continue
            for ti in range(TPB):
                for half in range(2):
                    pv_, bpv = pb.next()
                    for kc in range(KC):
                        k.mm(pv_, hT[:, kc, ti * 128:(ti + 1) * 128], w_in_sb[:, kc, 2048 + half * 512:2048 + (half + 1) * 512], kc == 0, kc == KC - 1, [bhT, b_win], [bpv])
                    k.copy("dve", v_all[:, ti, half * 512:(half + 1) * 512], pv_, [bpv], [bv_[ti]])
            for h in range(8):
                kst, bkst = ksts[h]
                tpk_, btpk_ = tpks[h % 2]
                for ti in range(TPB):
                    k.tr(tpk_[:, ti, :], kst[:, ti * 128:(ti + 1) * 128], ident_bf, [bkst, b_idb], [btpk_])
                for ti in range(TPB):
                    k.copy("dve", ks_all[:, ti, h, :], tpk_[:, ti, :], [btpk_], [bks[h]])
            for h in range(8):
                pq, bpq = pa.next()
                for kc in range(KC):
                    k.mm(pq, w_in_sb[:, kc, h * 128:(h + 1) * 128], hT[:, kc, :], kc == 0, kc == KC - 1, [b_win, bhT], [bpq])
                if DBG.get("ssub", 9) < 0:
                    continue
                ta, bta = tmpA.next()
                k.act(ta, pq, DBG.get("qf", AF.Silu), [bpq], [bta])
                if DBG.get("ssub", 9) < 1:
                    continue
                k.tt("pool", qE_all[:, h, :], ta, E_all[:, h, :], ALU.mult, [bta, bE[h]], [bqE[h]])
                if DBG.get("ssub", 9) < 2:
                    continue
                pg, bpg = pa.next()
                for kc in range(KC):
                    k.mm(pg, w_in_sb[:, kc, 3072 + h * 128:3072 + (h + 1) * 128], hT[:, kc, :], kc == 0, kc == KC - 1, [b_win, bhT], [bpg])
                k.act(gs_all[:, h, :], pg, AF.Silu, [bpg], [bgs[h]])
            if b + 1 < nbk_h:
                nxt_h = do_hT_h(b + 1)
            for ti in range(TPB):
                t = b * TPB + ti
                cs = slice(ti * 128, (ti + 1) * 128)
                oT, boT = oTr.next()
                oT3 = oT.rearrange("p (h t) -> p h t", t=128)
                atgs = [atr.next() for _ in range(2)]
                for g in range(2):
                    scb, bscb = scg[g]
                    for i_, h in enumerate(range(g * 4, g * 4 + 4)):
                        k.mm(scb[:, i_, :], kR_all[:, h, cs], qE_all[:, h, cs], True, True, [bkR[h], bqE[h]], [bscb])
                for g in range(2):
                    scb, bscb = scg[g]
                    atg, batg = atgs[g]
                    for i_ in range(4):
                        k.tt("dve", atg[:, i_, :], scb[:, i_, :], mask2_f, ALU.mult, [bscb, b_m2], [batg])
                for c in range(2):
                    cc = slice(c * 64, (c + 1) * 64)
                    pr = slice(c * 64, (c + 1) * 64)
                    ch = ti * 2 + c
                    for g in range(2):
                        atg, batg = atgs[g]
                        pob, bpob = pog[g]
                        dsb, bdsb = dsg[g]
                        for i_, h in enumerate(range(g * 4, g * 4 + 4)):
                            k.mm(pob[:, i_, cc], v_all[:, ti, h * 128:(h + 1) * 128], atg[:, i_, cc], True, False, [bv_[ti], batg], [bpob])
                            k.mm(pob[:, i_, cc], stb[:, h, :], qE_all[:, h, ti * 128 + c * 64: ti * 128 + (c + 1) * 64], False, True, [bstb[h], bqE[h]], [bpob])
                            k.mm(dsb[:, i_, :], ks_all[pr, ti, h, :], v_all[pr, ti, h * 128:(h + 1) * 128], True, True, [bks[h], bv_[ti]], [bdsb])
                    for g in range(2):
                        dsb, bdsb = dsg[g]
                        for i_, h in enumerate(range(g * 4, g * 4 + 4)):
                            k.stt("dve", st32[:, h, :], st32[:, h, :], Elast[:, h, ch:ch + 1], dsb[:, i_, :], ALU.mult, ALU.add, [bst32[h], bEl[h], bdsb], [bst32[h]])
                            k.copy("pool", stb[:, h, :], st32[:, h, :], [bst32[h]], [bstb[h]])
                for g in range(2):
                    pob, bpob = pog[g]
                    for i_, h in enumerate(range(g * 4, g * 4 + 4)):
                        k.copy("dve", oT3[:, h, :], pob[:, i_, :], [bpob], [boT])
                if DBG.get("stage", 9) < 5:
                    continue
                k.act(sq, oT, AF.Square, [boT], [bsq])
                for half in range(2):
                    k.mm(pbs[half][0], ones_bf, sq[:, half * 512:(half + 1) * 512], True, True, [b_1b, bsq], [pbs[half][1]])
                k.act(lnr, pbig, AF.Ln, [pbs[0][1], pbs[1][1]], [blnr], scale=1.0 / 128, bias=EPS)
                k.act(lnr, lnr, AF.Exp, [blnr], [blnr], scale=-0.5)
                k.stt("dve", lnr, oT, gcol[:, 0:1], lnr, ALU.mult, ALU.mult, [boT, b_gc, blnr], [blnr])
                k.tt("pool", on.rearrange("p (h t) -> p h t", t=128), lnr.rearrange("p (h t) -> p h t", t=128), gs_all[:, :, cs], ALU.mult, [blnr] + bgs, [bon])
                on3 = on.rearrange("p (h t) -> p h t", t=128)
                xres, bxres = xr.next()
                k.dma("sp", xres, x_src[t * 128:(t + 1) * 128, :], [bsrc], [bxres])
                ypair = []
                for half in range(2):
                    yp, byp = pbs[half]
                    for h in range(8):
                        k.mm(yp, on3[:, h, :], w_out_sb[:, h, half * 512:(half + 1) * 512], h == 0, h == 7, [bon, b_wout], [byp])
                    ypair.append((yp, byp))
                residual_out(ypair, xres, bxres, g1bc, b_g1, tmp, btmp, xnew, bxnew, x_dst, bdst, t, ss_out, b_ssout)

    def phase_ffn(l, x_src, bsrc, x_dst, bdst, ss_in, b_ssin, ss_out, b_ssout, vsc, vsh, vg):
        phase_reset()
        calc_rstd(ss_in, b_ssin)
        wup, b_wup = T([KC, 2 * DFF], BF16)
        wdn, b_wdn = T([NJ, D], BF16)
        load_w(wup, b_wup, w_up[l], KC)
        load_w(wdn, b_wdn, w_down[l], NJ)
        gbc, b_gbc = T([D], F32)
        g_bcast(gbc, b_gbc, vg)
        xr = Ring([T([D], F32) for _ in range(2)])
        hnr = Ring([T([D], BF16) for _ in range(2)])
        hTr = Ring([T([KC, TB + 2], BF16) for _ in range(2)])
        t0r = Ring([T([TB], F32) for _ in range(3)])
        ur = Ring([T([TB + 2], F32) for _ in range(3)])
        sgr = Ring([T([TB], F32) for _ in range(2)])
        actr = Ring([T([NJ, TB], BF16) for _ in range(2)])
        tmp, btmp = T([D], F32)
        xnew, bxnew = T([D], F32)
        tp, btp = BV(0, [KC, 128], BF16)
        pur = Ring([BV(1 + i, [TB + 2], F32) for i in range(4)])
        pbs = [BV(5, [512], F32), BV(6, [512], F32)]
        def do_hT(b, prev):
            hT, bhT = hTr.next()
            if prev is None:
                k.memset("pool", hT[:, :, 0:2], 0.0, [], [bhT])
            else:
                k.copy("pool", hT[:, :, 0:2], prev[0][:, :, TB:TB + 2], [prev[1]], [bhT])
            for ti in range(TPB):
                t = b * TPB + ti
                xt, bxt = xr.next()
                hn, bhn = hnr.next()
                k.dma("sp", xt, x_src[t * 128:(t + 1) * 128, :], [bsrc], [bxt])
                make_hT(xt, bxt, t, hn, bhn, tp, btp, [(vsc, vsh, hT[:, :, 2 + ti * 128:2 + (ti + 1) * 128], bhT)], act_heavy=True)
            return hT, bhT

        def do_up(hT, bhT):
            aT, baT = actr.next()
            for j in range(NJ):
                res = []
                for which, cidx in ((0, j), (1, NJ + j)):
                    pu, bpu = pur.next()
                    for kc in range(KC):
                        k.mm(pu, wup[:, kc, cidx * 128:(cidx + 1) * 128], hT[:, kc, :], kc == 0, kc == KC - 1, [b_wup, bhT], [bpu])
                    t0, bt0 = t0r.next()
                    u, bu = ur.next()
                    k.act(u, pu, AF.Identity, [bpu], [bu])
                    k.ts("dve", t0, u[:, 2:2 + TB], convc[:, l, 88 + cidx:89 + cidx], convc[:, l, 132 + cidx:133 + cidx], ALU.mult, ALU.add, [bu, b_cv], [bt0])
                    k.stt("dve", t0, u[:, 1:1 + TB], convc[:, l, 44 + cidx:45 + cidx], t0, ALU.mult, ALU.add, [bu, b_cv, bt0], [bt0])
                    k.stt("dve", t0, u[:, 0:TB], convc[:, l, cidx:cidx + 1], t0, ALU.mult, ALU.add, [bu, b_cv, bt0], [bt0])
                    res.append((t0, bt0))
                sg, bsg = sgr.next()
                k.act(sg, res[0][0], AF.Silu, [res[0][1]], [bsg])
                k.tt("pool", aT[:, j, :], sg, res[1][0], ALU.mult, [bsg, res[1][1]], [baT])
            return aT, baT

        def do_down(b, aT, baT):
            for ti in range(TPB):
                t = b * TPB + ti
                xres, bxres = xr.next()
                k.dma("sp", xres, x_src[t * 128:(t + 1) * 128, :], [bsrc], [bxres])
                ypair = []
                for half in range(2):
                    yp, byp = pbs[half]
                    for j in range(NJ):
                        k.mm(yp, aT[:, j, ti * 128:(ti + 1) * 128], wdn[:, j, half * 512:(half + 1) * 512], j == 0, j == NJ - 1, [baT, b_wdn], [byp])
                    ypair.append((yp, byp))
                residual_out(ypair, xres, bxres, gbc, b_gbc, tmp, btmp, xnew, bxnew, x_dst, bdst, t, ss_out, b_ssout)

        nbk = DBG.get('fnb', NB)
        hts = do_hT(0, None)
        ats = do_up(*hts)
        for b in range(nbk):
            if b + 1 < nbk:
                hts_n = do_hT(b + 1, hts)
                ats_n = do_up(*hts_n)
            do_down(b, *ats)
            if b + 1 < nbk:
                hts, ats = hts_n, ats_n

    def phase_fox_a(x_src, bsrc, ss_in, b_ssin):
        phase_reset()
        calc_rstd(ss_in, b_ssin)
        kvw, b_kvw = T([KC, 2056], BF16)
        wq, b_wq = T([KC, 2048], BF16)
        load_w(kvw, b_kvw, kv_w, KC)
        load_w(wq, b_wq, b_w_q, KC)
        xr = Ring([T([D], F32) for _ in range(3)])
        hnr = Ring([T([D], BF16) for _ in range(2)])
        hkvr = Ring([T([KC, TB], BF16) for _ in range(2)])
        h1r = Ring([T([KC, TB], BF16) for _ in range(2)])
        kTbr = Ring([T([8, TB], BF16) for _ in range(2)])
        qTbr = Ring([T([8, TB], BF16) for _ in range(2)])
        sqr = Ring([T([TB], BF16) for _ in range(2)])
        lnrr = Ring([T([TB], F32) for _ in range(2)])
        vbr = Ring([T([D], BF16) for _ in range(2)])
        gtr = Ring([T([D], F32) for _ in range(2)])
        ger = Ring([T([512], F32) for _ in range(2)])
        lfr = Ring([T([8], F32) for _ in range(2)])
        run, brun = T([8], F32)
        tp, btp = BV(0, [KC, 128], BF16)
        pa = Ring([BV(1, [TB], F32), BV(2, [TB], F32)])
        pssr = Ring([BV(3, [TB], F32), BV(4, [TB], F32)])
        pb = Ring([BV(5, [512], F32), BV(6, [512], F32)])
        pf8, bpf8 = BV(7, [8], F32, 0)
        pcum, bpcum = BV(7, [8], F32, 512)
        pcf, bpcf = BV(7, [8], F32, 1024)
        k.memset("pool", run, 0.0, [], [brun])
        def do_hT_a(b):
            hkv, bhkv = hkvr.next()
            h1, bh1 = h1r.next()
            for ti in range(TPB):
                t = b * TPB + ti
                xt, bxt = xr.next()
                hn, bhn = hnr.next()
                k.dma("sp", xt, x_src[t * 128:(t + 1) * 128, :], [bsrc], [bxt])
                tsl = slice(ti * 128, (ti + 1) * 128)
                make_hT(xt, bxt, t, hn, bhn, tp, btp, [(13, 12, hkv[:, :, tsl], bhkv), (7, 6, h1[:, :, tsl], bh1)])
            return hkv, bhkv, h1, bh1

        nxt_a = do_hT_a(0)
        for b in range(NB):
            hkv, bhkv, h1, bh1 = nxt_a
            kTb, bkTb = kTbr.next()
            qTb, bqTb = qTbr.next()
            jobs = [(kvw, b_kvw, hkv, bhkv, 1, kTb, bkTb, h) for h in range(8)] + [(wq, b_wq, h1, bh1, 2, qTb, bqTb, h) for h in range(8)]

            def proj(job):
                W, bW, hs, bhs, gi, dstb, bdstb, h = job
                pk, bpk = pa.next()
                for kc in range(KC):
                    k.mm(pk, W[:, kc, h * 128:(h + 1) * 128], hs[:, kc, :], kc == 0, kc == KC - 1, [bW, bhs], [bpk])
                return pk, bpk

            cur = proj(jobs[0])
            for n, job in enumerate(jobs):
                nxt = proj(jobs[n + 1]) if n + 1 < len(jobs) else None
                W, bW, hs, bhs, gi, dstb, bdstb, h = job
                pk, bpk = cur
                sq, bsq = sqr.next()
                k.act(sq, pk, AF.Square, [bpk], [bsq])
                pss, bpss = pssr.next()
                k.mm(pss, ones_bf, sq, True, True, [b_1b, bsq], [bpss])
                lnr, blnr = lnrr.next()
                k.act(lnr, pss, AF.Ln, [bpss], [blnr], scale=1.0 / 128, bias=EPS)
                k.act(lnr, lnr, AF.Exp, [blnr], [blnr], scale=-0.5)
                k.stt("dve", dstb[:, h, :], pk, gcol[:, gi:gi + 1], lnr, ALU.mult, ALU.mult, [bpk, b_gc, blnr], [bdstb])
                cur = nxt
            k.dma("pool", kT_d[b], kTb.rearrange("p h t -> p (h t)"), [bkTb], [bkT])
            k.dma("pool", qT_d[b], qTb.rearrange("p h t -> p (h t)"), [bqTb], [bqT])
            if b + 1 < NB:
                nxt_a = do_hT_a(b + 1)
            for ti in range(TPB):
                t = b * TPB + ti
                tsl = slice(ti * 128, (ti + 1) * 128)
                vb, bvb = vbr.next()
                for half in range(2):
                    pv_, bpv = pb.next()
                    for kc in range(KC):
                        k.mm(pv_, hkv[:, kc, tsl], kvw[:, kc, 1024 + half * 512:1024 + (half + 1) * 512], kc == 0, kc == KC - 1, [bhkv, b_kvw], [bpv])
                    k.copy("dve", vb[:, half * 512:(half + 1) * 512], pv_, [bpv], [bvb])
                k.dma("pool", v_d[t * 128:(t + 1) * 128, :], vb, [bvb], [bv])
                gt, bgt = gtr.next()
                for half in range(2):
                    pg, bpg = pb.next()
                    for kc in range(KC):
                        k.mm(pg, h1[:, kc, tsl], wq[:, kc, 1024 + half * 512:1024 + (half + 1) * 512], kc == 0, kc == KC - 1, [bh1, b_wq], [bpg])
                    ge, bge = ger.next()
                    k.act(ge, pg, AF.Exp, [bpg], [bge], scale=-1.0)
                    k.act(ge, ge, AF.Ln, [bge], [bge], bias=1.0)
                    k.act(gt[:, half * 512:(half + 1) * 512], ge, AF.Exp, [bge], [bgt], scale=-1.0)
                k.dma("pool", gate_d[t * 128:(t + 1) * 128, :], gt, [bgt], [bgate])
                for kc in range(KC):
                    k.mm(pf8, hkv[:, kc, tsl], kvw[:, kc, 2048:2056], kc == 0, kc == KC - 1, [bhkv, b_kvw], [bpf8])
                lf, blf = lfr.next()
                k.tt("dve", lf, pf8, bfbc, ALU.add, [bpf8, b_bf], [blf])
                k.act(lf, lf, AF.Exp, [blf], [blf], scale=-1.0)
                k.act(lf, lf, AF.Ln, [blf], [blf], bias=1.0)
                k.mm(pcum, tri_f, lf, True, False, [b_tri, blf], [bpcum])
                k.mm(pcum, ones_f, run, False, True, [b_1f, brun], [bpcum])
                k.copy("dve", ncum_all[:, t, :], pcum, [bpcum], [b_nc])
                k.tt("dve", run, run, lf, ALU.add, [brun, blf], [brun])
                k.mm(pcf, e0_f, ncum_all[:, t, :], True, True, [b_e0, b_nc], [bpcf])
                k.copy("dve", cfirst_all[:, t, :], pcf, [bpcf], [b_cf])

    def phase_fox_bc(x_src, bsrc, x_dst, bdst, ss_out, b_ssout):
        phase_reset()
        wo, b_wo = T([KC, D], BF16)
        load_w(wo, b_wo, b_w_out, KC)
        gbc, b_gbc = T([D], F32)
        g_bcast(gbc, b_gbc, 8)
        o_all = A.alloc([NT, D], BF16)
        bo = [Buf() for _ in range(NT)]
        kThr = Ring([T([S], BF16) for _ in range(2)])
        qThr = Ring([T([S], BF16) for _ in range(2)])
        vhr = Ring([T([NT, 132], BF16) for _ in range(2)])
        ghr = Ring([T([NT, 128], F32) for _ in range(2)])
        bir = Ring([T([NT, NT], F32) for _ in range(2)])
        ptr_ = Ring([T([256], BF16) for _ in range(4)])
        scr2 = Ring([T([NT // 2], F32) for _ in range(2)])
        cmbr = Ring([T([132], F32) for _ in range(2)])
        recr = Ring([T([1], F32) for _ in range(2)])
        xr = Ring([T([D], F32) for _ in range(2)])
        oTr = Ring([T([KC, 128], BF16) for _ in range(2)])
        tmp, btmp = T([D], F32)
        xnew, bxnew = T([D], F32)
        sr = Ring([BV(i, [256], F32) for i in range(3)])
        por = Ring([BV(3, [3, 132], F32), BV(4, [3, 132], F32)])
        tp, btp = BV(5, [KC, 128], BF16)
        pbs = [BV(6, [512], F32), BV(7, [512], F32)]
        for (vh, bvh) in vhr.items:
            k.memset("pool", vh[:, :, 128:129], 1.0, [], [bvh])
        vdv = v_d.rearrange("(j p) d -> p j d", p=128)
        gdv = gate_d.rearrange("(j p) d -> p j d", p=128)
        for h in range(8):
            kTh, bkTh = kThr.next()
            qTh, bqTh = qThr.next()
            vh, bvh = vhr.next()
            gh, bgh = ghr.next()
            bias, bbias = bir.next()
            k.dma("sp", kTh.rearrange("p (b t) -> p b t", t=TB), kT_d.rearrange("b d (h t) -> d b h t", h=8)[:, :, h, :], [bkT], [bkTh])
            k.dma("sp", qTh.rearrange("p (b t) -> p b t", t=TB), qT_d.rearrange("b d (h t) -> d b h t", h=8)[:, :, h, :], [bqT], [bqTh])
            k.dma("sp", vh[:, :, 0:128], vdv[:, :, h * 128:(h + 1) * 128], [bv], [bvh])
            k.dma("sp", gh, gdv[:, :, h * 128:(h + 1) * 128], [bgate], [bgh])
            for j in range(NT):
                k.ts("dve", bias[:, j, :], cfirst_all[:, :, h], -1.0, ncum_all[:, j, h:h + 1], ALU.mult, ALU.add, [b_cf, b_nc], [bbias])
            scol, bscol = scr2.next()
            cfv = cfirst_all[:, :, h].rearrange("p (m two) -> p m two", two=2)
            k.tt("dve", scol, cfv[:, :, 0], cfv[:, :, 1], ALU.subtract, [b_cf], [bscol])
            k.act(scol, scol, AF.Exp, [bscol], [bscol])
            jobs = []
            for m in range(NT // 2):
                Q0, Q1 = 2 * m, 2 * m + 1
                for jj in range(2 * m + 1):
                    jobs.append((Q0 * 128, 256, jj, Q0, jj == 2 * m, [(0, 0, jj == 0, jj == 2 * m), (1, 128, jj == 0, jj == 2 * m)], None))
                jobs.append((Q1 * 128, 128, Q1, Q1, True, [(2, 0, True, True)], m))
            LA = 2
            sbuf_of = {}
            acc_of = {}
            for n in range(len(jobs) + LA):
                if n < len(jobs):
                    q0, qw, jj, Qb, msk, tg, fin = jobs[n]
                    s_, bs = sr.next()
                    sbuf_of[n] = (s_, bs)
                    k.mm(s_[:, 0:qw], kTh[:, jj * 128:(jj + 1) * 128], qTh[:, q0:q0 + qw], True, True, [bkTh, bqTh], [bs])
                mi = n - LA
                if mi < 0:
                    continue
                q0, qw, jj, Qb, msk, tg, fin = jobs[mi]
                pm = q0 // 256
                if pm not in acc_of:
                    acc_of[pm] = por.next()
                accb, baccb = acc_of[pm]
                s_, bs = sbuf_of.pop(mi)
                pt, bpt = ptr_.next()
                k.act(pt[:, 0:qw], s_[:, 0:qw], AF.Exp, [bs, bbias], [bpt], bias=bias[:, jj, Qb:Qb + 1])
                if msk:
                    k.tt("pool", pt[:, 0:128], pt[:, 0:128], maskc_bf, ALU.mult, [bpt, b_mc], [bpt])
                for (ai, c0, st_, sp_) in tg:
                    k.mm(accb[:, ai, 0:129], pt[:, c0:c0 + 128], vh[:, jj, 0:129], st_, sp_, [bpt, bvh], [baccb])
                if fin is not None:
                    Q0, Q1 = 2 * fin, 2 * fin + 1
                    rec, brec = recr.next()
                    k.recip(rec, accb[:, 0, 128:129], [baccb], [brec])
                    k.stt("dve", o_all[:, Q0, h * 128:(h + 1) * 128], accb[:, 0, 0:128], rec, gh[:, Q0, :], ALU.mult, ALU.mult, [baccb, brec, bgh], [bo[Q0]])
                    cmb, bcmb = cmbr.next()
                    k.copy("dve", cmb[:, 0:129], accb[:, 2, 0:129], [baccb], [bcmb])
                    k.stt("dve", cmb[:, 0:129], accb[:, 1, 0:129], scol[:, fin:fin + 1], cmb[:, 0:129], ALU.mult, ALU.add, [baccb, bscol, bcmb], [bcmb])
                    rec, brec = recr.next()
                    k.recip(rec, cmb[:, 128:129], [bcmb], [brec])
                    k.stt("dve", o_all[:, Q1, h * 128:(h + 1) * 128], cmb[:, 0:128], rec, gh[:, Q1, :], ALU.mult, ALU.mult, [bcmb, brec, bgh], [bo[Q1]])
                    del acc_of[pm]
        for t in range(NT):
            for kc in range(KC):
                k.tr(tp[:, kc, :], o_all[:, t, kc * 128:(kc + 1) * 128], ident_bf, [bo[t], b_idb], [btp])
            oT, boT = oTr.next()
            k.copy("dve", oT.rearrange("p a b -> p (a b)"), tp.rearrange("p a b -> p (a b)"), [btp], [boT])
            xres, bxres = xr.next()
            k.dma("sp", xres, x_src[t * 128:(t + 1) * 128, :], [bsrc], [bxres])
            ypair = []
            for half in range(2):
                yp, byp = pbs[half]
                for kc in range(KC):
                    k.mm(yp, oT[:, kc, :], wo[:, kc, half * 512:(half + 1) * 512], kc == 0, kc == KC - 1, [boT, b_wo], [byp])
                ypair.append((yp, byp))
            residual_out(ypair, xres, bxres, gbc, b_gbc, tmp, btmp, xnew, bxnew, x_dst, bdst, t, ss_out, b_ssout)

    bxin = Buf("xin")
    calc = None
    if upto >= 1:
        phase_hgrn(x_in, bxin, x1_d, bx1, ss_a, b_ssa, ss_b, b_ssb)
    if upto >= 2:
        phase_ffn(0, x1_d, bx1, x2_d, bx2, ss_b, b_ssb, ss_a, b_ssa, 4, 3, 5)
    if upto >= 3:
        phase_fox_a(x2_d, bx2, ss_a, b_ssa)
    if upto >= 4:
        phase_fox_bc(x2_d, bx2, x3_d, bx3, ss_b, b_ssb)
    if upto >= 5:
        phase_ffn(1, x3_d, bx3, out_d, bout, ss_b, b_ssb, ss_a, b_ssa, 10, 9, 11)

    k.barrier(scr)
    fin_d = nc.dram_tensor("fin_d", [128, 1], F32, kind="Internal").ap()
    k.dma("sp", fin_d, scr, [], [Buf("fin")])
    P.finalize()
    sems = {e: [st.enter_context(nc.semaphore(f"s_{e}{i}")) for i in range(P.nepoch[e])] for e in CENG}
    dsems = [st.enter_context(nc.semaphore(f"d{i}")) for i in range(P.NDMA_SEM)]
    block = st.enter_context(nc.Block())
    P.emit(block, sems, dsems)
    st.close()
    return nc


_NC_CACHE = {}


def _in_maps(inputs):
    f = lambda a: np.ascontiguousarray(np.asarray(a, dtype=np.float32))
    x = f(inputs["x"])
    c = f(inputs["c"])
    shared = dict(
        ada_w=f(inputs["ada_w"]), ada_b=f(inputs["ada_b"]),
        a_w_in=f(inputs["a_w_in"]).reshape(D, 4096),
        a_lb_logits=f(inputs["a_lb_logits"]).reshape(16, 128),
        a_norm_g=f(inputs["a_norm_g"]).reshape(1, 128),
        a_w_out=f(inputs["a_w_out"]).reshape(D, D),
        kv_ada_w=f(inputs["kv_ada_w"]), kv_ada_b=f(inputs["kv_ada_b"]).reshape(1, 2 * D),
        kv_w=f(inputs["kv_w"]), kv_b_f=f(inputs["kv_b_f"]).reshape(1, 8),
        k_norm_g=f(inputs["k_norm_g"]).reshape(1, 128),
        b_w_q=f(inputs["b_w_q"]).reshape(D, 2 * D),
        q_norm_g=f(inputs["q_norm_g"]).reshape(1, 128),
        b_w_out=f(inputs["b_w_out"]).reshape(D, D),
        ffn_w_up=f(inputs["ffn_w_up"]),
        ffn_conv_w=f(inputs["ffn_conv_w"]).reshape(2, 132, 128),
        ffn_conv_b=f(inputs["ffn_conv_b"]).reshape(2, 44, 128),
        ffn_w_down=f(inputs["ffn_w_down"]),
    )
    maps = []
    for b in range(8):
        m = dict(shared)
        m["x"] = x[b]
        m["c"] = c[b].reshape(8, 128)
        maps.append(m)
    return maps


def kernel(**inputs):
    if "nc" not in _NC_CACHE:
        _NC_CACHE["nc"] = build_nc()
    nc = _NC_CACHE["nc"]
    res = run_bass_kernel_spmd(nc, _in_maps(inputs), core_ids=list(range(8)))
    return np.stack([np.asarray(r["out"], dtype=np.float32) for r in res.results], axis=0)
```

```python
import numpy as np
import concourse.bass as bass
import concourse.mybir as mybir
from concourse.bass_utils import run_bass_kernel_spmd

F32 = mybir.dt.float32
BF16 = mybir.dt.bfloat16
U8 = mybir.dt.uint8
AF = mybir.ActivationFunctionType
ALU = mybir.AluOpType
AX = mybir.AxisListType
DSZ = {F32: 4, BF16: 2, U8: 1}

CENG = ("pe", "act", "dve", "pool")
EPOCH = 12000
STRICT = False


class Buf:
    __slots__ = ("name", "w", "r")

    def __init__(self, name=""):
        self.name = name
        self.w = None
        self.r = {}


class Op:
    __slots__ = ("id", "eng", "fn", "deps", "dma", "seq", "sig", "cnt", "need", "clock", "dsem", "dval", "inc")


class Prog:
    def __init__(self, nc):
        self.nc = nc
        self.ops = []
        self.ndma = 0
        self.dma_last = {}
        self.NDMA_SEM = 40
        self.NHW = 24
        self.nsw = 0

    def op(self, eng, fn, reads=(), writes=(), dma=False):
        o = Op()
        o.id = len(self.ops)
        o.eng = eng
        o.fn = fn
        o.dma = dma
        o.sig = False
        o.cnt = None
        deps = {}
        for b in reads:
            if b.w is not None:
                deps[b.w] = True
        for b in writes:
            if b.w is not None:
                deps.setdefault(b.w, False)
            for r in b.r.values():
                for rid in r:
                    deps.setdefault(rid, False)
        if dma:
            if eng == "pool":
                slot = self.NHW + self.nsw % (self.NDMA_SEM - self.NHW)
                self.nsw += 1
            else:
                slot = self.ndma % self.NHW
                self.ndma += 1
            prev = self.dma_last.get(slot)
            if prev is not None:
                deps.setdefault(prev.id, False)
                o.dval = prev.dval + 16
            else:
                o.dval = 16
            o.dsem = slot
            self.dma_last[slot] = o
        deps.pop(o.id, None)
        o.deps = deps
        for b in reads:
            if dma:
                b.r.setdefault("dma", []).append(o.id)
            else:
                b.r[eng] = [o.id]
        for b in writes:
            b.w = o.id
            b.r = {}
        self.ops.append(o)
        return o

    def finalize(self):
        ops = self.ops
        seqc = {e: 0 for e in CENG}
        known = {e: {c: 0 for c in CENG} for e in CENG + ("sp",)}
        kdma = {e: set() for e in CENG + ("sp",)}
        for o in ops:
            A = o.eng
            if not o.dma:
                seqc[A] += 1
                o.seq = seqc[A]
            else:
                o.seq = 0
            kn = known[A]
            need = []
            dl = sorted(o.deps.items(), key=lambda kv: -kv[0])
            for xid, raw in dl:
                X = ops[xid]
                if X.dma:
                    if xid in kdma[A]:
                        continue
                    need.append(xid)
                    kdma[A].add(xid)
                    for c in CENG:
                        if X.clock[c] > kn[c]:
                            kn[c] = X.clock[c]
                    continue
                E = X.eng
                if X.seq <= kn[E]:
                    continue
                if (not o.dma) and E == A:
                    if A == "pe" or not (raw or STRICT):
                        continue
                need.append(xid)
                X.sig = True
                kn[E] = X.seq
                for c in CENG:
                    if X.clock[c] > kn[c]:
                        kn[c] = X.clock[c]
            o.need = need
            ck = dict(kn)
            if len(kdma[A]) > 512:
                kdma[A] = set(sorted(kdma[A])[-256:])
            o.clock = ck
        cnt = {e: 0 for e in CENG}
        for o in ops:
            if (not o.dma) and o.sig:
                cnt[o.eng] += 1
                o.cnt = cnt[o.eng]
        self.nepoch = {e: cnt[e] // EPOCH + 1 for e in CENG}

    def emit(self, block, sems, dsems):
        ops = self.ops

        def semval(X):
            if X.dma:
                return dsems[X.dsem], X.dval
            k = (X.cnt - 1) // EPOCH
            return sems[X.eng][k], (X.cnt - 1) % EPOCH + 1

        def run(engname):
            def body(e):
                for o in ops:
                    if o.eng != engname:
                        continue
                    for xid in o.need:
                        s, v = semval(ops[xid])
                        e.wait_ge(s, v)
                    ins = o.fn(e)
                    if o.dma:
                        ins.then_inc(dsems[o.dsem], 16)
                    elif o.sig:
                        s, _ = semval(o)
                        ins.then_inc(s, 1)
                if engname == "sp":
                    for slot, o in self.dma_last.items():
                        e.wait_ge(dsems[slot], o.dval)
            return body

        block.tensor(run("pe"))
        block.scalar(run("act"))
        block.vector(run("dve"))
        block.gpsimd(run("pool"))
        block.sync(run("sp"))


class Arena:
    def __init__(self, t, size, part=128):
        self.t = t
        self.size = size
        self.off = 0

    def alloc(self, free_shape, dtype, align=64):
        n = int(np.prod(free_shape))
        nb = n * DSZ[dtype]
        off = (self.off + align - 1) // align * align
        assert off + nb <= self.size, f"arena overflow {off + nb} > {self.size}"
        self.off = off + nb
        ap = self.t[:, off:off + nb].bitcast(dtype)
        if len(free_shape) == 2:
            ap = ap.rearrange("p (a b) -> p a b", a=free_shape[0], b=free_shape[1])
        elif len(free_shape) == 3:
            ap = ap.rearrange("p (a b c) -> p a b c", a=free_shape[0], b=free_shape[1], c=free_shape[2])
        return ap

    def mark(self):
        return self.off

    def reset(self, m):
        self.off = m


S = 4096
D = 1024
NT = 32
KC = 8
TB = 256
NB = S // TB
TPB = TB // 128
DFF = 2816
NJ = 22
EPS = 1e-6
ARENA = 212480
DBG = {}


class Ring:
    def __init__(self, items):
        self.items = items
        self.i = 0

    def next(self):
        it = self.items[self.i % len(self.items)]
        self.i += 1
        return it


class K:
    def __init__(self, nc):
        self.nc = nc
        self.P = Prog(nc)
        self.Y = Buf("phase")

    def _op(self, eng, fn, reads, writes, dma=False):
        return self.P.op(eng, fn, reads=list(reads) + [self.Y], writes=list(writes), dma=dma)

    def barrier(self, scr):
        self.P.op("dve", lambda e: e.memset(scr, 0.0), reads=[], writes=[self.Y])

    def mm(self, out, lhsT, rhs, start, stop, reads, writes):
        return self._op("pe", lambda e: e.matmul(out, lhsT=lhsT, rhs=rhs, start=start, stop=stop), reads, writes)

    def tr(self, out, in_, ident, reads, writes):
        return self._op("pe", lambda e: e.transpose(out=out, in_=in_, identity=ident), reads, writes)

    def act(self, out, in_, func, reads, writes, scale=1.0, bias=0.0, accum=None):
        if accum is None:
            return self._op("act", lambda e: e.activation(out=out, in_=in_, func=func, bias=bias, scale=scale), reads, writes)
        return self._op("act", lambda e: e.activation(out=out, in_=in_, func=func, bias=bias, scale=scale, accum_out=accum), reads, writes)

    def tt(self, eng, out, in0, in1, op, reads, writes):
        return self._op(eng, lambda e: e.tensor_tensor(out=out, in0=in0, in1=in1, op=op), reads, writes)

    def ts(self, eng, out, in0, s1, s2, op0, op1, reads, writes):
        if s2 is None:
            return self._op(eng, lambda e: e.tensor_scalar(out=out, in0=in0, scalar1=s1, scalar2=None, op0=op0), reads, writes)
        return self._op(eng, lambda e: e.tensor_scalar(out=out, in0=in0, scalar1=s1, scalar2=s2, op0=op0, op1=op1), reads, writes)

    def stt(self, eng, out, in0, scalar, in1, op0, op1, reads, writes):
        return self._op(eng, lambda e: e.scalar_tensor_tensor(out=out, in0=in0, scalar=scalar, in1=in1, op0=op0, op1=op1), reads, writes)

    def copy(self, eng, out, in_, reads, writes):
        if eng == "act":
            return self._op("act", lambda e: e.activation(out=out, in_=in_, func=AF.Identity), reads, writes)
        return self._op(eng, lambda e: e.tensor_copy(out=out, in_=in_), reads, writes)

    def recip(self, out, in_, reads, writes):
        return self._op("dve", lambda e: e.reciprocal(out=out, in_=in_), reads, writes)

    def memset(self, eng, ap, val, reads, writes):
        return self._op(eng, lambda e: e.memset(ap, val), reads, writes)

    def scan(self, out, d0, d1, reads, writes):
        return self._op("dve", lambda e: e.tensor_tensor_scan(out=out, data0=d0, data1=d1, initial=0.0, op0=ALU.mult, op1=ALU.add), reads, writes)

    def asel(self, out, in_, pattern, cmp, fill, base, cm, reads, writes):
        return self._op("pool", lambda e: e.affine_select(out=out, in_=in_, pattern=pattern, compare_op=cmp, fill=fill, base=base, channel_multiplier=cm), reads, writes)

    def dma(self, q, out, in_, reads, writes):
        return self._op(q, lambda e: e.dma_start(out=out, in_=in_), reads, writes, dma=True)


def build_nc(upto=99, debug=False):
    nc = bass.Bass("TRN2", target_bir_lowering=False)
    k = K(nc)
    P = k.P

    def din(name, shape):
        return nc.dram_tensor(name, shape, F32, kind="ExternalInput").ap()

    x_in = din("x", [S, D])
    c_in = din("c", [8, 128])
    ada_w = din("ada_w", [2, D, 6 * D])
    ada_b = din("ada_b", [2, 6 * D])
    a_w_in = din("a_w_in", [D, 4096])
    a_lb = din("a_lb_logits", [16, 128])
    a_ng = din("a_norm_g", [1, 128])
    a_w_out = din("a_w_out", [D, D])
    kv_ada_w = din("kv_ada_w", [D, 2 * D])
    kv_ada_b = din("kv_ada_b", [1, 2 * D])
    kv_w = din("kv_w", [D, 2056])
    kv_bf = din("kv_b_f", [1, 8])
    k_ng = din("k_norm_g", [1, 128])
    b_w_q = din("b_w_q", [D, 2 * D])
    q_ng = din("q_norm_g", [1, 128])
    b_w_out = din("b_w_out", [D, D])
    w_up = din("ffn_w_up", [2, D, 2 * DFF])
    conv_w = din("ffn_conv_w", [2, 132, 128])
    conv_b = din("ffn_conv_b", [2, 44, 128])
    w_down = din("ffn_w_down", [2, DFF, D])
    out_d = nc.dram_tensor("out", [S, D], F32, kind="ExternalOutput").ap()
    skind = "ExternalOutput" if debug else "Internal"
    modsD = nc.dram_tensor("modsD", [14, D], F32, kind=skind).ap()
    x1_d = nc.dram_tensor("x1", [S, D], F32, kind=skind).ap()
    x2_d = nc.dram_tensor("x2", [S, D], F32, kind=skind).ap()
    x3_d = nc.dram_tensor("x3", [S, D], F32, kind=skind).ap()
    kT_d = nc.dram_tensor("kT_d", [NB, 128, 8 * TB], BF16, kind="Internal").ap()
    qT_d = nc.dram_tensor("qT_d", [NB, 128, 8 * TB], BF16, kind="Internal").ap()
    v_d = nc.dram_tensor("v_d", [S, D], BF16, kind="Internal").ap()
    gate_d = nc.dram_tensor("gate_d", [S, D], F32, kind="Internal").ap()
    bx1, bx2, bx3, bmods, bkT, bqT, bv, bgate, bout = (Buf(n) for n in "x1 x2 x3 mods kT qT v gate out".split())

    import contextlib
    st = contextlib.ExitStack()
    arena_t = st.enter_context(nc.sbuf_tensor("arena", [128, ARENA], U8))
    ps_t = st.enter_context(nc.psum_tensor("psum", [128, 8 * 2048], U8))
    A = Arena(arena_t, ARENA)
    PS = Arena(ps_t, 8 * 2048)

    def T(shape, dt, name=""):
        return A.alloc(shape, dt), Buf(name)

    bankbuf = [Buf(f"bank{i}") for i in range(8)]

    def BV(bank, shape, dt, boff=0):
        n = int(np.prod(shape))
        nb = n * DSZ[dt]
        assert boff + nb <= 2048
        off = bank * 2048 + boff
        ap = ps_t[:, off:off + nb].bitcast(dt)
        if len(shape) == 2:
            ap = ap.rearrange("p (a b) -> p a b", a=shape[0], b=shape[1])
        return ap, bankbuf[bank]

    ident_f, b_idf = T([128], F32)
    ident_bf, b_idb = T([128], BF16)
    ones_f, b_1f = T([128], F32)
    ones_bf, b_1b = T([128], BF16)
    tri_f, b_tri = T([128], F32)
    e0_f, b_e0 = T([128], F32)
    mask2_f, b_m2 = T([128], F32)
    maskc_bf, b_mc = T([128], BF16)
    scanmsk, b_sm = T([TB], F32)
    modcol, b_mod = T([112], F32)
    ccol, b_cc = T([8], F32)
    cact, b_ca = T([8], F32)
    lbcol, b_lb = T([16], F32)
    omlb, b_omlb = T([8], F32)
    nomlb, b_nomlb = T([8], F32)
    gcol, b_gc = T([3], F32)
    convc, b_cv = T([2, 176], F32)
    ss_a, b_ssa = T([NT], F32)
    ss_b, b_ssb = T([NT], F32)
    rstd_all, b_rs = T([NT], F32)
    ncum_all, b_nc = T([NT, 8], F32)
    cfirst_all, b_cf = T([NT, 8], F32)
    bfbc, b_bf = T([8], F32)
    scr, b_scr = T([1], F32)
    junk, b_junk = T([D], BF16)
    pmark = A.mark()

    def phase_reset():
        A.reset(pmark)
        k.barrier(scr)

    def load_cols(dst, bdst, src, n, stg, bstg, pstg, bpstg, rd=()):
        k.dma("sp", stg[0:n, :], src, list(rd), [bstg])
        k.tr(pstg[:, 0:n], stg[0:n, :], ident_f[0:n, 0:n], [bstg, b_idf], [bpstg])
        k.copy("dve", dst, pstg[:, 0:n], [bpstg], [bdst])

    k.memset("pool", ident_f, 0.0, [], [b_idf])
    k.asel(ident_f, ident_f, [[-1, 128]], ALU.not_equal, 1.0, 0, 1, [b_idf], [b_idf])
    k.copy("dve", ident_bf, ident_f, [b_idf], [b_idb])
    k.memset("pool", ones_f, 1.0, [], [b_1f])
    k.memset("pool", ones_bf, 1.0, [], [b_1b])
    k.memset("pool", tri_f, 1.0, [], [b_tri])
    k.asel(tri_f, tri_f, [[1, 128]], ALU.is_ge, 0.0, 0, -1, [b_tri], [b_tri])
    k.copy("dve", maskc_bf, tri_f, [b_tri], [b_mc])
    k.copy("dve", mask2_f, tri_f, [b_tri], [b_m2])
    k.memset("dve", mask2_f[0:64, 64:128], 0.0, [b_m2], [b_m2])
    k.memset("pool", e0_f, 0.0, [], [b_e0])
    k.asel(e0_f, e0_f, [[0, 128]], ALU.not_equal, 1.0, 0, 1, [b_e0], [b_e0])
    k.memset("pool", scanmsk, 1.0, [], [b_sm])
    k.memset("pool", scanmsk.rearrange("p (c t) -> p c t", t=64)[:, :, 0:1], 0.0, [b_sm], [b_sm])
    k.memset("pool", ncum_all, 0.0, [], [b_nc])

    stg, b_stg = T([128], F32)
    pstg, b_pstg = BV(0, [128], F32)
    load_cols(ccol, b_cc, c_in, 8, stg, b_stg, pstg, b_pstg)
    load_cols(lbcol, b_lb, a_lb, 16, stg, b_stg, pstg, b_pstg)
    k.dma("sp", stg[0:1, :], a_ng, [], [b_stg])
    k.dma("sp", stg[1:2, :], k_ng, [], [b_stg])
    k.dma("sp", stg[2:3, :], q_ng, [], [b_stg])
    k.tr(pstg[:, 0:3], stg[0:3, :], ident_f[0:3, 0:3], [b_stg, b_idf], [b_pstg])
    k.copy("dve", gcol, pstg[:, 0:3], [b_pstg], [b_gc])
    for l in range(2):
        load_cols(convc[:, l, 0:128], b_cv, conv_w[l, 0:128, :], 128, stg, b_stg, pstg, b_pstg)
        load_cols(convc[:, l, 128:132], b_cv, conv_w[l, 128:132, :], 4, stg, b_stg, pstg, b_pstg)
        load_cols(convc[:, l, 132:176], b_cv, conv_b[l], 44, stg, b_stg, pstg, b_pstg)
    k.dma("sp", bfbc, kv_bf.partition_broadcast(128), [], [b_bf])
    k.tt("dve", lbcol[:, 0:8], lbcol[:, 8:16], lbcol[:, 0:8], ALU.subtract, [b_lb], [b_lb])
    k.act(lbcol[:, 0:8], lbcol[:, 0:8], AF.Exp, [b_lb], [b_lb])
    k.ts("dve", lbcol[:, 0:8], lbcol[:, 0:8], 1.0, None, ALU.add, None, [b_lb], [b_lb])
    k.recip(lbcol[:, 0:8], lbcol[:, 0:8], [b_lb], [b_lb])
    k.ts("dve", omlb, lbcol[:, 0:8], -1.0, 1.0, ALU.mult, ALU.add, [b_lb], [b_omlb])
    k.ts("dve", nomlb, lbcol[:, 0:8], 1.0, -1.0, ALU.mult, ALU.add, [b_lb], [b_nomlb])
    k.act(cact, ccol, AF.Silu, [b_cc], [b_ca])

    xr = Ring([T([D], F32) for _ in range(3)])

    def sumsq(xt, bxt, ssdst, bss, t):
        k.act(junk, xt, AF.Square, [bxt], [b_junk, bss], accum=ssdst[:, t:t + 1])

    for t in range(NT):
        xt, bxt = xr.next()
        k.dma("sp", xt, x_in[t * 128:(t + 1) * 128, :], [], [bxt])
        sumsq(xt, bxt, ss_a, b_ssa, t)

    wst = Ring([T([3072], F32) for _ in range(3)])
    brow, b_brow = T([3072], F32)
    mrow, b_mrow = T([3072], F32)
    prow = [BV(1 + i, [512], F32) for i in range(6)]
    modflat = modsD.rearrange("(o v) n -> o (v n)", o=1)
    for (Wd_, bd_, row0, width) in ((ada_w[0], ada_b[0:1, :], 0, 6144), (ada_w[1], ada_b[1:2, :], 6, 6144), (kv_ada_w, kv_ada_b, 12, 2048)):
        for c0 in range(0, width, 3072):
            wd = min(3072, width - c0)
            nn = wd // 512
            for kc in range(KC):
                s_, bs_ = wst.next()
                k.dma("sp", s_[:, 0:wd], Wd_[kc * 128:(kc + 1) * 128, c0:c0 + wd], [], [bs_])
                for n in range(nn):
                    k.mm(prow[n][0][0:1, :], cact[:, kc:kc + 1], s_[:, n * 512:(n + 1) * 512], kc == 0, kc == KC - 1, [b_ca, bs_], [prow[n][1]])
            k.dma("sp", brow[0:1, 0:wd], bd_[:, c0:c0 + wd], [], [b_brow])
            for n in range(nn):
                k.tt("dve", mrow[0:1, n * 512:(n + 1) * 512], prow[n][0][0:1, :], brow[0:1, n * 512:(n + 1) * 512], ALU.add, [prow[n][1], b_brow], [b_mrow])
            k.dma("sp", modflat[:, row0 * D + c0: row0 * D + c0 + wd], mrow[0:1, 0:wd], [b_mrow], [bmods])
    load_cols(modcol, b_mod, modsD.rearrange("v (c p) -> (v c) p", p=128), 112, stg, b_stg, pstg, b_pstg, rd=[bmods])
    k.ts("dve", gcol[:, 2:3], gcol[:, 2:3], 128.0 ** -0.5, None, ALU.mult, None, [b_gc], [b_gc])
    for v in (1, 4, 7, 10, 13):
        k.ts("dve", modcol[:, v * 8:(v + 1) * 8], modcol[:, v * 8:(v + 1) * 8], 1.0, None, ALU.add, None, [b_mod], [b_mod])

    def calc_rstd(ss, bss):
        k.act(rstd_all, ss, AF.Ln, [bss], [b_rs], scale=1.0 / D, bias=EPS)
        k.act(rstd_all, rstd_all, AF.Exp, [b_rs], [b_rs], scale=-0.5)

    def g_bcast(dst, bdst, v):
        k.dma("sp", dst, modsD[v:v + 1, :].partition_broadcast(128), [bmods], [bdst])

    def load_w(dst, bdst, src, nk, c0=0, c1=None):
        for kc in range(nk):
            if c1 is None:
                k.dma("pool", dst[:, kc, :], src[kc * 128:(kc + 1) * 128, :], [], [bdst])
            else:
                k.dma("pool", dst[:, kc, 0:c1 - c0], src[kc * 128:(kc + 1) * 128, c0:c1], [], [bdst])

    def make_hT(xt, bxt, t, hn, bhn, tp, btp, variants, act_heavy=False):
        if act_heavy:
            k.act(hn, xt, AF.Identity, [bxt, b_rs], [bhn], scale=rstd_all[:, t:t + 1])
        else:
            k.ts("dve", hn, xt, rstd_all[:, t:t + 1], None, ALU.mult, None, [bxt, b_rs], [bhn])
        for kc in range(KC):
            k.tr(tp[:, kc, :], hn[:, kc * 128:(kc + 1) * 128], ident_bf, [bhn, b_idb], [btp])
        for (vs, vh, dst, bdst) in variants:
            for kc in range(KC):
                sc = modcol[:, vs * 8 + kc: vs * 8 + kc + 1]
                sh = modcol[:, vh * 8 + kc: vh * 8 + kc + 1]
                if kc % 2 == 0 or (act_heavy and kc % 4 != 3):
                    k.act(dst[:, kc, :], tp[:, kc, :], AF.Identity, [btp, b_mod], [bdst], scale=sc, bias=sh)
                else:
                    k.ts("dve", dst[:, kc, :], tp[:, kc, :], sc, sh, ALU.mult, ALU.add, [btp, b_mod], [bdst])

    def residual_out(ypair, xres, bxres, gbc, b_gbc, tmp, btmp, xnew, bxnew, dst_d, bdst_d, t, ss_next, b_ssn):
        for half in range(2):
            yp, byp = ypair[half]
            k.tt("dve", tmp[:, half * 512:(half + 1) * 512], yp, gbc[:, half * 512:(half + 1) * 512], ALU.mult, [byp, b_gbc], [btmp])
        k.tt("pool", xnew, tmp, xres, ALU.add, [btmp, bxres], [bxnew])
        k.dma("pool", dst_d[t * 128:(t + 1) * 128, :], xnew, [bxnew], [bdst_d])
        sumsq(xnew, bxnew, ss_next, b_ssn, t)

    def phase_hgrn(x_src, bsrc, x_dst, bdst, ss_in, b_ssin, ss_out, b_ssout):
        phase_reset()
        calc_rstd(ss_in, b_ssin)
        w_in_sb, b_win = T([KC, 4096], BF16)
        w_out_sb, b_wout = T([KC, D], BF16)
        load_w(w_in_sb, b_win, a_w_in, KC)
        load_w(w_out_sb, b_wout, a_w_out, KC)
        g1bc, b_g1 = T([D], F32)
        g_bcast(g1bc, b_g1, 2)
        xr = Ring([T([D], F32) for _ in range(3)])
        hnr = Ring([T([D], BF16) for _ in range(2)])
        hTr = Ring([T([KC, TB], BF16) for _ in range(2)])
        tmpA = Ring([T([TB], F32) for _ in range(2)])
        tmpB = Ring([T([TB], F32) for _ in range(2)])
        tmpC = Ring([T([TB], F32) for _ in range(2)])
        kstr = Ring([T([TB], BF16) for _ in range(8)])
        E_all = A.alloc([8, TB], F32)
        kR_all = A.alloc([8, TB], BF16)
        qE_all = A.alloc([8, TB], BF16)
        gs_all = A.alloc([8, TB], F32)
        ks_all = A.alloc([TPB, 8, 128], BF16)
        v_all = A.alloc([TPB, D], BF16)
        Elast = A.alloc([8, TB // 64], F32)
        st32 = A.alloc([8, 128], F32)
        stb = A.alloc([8, 128], BF16)
        bE, bkR, bqE, bgs, bks, bEl, bst32, bstb = ([Buf() for _ in range(8)] for _ in range(8))
        bv_ = [Buf() for _ in range(TPB)]
        atr = Ring([T([4, 128], BF16) for _ in range(4)])
        oTr = Ring([T([D], F32) for _ in range(2)])
        sq, bsq = T([D], BF16)
        lnr, blnr = T([D], F32)
        on, bon = T([D], BF16)
        tmp, btmp = T([D], F32)
        xnew, bxnew = T([D], F32)
        tp, btp = BV(0, [KC, 128], BF16)
        pa = Ring([BV(1, [TB], F32), BV(2, [TB], F32)])
        pbs = [BV(3, [512], F32), BV(4, [512], F32)]
        pbig = ps_t[:, 3 * 2048:5 * 2048].bitcast(F32)
        pb = Ring(pbs)
        scg = [BV(5, [4, 128], F32), BV(1, [4, 128], F32)]
        pog = [BV(6, [4, 128], F32), BV(2, [4, 128], F32)]
        dsg = [BV(7, [4, 128], F32), BV(3, [4, 128], F32)]
        tpks = [BV(0, [TPB, 128], BF16), BV(5, [TPB, 128], BF16)]
        k.memset("pool", st32, 0.0, [], bst32)
        k.memset("pool", stb, 0.0, [], bstb)
        NCH = TB // 64
        def do_hT_h(b):
            hT, bhT = hTr.next()
            for ti in range(TPB):
                t = b * TPB + ti
                xt, bxt = xr.next()
                hn, bhn = hnr.next()
                k.dma("sp", xt, x_src[t * 128:(t + 1) * 128, :], [bsrc], [bxt])
                make_hT(xt, bxt, t, hn, bhn, tp, btp, [(1, 0, hT[:, :, ti * 128:(ti + 1) * 128], bhT)])
            return hT, bhT

        nbk_h = DBG.get("nb", NB)
        nxt_h = do_hT_h(0)
        for b in range(nbk_h):
            hT, bhT = nxt_h
            if DBG.get("stage", 9) < 1:
                continue
            ksts = []
            for h in range(DBG.get("nh", 8)):
                pf, bpf = pa.next()
                for kc in range(KC):
                    k.mm(pf, w_in_sb[:, kc, 1024 + h * 128:1024 + (h + 1) * 128], hT[:, kc, :], kc == 0, kc == KC - 1, [b_win, bhT], [bpf])
                ta, bta = tmpA.next()
                tb_, btb = tmpB.next()
                tc, btc = tmpC.next()
                k.act(ta, pf, AF.Exp, [bpf], [bta])
                if DBG.get("fsub", 9) < 1:
                    continue
                k.act(ta, ta, AF.Ln, [bta], [bta], bias=1.0)
                k.act(ta, ta, AF.Exp, [bta], [bta], scale=-1.0)
                k.act(tb_, ta, AF.Ln, [bta, b_nomlb], [btb], scale=nomlb[:, h:h + 1], bias=1.0)
                if DBG.get("fsub", 9) < 2:
                    continue
                k.scan(tc, scanmsk, tb_, [b_sm, btb], [btc])
                k.act(E_all[:, h, :], tc, AF.Exp, [btc], [bE[h]])
                k.act(tb_, tc, AF.Exp, [btc], [btb], scale=-1.0)
                k.stt("dve", kR_all[:, h, :], ta, omlb[:, h:h + 1], tb_, ALU.mult, ALU.mult, [bta, btb, b_omlb], [bkR[h]])
                if DBG.get("fsub", 9) < 3:
                    continue
                kst, bkst = kstr.next()
                Ev = E_all[:, h, :].rearrange("p (c t) -> p c t", t=64)
                k.tt("pool", kst.rearrange("p (c t) -> p c t", t=64), kR_all[:, h, :].rearrange("p (c t) -> p c t", t=64),
                     Ev[:, :, 63:64].to_broadcast([128, NCH, 64]), ALU.mult, [bkR[h], bE[h]], [bkst])
                k.copy("pool", Elast[:, h, :], Ev[:, :, 63], [bE[h]], [bEl[h]])
                ksts.append((kst, bkst))
            if DBG.get("stage", 9) < 2:
                continue
            for ti in range(TPB):
                for half in range(2):
                    pv_, bpv = pb.next()
                    for kc in range(KC):
                        k.mm(pv_, hT[:, kc, ti * 128:(ti + 1) * 128], w_in_sb[:, kc, 2048 + half * 512:2048 + (half + 1) * 512], kc == 0, kc == KC - 1, [bhT, b_win], [bpv])
                    k.copy("dve", v_all[:, ti, half * 512:(half + 1) * 512], pv_, [bpv], [bv_[ti]])
            for h in range(8):
                kst, bkst = ksts[h]
                tpk_, btpk_ = tpks[h % 2]
                for ti in range(TPB):
                    k.tr(tpk_[:, ti, :], kst[:, ti * 128:(ti + 1) * 128], ident_bf, [bkst, b_idb], [btpk_])
                for ti in range(TPB):
                    k.copy("dve", ks_all[:, ti, h, :], tpk_[:, ti, :], [btpk_], [bks[h]])
            for h in range(8):
                pq, bpq = pa.next()
                for kc in range(KC):
                    k.mm(pq, w_in_sb[:, kc, h * 128:(h + 1) * 128], hT[:, kc, :], kc == 0, kc == KC - 1, [b_win, bhT], [bpq])
                if DBG.get("ssub", 9) < 0:
                    continue
                ta, bta = tmpA.next()
                k.act(ta, pq, DBG.get("qf", AF.Silu), [bpq], [bta])
                if DBG.get("ssub", 9) < 1:
                    continue
                k.tt("pool", qE_all[:, h, :], ta, E_all[:, h, :], ALU.mult, [bta, bE[h]], [bqE[h]])
                if DBG.get("ssub", 9) < 2:
                    continue
                pg, bpg = pa.next()
                for kc in range(KC):
                    k.mm(pg, w_in_sb[:, kc, 3072 + h * 128:3072 + (h + 1) * 128], hT[:, kc, :], kc == 0, kc == KC - 1, [b_win, bhT], [bpg])
                k.act(gs_all[:, h, :], pg, AF.Silu, [bpg], [bgs[h]])
            if b + 1 < nbk_h:
                nxt_h = do_hT_h(b + 1)
            for ti in range(TPB):
                t = b * TPB + ti
                cs = slice(ti * 128, (ti + 1) * 128)
                oT, boT = oTr.next()
                oT3 = oT.rearrange("p (h t) -> p h t", t=128)
                atgs = [atr.next() for _ in range(2)]
                for g in range(2):
                    scb, bscb = scg[g]
                    for i_, h in enumerate(range(g * 4, g * 4 + 4)):
                        k.mm(scb[:, i_, :], kR_all[:, h, cs], qE_all[:, h, cs], True, True, [bkR[h], bqE[h]], [bscb])
                for g in range(2):
                    scb, bscb = scg[g]
                    atg, batg = atgs[g]
                    for i_ in range(4):
                        k.tt("dve", atg[:, i_, :], scb[:, i_, :], mask2_f, ALU.mult, [bscb, b_m2], [batg])
                for c in range(2):
                    cc = slice(c * 64, (c + 1) * 64)
                    pr = slice(c * 64, (c + 1) * 64)
                    ch = ti * 2 + c
                    for g in range(2):
                        atg, batg = atgs[g]
                        pob, bpob = pog[g]
                        dsb, bdsb = dsg[g]
                        for i_, h in enumerate(range(g * 4, g * 4 + 4)):
                            k.mm(pob[:, i_, cc], v_all[:, ti, h * 128:(h + 1) * 128], atg[:, i_, cc], True, False, [bv_[ti], batg], [bpob])
                            k.mm(pob[:, i_, cc], stb[:, h, :], qE_all[:, h, ti * 128 + c * 64: ti * 128 + (c + 1) * 64], False, True, [bstb[h], bqE[h]], [bpob])
                            k.mm(dsb[:, i_, :], ks_all[pr, ti, h, :], v_all[pr, ti, h * 128:(h + 1) * 128], True, True, [bks[h], bv_[ti]], [bdsb])
                    for g in range(2):
                        dsb, bdsb = dsg[g]
                        for i_, h in enumerate(range(g * 4, g * 4 + 4)):
                            k.stt("dve", st32[:, h, :], st32[:, h, :], Elast[:, h, ch:ch + 1], dsb[:, i_, :], ALU.mult, ALU.add, [bst32[h], bEl[h], bdsb], [bst32[h]])
                            k.copy("pool" if i_ % 2 == 0 else "act", stb[:, h, :], st32[:, h, :], [bst32[h]], [bstb[h]])
                for g in range(2):
                    pob, bpob = pog[g]
                    for i_, h in enumerate(range(g * 4, g * 4 + 4)):
                        k.copy("dve", oT3[:, h, :], pob[:, i_, :], [bpob], [boT])
                if DBG.get("stage", 9) < 5:
                    continue
                k.act(sq, oT, AF.Square, [boT], [bsq])
                for half in range(2):
                    k.mm(pbs[half][0], ones_bf, sq[:, half * 512:(half + 1) * 512], True, True, [b_1b, bsq], [pbs[half][1]])
                k.act(lnr, pbig, AF.Ln, [pbs[0][1], pbs[1][1]], [blnr], scale=1.0 / 128, bias=EPS)
                k.act(lnr, lnr, AF.Exp, [blnr], [blnr], scale=-0.5)
                k.stt("dve", lnr, oT, gcol[:, 0:1], lnr, ALU.mult, ALU.mult, [boT, b_gc, blnr], [blnr])
                k.tt("pool", on.rearrange("p (h t) -> p h t", t=128), lnr.rearrange("p (h t) -> p h t", t=128), gs_all[:, :, cs], ALU.mult, [blnr] + bgs, [bon])
                on3 = on.rearrange("p (h t) -> p h t", t=128)
                xres, bxres = xr.next()
                k.dma("sp", xres, x_src[t * 128:(t + 1) * 128, :], [bsrc], [bxres])
                ypair = []
                for half in range(2):
                    yp, byp = pbs[half]
                    for h in range(8):
                        k.mm(yp, on3[:, h, :], w_out_sb[:, h, half * 512:(half + 1) * 512], h == 0, h == 7, [bon, b_wout], [byp])
                    ypair.append((yp, byp))
                residual_out(ypair, xres, bxres, g1bc, b_g1, tmp, btmp, xnew, bxnew, x_dst, bdst, t, ss_out, b_ssout)

    def phase_ffn(l, x_src, bsrc, x_dst, bdst, ss_in, b_ssin, ss_out, b_ssout, vsc, vsh, vg):
        phase_reset()
        calc_rstd(ss_in, b_ssin)
        wup, b_wup = T([KC, 2 * DFF], BF16)
        wdn, b_wdn = T([NJ, D], BF16)
        load_w(wup, b_wup, w_up[l], KC)
        load_w(wdn, b_wdn, w_down[l], NJ)
        gbc, b_gbc = T([D], F32)
        g_bcast(gbc, b_gbc, vg)
        xr = Ring([T([D], F32) for _ in range(2)])
        hnr = Ring([T([D], BF16) for _ in range(2)])
        hTr = Ring([T([KC, TB + 2], BF16) for _ in range(2)])
        t0r = Ring([T([TB], F32) for _ in range(3)])
        ur = Ring([T([TB + 2], F32) for _ in range(3)])
        sgr = Ring([T([TB], F32) for _ in range(2)])
        actr = Ring([T([NJ, TB], BF16) for _ in range(2)])
        tmp, btmp = T([D], F32)
        xnew, bxnew = T([D], F32)
        tp, btp = BV(0, [KC, 128], BF16)
        pur = Ring([BV(1 + i, [TB + 2], F32) for i in range(4)])
        pbs = [BV(5, [512], F32), BV(6, [512], F32)]
        def do_hT(b, prev):
            hT, bhT = hTr.next()
            if prev is None:
                k.memset("pool", hT[:, :, 0:2], 0.0, [], [bhT])
            else:
                k.copy("pool", hT[:, :, 0:2], prev[0][:, :, TB:TB + 2], [prev[1]], [bhT])
            for ti in range(TPB):
                t = b * TPB + ti
                xt, bxt = xr.next()
                hn, bhn = hnr.next()
                k.dma("sp", xt, x_src[t * 128:(t + 1) * 128, :], [bsrc], [bxt])
                make_hT(xt, bxt, t, hn, bhn, tp, btp, [(vsc, vsh, hT[:, :, 2 + ti * 128:2 + (ti + 1) * 128], bhT)], act_heavy=True)
            return hT, bhT

        def do_up(hT, bhT):
            aT, baT = actr.next()
            for j in range(NJ):
                res = []
                for which, cidx in ((0, j), (1, NJ + j)):
                    pu, bpu = pur.next()
                    for kc in range(KC):
                        k.mm(pu, wup[:, kc, cidx * 128:(cidx + 1) * 128], hT[:, kc, :], kc == 0, kc == KC - 1, [b_wup, bhT], [bpu])
                    t0, bt0 = t0r.next()
                    u, bu = ur.next()
                    k.act(u, pu, AF.Identity, [bpu], [bu])
                    k.ts("dve", t0, u[:, 2:2 + TB], convc[:, l, 88 + cidx:89 + cidx], convc[:, l, 132 + cidx:133 + cidx], ALU.mult, ALU.add, [bu, b_cv], [bt0])
                    k.stt("dve", t0, u[:, 1:1 + TB], convc[:, l, 44 + cidx:45 + cidx], t0, ALU.mult, ALU.add, [bu, b_cv, bt0], [bt0])
                    k.stt("dve", t0, u[:, 0:TB], convc[:, l, cidx:cidx + 1], t0, ALU.mult, ALU.add, [bu, b_cv, bt0], [bt0])
                    res.append((t0, bt0))
                sg, bsg = sgr.next()
                k.act(sg, res[0][0], AF.Silu, [res[0][1]], [bsg])
                k.tt("pool", aT[:, j, :], sg, res[1][0], ALU.mult, [bsg, res[1][1]], [baT])
            return aT, baT

        def do_down(b, aT, baT):
            for ti in range(TPB):
                t = b * TPB + ti
                xres, bxres = xr.next()
                k.dma("sp", xres, x_src[t * 128:(t + 1) * 128, :], [bsrc], [bxres])
                ypair = []
                for half in range(2):
                    yp, byp = pbs[half]
                    for j in range(NJ):
                        k.mm(yp, aT[:, j, ti * 128:(ti + 1) * 128], wdn[:, j, half * 512:(half + 1) * 512], j == 0, j == NJ - 1, [baT, b_wdn], [byp])
                    ypair.append((yp, byp))
                residual_out(ypair, xres, bxres, gbc, b_gbc, tmp, btmp, xnew, bxnew, x_dst, bdst, t, ss_out, b_ssout)

        nbk = DBG.get('fnb', NB)
        hts = do_hT(0, None)
        ats = do_up(*hts)
        for b in range(nbk):
            if b + 1 < nbk:
                hts_n = do_hT(b + 1, hts)
                ats_n = do_up(*hts_n)
            do_down(b, *ats)
            if b + 1 < nbk:
                hts, ats = hts_n, ats_n

    def phase_fox_a(x_src, bsrc, ss_in, b_ssin):
        phase_reset()
        calc_rstd(ss_in, b_ssin)
        kvw, b_kvw = T([KC, 2056], BF16)
        wq, b_wq = T([KC, 2048], BF16)
        load_w(kvw, b_kvw, kv_w, KC)
        load_w(wq, b_wq, b_w_q, KC)
        xr = Ring([T([D], F32) for _ in range(3)])
        hnr = Ring([T([D], BF16) for _ in range(2)])
        hkvr = Ring([T([KC, TB], BF16) for _ in range(2)])
        h1r = Ring([T([KC, TB], BF16) for _ in range(2)])
        kTbr = Ring([T([8, TB], BF16) for _ in range(2)])
        qTbr = Ring([T([8, TB], BF16) for _ in range(2)])
        sqr = Ring([T([TB], BF16) for _ in range(3)])
        lnrr = Ring([T([TB], F32) for _ in range(3)])
        vbr = Ring([T([D], BF16) for _ in range(2)])
        gtr = Ring([T([D], F32) for _ in range(2)])
        ger = Ring([T([512], F32) for _ in range(2)])
        lfr = Ring([T([8], F32) for _ in range(2)])
        run, brun = T([8], F32)
        tp, btp = BV(0, [KC, 128], BF16)
        pa = Ring([BV(1, [TB], F32), BV(2, [TB], F32), BV(5, [TB], F32), BV(6, [TB], F32)])
        pssr = Ring([BV(3, [TB], F32), BV(4, [TB], F32)])
        pb = Ring([BV(5, [512], F32), BV(6, [512], F32)])
        pf8, bpf8 = BV(7, [8], F32, 0)
        pcum, bpcum = BV(7, [8], F32, 512)
        pcf, bpcf = BV(7, [8], F32, 1024)
        k.memset("pool", run, 0.0, [], [brun])
        def do_hT_a(b):
            hkv, bhkv = hkvr.next()
            h1, bh1 = h1r.next()
            for ti in range(TPB):
                t = b * TPB + ti
                xt, bxt = xr.next()
                hn, bhn = hnr.next()
                k.dma("sp", xt, x_src[t * 128:(t + 1) * 128, :], [bsrc], [bxt])
                tsl = slice(ti * 128, (ti + 1) * 128)
                make_hT(xt, bxt, t, hn, bhn, tp, btp, [(13, 12, hkv[:, :, tsl], bhkv), (7, 6, h1[:, :, tsl], bh1)])
            return hkv, bhkv, h1, bh1

        nxt_a = do_hT_a(0)
        for b in range(NB):
            hkv, bhkv, h1, bh1 = nxt_a
            kTb, bkTb = kTbr.next()
            qTb, bqTb = qTbr.next()
            jobs = [(kvw, b_kvw, hkv, bhkv, 1, kTb, bkTb, h) for h in range(8)] + [(wq, b_wq, h1, bh1, 2, qTb, bqTb, h) for h in range(8)]

            def proj(job):
                W, bW, hs, bhs, gi, dstb, bdstb, h = job
                pk, bpk = pa.next()
                for kc in range(KC):
                    k.mm(pk, W[:, kc, h * 128:(h + 1) * 128], hs[:, kc, :], kc == 0, kc == KC - 1, [bW, bhs], [bpk])
                return pk, bpk

            LAp = 2
            projd = {}
            for n in range(len(jobs) + LAp):
                if n < len(jobs):
                    projd[n] = proj(jobs[n])
                if n - LAp < 0:
                    continue
                job = jobs[n - LAp]
                W, bW, hs, bhs, gi, dstb, bdstb, h = job
                pk, bpk = projd.pop(n - LAp)
                sq, bsq = sqr.next()
                k.act(sq, pk, AF.Square, [bpk], [bsq])
                pss, bpss = pssr.next()
                k.mm(pss, ones_bf, sq, True, True, [b_1b, bsq], [bpss])
                lnr, blnr = lnrr.next()
                k.act(lnr, pss, AF.Ln, [bpss], [blnr], scale=1.0 / 128, bias=EPS)
                k.act(lnr, lnr, AF.Exp, [blnr], [blnr], scale=-0.5)
                k.stt("dve", dstb[:, h, :], pk, gcol[:, gi:gi + 1], lnr, ALU.mult, ALU.mult, [bpk, b_gc, blnr], [bdstb])
            k.dma("pool", kT_d[b], kTb.rearrange("p h t -> p (h t)"), [bkTb], [bkT])
            k.dma("pool", qT_d[b], qTb.rearrange("p h t -> p (h t)"), [bqTb], [bqT])
            if b + 1 < NB:
                nxt_a = do_hT_a(b + 1)
            for ti in range(TPB):
                t = b * TPB + ti
                tsl = slice(ti * 128, (ti + 1) * 128)
                vb, bvb = vbr.next()
                for half in range(2):
                    pv_, bpv = pb.next()
                    for kc in range(KC):
                        k.mm(pv_, hkv[:, kc, tsl], kvw[:, kc, 1024 + half * 512:1024 + (half + 1) * 512], kc == 0, kc == KC - 1, [bhkv, b_kvw], [bpv])
                    k.copy("dve", vb[:, half * 512:(half + 1) * 512], pv_, [bpv], [bvb])
                k.dma("pool", v_d[t * 128:(t + 1) * 128, :], vb, [bvb], [bv])
                gt, bgt = gtr.next()
                for half in range(2):
                    pg, bpg = pb.next()
                    for kc in range(KC):
                        k.mm(pg, h1[:, kc, tsl], wq[:, kc, 1024 + half * 512:1024 + (half + 1) * 512], kc == 0, kc == KC - 1, [bh1, b_wq], [bpg])
                    ge, bge = ger.next()
                    k.act(ge, pg, AF.Exp, [bpg], [bge], scale=-1.0)
                    k.act(ge, ge, AF.Ln, [bge], [bge], bias=1.0)
                    k.act(gt[:, half * 512:(half + 1) * 512], ge, AF.Exp, [bge], [bgt], scale=-1.0)
                k.dma("pool", gate_d[t * 128:(t + 1) * 128, :], gt, [bgt], [bgate])
                for kc in range(KC):
                    k.mm(pf8, hkv[:, kc, tsl], kvw[:, kc, 2048:2056], kc == 0, kc == KC - 1, [bhkv, b_kvw], [bpf8])
                lf, blf = lfr.next()
                k.tt("dve", lf, pf8, bfbc, ALU.add, [bpf8, b_bf], [blf])
                k.act(lf, lf, AF.Exp, [blf], [blf], scale=-1.0)
                k.act(lf, lf, AF.Ln, [blf], [blf], bias=1.0)
                k.mm(pcum, tri_f, lf, True, False, [b_tri, blf], [bpcum])
                k.mm(pcum, ones_f, run, False, True, [b_1f, brun], [bpcum])
                k.copy("dve", ncum_all[:, t, :], pcum, [bpcum], [b_nc])
                k.tt("dve", run, run, lf, ALU.add, [brun, blf], [brun])
                k.mm(pcf, e0_f, ncum_all[:, t, :], True, True, [b_e0, b_nc], [bpcf])
                k.copy("dve", cfirst_all[:, t, :], pcf, [bpcf], [b_cf])

    def phase_fox_bc(x_src, bsrc, x_dst, bdst, ss_out, b_ssout):
        phase_reset()
        wo, b_wo = T([KC, D], BF16)
        load_w(wo, b_wo, b_w_out, KC)
        gbc, b_gbc = T([D], F32)
        g_bcast(gbc, b_gbc, 8)
        o_all = A.alloc([NT, D], BF16)
        bo = [Buf() for _ in range(NT)]
        kThr = Ring([T([S], BF16) for _ in range(2)])
        qThr = Ring([T([S], BF16) for _ in range(2)])
        vhr = Ring([T([NT, 132], BF16) for _ in range(2)])
        ghr = Ring([T([NT, 128], F32) for _ in range(2)])
        bir = Ring([T([NT, NT], F32) for _ in range(2)])
        ptr_ = Ring([T([256], BF16) for _ in range(4)])
        scr2 = Ring([T([NT // 2], F32) for _ in range(2)])
        cmbr = Ring([T([132], F32) for _ in range(2)])
        recr = Ring([T([1], F32) for _ in range(2)])
        xr = Ring([T([D], F32) for _ in range(2)])
        oTr = Ring([T([KC, 128], BF16) for _ in range(2)])
        tmp, btmp = T([D], F32)
        xnew, bxnew = T([D], F32)
        sr = Ring([BV(i, [256], F32) for i in range(3)])
        por = Ring([BV(3, [3, 132], F32), BV(4, [3, 132], F32)])
        tp, btp = BV(5, [KC, 128], BF16)
        pbs = [BV(6, [512], F32), BV(7, [512], F32)]
        for (vh, bvh) in vhr.items:
            k.memset("pool", vh[:, :, 128:129], 1.0, [], [bvh])
        vdv = v_d.rearrange("(j p) d -> p j d", p=128)
        gdv = gate_d.rearrange("(j p) d -> p j d", p=128)
        for h in range(8):
            kTh, bkTh = kThr.next()
            qTh, bqTh = qThr.next()
            vh, bvh = vhr.next()
            gh, bgh = ghr.next()
            bias, bbias = bir.next()
            k.dma("sp", kTh.rearrange("p (b t) -> p b t", t=TB), kT_d.rearrange("b d (h t) -> d b h t", h=8)[:, :, h, :], [bkT], [bkTh])
            k.dma("sp", qTh.rearrange("p (b t) -> p b t", t=TB), qT_d.rearrange("b d (h t) -> d b h t", h=8)[:, :, h, :], [bqT], [bqTh])
            k.dma("sp", vh[:, :, 0:128], vdv[:, :, h * 128:(h + 1) * 128], [bv], [bvh])
            k.dma("sp", gh, gdv[:, :, h * 128:(h + 1) * 128], [bgate], [bgh])
            for j in range(NT):
                k.ts("dve", bias[:, j, :], cfirst_all[:, :, h], -1.0, ncum_all[:, j, h:h + 1], ALU.mult, ALU.add, [b_cf, b_nc], [bbias])
            scol, bscol = scr2.next()
            cfv = cfirst_all[:, :, h].rearrange("p (m two) -> p m two", two=2)
            k.tt("dve", scol, cfv[:, :, 0], cfv[:, :, 1], ALU.subtract, [b_cf], [bscol])
            k.act(scol, scol, AF.Exp, [bscol], [bscol])
            jobs = []
            for m in range(NT // 2):
                Q0, Q1 = 2 * m, 2 * m + 1
                for jj in range(2 * m + 1):
                    jobs.append((Q0 * 128, 256, jj, Q0, jj == 2 * m, [(0, 0, jj == 0, jj == 2 * m), (1, 128, jj == 0, jj == 2 * m)], None))
                jobs.append((Q1 * 128, 128, Q1, Q1, True, [(2, 0, True, True)], m))
            LA = 2
            sbuf_of = {}
            acc_of = {}
            for n in range(len(jobs) + LA):
                if n < len(jobs):
                    q0, qw, jj, Qb, msk, tg, fin = jobs[n]
                    s_, bs = sr.next()
                    sbuf_of[n] = (s_, bs)
                    k.mm(s_[:, 0:qw], kTh[:, jj * 128:(jj + 1) * 128], qTh[:, q0:q0 + qw], True, True, [bkTh, bqTh], [bs])
                mi = n - LA
                if mi < 0:
                    continue
                q0, qw, jj, Qb, msk, tg, fin = jobs[mi]
                pm = q0 // 256
                if pm not in acc_of:
                    acc_of[pm] = por.next()
                accb, baccb = acc_of[pm]
                s_, bs = sbuf_of.pop(mi)
                pt, bpt = ptr_.next()
                k.act(pt[:, 0:qw], s_[:, 0:qw], AF.Exp, [bs, bbias], [bpt], bias=bias[:, jj, Qb:Qb + 1])
                if msk:
                    k.tt("pool", pt[:, 0:128], pt[:, 0:128], maskc_bf, ALU.mult, [bpt, b_mc], [bpt])
                for (ai, c0, st_, sp_) in tg:
                    k.mm(accb[:, ai, 0:129], pt[:, c0:c0 + 128], vh[:, jj, 0:129], st_, sp_, [bpt, bvh], [baccb])
                if fin is not None:
                    Q0, Q1 = 2 * fin, 2 * fin + 1
                    rec, brec = recr.next()
                    k.recip(rec, accb[:, 0, 128:129], [baccb], [brec])
                    k.stt("dve", o_all[:, Q0, h * 128:(h + 1) * 128], accb[:, 0, 0:128], rec, gh[:, Q0, :], ALU.mult, ALU.mult, [baccb, brec, bgh], [bo[Q0]])
                    cmb, bcmb = cmbr.next()
                    k.copy("dve", cmb[:, 0:129], accb[:, 2, 0:129], [baccb], [bcmb])
                    k.stt("dve", cmb[:, 0:129], accb[:, 1, 0:129], scol[:, fin:fin + 1], cmb[:, 0:129], ALU.mult, ALU.add, [baccb, bscol, bcmb], [bcmb])
                    rec, brec = recr.next()
                    k.recip(rec, cmb[:, 128:129], [bcmb], [brec])
                    k.stt("dve", o_all[:, Q1, h * 128:(h + 1) * 128], cmb[:, 0:128], rec, gh[:, Q1, :], ALU.mult, ALU.mult, [bcmb, brec, bgh], [bo[Q1]])
                    del acc_of[pm]
        for t in range(NT):
            for kc in range(KC):
                k.tr(tp[:, kc, :], o_all[:, t, kc * 128:(kc + 1) * 128], ident_bf, [bo[t], b_idb], [btp])
            oT, boT = oTr.next()
            k.copy("dve", oT.rearrange("p a b -> p (a b)"), tp.rearrange("p a b -> p (a b)"), [btp], [boT])
            xres, bxres = xr.next()
            k.dma("sp", xres, x_src[t * 128:(t + 1) * 128, :], [bsrc], [bxres])
            ypair = []
            for half in range(2):
                yp, byp = pbs[half]
                for kc in range(KC):
                    k.mm(yp, oT[:, kc, :], wo[:, kc, half * 512:(half + 1) * 512], kc == 0, kc == KC - 1, [boT, b_wo], [byp])
                ypair.append((yp, byp))
            residual_out(ypair, xres, bxres, gbc, b_gbc, tmp, btmp, xnew, bxnew, x_dst, bdst, t, ss_out, b_ssout)

    bxin = Buf("xin")
    calc = None
    if upto >= 1:
        phase_hgrn(x_in, bxin, x1_d, bx1, ss_a, b_ssa, ss_b, b_ssb)
    if upto >= 2:
        phase_ffn(0, x1_d, bx1, x2_d, bx2, ss_b, b_ssb, ss_a, b_ssa, 4, 3, 5)
    if upto >= 3:
        phase_fox_a(x2_d, bx2, ss_a, b_ssa)
    if upto >= 4:
        phase_fox_bc(x2_d, bx2, x3_d, bx3, ss_b, b_ssb)
    if upto >= 5:
        phase_ffn(1, x3_d, bx3, out_d, bout, ss_b, b_ssb, ss_a, b_ssa, 10, 9, 11)

    k.barrier(scr)
    fin_d = nc.dram_tensor("fin_d", [128, 1], F32, kind="Internal").ap()
    k.dma("sp", fin_d, scr, [], [Buf("fin")])
    P.finalize()
    sems = {e: [st.enter_context(nc.semaphore(f"s_{e}{i}")) for i in range(P.nepoch[e])] for e in CENG}
    dsems = [st.enter_context(nc.semaphore(f"d{i}")) for i in range(P.NDMA_SEM)]
    block = st.enter_context(nc.Block())
    P.emit(block, sems, dsems)
    st.close()
    return nc


_NC_CACHE = {}


def _in_maps(inputs):
    f = lambda a: np.ascontiguousarray(np.asarray(a, dtype=np.float32))
    x = f(inputs["x"])
    c = f(inputs["c"])
    shared = dict(
        ada_w=f(inputs["ada_w"]), ada_b=f(inputs["ada_b"]),
        a_w_in=f(inputs["a_w_in"]).reshape(D, 4096),
        a_lb_logits=f(inputs["a_lb_logits"]).reshape(16, 128),
        a_norm_g=f(inputs["a_norm_g"]).reshape(1, 128),
        a_w_out=f(inputs["a_w_out"]).reshape(D, D),
        kv_ada_w=f(inputs["kv_ada_w"]), kv_ada_b=f(inputs["kv_ada_b"]).reshape(1, 2 * D),
        kv_w=f(inputs["kv_w"]), kv_b_f=f(inputs["kv_b_f"]).reshape(1, 8),
        k_norm_g=f(inputs["k_norm_g"]).reshape(1, 128),
        b_w_q=f(inputs["b_w_q"]).reshape(D, 2 * D),
        q_norm_g=f(inputs["q_norm_g"]).reshape(1, 128),
        b_w_out=f(inputs["b_w_out"]).reshape(D, D),
        ffn_w_up=f(inputs["ffn_w_up"]),
        ffn_conv_w=f(inputs["ffn_conv_w"]).reshape(2, 132, 128),
        ffn_conv_b=f(inputs["ffn_conv_b"]).reshape(2, 44, 128),
        ffn_w_down=f(inputs["ffn_w_down"]),
    )
    maps = []
    for b in range(8):
        m = dict(shared)
        m["x"] = x[b]
        m["c"] = c[b].reshape(8, 128)
        maps.append(m)
    return maps


def kernel(**inputs):
    if "nc" not in _NC_CACHE:
        _NC_CACHE["nc"] = build_nc()
    nc = _NC_CACHE["nc"]
    res = run_bass_kernel_spmd(nc, _in_maps(inputs), core_ids=list(range(8)))
    return np.stack([np.asarray(r["out"], dtype=np.float32) for r in res.results], axis=0)
```

```python
import numpy as np
import concourse.bass as bass
import concourse.mybir as mybir
from concourse.bass_utils import run_bass_kernel_spmd

F32 = mybir.dt.float32
BF16 = mybir.dt.bfloat16
U8 = mybir.dt.uint8
AF = mybir.ActivationFunctionType
ALU = mybir.AluOpType
AX = mybir.AxisListType
DSZ = {F32: 4, BF16: 2, U8: 1}

CENG = ("pe", "act", "dve", "pool")
EPOCH = 12000
STRICT = False


class Buf:
    __slots__ = ("name", "w", "r")

    def __init__(self, name=""):
        self.name = name
        self.w = None
        self.r = {}


class Op:
    __slots__ = ("id", "eng", "fn", "deps", "dma", "seq", "sig", "cnt", "need", "clock", "dsem", "dval", "inc")


class Prog:
    def __init__(self, nc):
        self.nc = nc
        self.ops = []
        self.ndma = 0
        self.dma_last = {}
        self.NDMA_SEM = 40
        self.NHW = 24
        self.nsw = 0

    def op(self, eng, fn, reads=(), writes=(), dma=False):
        o = Op()
        o.id = len(self.ops)
        o.eng = eng
        o.fn = fn
        o.dma = dma
        o.sig = False
        o.cnt = None
        deps = {}
        for b in reads:
            if b.w is not None:
                deps[b.w] = True
        for b in writes:
            if b.w is not None:
                deps.setdefault(b.w, False)
            for r in b.r.values():
                for rid in r:
                    deps.setdefault(rid, False)
        if dma:
            if eng == "pool":
                slot = self.NHW + self.nsw % (self.NDMA_SEM - self.NHW)
                self.nsw += 1
            else:
                slot = self.ndma % self.NHW
                self.ndma += 1
            prev = self.dma_last.get(slot)
            if prev is not None:
                deps.setdefault(prev.id, False)
                o.dval = prev.dval + 16
            else:
                o.dval = 16
            o.dsem = slot
            self.dma_last[slot] = o
        deps.pop(o.id, None)
        o.deps = deps
        for b in reads:
            if dma:
                b.r.setdefault("dma", []).append(o.id)
            else:
                b.r[eng] = [o.id]
        for b in writes:
            b.w = o.id
            b.r = {}
        self.ops.append(o)
        return o

    def finalize(self):
        ops = self.ops
        seqc = {e: 0 for e in CENG}
        known = {e: {c: 0 for c in CENG} for e in CENG + ("sp",)}
        kdma = {e: set() for e in CENG + ("sp",)}
        for o in ops:
            A = o.eng
            if not o.dma:
                seqc[A] += 1
                o.seq = seqc[A]
            else:
                o.seq = 0
            kn = known[A]
            need = []
            dl = sorted(o.deps.items(), key=lambda kv: -kv[0])
            for xid, raw in dl:
                X = ops[xid]
                if X.dma:
                    if xid in kdma[A]:
                        continue
                    need.append(xid)
                    kdma[A].add(xid)
                    for c in CENG:
                        if X.clock[c] > kn[c]:
                            kn[c] = X.clock[c]
                    continue
                E = X.eng
                if X.seq <= kn[E]:
                    continue
                if (not o.dma) and E == A:
                    if A == "pe" or not (raw or STRICT):
                        continue
                need.append(xid)
                X.sig = True
                kn[E] = X.seq
                for c in CENG:
                    if X.clock[c] > kn[c]:
                        kn[c] = X.clock[c]
            o.need = need
            ck = dict(kn)
            if len(kdma[A]) > 512:
                kdma[A] = set(sorted(kdma[A])[-256:])
            o.clock = ck
        cnt = {e: 0 for e in CENG}
        for o in ops:
            if (not o.dma) and o.sig:
                cnt[o.eng] += 1
                o.cnt = cnt[o.eng]
        self.nepoch = {e: cnt[e] // EPOCH + 1 for e in CENG}

    def emit(self, block, sems, dsems):
        ops = self.ops

        def semval(X):
            if X.dma:
                return dsems[X.dsem], X.dval
            k = (X.cnt - 1) // EPOCH
            return sems[X.eng][k], (X.cnt - 1) % EPOCH + 1

        def run(engname):
            def body(e):
                for o in ops:
                    if o.eng != engname:
                        continue
                    for xid in o.need:
                        s, v = semval(ops[xid])
                        e.wait_ge(s, v)
                    ins = o.fn(e)
                    if o.dma:
                        ins.then_inc(dsems[o.dsem], 16)
                    elif o.sig:
                        s, _ = semval(o)
                        ins.then_inc(s, 1)
                if engname == "sp":
                    for slot, o in self.dma_last.items():
                        e.wait_ge(dsems[slot], o.dval)
            return body

        block.tensor(run("pe"))
        block.scalar(run("act"))
        block.vector(run("dve"))
        block.gpsimd(run("pool"))
        block.sync(run("sp"))


class Arena:
    def __init__(self, t, size, part=128):
        self.t = t
        self.size = size
        self.off = 0

    def alloc(self, free_shape, dtype, align=64):
        n = int(np.prod(free_shape))
        nb = n * DSZ[dtype]
        off = (self.off + align - 1) // align * align
        assert off + nb <= self.size, f"arena overflow {off + nb} > {self.size}"
        self.off = off + nb
        ap = self.t[:, off:off + nb].bitcast(dtype)
        if len(free_shape) == 2:
            ap = ap.rearrange("p (a b) -> p a b", a=free_shape[0], b=free_shape[1])
        elif len(free_shape) == 3:
            ap = ap.rearrange("p (a b c) -> p a b c", a=free_shape[0], b=free_shape[1], c=free_shape[2])
        return ap

    def mark(self):
        return self.off

    def reset(self, m):
        self.off = m


S = 4096
D = 1024
NT = 32
KC = 8
TB = 256
NB = S // TB
TPB = TB // 128
DFF = 2816
NJ = 22
EPS = 1e-6
ARENA = 212480
DBG = {}


class Ring:
    def __init__(self, items):
        self.items = items
        self.i = 0

    def next(self):
        it = self.items[self.i % len(self.items)]
        self.i += 1
        return it


class K:
    def __init__(self, nc):
        self.nc = nc
        self.P = Prog(nc)
        self.Y = Buf("phase")

    def _op(self, eng, fn, reads, writes, dma=False):
        return self.P.op(eng, fn, reads=list(reads) + [self.Y], writes=list(writes), dma=dma)

    def barrier(self, scr):
        self.P.op("dve", lambda e: e.memset(scr, 0.0), reads=[], writes=[self.Y])

    def mm(self, out, lhsT, rhs, start, stop, reads, writes):
        return self._op("pe", lambda e: e.matmul(out, lhsT=lhsT, rhs=rhs, start=start, stop=stop), reads, writes)

    def tr(self, out, in_, ident, reads, writes):
        return self._op("pe", lambda e: e.transpose(out=out, in_=in_, identity=ident), reads, writes)

    def act(self, out, in_, func, reads, writes, scale=1.0, bias=0.0, accum=None):
        if accum is None:
            return self._op("act", lambda e: e.activation(out=out, in_=in_, func=func, bias=bias, scale=scale), reads, writes)
        return self._op("act", lambda e: e.activation(out=out, in_=in_, func=func, bias=bias, scale=scale, accum_out=accum), reads, writes)

    def tt(self, eng, out, in0, in1, op, reads, writes):
        return self._op(eng, lambda e: e.tensor_tensor(out=out, in0=in0, in1=in1, op=op), reads, writes)

    def ts(self, eng, out, in0, s1, s2, op0, op1, reads, writes):
        if s2 is None:
            return self._op(eng, lambda e: e.tensor_scalar(out=out, in0=in0, scalar1=s1, scalar2=None, op0=op0), reads, writes)
        return self._op(eng, lambda e: e.tensor_scalar(out=out, in0=in0, scalar1=s1, scalar2=s2, op0=op0, op1=op1), reads, writes)

    def stt(self, eng, out, in0, scalar, in1, op0, op1, reads, writes):
        return self._op(eng, lambda e: e.scalar_tensor_tensor(out=out, in0=in0, scalar=scalar, in1=in1, op0=op0, op1=op1), reads, writes)

    def copy(self, eng, out, in_, reads, writes):
        if eng == "act":
            return self._op("act", lambda e: e.activation(out=out, in_=in_, func=AF.Identity), reads, writes)
        return self._op(eng, lambda e: e.tensor_copy(out=out, in_=in_), reads, writes)

    def recip(self, out, in_, reads, writes):
        return self._op("dve", lambda e: e.reciprocal(out=out, in_=in_), reads, writes)

    def memset(self, eng, ap, val, reads, writes):
        return self._op(eng, lambda e: e.memset(ap, val), reads, writes)

    def scan(self, out, d0, d1, reads, writes):
        return self._op("dve", lambda e: e.tensor_tensor_scan(out=out, data0=d0, data1=d1, initial=0.0, op0=ALU.mult, op1=ALU.add), reads, writes)

    def asel(self, out, in_, pattern, cmp, fill, base, cm, reads, writes):
        return self._op("pool", lambda e: e.affine_select(out=out, in_=in_, pattern=pattern, compare_op=cmp, fill=fill, base=base, channel_multiplier=cm), reads, writes)

    def dma(self, q, out, in_, reads, writes):
        return self._op(q, lambda e: e.dma_start(out=out, in_=in_), reads, writes, dma=True)


def build_nc(upto=99, debug=False):
    nc = bass.Bass("TRN2", target_bir_lowering=False)
    k = K(nc)
    P = k.P

    def din(name, shape):
        return nc.dram_tensor(name, shape, F32, kind="ExternalInput").ap()

    x_in = din("x", [S, D])
    c_in = din("c", [8, 128])
    ada_w = din("ada_w", [2, D, 6 * D])
    ada_b = din("ada_b", [2, 6 * D])
    a_w_in = din("a_w_in", [D, 4096])
    a_lb = din("a_lb_logits", [16, 128])
    a_ng = din("a_norm_g", [1, 128])
    a_w_out = din("a_w_out", [D, D])
    kv_ada_w = din("kv_ada_w", [D, 2 * D])
    kv_ada_b = din("kv_ada_b", [1, 2 * D])
    kv_w = din("kv_w", [D, 2056])
    kv_bf = din("kv_b_f", [1, 8])
    k_ng = din("k_norm_g", [1, 128])
    b_w_q = din("b_w_q", [D, 2 * D])
    q_ng = din("q_norm_g", [1, 128])
    b_w_out = din("b_w_out", [D, D])
    w_up = din("ffn_w_up", [2, D, 2 * DFF])
    conv_w = din("ffn_conv_w", [2, 132, 128])
    conv_b = din("ffn_conv_b", [2, 44, 128])
    w_down = din("ffn_w_down", [2, DFF, D])
    out_d = nc.dram_tensor("out", [S, D], F32, kind="ExternalOutput").ap()
    skind = "ExternalOutput" if debug else "Internal"
    modsD = nc.dram_tensor("modsD", [14, D], F32, kind=skind).ap()
    x1_d = nc.dram_tensor("x1", [S, D], F32, kind=skind).ap()
    x2_d = nc.dram_tensor("x2", [S, D], F32, kind=skind).ap()
    x3_d = nc.dram_tensor("x3", [S, D], F32, kind=skind).ap()
    kT_d = nc.dram_tensor("kT_d", [NB, 128, 8 * TB], BF16, kind="Internal").ap()
    qT_d = nc.dram_tensor("qT_d", [NB, 128, 8 * TB], BF16, kind="Internal").ap()
    v_d = nc.dram_tensor("v_d", [S, D], BF16, kind="Internal").ap()
    gate_d = nc.dram_tensor("gate_d", [S, D], F32, kind="Internal").ap()
    bx1, bx2, bx3, bmods, bkT, bqT, bv, bgate, bout = (Buf(n) for n in "x1 x2 x3 mods kT qT v gate out".split())

    import contextlib
    st = contextlib.ExitStack()
    arena_t = st.enter_context(nc.sbuf_tensor("arena", [128, ARENA], U8))
    ps_t = st.enter_context(nc.psum_tensor("psum", [128, 8 * 2048], U8))
    A = Arena(arena_t, ARENA)
    PS = Arena(ps_t, 8 * 2048)

    def T(shape, dt, name=""):
        return A.alloc(shape, dt), Buf(name)

    bankbuf = [Buf(f"bank{i}") for i in range(8)]

    def BV(bank, shape, dt, boff=0):
        n = int(np.prod(shape))
        nb = n * DSZ[dt]
        assert boff + nb <= 2048
        off = bank * 2048 + boff
        ap = ps_t[:, off:off + nb].bitcast(dt)
        if len(shape) == 2:
            ap = ap.rearrange("p (a b) -> p a b", a=shape[0], b=shape[1])
        return ap, bankbuf[bank]

    ident_f, b_idf = T([128], F32)
    ident_bf, b_idb = T([128], BF16)
    ones_f, b_1f = T([128], F32)
    ones_bf, b_1b = T([128], BF16)
    tri_f, b_tri = T([128], F32)
    e0_f, b_e0 = T([128], F32)
    mask2_f, b_m2 = T([128], F32)
    maskc_bf, b_mc = T([128], BF16)
    scanmsk, b_sm = T([TB], F32)
    modcol, b_mod = T([112], F32)
    ccol, b_cc = T([8], F32)
    cact, b_ca = T([8], F32)
    lbcol, b_lb = T([16], F32)
    omlb, b_omlb = T([8], F32)
    nomlb, b_nomlb = T([8], F32)
    gcol, b_gc = T([3], F32)
    convc, b_cv = T([2, 176], F32)
    ss_a, b_ssa = T([NT], F32)
    ss_b, b_ssb = T([NT], F32)
    rstd_all, b_rs = T([NT], F32)
    ncum_all, b_nc = T([NT, 8], F32)
    cfirst_all, b_cf = T([NT, 8], F32)
    bfbc, b_bf = T([8], F32)
    scr, b_scr = T([1], F32)
    junk, b_junk = T([D], BF16)
    pmark = A.mark()

    def phase_reset():
        A.reset(pmark)
        k.barrier(scr)

    def load_cols(dst, bdst, src, n, stg, bstg, pstg, bpstg, rd=()):
        k.dma("sp", stg[0:n, :], src, list(rd), [bstg])
        k.tr(pstg[:, 0:n], stg[0:n, :], ident_f[0:n, 0:n], [bstg, b_idf], [bpstg])
        k.copy("dve", dst, pstg[:, 0:n], [bpstg], [bdst])

    k.memset("pool", ident_f, 0.0, [], [b_idf])
    k.asel(ident_f, ident_f, [[-1, 128]], ALU.not_equal, 1.0, 0, 1, [b_idf], [b_idf])
    k.copy("dve", ident_bf, ident_f, [b_idf], [b_idb])
    k.memset("pool", ones_f, 1.0, [], [b_1f])
    k.memset("pool", ones_bf, 1.0, [], [b_1b])
    k.memset("pool", tri_f, 1.0, [], [b_tri])
    k.asel(tri_f, tri_f, [[1, 128]], ALU.is_ge, 0.0, 0, -1, [b_tri], [b_tri])
    k.copy("dve", maskc_bf, tri_f, [b_tri], [b_mc])
    k.copy("dve", mask2_f, tri_f, [b_tri], [b_m2])
    k.memset("dve", mask2_f[0:64, 64:128], 0.0, [b_m2], [b_m2])
    k.memset("pool", e0_f, 0.0, [], [b_e0])
    k.asel(e0_f, e0_f, [[0, 128]], ALU.not_equal, 1.0, 0, 1, [b_e0], [b_e0])
    k.memset("pool", scanmsk, 1.0, [], [b_sm])
    k.memset("pool", scanmsk.rearrange("p (c t) -> p c t", t=64)[:, :, 0:1], 0.0, [b_sm], [b_sm])
    k.memset("pool", ncum_all, 0.0, [], [b_nc])

    stg, b_stg = T([128], F32)
    pstg, b_pstg = BV(0, [128], F32)
    load_cols(ccol, b_cc, c_in, 8, stg, b_stg, pstg, b_pstg)
    load_cols(lbcol, b_lb, a_lb, 16, stg, b_stg, pstg, b_pstg)
    k.dma("sp", stg[0:1, :], a_ng, [], [b_stg])
    k.dma("sp", stg[1:2, :], k_ng, [], [b_stg])
    k.dma("sp", stg[2:3, :], q_ng, [], [b_stg])
    k.tr(pstg[:, 0:3], stg[0:3, :], ident_f[0:3, 0:3], [b_stg, b_idf], [b_pstg])
    k.copy("dve", gcol, pstg[:, 0:3], [b_pstg], [b_gc])
    for l in range(2):
        load_cols(convc[:, l, 0:128], b_cv, conv_w[l, 0:128, :], 128, stg, b_stg, pstg, b_pstg)
        load_cols(convc[:, l, 128:132], b_cv, conv_w[l, 128:132, :], 4, stg, b_stg, pstg, b_pstg)
        load_cols(convc[:, l, 132:176], b_cv, conv_b[l], 44, stg, b_stg, pstg, b_pstg)
    k.dma("sp", bfbc, kv_bf.partition_broadcast(128), [], [b_bf])
    k.tt("dve", lbcol[:, 0:8], lbcol[:, 8:16], lbcol[:, 0:8], ALU.subtract, [b_lb], [b_lb])
    k.act(lbcol[:, 0:8], lbcol[:, 0:8], AF.Exp, [b_lb], [b_lb])
    k.ts("dve", lbcol[:, 0:8], lbcol[:, 0:8], 1.0, None, ALU.add, None, [b_lb], [b_lb])
    k.recip(lbcol[:, 0:8], lbcol[:, 0:8], [b_lb], [b_lb])
    k.ts("dve", omlb, lbcol[:, 0:8], -1.0, 1.0, ALU.mult, ALU.add, [b_lb], [b_omlb])
    k.ts("dve", nomlb, lbcol[:, 0:8], 1.0, -1.0, ALU.mult, ALU.add, [b_lb], [b_nomlb])
    k.act(cact, ccol, AF.Silu, [b_cc], [b_ca])

    xr = Ring([T([D], F32) for _ in range(3)])

    def sumsq(xt, bxt, ssdst, bss, t):
        k.act(junk, xt, AF.Square, [bxt], [b_junk, bss], accum=ssdst[:, t:t + 1])

    for t in range(NT):
        xt, bxt = xr.next()
        k.dma("sp", xt, x_in[t * 128:(t + 1) * 128, :], [], [bxt])
        sumsq(xt, bxt, ss_a, b_ssa, t)

    wst = Ring([T([3072], BF16) for _ in range(4)])
    cact_bf, b_cab = T([8], BF16)
    k.copy("dve", cact_bf, cact, [b_ca], [b_cab])
    brow, b_brow = T([3072], F32)
    mrow, b_mrow = T([3072], F32)
    prow = [BV(1 + i, [512], F32) for i in range(6)]
    modflat = modsD.rearrange("(o v) n -> o (v n)", o=1)
    for (Wd_, bd_, row0, width) in ((ada_w[0], ada_b[0:1, :], 0, 6144), (ada_w[1], ada_b[1:2, :], 6, 6144), (kv_ada_w, kv_ada_b, 12, 2048)):
        for c0 in range(0, width, 3072):
            wd = min(3072, width - c0)
            nn = wd // 512
            for kc in range(KC):
                s_, bs_ = wst.next()
                k.dma("pool", s_[:, 0:wd], Wd_[kc * 128:(kc + 1) * 128, c0:c0 + wd], [], [bs_])
                for n in range(nn):
                    k.mm(prow[n][0][0:1, :], cact_bf[:, kc:kc + 1], s_[:, n * 512:(n + 1) * 512], kc == 0, kc == KC - 1, [b_cab, bs_], [prow[n][1]])
            k.dma("sp", brow[0:1, 0:wd], bd_[:, c0:c0 + wd], [], [b_brow])
            for n in range(nn):
                k.tt("dve", mrow[0:1, n * 512:(n + 1) * 512], prow[n][0][0:1, :], brow[0:1, n * 512:(n + 1) * 512], ALU.add, [prow[n][1], b_brow], [b_mrow])
            k.dma("sp", modflat[:, row0 * D + c0: row0 * D + c0 + wd], mrow[0:1, 0:wd], [b_mrow], [bmods])
    load_cols(modcol, b_mod, modsD.rearrange("v (c p) -> (v c) p", p=128), 112, stg, b_stg, pstg, b_pstg, rd=[bmods])
    k.ts("dve", gcol[:, 2:3], gcol[:, 2:3], 128.0 ** -0.5, None, ALU.mult, None, [b_gc], [b_gc])
    for v in (1, 4, 7, 10, 13):
        k.ts("dve", modcol[:, v * 8:(v + 1) * 8], modcol[:, v * 8:(v + 1) * 8], 1.0, None, ALU.add, None, [b_mod], [b_mod])

    def calc_rstd(ss, bss):
        k.act(rstd_all, ss, AF.Ln, [bss], [b_rs], scale=1.0 / D, bias=EPS)
        k.act(rstd_all, rstd_all, AF.Exp, [b_rs], [b_rs], scale=-0.5)

    def g_bcast(dst, bdst, v):
        k.dma("sp", dst, modsD[v:v + 1, :].partition_broadcast(128), [bmods], [bdst])

    def load_w(dst, bdst, src, nk, c0=0, c1=None):
        for kc in range(nk):
            if c1 is None:
                k.dma("pool", dst[:, kc, :], src[kc * 128:(kc + 1) * 128, :], [], [bdst])
            else:
                k.dma("pool", dst[:, kc, 0:c1 - c0], src[kc * 128:(kc + 1) * 128, c0:c1], [], [bdst])

    def make_hT(xt, bxt, t, hn, bhn, tp, btp, variants, act_heavy=False):
        if act_heavy:
            k.act(hn, xt, AF.Identity, [bxt, b_rs], [bhn], scale=rstd_all[:, t:t + 1])
        else:
            k.ts("dve", hn, xt, rstd_all[:, t:t + 1], None, ALU.mult, None, [bxt, b_rs], [bhn])
        for kc in range(KC):
            k.tr(tp[:, kc, :], hn[:, kc * 128:(kc + 1) * 128], ident_bf, [bhn, b_idb], [btp])
        for (vs, vh, dst, bdst) in variants:
            for kc in range(KC):
                sc = modcol[:, vs * 8 + kc: vs * 8 + kc + 1]
                sh = modcol[:, vh * 8 + kc: vh * 8 + kc + 1]
                if kc % 2 == 0 or (act_heavy and kc % 4 != 3):
                    k.act(dst[:, kc, :], tp[:, kc, :], AF.Identity, [btp, b_mod], [bdst], scale=sc, bias=sh)
                else:
                    k.ts("dve", dst[:, kc, :], tp[:, kc, :], sc, sh, ALU.mult, ALU.add, [btp, b_mod], [bdst])

    def residual_out(ypair, xres, bxres, gbc, b_gbc, tmp, btmp, xnew, bxnew, dst_d, bdst_d, t, ss_next, b_ssn):
        for half in range(2):
            yp, byp = ypair[half]
            k.tt("dve", tmp[:, half * 512:(half + 1) * 512], yp, gbc[:, half * 512:(half + 1) * 512], ALU.mult, [byp, b_gbc], [btmp])
        k.tt("pool", xnew, tmp, xres, ALU.add, [btmp, bxres], [bxnew])
        k.dma("pool", dst_d[t * 128:(t + 1) * 128, :], xnew, [bxnew], [bdst_d])
        sumsq(xnew, bxnew, ss_next, b_ssn, t)

    def phase_hgrn(x_src, bsrc, x_dst, bdst, ss_in, b_ssin, ss_out, b_ssout):
        phase_reset()
        calc_rstd(ss_in, b_ssin)
        w_in_sb, b_win = T([KC, 4096], BF16)
        w_out_sb, b_wout = T([KC, D], BF16)
        load_w(w_in_sb, b_win, a_w_in, KC)
        load_w(w_out_sb, b_wout, a_w_out, KC)
        g1bc, b_g1 = T([D], F32)
        g_bcast(g1bc, b_g1, 2)
        xr = Ring([T([D], F32) for _ in range(3)])
        hnr = Ring([T([D], BF16) for _ in range(2)])
        hTr = Ring([T([KC, TB], BF16) for _ in range(2)])
        tmpA = Ring([T([TB], F32) for _ in range(2)])
        tmpB = Ring([T([TB], F32) for _ in range(2)])
        tmpC = Ring([T([TB], F32) for _ in range(2)])
        kstr = Ring([T([TB], BF16) for _ in range(8)])
        E_all = A.alloc([8, TB], F32)
        kR_all = A.alloc([8, TB], BF16)
        qE_all = A.alloc([8, TB], BF16)
        gs_all = A.alloc([8, TB], F32)
        ks_all = A.alloc([TPB, 8, 128], BF16)
        v_all = A.alloc([TPB, D], BF16)
        Elast = A.alloc([8, TB // 64], F32)
        st32 = A.alloc([8, 128], F32)
        stb = A.alloc([8, 128], BF16)
        bE, bkR, bqE, bgs, bks, bEl, bst32, bstb = ([Buf() for _ in range(8)] for _ in range(8))
        bv_ = [Buf() for _ in range(TPB)]
        atr = Ring([T([4, 128], BF16) for _ in range(4)])
        oTr = Ring([T([D], F32) for _ in range(2)])
        sq, bsq = T([D], BF16)
        lnr, blnr = T([D], F32)
        on, bon = T([D], BF16)
        tmp, btmp = T([D], F32)
        xnew, bxnew = T([D], F32)
        tp, btp = BV(0, [KC, 128], BF16)
        pa = Ring([BV(1, [TB], F32), BV(2, [TB], F32)])
        pbs = [BV(3, [512], F32), BV(4, [512], F32)]
        pbig = ps_t[:, 3 * 2048:5 * 2048].bitcast(F32)
        pb = Ring(pbs)
        scg = [BV(5, [4, 128], F32), BV(1, [4, 128], F32)]
        pog = [BV(6, [4, 128], F32), BV(2, [4, 128], F32)]
        dsg = [BV(7, [4, 128], F32), BV(3, [4, 128], F32)]
        tpks = [BV(0, [TPB, 128], BF16), BV(5, [TPB, 128], BF16)]
        k.memset("pool", st32, 0.0, [], bst32)
        k.memset("pool", stb, 0.0, [], bstb)
        NCH = TB // 64
        def do_hT_h(b):
            hT, bhT = hTr.next()
            for ti in range(TPB):
                t = b * TPB + ti
                xt, bxt = xr.next()
                hn, bhn = hnr.next()
                k.dma("sp", xt, x_src[t * 128:(t + 1) * 128, :], [bsrc], [bxt])
                make_hT(xt, bxt, t, hn, bhn, tp, btp, [(1, 0, hT[:, :, ti * 128:(ti + 1) * 128], bhT)])
            return hT, bhT

        nbk_h = DBG.get("nb", NB)
        nxt_h = do_hT_h(0)
        for b in range(nbk_h):
            hT, bhT = nxt_h
            if DBG.get("stage", 9) < 1:
                continue
            ksts = []
            for h in range(DBG.get("nh", 8)):
                pf, bpf = pa.next()
                for kc in range(KC):
                    k.mm(pf, w_in_sb[:, kc, 1024 + h * 128:1024 + (h + 1) * 128], hT[:, kc, :], kc == 0, kc == KC - 1, [b_win, bhT], [bpf])
                ta, bta = tmpA.next()
                tb_, btb = tmpB.next()
                tc, btc = tmpC.next()
                k.act(ta, pf, AF.Exp, [bpf], [bta])
                if DBG.get("fsub", 9) < 1:
                    continue
                k.act(ta, ta, AF.Ln, [bta], [bta], bias=1.0)
                k.act(ta, ta, AF.Exp, [bta], [bta], scale=-1.0)
                k.act(tb_, ta, AF.Ln, [bta, b_nomlb], [btb], scale=nomlb[:, h:h + 1], bias=1.0)
                if DBG.get("fsub", 9) < 2:
                    continue
                k.scan(tc, scanmsk, tb_, [b_sm, btb], [btc])
                k.act(E_all[:, h, :], tc, AF.Exp, [btc], [bE[h]])
                k.act(tb_, tc, AF.Exp, [btc], [btb], scale=-1.0)
                k.stt("dve", kR_all[:, h, :], ta, omlb[:, h:h + 1], tb_, ALU.mult, ALU.mult, [bta, btb, b_omlb], [bkR[h]])
                if DBG.get("fsub", 9) < 3:
                    continue
                kst, bkst = kstr.next()
                Ev = E_all[:, h, :].rearrange("p (c t) -> p c t", t=64)
                k.tt("pool", kst.rearrange("p (c t) -> p c t", t=64), kR_all[:, h, :].rearrange("p (c t) -> p c t", t=64),
                     Ev[:, :, 63:64].to_broadcast([128, NCH, 64]), ALU.mult, [bkR[h], bE[h]], [bkst])
                k.copy("pool", Elast[:, h, :], Ev[:, :, 63], [bE[h]], [bEl[h]])
                ksts.append((kst, bkst))
            if DBG.get("stage", 9) < 2:
                continue
            for ti in range(TPB):
                for half in range(2):
                    pv_, bpv = pb.next()
                    for kc in range(KC):
                        k.mm(pv_, hT[:, kc, ti * 128:(ti + 1) * 128], w_in_sb[:, kc, 2048 + half * 512:2048 + (half + 1) * 512], kc == 0, kc == KC - 1, [bhT, b_win], [bpv])
                    k.copy("dve", v_all[:, ti, half * 512:(half + 1) * 512], pv_, [bpv], [bv_[ti]])
            for h in range(8):
                kst, bkst = ksts[h]
                tpk_, btpk_ = tpks[h % 2]
                for ti in range(TPB):
                    k.tr(tpk_[:, ti, :], kst[:, ti * 128:(ti + 1) * 128], ident_bf, [bkst, b_idb], [btpk_])
                for ti in range(TPB):
                    k.copy("dve", ks_all[:, ti, h, :], tpk_[:, ti, :], [btpk_], [bks[h]])
            for h in range(8):
                pq, bpq = pa.next()
                for kc in range(KC):
                    k.mm(pq, w_in_sb[:, kc, h * 128:(h + 1) * 128], hT[:, kc, :], kc == 0, kc == KC - 1, [b_win, bhT], [bpq])
                if DBG.get("ssub", 9) < 0:
                    continue
                ta, bta = tmpA.next()
                k.act(ta, pq, DBG.get("qf", AF.Silu), [bpq], [bta])
                if DBG.get("ssub", 9) < 1:
                    continue
                k.tt("pool", qE_all[:, h, :], ta, E_all[:, h, :], ALU.mult, [bta, bE[h]], [bqE[h]])
                if DBG.get("ssub", 9) < 2:
                    continue
                pg, bpg = pa.next()
                for kc in range(KC):
                    k.mm(pg, w_in_sb[:, kc, 3072 + h * 128:3072 + (h + 1) * 128], hT[:, kc, :], kc == 0, kc == KC - 1, [b_win, bhT], [bpg])
                k.act(gs_all[:, h, :], pg, AF.Silu, [bpg], [bgs[h]])
            if b + 1 < nbk_h:
                nxt_h = do_hT_h(b + 1)
            for ti in range(TPB):
                t = b * TPB + ti
                cs = slice(ti * 128, (ti + 1) * 128)
                oT, boT = oTr.next()
                oT3 = oT.rearrange("p (h t) -> p h t", t=128)
                atgs = [atr.next() for _ in range(2)]
                for g in range(2):
                    scb, bscb = scg[g]
                    for i_, h in enumerate(range(g * 4, g * 4 + 4)):
                        k.mm(scb[:, i_, :], kR_all[:, h, cs], qE_all[:, h, cs], True, True, [bkR[h], bqE[h]], [bscb])
                for g in range(2):
                    scb, bscb = scg[g]
                    atg, batg = atgs[g]
                    for i_ in range(4):
                        k.tt("dve", atg[:, i_, :], scb[:, i_, :], mask2_f, ALU.mult, [bscb, b_m2], [batg])
                for c in range(2):
                    cc = slice(c * 64, (c + 1) * 64)
                    pr = slice(c * 64, (c + 1) * 64)
                    ch = ti * 2 + c
                    for g in range(2):
                        atg, batg = atgs[g]
                        pob, bpob = pog[g]
                        dsb, bdsb = dsg[g]
                        for i_, h in enumerate(range(g * 4, g * 4 + 4)):
                            k.mm(pob[:, i_, cc], v_all[:, ti, h * 128:(h + 1) * 128], atg[:, i_, cc], True, False, [bv_[ti], batg], [bpob])
                            k.mm(pob[:, i_, cc], stb[:, h, :], qE_all[:, h, ti * 128 + c * 64: ti * 128 + (c + 1) * 64], False, True, [bstb[h], bqE[h]], [bpob])
                            k.mm(dsb[:, i_, :], ks_all[pr, ti, h, :], v_all[pr, ti, h * 128:(h + 1) * 128], True, True, [bks[h], bv_[ti]], [bdsb])
                    for g in range(2):
                        dsb, bdsb = dsg[g]
                        for i_, h in enumerate(range(g * 4, g * 4 + 4)):
                            k.stt("dve", st32[:, h, :], st32[:, h, :], Elast[:, h, ch:ch + 1], dsb[:, i_, :], ALU.mult, ALU.add, [bst32[h], bEl[h], bdsb], [bst32[h]])
                            k.copy("pool" if i_ % 2 == 0 else "act", stb[:, h, :], st32[:, h, :], [bst32[h]], [bstb[h]])
                for g in range(2):
                    pob, bpob = pog[g]
                    for i_, h in enumerate(range(g * 4, g * 4 + 4)):
                        k.copy("dve", oT3[:, h, :], pob[:, i_, :], [bpob], [boT])
                if DBG.get("stage", 9) < 5:
                    continue
                k.act(sq, oT, AF.Square, [boT], [bsq])
                for half in range(2):
                    k.mm(pbs[half][0], ones_bf, sq[:, half * 512:(half + 1) * 512], True, True, [b_1b, bsq], [pbs[half][1]])
                k.act(lnr, pbig, AF.Ln, [pbs[0][1], pbs[1][1]], [blnr], scale=1.0 / 128, bias=EPS)
                k.act(lnr, lnr, AF.Exp, [blnr], [blnr], scale=-0.5)
                k.stt("dve", lnr, oT, gcol[:, 0:1], lnr, ALU.mult, ALU.mult, [boT, b_gc, blnr], [blnr])
                k.tt("pool", on.rearrange("p (h t) -> p h t", t=128), lnr.rearrange("p (h t) -> p h t", t=128), gs_all[:, :, cs], ALU.mult, [blnr] + bgs, [bon])
                on3 = on.rearrange("p (h t) -> p h t", t=128)
                xres, bxres = xr.next()
                k.dma("sp", xres, x_src[t * 128:(t + 1) * 128, :], [bsrc], [bxres])
                ypair = []
                for half in range(2):
                    yp, byp = pbs[half]
                    for h in range(8):
                        k.mm(yp, on3[:, h, :], w_out_sb[:, h, half * 512:(half + 1) * 512], h == 0, h == 7, [bon, b_wout], [byp])
                    ypair.append((yp, byp))
                residual_out(ypair, xres, bxres, g1bc, b_g1, tmp, btmp, xnew, bxnew, x_dst, bdst, t, ss_out, b_ssout)

    def phase_ffn(l, x_src, bsrc, x_dst, bdst, ss_in, b_ssin, ss_out, b_ssout, vsc, vsh, vg):
        phase_reset()
        calc_rstd(ss_in, b_ssin)
        wup, b_wup = T([KC, 2 * DFF], BF16)
        wdn, b_wdn = T([NJ, D], BF16)
        load_w(wup, b_wup, w_up[l], KC)
        load_w(wdn, b_wdn, w_down[l], NJ)
        gbc, b_gbc = T([D], F32)
        g_bcast(gbc, b_gbc, vg)
        xr = Ring([T([D], F32) for _ in range(2)])
        hnr = Ring([T([D], BF16) for _ in range(2)])
        hTr = Ring([T([KC, TB + 2], BF16) for _ in range(2)])
        t0r = Ring([T([TB], F32) for _ in range(3)])
        ur = Ring([T([TB + 2], F32) for _ in range(3)])
        sgr = Ring([T([TB], F32) for _ in range(2)])
        actr = Ring([T([NJ, TB], BF16) for _ in range(2)])
        tmp, btmp = T([D], F32)
        xnew, bxnew = T([D], F32)
        tp, btp = BV(0, [KC, 128], BF16)
        pur = Ring([BV(1 + i, [TB + 2], F32) for i in range(4)])
        pbs = [BV(5, [512], F32), BV(6, [512], F32)]
        def do_hT(b, prev):
            hT, bhT = hTr.next()
            if prev is None:
                k.memset("pool", hT[:, :, 0:2], 0.0, [], [bhT])
            else:
                k.copy("pool", hT[:, :, 0:2], prev[0][:, :, TB:TB + 2], [prev[1]], [bhT])
            for ti in range(TPB):
                t = b * TPB + ti
                xt, bxt = xr.next()
                hn, bhn = hnr.next()
                k.dma("sp", xt, x_src[t * 128:(t + 1) * 128, :], [bsrc], [bxt])
                make_hT(xt, bxt, t, hn, bhn, tp, btp, [(vsc, vsh, hT[:, :, 2 + ti * 128:2 + (ti + 1) * 128], bhT)], act_heavy=True)
            return hT, bhT

        def do_up(hT, bhT):
            aT, baT = actr.next()
            for j in range(NJ):
                res = []
                for which, cidx in ((0, j), (1, NJ + j)):
                    pu, bpu = pur.next()
                    for kc in range(KC):
                        k.mm(pu, wup[:, kc, cidx * 128:(cidx + 1) * 128], hT[:, kc, :], kc == 0, kc == KC - 1, [b_wup, bhT], [bpu])
                    t0, bt0 = t0r.next()
                    u, bu = ur.next()
                    k.act(u, pu, AF.Identity, [bpu], [bu])
                    k.ts("dve", t0, u[:, 2:2 + TB], convc[:, l, 88 + cidx:89 + cidx], convc[:, l, 132 + cidx:133 + cidx], ALU.mult, ALU.add, [bu, b_cv], [bt0])
                    k.stt("dve", t0, u[:, 1:1 + TB], convc[:, l, 44 + cidx:45 + cidx], t0, ALU.mult, ALU.add, [bu, b_cv, bt0], [bt0])
                    k.stt("dve", t0, u[:, 0:TB], convc[:, l, cidx:cidx + 1], t0, ALU.mult, ALU.add, [bu, b_cv, bt0], [bt0])
                    res.append((t0, bt0))
                sg, bsg = sgr.next()
                k.act(sg, res[0][0], AF.Silu, [res[0][1]], [bsg])
                k.tt("pool", aT[:, j, :], sg, res[1][0], ALU.mult, [bsg, res[1][1]], [baT])
            return aT, baT

        def do_down(b, aT, baT):
            for ti in range(TPB):
                t = b * TPB + ti
                xres, bxres = xr.next()
                k.dma("sp", xres, x_src[t * 128:(t + 1) * 128, :], [bsrc], [bxres])
                ypair = []
                for half in range(2):
                    yp, byp = pbs[half]
                    for j in range(NJ):
                        k.mm(yp, aT[:, j, ti * 128:(ti + 1) * 128], wdn[:, j, half * 512:(half + 1) * 512], j == 0, j == NJ - 1, [baT, b_wdn], [byp])
                    ypair.append((yp, byp))
                residual_out(ypair, xres, bxres, gbc, b_gbc, tmp, btmp, xnew, bxnew, x_dst, bdst, t, ss_out, b_ssout)

        nbk = DBG.get('fnb', NB)
        hts = do_hT(0, None)
        ats = do_up(*hts)
        for b in range(nbk):
            if b + 1 < nbk:
                hts_n = do_hT(b + 1, hts)
                ats_n = do_up(*hts_n)
            do_down(b, *ats)
            if b + 1 < nbk:
                hts, ats = hts_n, ats_n

    def phase_fox_a(x_src, bsrc, ss_in, b_ssin):
        phase_reset()
        calc_rstd(ss_in, b_ssin)
        kvw, b_kvw = T([KC, 2056], BF16)
        wq, b_wq = T([KC, 2048], BF16)
        load_w(kvw, b_kvw, kv_w, KC)
        load_w(wq, b_wq, b_w_q, KC)
        xr = Ring([T([D], F32) for _ in range(3)])
        hnr = Ring([T([D], BF16) for _ in range(2)])
        hkvr = Ring([T([KC, TB], BF16) for _ in range(2)])
        h1r = Ring([T([KC, TB], BF16) for _ in range(2)])
        kTbr = Ring([T([8, TB], BF16) for _ in range(2)])
        qTbr = Ring([T([8, TB], BF16) for _ in range(2)])
        sqr = Ring([T([TB], BF16) for _ in range(3)])
        lnrr = Ring([T([TB], F32) for _ in range(3)])
        vbr = Ring([T([D], BF16) for _ in range(2)])
        gtr = Ring([T([D], F32) for _ in range(2)])
        ger = Ring([T([512], F32) for _ in range(2)])
        lfr = Ring([T([8], F32) for _ in range(2)])
        run, brun = T([8], F32)
        tp, btp = BV(0, [KC, 128], BF16)
        pa = Ring([BV(1, [TB], F32), BV(2, [TB], F32), BV(5, [TB], F32), BV(6, [TB], F32)])
        pssr = Ring([BV(3, [TB], F32), BV(4, [TB], F32)])
        pb = Ring([BV(5, [512], F32), BV(6, [512], F32)])
        pf8, bpf8 = BV(7, [8], F32, 0)
        pcum, bpcum = BV(7, [8], F32, 512)
        pcf, bpcf = BV(7, [8], F32, 1024)
        k.memset("pool", run, 0.0, [], [brun])
        def do_hT_a(b):
            hkv, bhkv = hkvr.next()
            h1, bh1 = h1r.next()
            for ti in range(TPB):
                t = b * TPB + ti
                xt, bxt = xr.next()
                hn, bhn = hnr.next()
                k.dma("sp", xt, x_src[t * 128:(t + 1) * 128, :], [bsrc], [bxt])
                tsl = slice(ti * 128, (ti + 1) * 128)
                make_hT(xt, bxt, t, hn, bhn, tp, btp, [(13, 12, hkv[:, :, tsl], bhkv), (7, 6, h1[:, :, tsl], bh1)])
            return hkv, bhkv, h1, bh1

        nxt_a = do_hT_a(0)
        for b in range(NB):
            hkv, bhkv, h1, bh1 = nxt_a
            kTb, bkTb = kTbr.next()
            qTb, bqTb = qTbr.next()
            jobs = [(kvw, b_kvw, hkv, bhkv, 1, kTb, bkTb, h) for h in range(8)] + [(wq, b_wq, h1, bh1, 2, qTb, bqTb, h) for h in range(8)]

            def proj(job):
                W, bW, hs, bhs, gi, dstb, bdstb, h = job
                pk, bpk = pa.next()
                for kc in range(KC):
                    k.mm(pk, W[:, kc, h * 128:(h + 1) * 128], hs[:, kc, :], kc == 0, kc == KC - 1, [bW, bhs], [bpk])
                return pk, bpk

            LAp = 2
            projd = {}
            for n in range(len(jobs) + LAp):
                if n < len(jobs):
                    projd[n] = proj(jobs[n])
                if n - LAp < 0:
                    continue
                job = jobs[n - LAp]
                W, bW, hs, bhs, gi, dstb, bdstb, h = job
                pk, bpk = projd.pop(n - LAp)
                sq, bsq = sqr.next()
                k.act(sq, pk, AF.Square, [bpk], [bsq])
                pss, bpss = pssr.next()
                k.mm(pss, ones_bf, sq, True, True, [b_1b, bsq], [bpss])
                lnr, blnr = lnrr.next()
                k.act(lnr, pss, AF.Ln, [bpss], [blnr], scale=1.0 / 128, bias=EPS)
                k.act(lnr, lnr, AF.Exp, [blnr], [blnr], scale=-0.5)
                k.stt("dve", dstb[:, h, :], pk, gcol[:, gi:gi + 1], lnr, ALU.mult, ALU.mult, [bpk, b_gc, blnr], [bdstb])
            k.dma("pool", kT_d[b], kTb.rearrange("p h t -> p (h t)"), [bkTb], [bkT])
            k.dma("pool", qT_d[b], qTb.rearrange("p h t -> p (h t)"), [bqTb], [bqT])
            if b + 1 < NB:
                nxt_a = do_hT_a(b + 1)
            for ti in range(TPB):
                t = b * TPB + ti
                tsl = slice(ti * 128, (ti + 1) * 128)
                vb, bvb = vbr.next()
                for half in range(2):
                    pv_, bpv = pb.next()
                    for kc in range(KC):
                        k.mm(pv_, hkv[:, kc, tsl], kvw[:, kc, 1024 + half * 512:1024 + (half + 1) * 512], kc == 0, kc == KC - 1, [bhkv, b_kvw], [bpv])
                    k.copy("dve", vb[:, half * 512:(half + 1) * 512], pv_, [bpv], [bvb])
                k.dma("pool", v_d[t * 128:(t + 1) * 128, :], vb, [bvb], [bv])
                gt, bgt = gtr.next()
                for half in range(2):
                    pg, bpg = pb.next()
                    for kc in range(KC):
                        k.mm(pg, h1[:, kc, tsl], wq[:, kc, 1024 + half * 512:1024 + (half + 1) * 512], kc == 0, kc == KC - 1, [bh1, b_wq], [bpg])
                    ge, bge = ger.next()
                    k.act(ge, pg, AF.Exp, [bpg], [bge], scale=-1.0)
                    k.act(ge, ge, AF.Ln, [bge], [bge], bias=1.0)
                    k.act(gt[:, half * 512:(half + 1) * 512], ge, AF.Exp, [bge], [bgt], scale=-1.0)
                k.dma("pool", gate_d[t * 128:(t + 1) * 128, :], gt, [bgt], [bgate])
                for kc in range(KC):
                    k.mm(pf8, hkv[:, kc, tsl], kvw[:, kc, 2048:2056], kc == 0, kc == KC - 1, [bhkv, b_kvw], [bpf8])
                lf, blf = lfr.next()
                k.tt("dve", lf, pf8, bfbc, ALU.add, [bpf8, b_bf], [blf])
                k.act(lf, lf, AF.Exp, [blf], [blf], scale=-1.0)
                k.act(lf, lf, AF.Ln, [blf], [blf], bias=1.0)
                k.mm(pcum, tri_f, lf, True, False, [b_tri, blf], [bpcum])
                k.mm(pcum, ones_f, run, False, True, [b_1f, brun], [bpcum])
                k.copy("dve", ncum_all[:, t, :], pcum, [bpcum], [b_nc])
                k.tt("dve", run, run, lf, ALU.add, [brun, blf], [brun])
                k.mm(pcf, e0_f, ncum_all[:, t, :], True, True, [b_e0, b_nc], [bpcf])
                k.copy("dve", cfirst_all[:, t, :], pcf, [bpcf], [b_cf])

    def phase_fox_bc(x_src, bsrc, x_dst, bdst, ss_out, b_ssout):
        phase_reset()
        wo, b_wo = T([KC, D], BF16)
        load_w(wo, b_wo, b_w_out, KC)
        gbc, b_gbc = T([D], F32)
        g_bcast(gbc, b_gbc, 8)
        o_all = A.alloc([NT, D], BF16)
        bo = [Buf() for _ in range(NT)]
        kThr = Ring([T([S], BF16) for _ in range(2)])
        qThr = Ring([T([S], BF16) for _ in range(2)])
        vhr = Ring([T([NT, 132], BF16) for _ in range(2)])
        ghr = Ring([T([NT, 128], F32) for _ in range(2)])
        bir = Ring([T([NT, NT], F32) for _ in range(2)])
        ptr_ = Ring([T([256], BF16) for _ in range(4)])
        scr2 = Ring([T([NT // 2], F32) for _ in range(2)])
        cmbr = Ring([T([132], F32) for _ in range(2)])
        recr = Ring([T([1], F32) for _ in range(2)])
        xr = Ring([T([D], F32) for _ in range(2)])
        oTr = Ring([T([KC, 128], BF16) for _ in range(2)])
        tmp, btmp = T([D], F32)
        xnew, bxnew = T([D], F32)
        sr = Ring([BV(i, [256], F32) for i in range(3)])
        por = Ring([BV(3, [3, 132], F32), BV(4, [3, 132], F32)])
        tp, btp = BV(5, [KC, 128], BF16)
        pbs = [BV(6, [512], F32), BV(7, [512], F32)]
        for (vh, bvh) in vhr.items:
            k.memset("pool", vh[:, :, 128:129], 1.0, [], [bvh])
        vdv = v_d.rearrange("(j p) d -> p j d", p=128)
        gdv = gate_d.rearrange("(j p) d -> p j d", p=128)
        for h in range(8):
            kTh, bkTh = kThr.next()
            qTh, bqTh = qThr.next()
            vh, bvh = vhr.next()
            gh, bgh = ghr.next()
            bias, bbias = bir.next()
            k.dma("sp", kTh.rearrange("p (b t) -> p b t", t=TB), kT_d.rearrange("b d (h t) -> d b h t", h=8)[:, :, h, :], [bkT], [bkTh])
            k.dma("sp", qTh.rearrange("p (b t) -> p b t", t=TB), qT_d.rearrange("b d (h t) -> d b h t", h=8)[:, :, h, :], [bqT], [bqTh])
            k.dma("sp", vh[:, :, 0:128], vdv[:, :, h * 128:(h + 1) * 128], [bv], [bvh])
            k.dma("sp", gh, gdv[:, :, h * 128:(h + 1) * 128], [bgate], [bgh])
            for j in range(NT):
                k.ts("dve", bias[:, j, :], cfirst_all[:, :, h], -1.0, ncum_all[:, j, h:h + 1], ALU.mult, ALU.add, [b_cf, b_nc], [bbias])
            scol, bscol = scr2.next()
            cfv = cfirst_all[:, :, h].rearrange("p (m two) -> p m two", two=2)
            k.tt("dve", scol, cfv[:, :, 0], cfv[:, :, 1], ALU.subtract, [b_cf], [bscol])
            k.act(scol, scol, AF.Exp, [bscol], [bscol])
            jobs = []
            for m in range(NT // 2):
                Q0, Q1 = 2 * m, 2 * m + 1
                for jj in range(2 * m + 1):
                    jobs.append((Q0 * 128, 256, jj, Q0, jj == 2 * m, [(0, 0, jj == 0, jj == 2 * m), (1, 128, jj == 0, jj == 2 * m)], None))
                jobs.append((Q1 * 128, 128, Q1, Q1, True, [(2, 0, True, True)], m))
            LA = 2
            sbuf_of = {}
            acc_of = {}
            for n in range(len(jobs) + LA):
                if n < len(jobs):
                    q0, qw, jj, Qb, msk, tg, fin = jobs[n]
                    s_, bs = sr.next()
                    sbuf_of[n] = (s_, bs)
                    k.mm(s_[:, 0:qw], kTh[:, jj * 128:(jj + 1) * 128], qTh[:, q0:q0 + qw], True, True, [bkTh, bqTh], [bs])
                mi = n - LA
                if mi < 0:
                    continue
                q0, qw, jj, Qb, msk, tg, fin = jobs[mi]
                pm = q0 // 256
                if pm not in acc_of:
                    acc_of[pm] = por.next()
                accb, baccb = acc_of[pm]
                s_, bs = sbuf_of.pop(mi)
                pt, bpt = ptr_.next()
                k.act(pt[:, 0:qw], s_[:, 0:qw], AF.Exp, [bs, bbias], [bpt], bias=bias[:, jj, Qb:Qb + 1])
                if msk:
                    k.tt("pool", pt[:, 0:128], pt[:, 0:128], maskc_bf, ALU.mult, [bpt, b_mc], [bpt])
                for (ai, c0, st_, sp_) in tg:
                    k.mm(accb[:, ai, 0:129], pt[:, c0:c0 + 128], vh[:, jj, 0:129], st_, sp_, [bpt, bvh], [baccb])
                if fin is not None:
                    Q0, Q1 = 2 * fin, 2 * fin + 1
                    rec, brec = recr.next()
                    k.recip(rec, accb[:, 0, 128:129], [baccb], [brec])
                    k.stt("dve", o_all[:, Q0, h * 128:(h + 1) * 128], accb[:, 0, 0:128], rec, gh[:, Q0, :], ALU.mult, ALU.mult, [baccb, brec, bgh], [bo[Q0]])
                    cmb, bcmb = cmbr.next()
                    k.copy("dve", cmb[:, 0:129], accb[:, 2, 0:129], [baccb], [bcmb])
                    k.stt("dve", cmb[:, 0:129], accb[:, 1, 0:129], scol[:, fin:fin + 1], cmb[:, 0:129], ALU.mult, ALU.add, [baccb, bscol, bcmb], [bcmb])
                    rec, brec = recr.next()
                    k.recip(rec, cmb[:, 128:129], [bcmb], [brec])
                    k.stt("dve", o_all[:, Q1, h * 128:(h + 1) * 128], cmb[:, 0:128], rec, gh[:, Q1, :], ALU.mult, ALU.mult, [bcmb, brec, bgh], [bo[Q1]])
                    del acc_of[pm]
        for t in range(NT):
            for kc in range(KC):
                k.tr(tp[:, kc, :], o_all[:, t, kc * 128:(kc + 1) * 128], ident_bf, [bo[t], b_idb], [btp])
            oT, boT = oTr.next()
            k.copy("dve", oT.rearrange("p a b -> p (a b)"), tp.rearrange("p a b -> p (a b)"), [btp], [boT])
            xres, bxres = xr.next()
            k.dma("sp", xres, x_src[t * 128:(t + 1) * 128, :], [bsrc], [bxres])
            ypair = []
            for half in range(2):
                yp, byp = pbs[half]
                for kc in range(KC):
                    k.mm(yp, oT[:, kc, :], wo[:, kc, half * 512:(half + 1) * 512], kc == 0, kc == KC - 1, [boT, b_wo], [byp])
                ypair.append((yp, byp))
            residual_out(ypair, xres, bxres, gbc, b_gbc, tmp, btmp, xnew, bxnew, x_dst, bdst, t, ss_out, b_ssout)

    bxin = Buf("xin")
    calc = None
    if upto >= 1:
        phase_hgrn(x_in, bxin, x1_d, bx1, ss_a, b_ssa, ss_b, b_ssb)
    if upto >= 2:
        phase_ffn(0, x1_d, bx1, x2_d, bx2, ss_b, b_ssb, ss_a, b_ssa, 4, 3, 5)
    if upto >= 3:
        phase_fox_a(x2_d, bx2, ss_a, b_ssa)
    if upto >= 4:
        phase_fox_bc(x2_d, bx2, x3_d, bx3, ss_b, b_ssb)
    if upto >= 5:
        phase_ffn(1, x3_d, bx3, out_d, bout, ss_b, b_ssb, ss_a, b_ssa, 10, 9, 11)

    k.barrier(scr)
    fin_d = nc.dram_tensor("fin_d", [128, 1], F32, kind="Internal").ap()
    k.dma("sp", fin_d, scr, [], [Buf("fin")])
    P.finalize()
    sems = {e: [st.enter_context(nc.semaphore(f"s_{e}{i}")) for i in range(P.nepoch[e])] for e in CENG}
    dsems = [st.enter_context(nc.semaphore(f"d{i}")) for i in range(P.NDMA_SEM)]
    block = st.enter_context(nc.Block())
    P.emit(block, sems, dsems)
    st.close()
    return nc


_NC_CACHE = {}


def _in_maps(inputs):
    f = lambda a: np.ascontiguousarray(np.asarray(a, dtype=np.float32))
    x = f(inputs["x"])
    c = f(inputs["c"])
    shared = dict(
        ada_w=f(inputs["ada_w"]), ada_b=f(inputs["ada_b"]),
        a_w_in=f(inputs["a_w_in"]).reshape(D, 4096),
        a_lb_logits=f(inputs["a_lb_logits"]).reshape(16, 128),
        a_norm_g=f(inputs["a_norm_g"]).reshape(1, 128),
        a_w_out=f(inputs["a_w_out"]).reshape(D, D),
        kv_ada_w=f(inputs["kv_ada_w"]), kv_ada_b=f(inputs["kv_ada_b"]).reshape(1, 2 * D),
        kv_w=f(inputs["kv_w"]), kv_b_f=f(inputs["kv_b_f"]).reshape(1, 8),
        k_norm_g=f(inputs["k_norm_g"]).reshape(1, 128),
        b_w_q=f(inputs["b_w_q"]).reshape(D, 2 * D),
        q_norm_g=f(inputs["q_norm_g"]).reshape(1, 128),
        b_w_out=f(inputs["b_w_out"]).reshape(D, D),
        ffn_w_up=f(inputs["ffn_w_up"]),
        ffn_conv_w=f(inputs["ffn_conv_w"]).reshape(2, 132, 128),
        ffn_conv_b=f(inputs["ffn_conv_b"]).reshape(2, 44, 128),
        ffn_w_down=f(inputs["ffn_w_down"]),
    )
    maps = []
    for b in range(8):
        m = dict(shared)
        m["x"] = x[b]
        m["c"] = c[b].reshape(8, 128)
        maps.append(m)
    return maps


def kernel(**inputs):
    if "nc" not in _NC_CACHE:
        _NC_CACHE["nc"] = build_nc()
    nc = _NC_CACHE["nc"]
    res = run_bass_kernel_spmd(nc, _in_maps(inputs), core_ids=list(range(8)))
    return np.stack([np.asarray(r["out"], dtype=np.float32) for r in res.results], axis=0)
```
